# Optimizing a Trainium2 kernel written in Bass

```python
import math
import jax, jax.numpy as jnp
from jax import lax
import numpy as np

D_MODEL = 1024
BATCH = 8
SEQ = 4096
DEPTH = 1

CHUNK = 64
N_MEM = 256
ATT_HEADS = 8
ATT_KV_HEADS = 2
ATT_HEAD_DIM = 64
IDX_HEADS = 4
IDX_DIM = 64
TOPK_MAX = 256
Q_BLOCK = 128
ROPE_THETA = 10000.0
S5_WIDTH = 512
S5_GROUP = 16
S5_GROUPS = S5_WIDTH // S5_GROUP
S5_STATE = 64
MEM_HEADS = 4
MEM_HEAD_DIM = 128
FFN_DIM = 2816
CONV_WIDTH = 3
N_BRANCH = 3
ALPHA = (2.0 * DEPTH) ** 0.25
BETA = (8.0 * DEPTH) ** -0.25
LN_EPS = 1e-5
NEG = -1e30

IN_WIDTHS = (ATT_HEADS * ATT_HEAD_DIM, ATT_KV_HEADS * ATT_HEAD_DIM, ATT_KV_HEADS * ATT_HEAD_DIM,
             IDX_HEADS * IDX_DIM, IDX_DIM, IDX_HEADS, S5_WIDTH, MEM_HEADS * MEM_HEAD_DIM)
IN_WIDTH = sum(IN_WIDTHS)

kernel_name = "hybrid_dsa_s5_memattn_convffn_deepnorm"


def layer_norm(x, g, b):
    xf = x.astype(jnp.float32)
    mu = jnp.mean(xf, axis=-1, keepdims=True)
    var = jnp.mean(jnp.square(xf - mu), axis=-1, keepdims=True)
    return ((xf - mu) * lax.rsqrt(var + LN_EPS) * g.astype(jnp.float32) + b.astype(jnp.float32)).astype(x.dtype)


def rope(x, positions):
    half = x.shape[-1] // 2
    inv_freq = ROPE_THETA ** (-jnp.arange(half, dtype=jnp.float32) / half)
    ang = positions.astype(jnp.float32)[..., None] * inv_freq
    cos = jnp.cos(ang)[:, :, None, :]
    sin = jnp.sin(ang)[:, :, None, :]
    xf = x.astype(jnp.float32)
    x1, x2 = xf[..., :half], xf[..., half:]
    return jnp.concatenate([x1 * cos - x2 * sin, x2 * cos + x1 * sin], axis=-1).astype(x.dtype)


def dsa_attention(q, k, v, q_idx, k_idx, w_idx):
    B, S = q.shape[0], q.shape[1]
    topk = min(TOPK_MAX, S // 4)
    n_blocks = S // Q_BLOCK
    rep = ATT_HEADS // ATT_KV_HEADS
    scale = ATT_HEAD_DIM ** -0.5

    def to_blocks(a):
        return a.reshape(B, n_blocks, Q_BLOCK, *a.shape[2:]).swapaxes(0, 1)

    q_b = to_blocks(q.reshape(B, S, ATT_KV_HEADS, rep, ATT_HEAD_DIM))
    qi_b = to_blocks(q_idx)
    w_b = to_blocks(w_idx)
    key_chunk = jnp.arange(S) // CHUNK
    b_ix = jnp.arange(B)[:, None, None]
    k_idx_f = k_idx.astype(jnp.float32)

    def one_block(args):
        blk, q_blk, qi_blk, w_blk = args
        t = blk * Q_BLOCK + jnp.arange(Q_BLOCK)
        q_chunk = t // CHUNK
        logits = jnp.einsum('bqhd,bsd->bqhs', qi_blk.astype(jnp.float32), k_idx_f) * (IDX_DIM ** -0.5)
        score = jnp.einsum('bqh,bqhs->bqs', w_blk.astype(jnp.float32) * (IDX_HEADS ** -0.5), jax.nn.relu(logits))
        admissible = key_chunk[None, :] <= q_chunk[:, None]
        score = jnp.where(admissible[None], score, NEG)
        _, idx = lax.top_k(score, topk)
        valid = key_chunk[idx] <= q_chunk[None, :, None]
        k_sel = k[b_ix, idx]
        v_sel = v[b_ix, idx]
        s = jnp.einsum('bqgrd,bqkgd->bqgrk', q_blk, k_sel).astype(jnp.float32) * scale
        s = jnp.where(valid[:, :, None, None, :], s, NEG)
        p = jax.nn.softmax(s, axis=-1).astype(v.dtype)
        return jnp.einsum('bqgrk,bqkgd->bqgrd', p, v_sel)

    out = lax.map(one_block, (jnp.arange(n_blocks), q_b, qi_b, w_b))
    return out.swapaxes(0, 1).reshape(B, S, ATT_HEADS * ATT_HEAD_DIM)


def s5_branch(u, lam_re, lam_im, log_dt, b_re, b_im, c_re, c_im, d_skip, w_glu, b_glu):
    B, S, _ = u.shape
    uf = u.astype(jnp.float32).reshape(B, S, S5_GROUPS, S5_GROUP)
    lam = lax.complex(jnp.minimum(lam_re.astype(jnp.float32), -1e-4), lam_im.astype(jnp.float32))
    dt = jnp.exp(log_dt.astype(jnp.float32))[:, None]
    lam_bar = jnp.exp(lam * dt)
    b_mat = lax.complex(b_re.astype(jnp.float32), b_im.astype(jnp.float32))
    c_mat = lax.complex(c_re.astype(jnp.float32), c_im.astype(jnp.float32))
    b_bar = ((lam_bar - 1.0) / lam)[..., None] * b_mat
    bu = jnp.einsum('gph,bsgh->bsgp', b_bar, uf.astype(jnp.complex64))
    a = jnp.broadcast_to(lam_bar, (1, S, S5_GROUPS, S5_STATE))

    def combine(e1, e2):
        a1, x1 = e1
        a2, x2 = e2
        return a2 * a1, a2 * x1 + x2

    _, h = lax.associative_scan(combine, (a, bu), axis=1)
    y = jnp.einsum('ghp,bsgp->bsgh', c_mat, h).real + d_skip.astype(jnp.float32).reshape(S5_GROUPS, S5_GROUP) * uf
    y = jax.nn.gelu(y.reshape(B, S, S5_WIDTH))
    y = y * jax.nn.sigmoid(y @ w_glu.astype(jnp.float32) + b_glu.astype(jnp.float32))
    return y.astype(u.dtype)


def memory_attention(q_mem, mem, w_mem_kv):
    B, S, _ = q_mem.shape
    kv = (mem @ w_mem_kv).reshape(B, mem.shape[1], 2, MEM_HEADS, MEM_HEAD_DIM)
    k_m, v_m = kv[:, :, 0], kv[:, :, 1]
    q = q_mem.reshape(B, S, MEM_HEADS, MEM_HEAD_DIM)
    s = jnp.einsum('bqhd,bmhd->bhqm', q, k_m).astype(jnp.float32) * (MEM_HEAD_DIM ** -0.5)
    p = jax.nn.softmax(s, axis=-1).astype(v_m.dtype)
    return jnp.einsum('bhqm,bmhd->bqhd', p, v_m).reshape(B, S, MEM_HEADS * MEM_HEAD_DIM)


def hybrid_mixer(x, mem, positions, w_in, w_gate, b_gate, lam_re, lam_im, log_dt, b_re, b_im,
                 c_re, c_im, d_skip, w_glu, b_glu, w_mem_kv, w_proj_a, w_proj_b, w_proj_c, w_out):
    B, S, D = x.shape
    offsets = np.cumsum(IN_WIDTHS)[:-1].tolist()
    q_att, k_att, v_att, q_idx, k_idx, w_idx, u_s5, q_mem = jnp.split(x @ w_in, offsets, axis=-1)
    q_att = rope(q_att.reshape(B, S, ATT_HEADS, ATT_HEAD_DIM), positions)
    k_att = rope(k_att.reshape(B, S, ATT_KV_HEADS, ATT_HEAD_DIM), positions)
    v_att = v_att.reshape(B, S, ATT_KV_HEADS, ATT_HEAD_DIM)
    q_idx = rope(q_idx.reshape(B, S, IDX_HEADS, IDX_DIM), positions)
    k_idx = rope(k_idx[:, :, None, :], positions)[:, :, 0]
    y_a = dsa_attention(q_att, k_att, v_att, q_idx, k_idx, w_idx) @ w_proj_a
    y_b = s5_branch(u_s5, lam_re, lam_im, log_dt, b_re, b_im, c_re, c_im, d_skip, w_glu, b_glu) @ w_proj_b
    y_c = memory_attention(q_mem, mem, w_mem_kv) @ w_proj_c
    gates = jax.nn.sigmoid((x @ w_gate + b_gate).astype(jnp.float32)).reshape(B, S, N_BRANCH, D).astype(x.dtype)
    merged = gates[:, :, 0] * y_a + gates[:, :, 1] * y_b + gates[:, :, 2] * y_c
    return merged @ w_out


def conv_ffn(x, w_up, conv_w, conv_b, w_down):
    S = x.shape[1]
    h = x @ w_up
    hp = jnp.pad(h, ((0, 0), (CONV_WIDTH - 1, 0), (0, 0)))
    h = sum(conv_w[j] * hp[:, j:j + S] for j in range(CONV_WIDTH)) + conv_b
    g, up = jnp.split(h, 2, axis=-1)
    return (jax.nn.silu(g) * up) @ w_down


def setup_inputs(seed: int = 0) -> dict:
    key = jax.random.key(seed)
    ks = jax.random.split(key, 32)
    f32 = jnp.float32
    D, L = D_MODEL, DEPTH

    def nrm(k, shape, scale):
        return jax.random.normal(k, shape, f32) * scale

    x = nrm(ks[0], (BATCH, SEQ, D), 1.0)
    mem = nrm(ks[1], (BATCH, N_MEM, D), 1.0)
    start = jax.random.randint(ks[2], (BATCH, 1), 0, 16) * CHUNK
    positions = (start + jnp.arange(SEQ, dtype=jnp.int32)[None, :]).astype(jnp.int32)
    n_idx = jnp.arange(S5_STATE, dtype=f32)
    return {
        "x": x,
        "mem": mem,
        "positions": positions,
        "w_in": nrm(ks[3], (L, D, IN_WIDTH), D ** -0.5),
        "w_gate": nrm(ks[4], (L, D, N_BRANCH * D), D ** -0.5),
        "b_gate": nrm(ks[5], (L, N_BRANCH * D), 0.02),
        "s5_lam_re": -0.5 + nrm(ks[6], (L, S5_GROUPS, S5_STATE), 0.01),
        "s5_lam_im": math.pi * n_idx + nrm(ks[7], (L, S5_GROUPS, S5_STATE), 0.01),
        "s5_log_dt": jax.random.uniform(ks[8], (L, S5_GROUPS), f32, math.log(1e-3), math.log(1e-1)),
        "s5_b_re": nrm(ks[9], (L, S5_GROUPS, S5_STATE, S5_GROUP), (0.5 / S5_GROUP) ** 0.5),
        "s5_b_im": nrm(ks[10], (L, S5_GROUPS, S5_STATE, S5_GROUP), (0.5 / S5_GROUP) ** 0.5),
        "s5_c_re": nrm(ks[11], (L, S5_GROUPS, S5_GROUP, S5_STATE), (0.5 / S5_STATE) ** 0.5),
        "s5_c_im": nrm(ks[12], (L, S5_GROUPS, S5_GROUP, S5_STATE), (0.5 / S5_STATE) ** 0.5),
        "s5_d": nrm(ks[13], (L, S5_WIDTH), 1.0),
        "w_glu": nrm(ks[14], (L, S5_WIDTH, S5_WIDTH), S5_WIDTH ** -0.5),
        "b_glu": nrm(ks[15], (L, S5_WIDTH), 0.02),
        "w_mem_kv": nrm(ks[16], (L, D, 2 * MEM_HEADS * MEM_HEAD_DIM), D ** -0.5),
        "w_proj_a": nrm(ks[17], (L, ATT_HEADS * ATT_HEAD_DIM, D), (ATT_HEADS * ATT_HEAD_DIM) ** -0.5),
        "w_proj_b": nrm(ks[18], (L, S5_WIDTH, D), S5_WIDTH ** -0.5),
        "w_proj_c": nrm(ks[19], (L, MEM_HEADS * MEM_HEAD_DIM, D), (MEM_HEADS * MEM_HEAD_DIM) ** -0.5),
        "w_out": nrm(ks[20], (L, D, D), BETA * D ** -0.5),
        "ln1_g": 1.0 + nrm(ks[21], (L, D), 0.02),
        "ln1_b": nrm(ks[22], (L, D), 0.02),
        "w_up": nrm(ks[23], (L, D, 2 * FFN_DIM), D ** -0.5),
        "conv_w": nrm(ks[24], (L, CONV_WIDTH, 2 * FFN_DIM), CONV_WIDTH ** -0.5),
        "conv_b": nrm(ks[25], (L, 2 * FFN_DIM), 0.02),
        "w_down": nrm(ks[26], (L, FFN_DIM, D), BETA * FFN_DIM ** -0.5),
        "ln2_g": 1.0 + nrm(ks[27], (L, D), 0.02),
        "ln2_b": nrm(ks[28], (L, D), 0.02),
    }


def reference(x, mem, positions, w_in, w_gate, b_gate, s5_lam_re, s5_lam_im, s5_log_dt, s5_b_re, s5_b_im,
              s5_c_re, s5_c_im, s5_d, w_glu, b_glu, w_mem_kv, w_proj_a, w_proj_b, w_proj_c, w_out,
              ln1_g, ln1_b, w_up, conv_w, conv_b, w_down, ln2_g, ln2_b):
    h = x
    for l in range(DEPTH):
        mix = hybrid_mixer(h, mem, positions, w_in[l], w_gate[l], b_gate[l], s5_lam_re[l], s5_lam_im[l],
                           s5_log_dt[l], s5_b_re[l], s5_b_im[l], s5_c_re[l], s5_c_im[l], s5_d[l],
                           w_glu[l], b_glu[l], w_mem_kv[l], w_proj_a[l], w_proj_b[l], w_proj_c[l], w_out[l])
        h = layer_norm(ALPHA * h + mix, ln1_g[l], ln1_b[l])
        f = conv_ffn(h, w_up[l], conv_w[l], conv_b[l], w_down[l])
        h = layer_norm(ALPHA * h + f, ln2_g[l], ln2_b[l])
    return h
```

```python
from contextlib import ExitStack
import numpy as np
import concourse.bass as bass
import concourse.mybir as mybir
from concourse.bass_utils import run_bass_kernel_spmd

F32 = mybir.dt.float32
BF16 = mybir.dt.bfloat16
I32 = mybir.dt.int32
AF = mybir.ActivationFunctionType
ALU = mybir.AluOpType

S = 4096
D = 1024
TT = 512
NTT = S // TT
ALPHA = 2.0 ** 0.25
LN_EPS = 1e-5
TWO_PI = 6.283185307179586
SIN_SC = 6.2831840
SIN_BI = -3.1415920
EPOCH = 30000
NEG_FILL = -3.0e38


class Prog:
    ENGS = ("pe", "act", "dve", "pool", "sp")

    def __init__(self, nc, es, ndma=8):
        self.nc = nc
        self.es = es
        self.ops = {e: [] for e in self.ENGS}
        self.count = {e: 0 for e in self.ENGS}
        self.sems = {}
        self.ndma = ndma
        self.dsems = {}
        self.dcount = {e: 0 for e in self.ENGS}
        self.dtarget = {}
        self.waited = {e: {} for e in self.ENGS}
        self.lastw = {}
        self.readers = {}

    def _sem(self, eng, epoch):
        k = (eng, epoch)
        if k not in self.sems:
            self.sems[k] = self.es.enter_context(self.nc.semaphore(f"s_{eng}_{epoch}"))
        return self.sems[k]

    def _dsem(self, eng, r):
        k = (eng, r)
        if k not in self.dsems:
            self.dsems[k] = self.es.enter_context(self.nc.semaphore(f"d_{eng}_{r}"))
            self.dtarget[k] = 0
        return self.dsems[k]

    def _need(self, eng, ev, waits):
        if ev is None:
            return
        sem, val, src = ev
        if src == "pe" and eng == "pe":
            return
        w = self.waited[eng]
        if w.get(id(sem), 0) >= val:
            return
        w[id(sem)] = val
        waits.append((sem, val))

    def op(self, eng, fn, r=(), w=(), dma=False):
        waits = []
        for k in r:
            self._need(eng, self.lastw.get(k), waits)
        for k in w:
            self._need(eng, self.lastw.get(k), waits)
            for ev in self.readers.get(k, {}).values():
                self._need(eng, ev, waits)
        if dma:
            i = self.dcount[eng]
            self.dcount[eng] += 1
            rr = i % self.ndma
            sem = self._dsem(eng, rr)
            prev = self.dtarget[(eng, rr)]
            if prev > 0:
                self._need(eng, (sem, prev, "dma"), waits)
            self.dtarget[(eng, rr)] = prev + 16
            ev = (sem, prev + 16, "dma")
            inc = 16
        else:
            self.count[eng] += 1
            ep, v = divmod(self.count[eng] - 1, EPOCH)
            sem = self._sem(eng, ep)
            ev = (sem, v + 1, eng)
            inc = 1
        best = {}
        for s_, v_ in waits:
            if id(s_) not in best or best[id(s_)][1] < v_:
                best[id(s_)] = (s_, v_)
        self.ops[eng].append((list(best.values()), fn, sem, inc))
        for k in r:
            self.readers.setdefault(k, {})[(eng, id(sem))] = ev
        for k in w:
            self.lastw[k] = ev
            self.readers[k] = {}
        return ev

    def wait_all(self, eng, keys):
        waits = []
        for k in keys:
            self._need(eng, self.lastw.get(k), waits)
        self.ops[eng].append((waits, None, None, 0))

    def emit(self):
        nc = self.nc
        ops = self.ops
        self.ops = {e: [] for e in self.ENGS}

        def run(engname):
            def _f(eng):
                for wl, fn, sem, inc in ops[engname]:
                    for s_, v_ in wl:
                        eng.wait_ge(s_, v_)
                    if fn is not None:
                        fn(eng).then_inc(sem, inc)
            return _f

        with nc.Block() as block:
            block.tensor(run("pe"))
            block.scalar(run("act"))
            block.vector(run("dve"))
            block.gpsimd(run("pool"))
            block.sync(run("sp"))


class K:
    def __init__(self, nc, P):
        self.nc = nc
        self.P = P
        self.ring = []
        self.ri = 0

    def sb(self, es, name, shape, dt):
        return es.enter_context(self.nc.sbuf_tensor("sb_" + name, shape, dt))

    def ps(self):
        t = self.ring[self.ri % len(self.ring)]
        self.ri += 1
        return t

    def mm(self, out, lhsT, rhs, start=True, stop=True, r=(), w=()):
        self.P.op("pe", lambda e: e.matmul(out, lhsT=lhsT, rhs=rhs, start=start, stop=stop), r=r, w=w)

    def tr(self, out, in_, ident, r=(), w=()):
        self.P.op("pe", lambda e: e.transpose(out, in_, ident), r=r, w=w)

    def act(self, out, in_, func, r=(), w=(), **kw):
        self.P.op("act", lambda e: e.activation(out=out, in_=in_, func=func, **kw), r=r, w=w)

    def tt(self, eng, out, in0, in1, op, r=(), w=()):
        self.P.op(eng, lambda e: e.tensor_tensor(out=out, in0=in0, in1=in1, op=op), r=r, w=w)

    def ts(self, eng, out, in0, s1, op0, s2=None, op1=None, r=(), w=()):
        if op1 is None:
            self.P.op(eng, lambda e: e.tensor_scalar(out=out, in0=in0, scalar1=s1, scalar2=None, op0=op0), r=r, w=w)
        else:
            self.P.op(eng, lambda e: e.tensor_scalar(out=out, in0=in0, scalar1=s1, scalar2=s2, op0=op0, op1=op1), r=r, w=w)

    def stt(self, eng, out, in0, scalar, in1, op0, op1, r=(), w=()):
        self.P.op(eng, lambda e: e.scalar_tensor_tensor(out=out, in0=in0, scalar=scalar, in1=in1, op0=op0, op1=op1), r=r, w=w)

    def cp(self, eng, out, in_, r=(), w=()):
        if eng == "act":
            self.act(out, in_, AF.Copy, r=r, w=w)
        else:
            self.P.op(eng, lambda e: e.tensor_copy(out=out, in_=in_), r=r, w=w)

    def recip(self, out, in_, r=(), w=()):
        self.P.op("dve", lambda e: e.reciprocal(out=out, in_=in_), r=r, w=w)

    def memset(self, eng, ap, val, w=()):
        self.P.op(eng, lambda e: e.memset(ap, val), w=w)

    def dma(self, eng, out, in_, r=(), w=()):
        self.P.op(eng, lambda e: e.dma_start(out=out, in_=in_), r=r, w=w, dma=True)


def kt_view(dram_ap, cols):
    return dram_ap.rearrange("(kt p) s -> p kt s", p=128)[:, :, cols]


def sin_reduced(k, out, in_, mul, quarter, shape, tmp, negpi, rk, wk):
    y, ki, kf = tmp
    off = 8.5 + (0.25 if quarter else 0.0)
    if isinstance(mul, float):
        k.ts("dve", y, in_, mul / TWO_PI, ALU.mult, off, ALU.add, r=rk, w=["sr_y"])
    else:
        k.ts("dve", y, in_, mul, ALU.mult, off, ALU.add, r=rk, w=["sr_y"])
    k.cp("dve", ki, y, r=["sr_y"], w=["sr_k"])
    k.cp("dve", kf, ki, r=["sr_k"], w=["sr_kf"])
    k.tt("dve", y, y, kf, ALU.subtract, r=["sr_y", "sr_kf"], w=["sr_y"])
    k.ts("dve", kf, y, 0.0, ALU.is_lt, r=["sr_y"], w=["sr_kf"])
    k.tt("dve", y, y, kf, ALU.add, r=["sr_y", "sr_kf"], w=["sr_y"])
    k.act(out, y, AF.Sin, r=["sr_y", "negpi"], w=wk, bias=negpi, scale=SIN_SC)


def layer_norm(k, es_tmp, z, zkey, gvec, bvec, out32, o32key, out16, o16key):
    zb, zsq, onesD, meanS, m2, var, rstd, tmps, epsb = es_tmp
    for kt in range(8):
        k.act(zb[:, kt, :], z[:, kt, :], AF.Copy, r=[zkey], w=[f"zb{kt}"])
        k.act(zsq[:, kt, :], z[:, kt, :], AF.Square, r=[zkey], w=[f"zsq{kt}"])
    psM, kM = k.ps()
    for kt in range(8):
        k.mm(psM[:, :], onesD[:, :], zb[:, kt, :], start=(kt == 0), stop=(kt == 7), r=[f"zb{kt}", "onesD"], w=[kM])
    psQ, kQ = k.ps()
    for kt in range(8):
        k.mm(psQ[:, :], onesD[:, :], zsq[:, kt, :], start=(kt == 0), stop=(kt == 7), r=[f"zsq{kt}", "onesD"], w=[kQ])
    k.act(meanS[:, :], psM[:, :], AF.Copy, r=[kM], w=["meanS"])
    k.tt("dve", m2[:, :], meanS[:, :], meanS[:, :], ALU.mult, r=["meanS"], w=["m2"])
    k.tt("dve", var[:, :], psQ[:, :], m2[:, :], ALU.subtract, r=[kQ, "m2"], w=["var"])
    k.act(var[:, :], var[:, :], AF.Sqrt, r=["var", "epsb"], w=["var"], bias=epsb[:, 0:1])
    k.recip(rstd[:, :], var[:, :], r=["var"], w=["rstd"])
    for kt in range(8):
        t = tmps[kt % 2]
        tk = f"lnt{kt % 2}"
        k.tt("dve", t[:, :], z[:, kt, :], meanS[:, :], ALU.subtract, r=[zkey, "meanS"], w=[tk])
        k.tt("dve", t[:, :], t[:, :], rstd[:, :], ALU.mult, r=[tk, "rstd"], w=[tk])
        k.act(out32[:, kt, :], t[:, :], AF.Identity, r=[tk, "lnvec"], w=[o32key],
              scale=gvec[:, kt:kt + 1], bias=bvec[:, kt:kt + 1])
        if out16 is not None:
            k.cp("pool", out16(kt), out32[:, kt, :], r=[o32key], w=[o16key])


def phase_mix(k, T):
    nc, P = k.nc, k.P
    with ExitStack() as es:
        sb = lambda n, s, d: k.sb(es, n, s, d)
        xf = sb("xf", [128, 8, TT], F32)
        xb = sb("xb4", [128, 8, TT], BF16)
        brt = [sb(f"brt{i}", [128, 4, TT], BF16) for i in range(3)]
        gs = [sb(f"gs{i}", [128, TT], F32) for i in range(3)]
        m1 = sb("m1", [128, TT], F32)
        tA = sb("tA", [128, TT], F32)
        tB = sb("tB", [128, TT], F32)
        merged = sb("merged", [128, 8, TT], BF16)
        zb = sb("zb", [128, 8, TT], BF16)
        zsq = sb("zsq", [128, 8, TT], BF16)
        onesD = sb("onesD", [128, 128], BF16)
        meanS = sb("meanS", [128, TT], F32)
        m2 = sb("m2", [128, TT], F32)
        var = sb("var", [128, TT], F32)
        rstd = sb("rstd", [128, TT], F32)
        lnt = [sb(f"lnt{i}", [128, TT], F32) for i in range(2)]
        epsb = sb("epsb", [128, 1], F32)
        h1 = sb("h1", [128, 8, TT], F32)
        h1b = sb("h1b", [128, 8, TT + 2], BF16)
        actb = sb("actb", [128, 22, TT], BF16)
        c0 = [sb(f"c0_{i}", [128, 256], F32) for i in range(2)]
        c1 = [sb(f"c1_{i}", [128, 256], F32) for i in range(2)]
        sg = sb("sg", [128, 256], F32)
        ot = sb("ot", [128, 8, TT], F32)
        NSLOT = 3
        wring = [sb(f"wr{i}", [128, 4608], BF16) for i in range(NSLOT)]
        bgate = sb("bgate", [128, 24], F32)
        lnv = sb("lnv", [128, 32], F32)
        cw = sb("cw", [128, 44, 3], F32)
        cb = sb("cb", [128, 44], F32)
        ln_tmp = (zb, zsq, onesD, meanS, m2, var, rstd, lnt, epsb)

        k.dma("sp", bgate[:, :], T["b_gate_l"], w=["bgate"])
        k.dma("sp", lnv[:, :], T["ln_l"], w=["lnvec"])
        k.dma("sp", cw[:, :, :], T["conv_w_l"], w=["cw"])
        k.dma("sp", cb[:, :], T["conv_b_l"], w=["cw"])
        k.memset("pool", onesD[:, :], 1.0 / 1024.0, w=["onesD"])
        k.memset("pool", epsb[:, :], LN_EPS, w=["epsb"])
        k.memset("pool", h1b[:, :, 0:2], 0.0, w=["h1b"])

        chunks = []
        for tt_ in range(NTT):
            for n in range(8):
                chunks.append(("A", n))
            for n in range(8):
                chunks.append(("O", n))
            for i in range(22):
                chunks.append(("U", i))
            for n in range(8):
                chunks.append(("Dn", n))
        csize = {"A": 4608, "O": 1024, "U": 2048, "Dn": 2816}
        csrc = {"A": T["wA_bf"], "O": T["wO_bf"], "U": T["wU_bf"], "Dn": T["wD_bf"]}
        state = {"next": 0}

        def load_chunk():
            i = state["next"]
            if i >= len(chunks):
                return
            kind, idx = chunks[i]
            slot = i % NSLOT
            k.dma("sp", wring[slot][:, 0:csize[kind]], csrc[kind][idx], r=[f"wbf_{kind}{idx}"], w=[f"wr{slot}"])
            state["next"] += 1

        ci = {"i": 0}

        def get_chunk():
            i = ci["i"]
            ci["i"] += 1
            return wring[i % NSLOT], f"wr{i % NSLOT}"

        for _ in range(NSLOT):
            load_chunk()

        def load_inputs(t_):
            tk_ = slice(t_ * TT, (t_ + 1) * TT)
            k.dma("sp", xf[:, :, :], kt_view(T["xT"], tk_), w=["xf"])
            k.dma("pool", xb[:, :, :], kt_view(T["xT"], tk_), w=["xb4"])
            for i in range(3):
                k.dma("sp", brt[i][:, :, :], kt_view(T["br"][i], tk_), r=[f"br{i}_{t_}_{q}" for q in range(4)], w=[f"brt{i}"])

        load_inputs(0)
        for tt_ in range(NTT):
            tok = slice(tt_ * TT, (tt_ + 1) * TT)
            for n in range(8):
                wt, wk = get_chunk()
                psy = []
                for i in range(3):
                    py, ky = k.ps()
                    for kt in range(4):
                        c = (4 * i + kt) * 128
                        k.mm(py[:, :], wt[:, c:c + 128], brt[i][:, kt, :], start=(kt == 0), stop=(kt == 3),
                             r=[wk, f"brt{i}"], w=[ky])
                    pg, kg = k.ps()
                    for kt in range(8):
                        c = (12 + 8 * i + kt) * 128
                        k.mm(pg[:, :], wt[:, c:c + 128], xb[:, kt, :], start=(kt == 0), stop=(kt == 7),
                             r=[wk, "xb4"], w=[kg])
                    k.act(gs[i][:, :], pg[:, :], AF.Sigmoid, r=[kg, "bgate"], w=[f"gs{i}"],
                          bias=bgate[:, i * 8 + n:i * 8 + n + 1])
                    psy.append((py, ky))
                load_chunk()
                k.tt("dve", m1[:, :], psy[0][0][:, :], gs[0][:, :], ALU.mult, r=[psy[0][1], "gs0"], w=["m1"])
                k.tt("dve", tA[:, :], psy[1][0][:, :], gs[1][:, :], ALU.mult, r=[psy[1][1], "gs1"], w=["tA"])
                k.tt("dve", tB[:, :], psy[2][0][:, :], gs[2][:, :], ALU.mult, r=[psy[2][1], "gs2"], w=["tB"])
                k.tt("pool", m1[:, :], m1[:, :], tA[:, :], ALU.add, r=["m1", "tA"], w=["m1"])
                k.tt("pool", merged[:, n, :], m1[:, :], tB[:, :], ALU.add, r=["m1", "tB"], w=[f"merged{n}"])
            for n in range(8):
                wt, wk = get_chunk()
                px, kx = k.ps()
                for kt in range(8):
                    k.mm(px[:, :], wt[:, kt * 128:(kt + 1) * 128], merged[:, kt, :], start=(kt == 0), stop=(kt == 7),
                         r=[wk, f"merged{kt}"], w=[kx])
                load_chunk()
                k.stt("dve", xf[:, n, :], xf[:, n, :], ALPHA, px[:, :], ALU.mult, ALU.add, r=["xf", kx], w=["xf"])
            if tt_ > 0:
                k.cp("pool", h1b[:, :, 0:2], h1b[:, :, TT:TT + 2], r=["h1b"], w=["h1b"])
            layer_norm(k, ln_tmp, xf, "xf", lnv[:, 0:8], lnv[:, 8:16], h1, "h1",
                       lambda kt: h1b[:, kt, 2:TT + 2], "h1b")
            if tt_ + 1 < NTT:
                load_inputs(tt_ + 1)
            for i in range(22):
                wt, wk = get_chunk()
                for hf in range(2):
                    res = []
                    for part in range(2):
                        pp, kp = k.ps()
                        for kt in range(8):
                            c = (part * 8 + kt) * 128
                            k.mm(pp[:, 0:258], wt[:, c:c + 128], h1b[:, kt, hf * 256:hf * 256 + 258],
                                 start=(kt == 0), stop=(kt == 7), r=[wk, "h1b"], w=[kp])
                        ch = i + 22 * part
                        a0, a1 = c0[part], c1[part]
                        k.act(a0[:, :], pp[:, 2:258], AF.Identity, r=[kp, "cw"], w=[f"c0_{part}"],
                              scale=cw[:, ch, 2:3], bias=cb[:, ch:ch + 1])
                        k.stt("dve", a1[:, :], pp[:, 1:257], cw[:, ch, 1:2], a0[:, :], ALU.mult, ALU.add,
                              r=[kp, "cw", f"c0_{part}"], w=[f"c1_{part}"])
                        k.stt("dve", a0[:, :], pp[:, 0:256], cw[:, ch, 0:1], a1[:, :], ALU.mult, ALU.add,
                              r=[kp, "cw", f"c1_{part}"], w=[f"c0_{part}"])
                        res.append(a0)
                    k.act(sg[:, :], res[0][:, :], AF.Silu, r=["c0_0"], w=["sg"])
                    k.tt("pool", actb[:, i, hf * 256:(hf + 1) * 256], sg[:, :], res[1][:, :], ALU.mult,
                         r=["sg", "c0_1"], w=[f"actb{i}"])
                load_chunk()
            for n in range(8):
                wt, wk = get_chunk()
                pd, kd = k.ps()
                for kt in range(22):
                    k.mm(pd[:, :], wt[:, kt * 128:(kt + 1) * 128], actb[:, kt, :], start=(kt == 0), stop=(kt == 21),
                         r=[wk, f"actb{kt}"], w=[kd])
                load_chunk()
                k.stt("dve", h1[:, n, :], h1[:, n, :], ALPHA, pd[:, :], ALU.mult, ALU.add, r=["h1", kd], w=["h1"])
            layer_norm(k, ln_tmp, h1, "h1", lnv[:, 16:24], lnv[:, 24:32], ot, "ot", None, None)
            k.dma("act", kt_view(T["outT"], tok), ot[:, :, :], r=["ot"], w=[f"outT{tt_}"])
        P.wait_all("sp", [f"outT{t_}" for t_ in range(NTT)])
        P.emit()


def phase_mem(k, T):
    nc, P = k.nc, k.P
    with ExitStack() as es:
        sb = lambda n, s, d: k.sb(es, n, s, d)
        memb = sb("memb", [128, 8, 256], BF16)
        wq = sb("wq_mem", [128, 4, 1024], BF16)
        wkk = sb("wk_mem", [128, 4, 1024], BF16)
        wv = sb("wv_mem", [128, 8, 512], BF16)
        kmT = sb("kmT", [128, 4, 256], BF16)
        vm = sb("vm", [128, 2, 512], BF16)
        ones = sb("ones_mem", [128, 128], BF16)
        xbs = [sb(f"xb2_{i}", [128, 8, TT], BF16) for i in range(2)]
        qm = [sb(f"qm{i}", [128, TT], BF16) for i in range(2)]
        pT = [sb(f"pT{i}", [128, 2, TT], BF16) for i in range(2)]
        rec = sb("rec_mem", [128, TT], F32)
        ob = [sb(f"ob{i}", [128, TT], BF16) for i in range(2)]

        k.dma("pool", memb[:, :, :], T["memT"].rearrange("(kt p) m -> p kt m", p=128), w=["memb"])
        k.dma("pool", wq[:, :, :], T["w_qmem_l"], w=["wq_mem"])
        k.dma("pool", wkk[:, :, :], T["w_kmem_l"], w=["wk_mem"])
        k.dma("pool", wv[:, :, :], T["w_vmem_l"], w=["wv_mem"])
        k.memset("dve", ones[:, :], 1.0, w=["ones_mem"])
        for h in range(4):
            pk, kk = k.ps()
            for kt in range(8):
                k.mm(pk[:, 0:256], wkk[:, h, kt * 128:(kt + 1) * 128], memb[:, kt, :], start=(kt == 0), stop=(kt == 7),
                     r=["wk_mem", "memb"], w=[kk])
            k.cp("act", kmT[:, h, :], pk[:, 0:256], r=[kk], w=["kmT"])
        for mt in range(2):
            pv, kv = k.ps()
            for kt in range(8):
                k.mm(pv[:, :], memb[:, kt, mt * 128:(mt + 1) * 128], wv[:, kt, :], start=(kt == 0), stop=(kt == 7),
                     r=["wv_mem", "memb"], w=[kv])
            k.cp("act", vm[:, mt, :], pv[:, :], r=[kv], w=["vm"])
        sc = 128.0 ** -0.5
        it = 0
        for tt_ in range(NTT):
            tok = slice(tt_ * TT, (tt_ + 1) * TT)
            xb = xbs[tt_ % 2]
            xk = f"xb2_{tt_ % 2}"
            k.dma("pool", xb[:, :, :], kt_view(T["xT"], tok), w=[xk])
            for h in range(4):
                q_, qk = qm[it % 2], f"qm{it % 2}"
                p_, pk_ = pT[it % 2], f"pT{it % 2}"
                o_, ok_ = ob[it % 2], f"ob{it % 2}"
                it += 1
                pq, kq = k.ps()
                for kt in range(8):
                    k.mm(pq[:, :], wq[:, h, kt * 128:(kt + 1) * 128], xb[:, kt, :], start=(kt == 0), stop=(kt == 7),
                         r=["wq_mem", xk], w=[kq])
                k.cp("act", q_[:, :], pq[:, :], r=[kq], w=[qk])
                for mt in range(2):
                    ps_, ks_ = k.ps()
                    k.mm(ps_[:, :], kmT[:, h, mt * 128:(mt + 1) * 128], q_[:, :], r=["kmT", qk], w=[ks_])
                    k.act(p_[:, mt, :], ps_[:, :], AF.Exp, r=[ks_], w=[pk_], scale=sc)
                po, ko = k.ps()
                for mt in range(2):
                    k.mm(po[:, :], vm[:, mt, h * 128:(h + 1) * 128], p_[:, mt, :], start=(mt == 0), stop=(mt == 1),
                         r=["vm", pk_], w=[ko])
                pd, kd = k.ps()
                for mt in range(2):
                    k.mm(pd[:, :], ones[:, :], p_[:, mt, :], start=(mt == 0), stop=(mt == 1),
                         r=["ones_mem", pk_], w=[kd])
                k.recip(rec[:, :], pd[:, :], r=[kd], w=["rec_mem"])
                k.tt("dve", o_[:, :], po[:, :], rec[:, :], ALU.mult, r=[ko, "rec_mem"], w=[ok_])
                k.dma("sp", T["br"][2][h * 128:(h + 1) * 128, tok], o_[:, :], r=[ok_], w=[f"br2_{tt_}_{h}"])
        P.emit()


def cmul(k, eng, outr, outi, ar, ai, br_, bi_, t1, t2, r, w):
    k.tt(eng, t1, ar, br_, ALU.mult, r=r, w=["cm_t1"])
    k.tt(eng, t2, ai, bi_, ALU.mult, r=r, w=["cm_t2"])
    k.tt(eng, outr, t1, t2, ALU.subtract, r=["cm_t1", "cm_t2"], w=w)
    k.tt(eng, t1, ar, bi_, ALU.mult, r=r + ["cm_t1"], w=["cm_t1"])
    k.tt(eng, t2, ai, br_, ALU.mult, r=r + ["cm_t2"], w=["cm_t2"])
    k.tt(eng, outi, t1, t2, ALU.add, r=["cm_t1", "cm_t2"], w=w)


def phase_s5(k, T):
    nc, P = k.nc, k.P
    with ExitStack() as es:
        sbp = lambda n, s, d: k.sb(es, n, s, d)
        es1 = ExitStack()
        sb = lambda n, s, d: k.sb(es1, n, s, d)
        I16 = sbp("I16", [128, 128], BF16)
        J16 = sbp("J16", [128, 128], BF16)
        Sel = sbp("Sel", [128, 64, 128], BF16)
        SelI = sbp("SelI", [128, 64, 128], BF16)
        wu = sbp("wu", [128, 4, 1024], BF16)
        wglu = sbp("wglu", [128, 4, 512], BF16)
        bglu = sbp("bglu", [128, 4], F32)
        LR = sbp("LR", [128, 32, 9], F32)
        LI = sbp("LI", [128, 32, 9], F32)
        Mb = sbp("Mb", [128, 32, 128], BF16)
        BmT = sbp("BmT", [128, 32, 128], BF16)
        Cmb = sbp("Cmb", [128, 32, 128], BF16)
        I32t = sb("I32t", [128, 128], F32)
        Mc = sb("Mc", [128, 128], F32)
        lre = sb("lre", [128, 32], F32)
        lim = sb("lim", [128, 32], F32)
        ldt = sb("ldt", [128, 32], F32)
        dP = sb("dP", [128, 32], F32)
        Br = sb("Br", [128, 32, 16], F32)
        Bi = sb("Bi", [128, 32, 16], F32)
        Cr = sb("Cr", [128, 32, 16], F32)
        Ci = sb("Ci", [128, 32, 16], F32)
        negpi = sb("negpi", [128, 1], F32)
        k.dma("sp", I32t[:, :], T["I128"], w=["I32t"])
        k.dma("pool", I16[:, :], T["I128"], w=["I16"])
        k.dma("pool", J16[:, :], T["J128"], w=["J16"])
        k.dma("sp", Mc[:, :], T["Mc"], w=["Mc"])
        k.dma("pool", Sel[:, :, :], T["Sel"], w=["Sel"])
        k.dma("pool", SelI[:, :, :], T["SelI"], w=["SelI"])
        for nm, t_ in (("lre", lre), ("lim", lim), ("ldt", ldt), ("dP", dP)):
            k.dma("sp", t_[:, :], T[nm], w=[nm])
        for nm, t_ in (("Br", Br), ("Bi", Bi), ("Cr", Cr), ("Ci", Ci)):
            k.dma("sp", t_[:, :, :], T[nm], w=[nm])
        k.dma("pool", wu[:, :, :], T["w_u_l"], w=["wu"])
        k.dma("pool", wglu[:, :, :], T["w_glu_l"], w=["wglu"])
        k.dma("sp", bglu[:, :], T["b_glu_l"], w=["bglu"])
        k.memset("pool", negpi[:, :], SIN_BI, w=["negpi"])

        a_ = sb("s5a", [128, 32], F32)
        th = sb("s5th", [128, 32], F32)
        dt_ = sb("s5dt", [128, 32], F32)
        y_ = sb("sr_y", [128, 32], F32)
        ki_ = sb("sr_k", [128, 32], I32)
        kf_ = sb("sr_kf", [128, 32], F32)
        sn = sb("s5sn", [128, 32], F32)
        cs = sb("s5cs", [128, 32], F32)
        mg = sb("s5mg", [128, 32], F32)
        t1 = sb("cm_t1", [128, 32], F32)
        t2 = sb("cm_t2", [128, 32], F32)
        PR = sb("PR", [128, 32, 9], F32)
        PI = sb("PI", [128, 32, 9], F32)
        NR = sb("NR", [128, 32, 8], F32)
        NI = sb("NI", [128, 32, 8], F32)
        bR = sb("betR", [128, 32], F32)
        bI = sb("betI", [128, 32], F32)
        inv = sb("s5inv", [128, 32], F32)

        k.ts("dve", lre[:, :], lre[:, :], -1e-4, ALU.min, r=["lre"], w=["lre"])
        k.act(dt_[:, :], ldt[:, :], AF.Exp, r=["ldt"], w=["s5dt"])
        k.tt("dve", a_[:, :], lre[:, :], dt_[:, :], ALU.mult, r=["lre", "s5dt"], w=["s5a"])
        k.tt("dve", th[:, :], lim[:, :], dt_[:, :], ALU.mult, r=["lim", "s5dt"], w=["s5th"])
        k.act(mg[:, :], a_[:, :], AF.Exp, r=["s5a"], w=["s5mg"])
        tmp3 = (y_[:, :], ki_[:, :], kf_[:, :])
        sin_reduced(k, sn[:, :], th[:, :], 1.0, False, None, tmp3, negpi[:, 0:1], ["s5th"], ["s5sn"])
        sin_reduced(k, cs[:, :], th[:, :], 1.0, True, None, tmp3, negpi[:, 0:1], ["s5th"], ["s5cs"])
        k.memset("dve", PR[:, :, 0], 1.0, w=["PRI"])
        k.memset("dve", PI[:, :, 0], 0.0, w=["PRI"])
        k.tt("dve", PR[:, :, 1], mg[:, :], cs[:, :], ALU.mult, r=["s5mg", "s5cs"], w=["PRI"])
        k.tt("dve", PI[:, :, 1], mg[:, :], sn[:, :], ALU.mult, r=["s5mg", "s5sn"], w=["PRI"])
        for n in range(2, 9):
            cmul(k, "dve", PR[:, :, n], PI[:, :, n], PR[:, :, n - 1], PI[:, :, n - 1], PR[:, :, 1], PI[:, :, 1],
                 t1[:, :], t2[:, :], ["PRI"], ["PRI"])
        k.cp("dve", LR[:, :, 0], PR[:, :, 8], r=["PRI"], w=["LRI"])
        k.cp("dve", LI[:, :, 0], PI[:, :, 8], r=["PRI"], w=["LRI"])
        for l in range(1, 9):
            cmul(k, "dve", LR[:, :, l], LI[:, :, l], LR[:, :, l - 1], LI[:, :, l - 1], LR[:, :, l - 1], LI[:, :, l - 1],
                 t1[:, :], t2[:, :], ["LRI"], ["LRI"])
        for n in range(8):
            k.act(inv[:, :], a_[:, :], AF.Exp, r=["s5a"], w=["s5inv"], scale=-2.0 * (n + 1))
            k.tt("dve", NR[:, :, n], PR[:, :, n + 1], inv[:, :], ALU.mult, r=["PRI", "s5inv"], w=["NRI"])
            k.stt("dve", NI[:, :, n], PI[:, :, n + 1], -1.0, inv[:, :], ALU.mult, ALU.mult, r=["PRI", "s5inv"], w=["NRI"])
        nr_ = sb("s5nr", [128, 32], F32)
        den = sb("s5den", [128, 32], F32)
        k.ts("dve", nr_[:, :], PR[:, :, 1], -1.0, ALU.add, r=["PRI"], w=["s5nr"])
        k.tt("dve", den[:, :], lre[:, :], lre[:, :], ALU.mult, r=["lre"], w=["s5den"])
        k.tt("dve", t1[:, :], lim[:, :], lim[:, :], ALU.mult, r=["lim"], w=["cm_t1"])
        k.tt("dve", den[:, :], den[:, :], t1[:, :], ALU.add, r=["s5den", "cm_t1"], w=["s5den"])
        k.recip(den[:, :], den[:, :], r=["s5den"], w=["s5den"])
        k.tt("dve", t1[:, :], nr_[:, :], lre[:, :], ALU.mult, r=["s5nr", "lre"], w=["cm_t1"])
        k.tt("dve", t2[:, :], PI[:, :, 1], lim[:, :], ALU.mult, r=["PRI", "lim"], w=["cm_t2"])
        k.tt("dve", t1[:, :], t1[:, :], t2[:, :], ALU.add, r=["cm_t1", "cm_t2"], w=["cm_t1"])
        k.tt("dve", bR[:, :], t1[:, :], den[:, :], ALU.mult, r=["cm_t1", "s5den"], w=["betR"])
        k.tt("dve", t1[:, :], PI[:, :, 1], lre[:, :], ALU.mult, r=["PRI", "lre"], w=["cm_t1"])
        k.tt("dve", t2[:, :], nr_[:, :], lim[:, :], ALU.mult, r=["s5nr", "lim"], w=["cm_t2"])
        k.tt("dve", t1[:, :], t1[:, :], t2[:, :], ALU.subtract, r=["cm_t1", "cm_t2"], w=["cm_t1"])
        k.tt("dve", bI[:, :], t1[:, :], den[:, :], ALU.mult, r=["cm_t1", "s5den"], w=["betI"])
        bbr = sb("bbr", [128, 32, 16], F32)
        bbi = sb("bbi", [128, 32, 16], F32)
        u1 = sb("s5u1", [128, 32, 16], F32)
        u2 = sb("s5u2", [128, 32, 16], F32)
        bc16 = lambda t_: t_[:, :].unsqueeze(2).to_broadcast([128, 32, 16])
        cmul(k, "dve", bbr[:, :, :], bbi[:, :, :], bc16(bR), bc16(bI), Br[:, :, :], Bi[:, :, :],
             u1[:, :, :], u2[:, :, :], ["betR", "betI", "Br", "Bi"], ["bb"])
        Z = sb("Zm", [128, 32, 8, 16], F32)
        W = sb("Wm", [128, 32, 8, 16], F32)
        Cm = sb("Cm", [128, 32, 8, 16], F32)
        v1 = sb("s5v1", [128, 32, 8, 16], F32)
        v2 = sb("s5v2", [128, 32, 8, 16], F32)

        def big_cmul(dst, dkey, Xr, Xi, Tr, Ti, neg_im, rk):
            xb_ = lambda t_, lo, hi: t_[lo:hi, :, :].unsqueeze(2).to_broadcast([hi - lo, 32, 8, 16])
            tb_ = lambda t_, lo, hi: t_[lo:hi].unsqueeze(3).to_broadcast([hi - lo, 32, 8, 16])
            k.tt("dve", v1[0:64], xb_(Xr, 0, 64), tb_(Tr, 0, 64), ALU.mult, r=rk, w=["s5v1"])
            k.tt("dve", v2[0:64], xb_(Xi, 0, 64), tb_(Ti, 0, 64), ALU.mult, r=rk, w=["s5v2"])
            k.tt("dve", dst[0:64], v1[0:64], v2[0:64], ALU.subtract, r=["s5v1", "s5v2"], w=[dkey])
            k.tt("dve", v1[64:128], xb_(Xi, 64, 128), tb_(Tr, 64, 128), ALU.mult, r=rk + ["s5v1"], w=["s5v1"])
            k.tt("dve", v2[64:128], xb_(Xr, 64, 128), tb_(Ti, 64, 128), ALU.mult, r=rk + ["s5v2"], w=["s5v2"])
            if neg_im:
                k.stt("dve", dst[64:128], v1[64:128], -1.0, v2[64:128], ALU.mult, ALU.subtract,
                      r=["s5v1", "s5v2"], w=[dkey])
            else:
                k.tt("dve", dst[64:128], v1[64:128], v2[64:128], ALU.add, r=["s5v1", "s5v2"], w=[dkey])

        big_cmul(Z, "Zm", bbr, bbi, NR[:, :, 0:8], NI[:, :, 0:8], False, ["bb", "NRI"])
        PRrev = sb("PRrev", [128, 32, 8], F32)
        PIrev = sb("PIrev", [128, 32, 8], F32)
        for tau in range(8):
            k.cp("pool", PRrev[:, :, tau], PR[:, :, 7 - tau], r=["PRI"], w=["Prev"])
            k.cp("pool", PIrev[:, :, tau], PI[:, :, 7 - tau], r=["PRI"], w=["Prev"])
        big_cmul(W, "Wm", bbr, bbi, PRrev[:, :, :], PIrev[:, :, :], False, ["bb", "Prev"])
        big_cmul(Cm, "Cm", Cr, Ci, PR[:, :, 1:9], PI[:, :, 1:9], True, ["Cr", "Ci", "PRI"])

        mt_ = sb("s5mt", [128, 4, 128], F32)
        k.cp("act", Cmb[:, :, :], Cm[:, :, :, :].rearrange("p g r i -> p g (r i)"), r=["Cm"], w=["Cmb"])
        for g4 in range(8):
            pm, km = k.ps()
            for gg in range(4):
                g = g4 * 4 + gg
                k.mm(pm[:, gg * 128:(gg + 1) * 128], Z[:, g].rearrange("p t j -> p (t j)"),
                     Cm[:, g].rearrange("p r i -> p (r i)"), r=["Zm", "Cm"], w=[km])
            k.tt("dve", mt_[:, :, :], pm[:, :].rearrange("p (a b) -> p a b", a=4),
                 Mc[:, :].unsqueeze(1).to_broadcast([128, 4, 128]), ALU.mult, r=[km, "Mc"], w=["s5mt"])
            for gg in range(4):
                g = g4 * 4 + gg
                k.stt("dve", Mb[:, g, :], I32t[:, :], dP[:, g:g + 1], mt_[:, gg, :], ALU.mult, ALU.add,
                      r=["I32t", "dP", "s5mt"], w=["Mb"])
            pw, kw = k.ps()
            for gg in range(4):
                g = g4 * 4 + gg
                k.tr(pw[:, gg * 128:(gg + 1) * 128], W[:, g].rearrange("p t j -> p (t j)"), I32t[:, :],
                     r=["Wm", "I32t"], w=[kw])
            k.cp("act", BmT[:, g4 * 4:(g4 + 1) * 4, :], pw[:, :].rearrange("p (a b) -> p a b", a=4), r=[kw], w=["BmT"])

        P.emit()
        es1.close()
        sb = sbp
        uTp = sb("uTp", [128, 4, 8, 512], BF16)
        ysT = sb("ysT", [128, 4, S], BF16)
        xbs = [sb(f"xb1_{i}", [128, 8, TT], BF16) for i in range(2)]
        for tt_ in range(NTT):
            tok = slice(tt_ * TT, (tt_ + 1) * TT)
            xb = xbs[tt_ % 2]
            xk = f"xb1_{tt_ % 2}"
            k.dma("pool", xb[:, :, :], kt_view(T["xT"], tok), w=[xk])
            for t in range(4):
                pu, ku = k.ps()
                for kt in range(8):
                    k.mm(pu[:, :], wu[:, t, kt * 128:(kt + 1) * 128], xb[:, kt, :], start=(kt == 0), stop=(kt == 7),
                         r=["wu", xk], w=[ku])
                k.cp("act", uTp[:, t, :, tt_ * 64:(tt_ + 1) * 64], pu[:, :].rearrange("p (c t) -> p t c", t=8),
                     r=[ku], w=[f"uTp{t}"])

        Ug = [sb(f"Ug{i}", [128, 512], BF16) for i in range(8)]
        Hb = [sb(f"Hb{i}", [128, 513], BF16) for i in range(8)]
        Yg = [sb(f"Yg{i}", [128, 512], BF16) for i in range(8)]
        Rt = [sb(f"Rt{i}", [128, 128], F32) for i in range(2)]
        Rj = [sb(f"Rj{i}", [128, 128], F32) for i in range(2)]
        Rm = [sb(f"Rm{i}", [128, 128], BF16) for i in range(4)]
        for i in range(8):
            k.memset("pool", Hb[i][:, 0:1], 0.0, w=[f"Hb{i}"])
        ri = 0
        for t in range(4):
            for gl in range(8):
                g = 8 * t + gl
                pu, ku = k.ps()
                for tau in range(8):
                    k.mm(pu[:, :], Sel[:, gl * 8 + tau, :], uTp[:, t, tau, :], start=(tau == 0), stop=(tau == 7),
                         r=["Sel", f"uTp{t}"], w=[ku])
                k.cp("act", Ug[gl][:, :], pu[:, :], r=[ku], w=[f"Ug{gl}"])
                ph, kh = k.ps()
                k.mm(ph[:, :], BmT[:, g, :], Ug[gl][:, :], r=["BmT", f"Ug{gl}"], w=[kh])
                k.cp("act", Hb[gl][:, 1:513], ph[:, :], r=[kh], w=[f"Hb{gl}"])
            for l in range(9):
                d = 1 << l
                for gl in range(8):
                    g = 8 * t + gl
                    rt, rtk = Rt[ri % 2], f"Rt{ri % 2}"
                    rm, rmk = Rm[ri % 4], f"Rm{ri % 4}"
                    rj, rjk = Rj[ri % 2], f"Rj{ri % 2}"
                    ri += 1
                    k.ts("pool", rt[:, :], I16[:, :], LR[:, g, l:l + 1], ALU.mult, r=["I16", "LRI"], w=[rtk])
                    k.ts("pool", rj[:, :], J16[:, :], LI[:, g, l:l + 1], ALU.mult, r=["J16", "LRI"], w=[rjk])
                    k.tt("pool", rm[:, :], rt[:, :], rj[:, :], ALU.add, r=[rtk, rjk], w=[rmk])
                    ps_, ks_ = k.ps()
                    k.mm(ps_[:, 0:512 - d], rm[:, :], Hb[gl][:, 1:513 - d], r=[rmk, f"Hb{gl}"], w=[ks_])
                    k.tt("dve", Hb[gl][:, 1 + d:513], Hb[gl][:, 1 + d:513], ps_[:, 0:512 - d], ALU.add,
                         r=[ks_, f"Hb{gl}"], w=[f"Hb{gl}"])
            for gl in range(8):
                g = 8 * t + gl
                py, ky = k.ps()
                k.mm(py[:, :], Mb[:, g, :], Ug[gl][:, :], start=True, stop=False, r=["Mb", f"Ug{gl}"], w=[ky])
                k.mm(py[:, :], Cmb[:, g, :], Hb[gl][:, 0:512], start=False, stop=True, r=["Cmb", f"Hb{gl}"], w=[ky])
                k.act(Yg[gl][:, :], py[:, :], AF.Gelu_apprx_tanh, r=[ky], w=[f"Yg{gl}"])
            for r_ in range(8):
                pt, kt_ = k.ps()
                for gl in range(8):
                    k.mm(pt[:, :], SelI[:, gl * 8 + r_, :], Yg[gl][:, :], start=(gl == 0), stop=(gl == 7),
                         r=["SelI", f"Yg{gl}"], w=[kt_])
                k.cp("act", ysT[:, t, :].rearrange("p (c r) -> p r c", r=8)[:, r_, :], pt[:, :], r=[kt_], w=[f"ysT{t}"])
        sgl = [sb(f"sgl{i}", [128, TT], F32) for i in range(2)]
        og = [sb(f"og{i}", [128, TT], BF16) for i in range(2)]
        it = 0
        for tt_ in range(NTT):
            tok = slice(tt_ * TT, (tt_ + 1) * TT)
            for n in range(4):
                pz, kz = k.ps()
                for kt in range(4):
                    k.mm(pz[:, :], wglu[:, n, kt * 128:(kt + 1) * 128], ysT[:, kt, tok], start=(kt == 0), stop=(kt == 3),
                         r=["wglu", f"ysT{kt}"], w=[kz])
                s_, sk = sgl[it % 2], f"sgl{it % 2}"
                o_, ok_ = og[it % 2], f"og{it % 2}"
                it += 1
                k.act(s_[:, :], pz[:, :], AF.Sigmoid, r=[kz, "bglu"], w=[sk], bias=bglu[:, n:n + 1])
                k.tt("dve", o_[:, :], ysT[:, n, tok], s_[:, :], ALU.mult, r=[f"ysT{n}", sk], w=[ok_])
                k.dma("sp", T["br"][1][n * 128:(n + 1) * 128, tok], o_[:, :], r=[ok_], w=[f"br1_{tt_}_{n}"])
        P.emit()


def phase_att(k, T):
    nc, P = k.nc, k.P
    with ExitStack() as es:
        sb = lambda n, s, d: k.sb(es, n, s, d)
        qT = sb("qT", [128, 4, S], BF16)
        kT = sb("kT", [128, 2, S], BF16)
        qiT = sb("qiT", [128, 2, S], BF16)
        kiT = sb("kiT", [128, S], BF16)
        V1 = sb("V1", [128, 32, 2, 65], BF16)
        widx = sb("widx", [128, 32, 4], F32)
        I16 = sb("I16a", [128, 128], BF16)
        k.dma("pool", I16[:, :], T["I128"], w=["I16a"])
        k.memset("dve", V1[:, :, :, 64:65], 1.0, w=["V1"])
        dests = [(qT, 0), (qT, 1), (qT, 2), (qT, 3), (kT, 0), (kT, 1), (qiT, 0), (qiT, 1), (kiT, None)]
        dkeys = ["qT", "qT", "qT", "qT", "kT", "kT", "qiT", "qiT", "kiT"]
        with ExitStack() as es2:
            sb2 = lambda n, s, d: k.sb(es2, n, s, d)
            watt = sb2("watt", [128, 18, 1024], BF16)
            wvw = sb2("wvw", [128, 8, 132], BF16)
            cosTs = [sb2(f"cosT{i}", [128, TT], F32) for i in range(2)]
            sinTs = [sb2(f"sinT{i}", [128, TT], F32) for i in range(2)]
            posi = sb2("posi", [128, TT], I32)
            posf = sb2("posf", [128, TT], F32)
            y_ = sb2("sr_y3", [128, TT], F32)
            ki_ = sb2("sr_k3", [128, TT], I32)
            kf_ = sb2("sr_kf3", [128, TT], F32)
            ifr = sb2("ifr", [128, 2], F32)
            negpi = sb2("negpi3", [128, 1], F32)
            xbs = [sb2(f"xb3_{i}", [128, 8, TT], BF16) for i in range(2)]
            r1 = [sb2(f"r1_{i}", [128, TT], F32) for i in range(2)]
            r2 = [sb2(f"r2_{i}", [128, TT], F32) for i in range(2)]
            for i in range(18):
                k.dma("pool", watt[:, i, :], T["w_att_l"][i], w=[f"watt{i}"])
            k.dma("pool", wvw[:, :, :], T["w_vw_l"], w=["wvw"])
            k.dma("sp", ifr[:, :], T["ifr"], w=["ifr"])
            k.memset("pool", negpi[:, :], SIN_BI, w=["negpi"])
            tmp3 = (y_[:, :], ki_[:, :], kf_[:, :])
            it = 0
            for tt_ in range(NTT):
                tok = slice(tt_ * TT, (tt_ + 1) * TT)
                xb = xbs[tt_ % 2]
                xk = f"xb3_{tt_ % 2}"
                k.dma("pool", xb[:, :, :], kt_view(T["xT"], tok), w=[xk])
                cosT, ck = cosTs[tt_ % 2], f"cosT{tt_ % 2}"
                sinT, sk_ = sinTs[tt_ % 2], f"sinT{tt_ % 2}"
                k.dma("sp", posi[:, :], T["pos"][:, tok].partition_broadcast(128), w=["posi"])
                k.cp("dve", posf[:, :], posi[:, :], r=["posi"], w=["posf"])
                sin_reduced(k, sinT[:, :], posf[:, :], ifr[:, 0:1], False, None, tmp3, negpi[:, 0:1], ["posf", "ifr"], [sk_])
                sin_reduced(k, cosT[:, :], posf[:, :], ifr[:, 0:1], True, None, tmp3, negpi[:, 0:1], ["posf", "ifr"], [ck])
                k.ts("dve", sinT[:, :], sinT[:, :], ifr[:, 1:2], ALU.mult, r=[sk_, "ifr"], w=[sk_])
                for i in range(9):
                    p1, k1 = k.ps()
                    for kt in range(8):
                        k.mm(p1[:, :], watt[:, 2 * i, kt * 128:(kt + 1) * 128], xb[:, kt, :], start=(kt == 0), stop=(kt == 7),
                             r=[f"watt{2 * i}", xk], w=[k1])
                    p2, k2 = k.ps()
                    for kt in range(8):
                        k.mm(p2[:, :], watt[:, 2 * i + 1, kt * 128:(kt + 1) * 128], xb[:, kt, :], start=(kt == 0), stop=(kt == 7),
                             r=[f"watt{2 * i + 1}", xk], w=[k2])
                    a1, a1k = r1[it % 2], f"r1_{it % 2}"
                    a2, a2k = r2[it % 2], f"r2_{it % 2}"
                    it += 1
                    k.tt("dve", a1[:, :], p1[:, :], cosT[:, :], ALU.mult, r=[k1, ck], w=[a1k])
                    k.tt("dve", a2[:, :], p2[:, :], sinT[:, :], ALU.mult, r=[k2, sk_], w=[a2k])
                    dt_, di = dests[i]
                    dst = dt_[:, di, tok] if di is not None else dt_[:, tok]
                    k.tt("pool", dst, a1[:, :], a2[:, :], ALU.add, r=[a1k, a2k], w=[dkeys[i]])
                for sbk in range(4):
                    sblk = tt_ * 4 + sbk
                    pv, kv = k.ps()
                    for kt in range(8):
                        k.mm(pv[:, 0:132], xb[:, kt, sbk * 128:(sbk + 1) * 128], wvw[:, kt, :], start=(kt == 0), stop=(kt == 7),
                             r=["wvw", xk], w=[kv])
                    k.cp("act", V1[:, sblk, :, 0:64], pv[:, 0:128].rearrange("p (g d) -> p g d", g=2), r=[kv], w=["V1"])
                    k.act(widx[:, sblk, :], pv[:, 128:132], AF.Copy, r=[kv], w=["widx"], scale=0.5)
            P.emit()

        scoreA = sb("scoreA", [128, S], F32)
        work = sb("work", [128, S], F32)
        negidx = sb("negidx", [128, S], F32)
        masks = [sb(f"mask{i}", [128, S], BF16) for i in range(2)]
        maskTs = [sb(f"maskT{i}", [128, S], BF16) for i in range(2)]
        rl = [sb(f"rl{i}", [128, 512], F32) for i in range(2)]
        Pt = [sb(f"Pt{i}", [128, 512], BF16) for i in range(3)]
        m8 = [sb(f"m8_{i}", [128, 8], F32) for i in range(2)]
        thrc = sb("thrc", [128, 1], F32)
        rec = sb("rec_a", [128, 8], F32)
        attn = sb("attn_tm", [128, 8, 64], BF16)
        aT = [sb(f"aT{i}", [128, 4, 128], BF16) for i in range(2)]
        k.dma("sp", negidx[:, :], T["negidx"].partition_broadcast(128), w=["negidx"])
        k.memset("pool", thrc[:, :], -1.0e29, w=["thrc"])
        psO = k.psO
        psB = k.psB
        ring_save = k.ring
        k.ring = k.ring[:5]

        def indexer(b):
            N = 128 * (b + 1)
            tb = slice(b * 128, (b + 1) * 128)
            nch = (N + 511) // 512
            for c in range(nch):
                c0_ = c * 512
                cw_ = min(512, N - c0_)
                for h in range(4):
                    base = 64 * (h % 2)
                    pl, kl = k.ps()
                    k.mm(pl[:, 0:cw_], qiT[base:base + 64, h // 2, tb], kiT[base:base + 64, c0_:c0_ + cw_],
                         r=["qiT", "kiT"], w=[kl])
                    t_, tk = rl[(c * 4 + h) % 2], f"rl{(c * 4 + h) % 2}"
                    k.act(t_[:, 0:cw_], pl[:, 0:cw_], AF.Relu, r=[kl], w=[tk], scale=0.125)
                    other = negidx if h == 0 else scoreA
                    okey = "negidx" if h == 0 else "scoreA"
                    k.stt("dve", scoreA[:, c0_:c0_ + cw_], t_[:, 0:cw_], widx[:, b, h:h + 1], other[:, c0_:c0_ + cw_],
                          ALU.mult, ALU.add, r=[tk, "widx", okey], w=["scoreA"])
            k.memset("dve", scoreA[0:64, N - 64:N], -1.0e30, w=["scoreA"])

        def select(b):
            N = 128 * (b + 1)
            nsb = b + 1
            mask, mk = masks[b % 2], f"mask{b % 2}"
            maskT, mtk = maskTs[b % 2], f"maskT{b % 2}"
            if b >= 2:
                src, skey = scoreA, "scoreA"
                for rd in range(32):
                    mm8, m8k = m8[rd % 2], f"m8_{rd % 2}"
                    P.op("dve", lambda e, o=mm8[:, :], i_=src[:, 0:N]: e.max(out=o, in_=i_), r=[skey], w=[m8k])
                    if rd < 31:
                        P.op("dve", lambda e, o=work[:, 0:N], rp=mm8[:, :], iv=src[:, 0:N]:
                             e.match_replace(out=o, in_to_replace=rp, in_values=iv, imm_value=NEG_FILL),
                             r=[skey, m8k], w=["work"])
                        src, skey = work, "work"
                thr, thk = m8[31 % 2][:, 7:8], f"m8_{31 % 2}"
            else:
                thr, thk = thrc[:, 0:1], "thrc"
            k.ts("dve", mask[:, 0:N], scoreA[:, 0:N], thr, ALU.is_ge, r=["scoreA", thk], w=[mk])
            for j0 in range(0, nsb, 8):
                nj = min(8, nsb - j0)
                pb, kb = psB
                for jj in range(nj):
                    j = j0 + jj
                    k.tr(pb[:, jj * 128:(jj + 1) * 128], mask[:, j * 128:(j + 1) * 128], I16[:, :], r=[mk, "I16a"], w=[kb])
                k.cp("act", maskT[:, j0 * 128:(j0 + nj) * 128], pb[:, 0:nj * 128], r=[kb], w=[mtk])

        def attend(b):
            nsb = b + 1
            tb = slice(b * 128, (b + 1) * 128)
            maskT, mtk = maskTs[b % 2], f"maskT{b % 2}"
            pi_ = 0
            for h in range(8):
                g = h // 4
                base = 64 * (h % 2)
                po, ko = psO[h // 4]
                for j0 in range(0, nsb, 4):
                    nj = min(4, nsb - j0)
                    ps_, ks_ = k.ps()
                    for jj in range(nj):
                        j = j0 + jj
                        k.mm(ps_[:, jj * 128:(jj + 1) * 128], kT[base:base + 64, g, j * 128:(j + 1) * 128],
                             qT[base:base + 64, h // 2, tb], r=["kT", "qT"], w=[ks_])
                    p_, pk_ = Pt[pi_ % 3], f"Pt{pi_ % 3}"
                    pi_ += 1
                    k.act(p_[:, 0:nj * 128], ps_[:, 0:nj * 128], AF.Exp, r=[ks_], w=[pk_], scale=0.125)
                    k.tt("pool", p_[:, 0:nj * 128], p_[:, 0:nj * 128], maskT[:, j0 * 128:(j0 + nj) * 128], ALU.mult,
                         r=[pk_, mtk], w=[pk_])
                    for jj in range(nj):
                        j = j0 + jj
                        k.mm(po[:, (h % 4) * 65:(h % 4) * 65 + 65], p_[:, jj * 128:(jj + 1) * 128], V1[:, j, g, :],
                             start=(j == 0), stop=(j == nsb - 1), r=[pk_, "V1"], w=[ko])

        def finish(b):
            tb = slice(b * 128, (b + 1) * 128)
            for hb in range(2):
                po, ko = psO[hb]
                pv = po[:, 0:260].rearrange("p (h e) -> p h e", h=4)
                k.recip(rec[:, hb * 4:(hb + 1) * 4], pv[:, :, 64], r=[ko], w=["rec_a"])
                k.tt("dve", attn[:, hb * 4:(hb + 1) * 4, :], pv[:, :, 0:64],
                     rec[:, hb * 4:(hb + 1) * 4].unsqueeze(2).to_broadcast([128, 4, 64]), ALU.mult,
                     r=[ko, "rec_a"], w=["attn_tm"])
            pb, kb = psB
            a_, ak = aT[b % 2], f"aT{b % 2}"
            af = attn[:, :, :].rearrange("p h d -> p (h d)")
            for t in range(4):
                k.tr(pb[:, t * 128:(t + 1) * 128], af[:, t * 128:(t + 1) * 128], I16[:, :], r=["attn_tm", "I16a"], w=[kb])
            k.cp("act", a_[:, :, :], pb[:, 0:512].rearrange("p (t s) -> p t s", t=4), r=[kb], w=[ak])
            k.dma("sp", kt_view(T["br"][0], tb), a_[:, :, :], r=[ak], w=[f"br0_{b // 4}_{b % 4}"])

        import os as _os
        NB = int(_os.environ.get("ATT_NB", S // 128))
        indexer(0)
        select(0)
        for b in range(NB):
            if b + 1 < NB:
                indexer(b + 1)
            attend(b)
            if b + 1 < NB:
                select(b + 1)
            finish(b)
        k.ring = ring_save
        P.emit()


def build_program(shapes, phases=("s5", "mem", "att", "mix"), br_mode="internal"):
    nc = bass.Bass("TRN2", target_bir_lowering=False)
    T = {}
    for name, (shape, dt) in shapes.items():
        T[name] = nc.dram_tensor(name, list(shape), dt, kind="ExternalInput").ap()
    T["outT"] = nc.dram_tensor("outT", [D, S], F32, kind="ExternalOutput").ap()
    if br_mode == "internal":
        brt = nc.dram_tensor("br", [3, 512, S], BF16).ap()
    elif br_mode == "output":
        brt = nc.dram_tensor("br", [3, 512, S], BF16, kind="ExternalOutput").ap()
    else:
        brt = nc.dram_tensor("br", [3, 512, S], BF16, kind="ExternalInput").ap()
    T["br"] = [brt[i] for i in range(3)]
    T["wA_bf"] = nc.dram_tensor("wA_bf", [8, 128, 4608], BF16).ap()
    T["wO_bf"] = nc.dram_tensor("wO_bf", [8, 128, 1024], BF16).ap()
    T["wU_bf"] = nc.dram_tensor("wU_bf", [22, 128, 2048], BF16).ap()
    T["wD_bf"] = nc.dram_tensor("wD_bf", [8, 128, 2816], BF16).ap()
    with ExitStack() as es:
        P = Prog(nc, es)
        k = K(nc, P)
        for i in range(5):
            t = es.enter_context(nc.psum_tensor(f"psf{i}", [128, 512], F32))
            k.ring.append((t, f"psf{i}"))
        o0 = es.enter_context(nc.psum_tensor("psf5", [128, 512], F32))
        o1 = es.enter_context(nc.psum_tensor("psf6", [128, 512], F32))
        pb = es.enter_context(nc.psum_tensor("psb", [128, 1024], BF16))
        k.psO = [(o0, "psf5"), (o1, "psf6")]
        k.psB = (pb, "psb")
        k.ring.append((o0, "psf5"))
        k.ring.append((o1, "psf6"))
        if "mix" in phases:
            for i in range(8):
                k.dma("pool", T["wA_bf"][i], T["wA_l"][i], w=[f"wbf_A{i}"])
            for i in range(8):
                k.dma("pool", T["wO_bf"][i], T["wO_l"][i], w=[f"wbf_O{i}"])
            for i in range(22):
                k.dma("pool", T["wU_bf"][i], T["wU_l"][i], w=[f"wbf_U{i}"])
            for i in range(8):
                k.dma("pool", T["wD_bf"][i], T["wD_l"][i], w=[f"wbf_Dn{i}"])
        if "s5" in phases:
            phase_s5(k, T)
        if "mem" in phases:
            phase_mem(k, T)
        if "att" in phases:
            phase_att(k, T)
        if "mix" in phases:
            phase_mix(k, T)
        else:
            P.wait_all("sp", [kk_ for kk_ in P.lastw if kk_.startswith("br")])
            P.emit()
    return nc


def tile_lhsT(W):
    Kd, N = W.shape
    return np.ascontiguousarray(W.reshape(Kd // 128, 128, N // 128, 128).transpose(2, 1, 0, 3).reshape(N // 128, 128, (Kd // 128) * 128))


def rhs_layout(W):
    Kd, N = W.shape
    return np.ascontiguousarray(W.reshape(Kd // 128, 128, N).transpose(1, 0, 2))


def prep_shared(inp):
    f = np.float32
    sh = {}
    w_in = inp["w_in"][0]
    def hc(base, h):
        return base + h * 64 + np.arange(64)
    def sw(c):
        return np.concatenate([c[32:], c[:32]])
    tiles = []
    for j in range(4):
        a, b = hc(0, 2 * j), hc(0, 2 * j + 1)
        tiles += [np.concatenate([a, b]), np.concatenate([sw(a), sw(b)])]
    for g in range(2):
        a = hc(512, g)
        tiles += [np.concatenate([a, a]), np.concatenate([sw(a), sw(a)])]
    for j in range(2):
        a, b = hc(768, 2 * j), hc(768, 2 * j + 1)
        tiles += [np.concatenate([a, b]), np.concatenate([sw(a), sw(b)])]
    a = 1024 + np.arange(64)
    tiles += [np.concatenate([a, a]), np.concatenate([sw(a), sw(a)])]
    cols = np.concatenate(tiles)
    sh["w_att_l"] = tile_lhsT(w_in[:, cols])
    sh["w_vw_l"] = rhs_layout(np.concatenate([w_in[:, 640:768], w_in[:, 1088:1092]], axis=1))
    sh["w_u_l"] = np.ascontiguousarray(tile_lhsT(w_in[:, 1092:1604]).transpose(1, 0, 2))
    sh["w_qmem_l"] = np.ascontiguousarray(tile_lhsT(w_in[:, 1604:2116]).transpose(1, 0, 2))
    wkv = inp["w_mem_kv"][0]
    sh["w_kmem_l"] = np.ascontiguousarray(tile_lhsT(wkv[:, 0:512]).transpose(1, 0, 2))
    sh["w_vmem_l"] = rhs_layout(wkv[:, 512:1024])
    sh["w_glu_l"] = np.ascontiguousarray(tile_lhsT(inp["w_glu"][0]).transpose(1, 0, 2))
    sh["b_glu_l"] = np.ascontiguousarray(inp["b_glu"][0].reshape(4, 128).T)
    p = np.arange(128)
    inv_freq = (10000.0 ** (-(np.arange(32, dtype=np.float32)) / np.float32(32))).astype(np.float32)
    ifr = np.zeros((128, 2), f)
    ifr[:, 0] = (inv_freq[p % 32].astype(np.float64) / TWO_PI).astype(f)
    ifr[:, 1] = np.where((p % 64) < 32, -1.0, 1.0)
    sh["ifr"] = ifr
    sh["negidx"] = (-(np.arange(S, dtype=np.float64) + 1.0) * 1e-30).astype(f).reshape(1, S)
    dup = lambda a_: np.ascontiguousarray(np.concatenate([a_, a_], axis=0).astype(f))
    sh["lre"] = dup(inp["s5_lam_re"][0].T)
    sh["lim"] = dup(inp["s5_lam_im"][0].T)
    sh["ldt"] = np.ascontiguousarray(np.broadcast_to(inp["s5_log_dt"][0][None, :], (128, 32)).astype(f))
    sh["Br"] = dup(inp["s5_b_re"][0].transpose(1, 0, 2))
    sh["Bi"] = dup(inp["s5_b_im"][0].transpose(1, 0, 2))
    sh["Cr"] = dup(inp["s5_c_re"][0].transpose(2, 0, 1))
    sh["Ci"] = dup(inp["s5_c_im"][0].transpose(2, 0, 1))
    sh["dP"] = np.ascontiguousarray(np.tile(inp["s5_d"][0].reshape(32, 16).T, (8, 1)).astype(f))
    sh["I128"] = np.eye(128, dtype=f)
    J = np.zeros((128, 128), f)
    for q in range(64):
        J[q, 64 + q] = 1.0
        J[64 + q, q] = -1.0
    sh["J128"] = J
    tau = np.arange(128) // 16
    sh["Mc"] = (tau[None, :] >= tau[:, None]).astype(f)
    Sel = np.zeros((128, 64, 128), f)
    SelI = np.zeros((128, 64, 128), f)
    for gl in range(8):
        for t in range(8):
            for j in range(16):
                Sel[gl * 16 + j, gl * 8 + t, t * 16 + j] = 1.0
                SelI[t * 16 + j, gl * 8 + t, gl * 16 + j] = 1.0
    sh["Sel"] = Sel
    sh["SelI"] = SelI
    wg = inp["w_gate"][0]
    pa = [tile_lhsT(inp[n][0]) for n in ("w_proj_a", "w_proj_b", "w_proj_c")]
    ga = [tile_lhsT(wg[:, i * 1024:(i + 1) * 1024]) for i in range(3)]
    sh["wA_l"] = np.ascontiguousarray(np.concatenate(pa + ga, axis=2))
    sh["wO_l"] = tile_lhsT(inp["w_out"][0])
    wu = tile_lhsT(inp["w_up"][0])
    sh["wU_l"] = np.ascontiguousarray(np.concatenate([wu[0:22], wu[22:44]], axis=2))
    sh["wD_l"] = tile_lhsT(inp["w_down"][0])
    sh["b_gate_l"] = np.ascontiguousarray(inp["b_gate"][0].reshape(3, 8, 128).transpose(2, 0, 1).reshape(128, 24))
    vec = lambda v: v.reshape(8, 128).T
    sh["ln_l"] = np.ascontiguousarray(np.concatenate([vec(inp["ln1_g"][0]), vec(inp["ln1_b"][0]),
                                                      vec(inp["ln2_g"][0]), vec(inp["ln2_b"][0])], axis=1))
    sh["conv_w_l"] = np.ascontiguousarray(inp["conv_w"][0].T.reshape(44, 128, 3).transpose(1, 0, 2))
    sh["conv_b_l"] = np.ascontiguousarray(inp["conv_b"][0].reshape(44, 128).T)
    return {k_: np.ascontiguousarray(v.astype(f)) for k_, v in sh.items()}


def prep_core(inp, b):
    return {
        "xT": np.ascontiguousarray(inp["x"][b].T),
        "memT": np.ascontiguousarray(inp["mem"][b].T),
        "pos": np.ascontiguousarray(inp["positions"][b].reshape(1, S).astype(np.int32)),
    }


def kernel(**inputs):
    inp = {k_: np.asarray(v) for k_, v in inputs.items()}
    shared = prep_shared(inp)
    B = inp["x"].shape[0]
    in_maps = []
    for b in range(B):
        m = dict(shared)
        m.update(prep_core(inp, b))
        in_maps.append(m)
    shapes = {k_: (v.shape, I32 if v.dtype == np.int32 else F32) for k_, v in in_maps[0].items()}
    nc = build_program(shapes)
    res = run_bass_kernel_spmd(nc, in_maps, core_ids=list(range(B)))
    out = np.stack([np.asarray(r["outT"]).T for r in res.results], axis=0)
    return np.ascontiguousarray(out.astype(np.float32))
```

```python
from contextlib import ExitStack
import numpy as np
import concourse.bass as bass
import concourse.mybir as mybir
from concourse.bass_utils import run_bass_kernel_spmd

F32 = mybir.dt.float32
BF16 = mybir.dt.bfloat16
I32 = mybir.dt.int32
AF = mybir.ActivationFunctionType
ALU = mybir.AluOpType

S = 4096
D = 1024
TT = 512
NTT = S // TT
ALPHA = 2.0 ** 0.25
LN_EPS = 1e-5
TWO_PI = 6.283185307179586
SIN_SC = 6.2831840
SIN_BI = -3.1415920
EPOCH = 30000
NEG_FILL = -3.0e38


class Prog:
    ENGS = ("pe", "act", "dve", "pool", "sp")

    def __init__(self, nc, es, ndma=8):
        self.nc = nc
        self.es = es
        self.ops = {e: [] for e in self.ENGS}
        self.count = {e: 0 for e in self.ENGS}
        self.sems = {}
        self.ndma = ndma
        self.dsems = {}
        self.dcount = {e: 0 for e in self.ENGS}
        self.dtarget = {}
        self.waited = {e: {} for e in self.ENGS}
        self.lastw = {}
        self.readers = {}

    def _sem(self, eng, epoch):
        k = (eng, epoch)
        if k not in self.sems:
            self.sems[k] = self.es.enter_context(self.nc.semaphore(f"s_{eng}_{epoch}"))
        return self.sems[k]

    def _dsem(self, eng, r):
        k = (eng, r)
        if k not in self.dsems:
            self.dsems[k] = self.es.enter_context(self.nc.semaphore(f"d_{eng}_{r}"))
            self.dtarget[k] = 0
        return self.dsems[k]

    def _need(self, eng, ev, waits):
        if ev is None:
            return
        sem, val, src = ev
        if src == "pe" and eng == "pe":
            return
        w = self.waited[eng]
        if w.get(id(sem), 0) >= val:
            return
        w[id(sem)] = val
        waits.append((sem, val))

    def op(self, eng, fn, r=(), w=(), dma=False):
        waits = []
        for k in r:
            self._need(eng, self.lastw.get(k), waits)
        for k in w:
            self._need(eng, self.lastw.get(k), waits)
            for ev in self.readers.get(k, {}).values():
                self._need(eng, ev, waits)
        if dma:
            i = self.dcount[eng]
            self.dcount[eng] += 1
            rr = i % self.ndma
            sem = self._dsem(eng, rr)
            prev = self.dtarget[(eng, rr)]
            if prev > 0:
                self._need(eng, (sem, prev, "dma"), waits)
            self.dtarget[(eng, rr)] = prev + 16
            ev = (sem, prev + 16, "dma")
            inc = 16
        else:
            self.count[eng] += 1
            ep, v = divmod(self.count[eng] - 1, EPOCH)
            sem = self._sem(eng, ep)
            ev = (sem, v + 1, eng)
            inc = 1
        best = {}
        for s_, v_ in waits:
            if id(s_) not in best or best[id(s_)][1] < v_:
                best[id(s_)] = (s_, v_)
        self.ops[eng].append((list(best.values()), fn, sem, inc))
        for k in r:
            self.readers.setdefault(k, {})[(eng, id(sem))] = ev
        for k in w:
            self.lastw[k] = ev
            self.readers[k] = {}
        return ev

    def wait_all(self, eng, keys):
        waits = []
        for k in keys:
            self._need(eng, self.lastw.get(k), waits)
        self.ops[eng].append((waits, None, None, 0))

    def drain_dmas(self, eng="sp"):
        waits = []
        for (qe, r_), sem in self.dsems.items():
            tgt = self.dtarget[(qe, r_)]
            if tgt > 0:
                self._need(eng, (sem, tgt, "dma"), waits)
        self.ops[eng].append((waits, None, None, 0))

    def emit(self):
        self.drain_dmas("sp")
        nc = self.nc
        ops = self.ops
        self.ops = {e: [] for e in self.ENGS}

        def run(engname):
            def _f(eng):
                for wl, fn, sem, inc in ops[engname]:
                    for s_, v_ in wl:
                        eng.wait_ge(s_, v_)
                    if fn is not None:
                        fn(eng).then_inc(sem, inc)
            return _f

        with nc.Block() as block:
            block.tensor(run("pe"))
            block.scalar(run("act"))
            block.vector(run("dve"))
            block.gpsimd(run("pool"))
            block.sync(run("sp"))


class K:
    def __init__(self, nc, P):
        self.nc = nc
        self.P = P
        self.ring = []
        self.ri = 0

    def sb(self, es, name, shape, dt):
        return es.enter_context(self.nc.sbuf_tensor("sb_" + name, shape, dt))

    def ps(self):
        t = self.ring[self.ri % len(self.ring)]
        self.ri += 1
        return t

    def mm(self, out, lhsT, rhs, start=True, stop=True, r=(), w=()):
        self.P.op("pe", lambda e: e.matmul(out, lhsT=lhsT, rhs=rhs, start=start, stop=stop), r=r, w=w)

    def tr(self, out, in_, ident, r=(), w=()):
        self.P.op("pe", lambda e: e.transpose(out, in_, ident), r=r, w=w)

    def act(self, out, in_, func, r=(), w=(), **kw):
        self.P.op("act", lambda e: e.activation(out=out, in_=in_, func=func, **kw), r=r, w=w)

    def tt(self, eng, out, in0, in1, op, r=(), w=()):
        self.P.op(eng, lambda e: e.tensor_tensor(out=out, in0=in0, in1=in1, op=op), r=r, w=w)

    def ts(self, eng, out, in0, s1, op0, s2=None, op1=None, r=(), w=()):
        if op1 is None:
            self.P.op(eng, lambda e: e.tensor_scalar(out=out, in0=in0, scalar1=s1, scalar2=None, op0=op0), r=r, w=w)
        else:
            self.P.op(eng, lambda e: e.tensor_scalar(out=out, in0=in0, scalar1=s1, scalar2=s2, op0=op0, op1=op1), r=r, w=w)

    def stt(self, eng, out, in0, scalar, in1, op0, op1, r=(), w=()):
        self.P.op(eng, lambda e: e.scalar_tensor_tensor(out=out, in0=in0, scalar=scalar, in1=in1, op0=op0, op1=op1), r=r, w=w)

    def cp(self, eng, out, in_, r=(), w=()):
        if eng == "act":
            self.act(out, in_, AF.Copy, r=r, w=w)
        else:
            self.P.op(eng, lambda e: e.tensor_copy(out=out, in_=in_), r=r, w=w)

    def recip(self, out, in_, r=(), w=()):
        self.P.op("dve", lambda e: e.reciprocal(out=out, in_=in_), r=r, w=w)

    def memset(self, eng, ap, val, w=()):
        self.P.op(eng, lambda e: e.memset(ap, val), w=w)

    def dma(self, eng, out, in_, r=(), w=()):
        self.P.op(eng, lambda e: e.dma_start(out=out, in_=in_), r=r, w=w, dma=True)


def kt_view(dram_ap, cols):
    return dram_ap.rearrange("(kt p) s -> p kt s", p=128)[:, :, cols]


def sin_reduced(k, out, in_, mul, quarter, shape, tmp, negpi, rk, wk):
    y, ki, kf = tmp
    off = 8.5 + (0.25 if quarter else 0.0)
    if isinstance(mul, float):
        k.ts("dve", y, in_, mul / TWO_PI, ALU.mult, off, ALU.add, r=rk, w=["sr_y"])
    else:
        k.ts("dve", y, in_, mul, ALU.mult, off, ALU.add, r=rk, w=["sr_y"])
    k.cp("dve", ki, y, r=["sr_y"], w=["sr_k"])
    k.cp("dve", kf, ki, r=["sr_k"], w=["sr_kf"])
    k.tt("dve", y, y, kf, ALU.subtract, r=["sr_y", "sr_kf"], w=["sr_y"])
    k.ts("dve", kf, y, 0.0, ALU.is_lt, r=["sr_y"], w=["sr_kf"])
    k.tt("dve", y, y, kf, ALU.add, r=["sr_y", "sr_kf"], w=["sr_y"])
    k.act(out, y, AF.Sin, r=["sr_y", "negpi"], w=wk, bias=negpi, scale=SIN_SC)


def layer_norm(k, es_tmp, z, zkey, gvec, bvec, out32, o32key, out16, o16key):
    zb, zsq, onesD, meanS, m2, var, rstd, tmps, epsb = es_tmp
    for kt in range(8):
        k.act(zb[:, kt, :], z[:, kt, :], AF.Copy, r=[zkey], w=[f"zb{kt}"])
        k.act(zsq[:, kt, :], z[:, kt, :], AF.Square, r=[zkey], w=[f"zsq{kt}"])
    psM, kM = k.ps()
    for kt in range(8):
        k.mm(psM[:, :], onesD[:, :], zb[:, kt, :], start=(kt == 0), stop=(kt == 7), r=[f"zb{kt}", "onesD"], w=[kM])
    psQ, kQ = k.ps()
    for kt in range(8):
        k.mm(psQ[:, :], onesD[:, :], zsq[:, kt, :], start=(kt == 0), stop=(kt == 7), r=[f"zsq{kt}", "onesD"], w=[kQ])
    k.act(meanS[:, :], psM[:, :], AF.Copy, r=[kM], w=["meanS"])
    k.tt("dve", m2[:, :], meanS[:, :], meanS[:, :], ALU.mult, r=["meanS"], w=["m2"])
    k.tt("dve", var[:, :], psQ[:, :], m2[:, :], ALU.subtract, r=[kQ, "m2"], w=["var"])
    k.act(var[:, :], var[:, :], AF.Sqrt, r=["var", "epsb"], w=["var"], bias=epsb[:, 0:1])
    k.recip(rstd[:, :], var[:, :], r=["var"], w=["rstd"])
    for kt in range(8):
        t = tmps[kt % 2]
        tk = f"lnt{kt % 2}"
        k.tt("dve", t[:, :], z[:, kt, :], meanS[:, :], ALU.subtract, r=[zkey, "meanS"], w=[tk])
        k.tt("dve", t[:, :], t[:, :], rstd[:, :], ALU.mult, r=[tk, "rstd"], w=[tk])
        k.act(out32[:, kt, :], t[:, :], AF.Identity, r=[tk, "lnvec"], w=[o32key],
              scale=gvec[:, kt:kt + 1], bias=bvec[:, kt:kt + 1])
        if out16 is not None:
            k.cp("pool", out16(kt), out32[:, kt, :], r=[o32key], w=[o16key])


def phase_mix(k, T):
    nc, P = k.nc, k.P
    with ExitStack() as es:
        sb = lambda n, s, d: k.sb(es, n, s, d)
        xf = sb("xf", [128, 8, TT], F32)
        xb = sb("xb4", [128, 8, TT], BF16)
        brt = [sb(f"brt{i}", [128, 4, TT], BF16) for i in range(3)]
        gs = [sb(f"gs{i}", [128, TT], F32) for i in range(3)]
        m1 = sb("m1", [128, TT], F32)
        tA = sb("tA", [128, TT], F32)
        tB = sb("tB", [128, TT], F32)
        merged = sb("merged", [128, 8, TT], BF16)
        zb = sb("zb", [128, 8, TT], BF16)
        zsq = sb("zsq", [128, 8, TT], BF16)
        onesD = sb("onesD", [128, 128], BF16)
        meanS = sb("meanS", [128, TT], F32)
        m2 = sb("m2", [128, TT], F32)
        var = sb("var", [128, TT], F32)
        rstd = sb("rstd", [128, TT], F32)
        lnt = [sb(f"lnt{i}", [128, TT], F32) for i in range(2)]
        epsb = sb("epsb", [128, 1], F32)
        h1 = sb("h1", [128, 8, TT], F32)
        h1b = sb("h1b", [128, 8, TT + 2], BF16)
        actb = sb("actb", [128, 22, TT], BF16)
        c0 = [sb(f"c0_{i}", [128, 256], F32) for i in range(2)]
        c1 = [sb(f"c1_{i}", [128, 256], F32) for i in range(2)]
        sg = sb("sg", [128, 256], F32)
        ot = sb("ot", [128, 8, TT], F32)
        NSLOT = 3
        wring = [sb(f"wr{i}", [128, 4608], BF16) for i in range(NSLOT)]
        bgate = sb("bgate", [128, 24], F32)
        lnv = sb("lnv", [128, 32], F32)
        cw = sb("cw", [128, 44, 3], F32)
        cb = sb("cb", [128, 44], F32)
        ln_tmp = (zb, zsq, onesD, meanS, m2, var, rstd, lnt, epsb)

        k.dma("sp", bgate[:, :], T["b_gate_l"], w=["bgate"])
        k.dma("sp", lnv[:, :], T["ln_l"], w=["lnvec"])
        k.dma("sp", cw[:, :, :], T["conv_w_l"], w=["cw"])
        k.dma("sp", cb[:, :], T["conv_b_l"], w=["cw"])
        k.memset("pool", onesD[:, :], 1.0 / 1024.0, w=["onesD"])
        k.memset("pool", epsb[:, :], LN_EPS, w=["epsb"])
        k.memset("pool", h1b[:, :, 0:2], 0.0, w=["h1b"])

        chunks = []
        for tt_ in range(NTT):
            for n in range(8):
                chunks.append(("A", n))
            for n in range(8):
                chunks.append(("O", n))
            for i in range(22):
                chunks.append(("U", i))
            for n in range(8):
                chunks.append(("Dn", n))
        csize = {"A": 4608, "O": 1024, "U": 2048, "Dn": 2816}
        csrc = {"A": T["wA_bf"], "O": T["wO_bf"], "U": T["wU_bf"], "Dn": T["wD_bf"]}
        state = {"next": 0}

        def load_chunk():
            i = state["next"]
            if i >= len(chunks):
                return
            kind, idx = chunks[i]
            slot = i % NSLOT
            k.dma("sp", wring[slot][:, 0:csize[kind]], csrc[kind][idx], r=[f"wbf_{kind}{idx}"], w=[f"wr{slot}"])
            state["next"] += 1

        ci = {"i": 0}

        def get_chunk():
            i = ci["i"]
            ci["i"] += 1
            return wring[i % NSLOT], f"wr{i % NSLOT}"

        for _ in range(NSLOT):
            load_chunk()

        def load_inputs(t_):
            tk_ = slice(t_ * TT, (t_ + 1) * TT)
            k.dma("sp", xf[:, :, :], kt_view(T["xT"], tk_), w=["xf"])
            k.dma("pool", xb[:, :, :], kt_view(T["xT"], tk_), w=["xb4"])
            for i in range(3):
                k.dma("sp", brt[i][:, :, :], kt_view(T["br"][i], tk_), r=[f"br{i}_{t_}_{q}" for q in range(4)], w=[f"brt{i}"])

        load_inputs(0)
        for tt_ in range(NTT):
            tok = slice(tt_ * TT, (tt_ + 1) * TT)
            for n in range(8):
                wt, wk = get_chunk()
                psy = []
                for i in range(3):
                    py, ky = k.ps()
                    for kt in range(4):
                        c = (4 * i + kt) * 128
                        k.mm(py[:, :], wt[:, c:c + 128], brt[i][:, kt, :], start=(kt == 0), stop=(kt == 3),
                             r=[wk, f"brt{i}"], w=[ky])
                    pg, kg = k.ps()
                    for kt in range(8):
                        c = (12 + 8 * i + kt) * 128
                        k.mm(pg[:, :], wt[:, c:c + 128], xb[:, kt, :], start=(kt == 0), stop=(kt == 7),
                             r=[wk, "xb4"], w=[kg])
                    k.act(gs[i][:, :], pg[:, :], AF.Sigmoid, r=[kg, "bgate"], w=[f"gs{i}"],
                          bias=bgate[:, i * 8 + n:i * 8 + n + 1])
                    psy.append((py, ky))
                load_chunk()
                k.tt("dve", m1[:, :], psy[0][0][:, :], gs[0][:, :], ALU.mult, r=[psy[0][1], "gs0"], w=["m1"])
                k.tt("dve", tA[:, :], psy[1][0][:, :], gs[1][:, :], ALU.mult, r=[psy[1][1], "gs1"], w=["tA"])
                k.tt("dve", tB[:, :], psy[2][0][:, :], gs[2][:, :], ALU.mult, r=[psy[2][1], "gs2"], w=["tB"])
                k.tt("pool", m1[:, :], m1[:, :], tA[:, :], ALU.add, r=["m1", "tA"], w=["m1"])
                k.tt("pool", merged[:, n, :], m1[:, :], tB[:, :], ALU.add, r=["m1", "tB"], w=[f"merged{n}"])
            for n in range(8):
                wt, wk = get_chunk()
                px, kx = k.ps()
                for kt in range(8):
                    k.mm(px[:, :], wt[:, kt * 128:(kt + 1) * 128], merged[:, kt, :], start=(kt == 0), stop=(kt == 7),
                         r=[wk, f"merged{kt}"], w=[kx])
                load_chunk()
                k.stt("dve", xf[:, n, :], xf[:, n, :], ALPHA, px[:, :], ALU.mult, ALU.add, r=["xf", kx], w=["xf"])
            if tt_ > 0:
                k.cp("pool", h1b[:, :, 0:2], h1b[:, :, TT:TT + 2], r=["h1b"], w=["h1b"])
            layer_norm(k, ln_tmp, xf, "xf", lnv[:, 0:8], lnv[:, 8:16], h1, "h1",
                       lambda kt: h1b[:, kt, 2:TT + 2], "h1b")
            if tt_ + 1 < NTT:
                load_inputs(tt_ + 1)
            for i in range(22):
                wt, wk = get_chunk()
                for hf in range(2):
                    res = []
                    for part in range(2):
                        pp, kp = k.ps()
                        for kt in range(8):
                            c = (part * 8 + kt) * 128
                            k.mm(pp[:, 0:258], wt[:, c:c + 128], h1b[:, kt, hf * 256:hf * 256 + 258],
                                 start=(kt == 0), stop=(kt == 7), r=[wk, "h1b"], w=[kp])
                        ch = i + 22 * part
                        a0, a1 = c0[part], c1[part]
                        k.act(a0[:, :], pp[:, 2:258], AF.Identity, r=[kp, "cw"], w=[f"c0_{part}"],
                              scale=cw[:, ch, 2:3], bias=cb[:, ch:ch + 1])
                        k.stt("dve", a1[:, :], pp[:, 1:257], cw[:, ch, 1:2], a0[:, :], ALU.mult, ALU.add,
                              r=[kp, "cw", f"c0_{part}"], w=[f"c1_{part}"])
                        k.stt("dve", a0[:, :], pp[:, 0:256], cw[:, ch, 0:1], a1[:, :], ALU.mult, ALU.add,
                              r=[kp, "cw", f"c1_{part}"], w=[f"c0_{part}"])
                        res.append(a0)
                    k.act(sg[:, :], res[0][:, :], AF.Silu, r=["c0_0"], w=["sg"])
                    k.tt("pool", actb[:, i, hf * 256:(hf + 1) * 256], sg[:, :], res[1][:, :], ALU.mult,
                         r=["sg", "c0_1"], w=[f"actb{i}"])
                load_chunk()
            for n in range(8):
                wt, wk = get_chunk()
                pd, kd = k.ps()
                for kt in range(22):
                    k.mm(pd[:, :], wt[:, kt * 128:(kt + 1) * 128], actb[:, kt, :], start=(kt == 0), stop=(kt == 21),
                         r=[wk, f"actb{kt}"], w=[kd])
                load_chunk()
                k.stt("dve", h1[:, n, :], h1[:, n, :], ALPHA, pd[:, :], ALU.mult, ALU.add, r=["h1", kd], w=["h1"])
            layer_norm(k, ln_tmp, h1, "h1", lnv[:, 16:24], lnv[:, 24:32], ot, "ot", None, None)
            k.dma("act", kt_view(T["outT"], tok), ot[:, :, :], r=["ot"], w=[f"outT{tt_}"])
        P.wait_all("sp", [f"outT{t_}" for t_ in range(NTT)])
        P.emit()


def phase_mem(k, T):
    nc, P = k.nc, k.P
    with ExitStack() as es:
        sb = lambda n, s, d: k.sb(es, n, s, d)
        memb = sb("memb", [128, 8, 256], BF16)
        wq = sb("wq_mem", [128, 4, 1024], BF16)
        wkk = sb("wk_mem", [128, 4, 1024], BF16)
        wv = sb("wv_mem", [128, 8, 512], BF16)
        kmT = sb("kmT", [128, 4, 256], BF16)
        vm = sb("vm", [128, 2, 512], BF16)
        ones = sb("ones_mem", [128, 128], BF16)
        xbs = [sb(f"xb2_{i}", [128, 8, TT], BF16) for i in range(2)]
        qm = [sb(f"qm{i}", [128, TT], BF16) for i in range(2)]
        pT = [sb(f"pT{i}", [128, 2, TT], BF16) for i in range(2)]
        rec = sb("rec_mem", [128, TT], F32)
        ob = [sb(f"ob{i}", [128, TT], BF16) for i in range(2)]

        k.dma("pool", memb[:, :, :], T["memT"].rearrange("(kt p) m -> p kt m", p=128), w=["memb"])
        k.dma("pool", wq[:, :, :], T["w_qmem_l"], w=["wq_mem"])
        k.dma("pool", wkk[:, :, :], T["w_kmem_l"], w=["wk_mem"])
        k.dma("pool", wv[:, :, :], T["w_vmem_l"], w=["wv_mem"])
        k.memset("dve", ones[:, :], 1.0, w=["ones_mem"])
        for h in range(4):
            pk, kk = k.ps()
            for kt in range(8):
                k.mm(pk[:, 0:256], wkk[:, h, kt * 128:(kt + 1) * 128], memb[:, kt, :], start=(kt == 0), stop=(kt == 7),
                     r=["wk_mem", "memb"], w=[kk])
            k.cp("act", kmT[:, h, :], pk[:, 0:256], r=[kk], w=["kmT"])
        for mt in range(2):
            pv, kv = k.ps()
            for kt in range(8):
                k.mm(pv[:, :], memb[:, kt, mt * 128:(mt + 1) * 128], wv[:, kt, :], start=(kt == 0), stop=(kt == 7),
                     r=["wv_mem", "memb"], w=[kv])
            k.cp("act", vm[:, mt, :], pv[:, :], r=[kv], w=["vm"])
        sc = 128.0 ** -0.5
        it = 0
        for tt_ in range(NTT):
            tok = slice(tt_ * TT, (tt_ + 1) * TT)
            xb = xbs[tt_ % 2]
            xk = f"xb2_{tt_ % 2}"
            k.dma("pool", xb[:, :, :], kt_view(T["xT"], tok), w=[xk])
            for h in range(4):
                q_, qk = qm[it % 2], f"qm{it % 2}"
                p_, pk_ = pT[it % 2], f"pT{it % 2}"
                o_, ok_ = ob[it % 2], f"ob{it % 2}"
                it += 1
                pq, kq = k.ps()
                for kt in range(8):
                    k.mm(pq[:, :], wq[:, h, kt * 128:(kt + 1) * 128], xb[:, kt, :], start=(kt == 0), stop=(kt == 7),
                         r=["wq_mem", xk], w=[kq])
                k.cp("act", q_[:, :], pq[:, :], r=[kq], w=[qk])
                for mt in range(2):
                    ps_, ks_ = k.ps()
                    k.mm(ps_[:, :], kmT[:, h, mt * 128:(mt + 1) * 128], q_[:, :], r=["kmT", qk], w=[ks_])
                    k.act(p_[:, mt, :], ps_[:, :], AF.Exp, r=[ks_], w=[pk_], scale=sc)
                po, ko = k.ps()
                for mt in range(2):
                    k.mm(po[:, :], vm[:, mt, h * 128:(h + 1) * 128], p_[:, mt, :], start=(mt == 0), stop=(mt == 1),
                         r=["vm", pk_], w=[ko])
                pd, kd = k.ps()
                for mt in range(2):
                    k.mm(pd[:, :], ones[:, :], p_[:, mt, :], start=(mt == 0), stop=(mt == 1),
                         r=["ones_mem", pk_], w=[kd])
                k.recip(rec[:, :], pd[:, :], r=[kd], w=["rec_mem"])
                k.tt("dve", o_[:, :], po[:, :], rec[:, :], ALU.mult, r=[ko, "rec_mem"], w=[ok_])
                k.dma("sp", T["br"][2][h * 128:(h + 1) * 128, tok], o_[:, :], r=[ok_], w=[f"br2_{tt_}_{h}"])
        P.emit()


def cmul(k, eng, outr, outi, ar, ai, br_, bi_, t1, t2, r, w):
    k.tt(eng, t1, ar, br_, ALU.mult, r=r, w=["cm_t1"])
    k.tt(eng, t2, ai, bi_, ALU.mult, r=r, w=["cm_t2"])
    k.tt(eng, outr, t1, t2, ALU.subtract, r=["cm_t1", "cm_t2"], w=w)
    k.tt(eng, t1, ar, bi_, ALU.mult, r=r + ["cm_t1"], w=["cm_t1"])
    k.tt(eng, t2, ai, br_, ALU.mult, r=r + ["cm_t2"], w=["cm_t2"])
    k.tt(eng, outi, t1, t2, ALU.add, r=["cm_t1", "cm_t2"], w=w)


def phase_s5(k, T):
    nc, P = k.nc, k.P
    with ExitStack() as es:
        sbp = lambda n, s, d: k.sb(es, n, s, d)
        es1 = ExitStack()
        sb = lambda n, s, d: k.sb(es1, n, s, d)
        I16 = sbp("I16", [128, 128], BF16)
        J16 = sbp("J16", [128, 128], BF16)
        Sel = sbp("Sel", [128, 64, 128], BF16)
        SelI = sbp("SelI", [128, 64, 128], BF16)
        wu = sbp("wu", [128, 4, 1024], BF16)
        wglu = sbp("wglu", [128, 4, 512], BF16)
        bglu = sbp("bglu", [128, 4], F32)
        LR = sbp("LR", [128, 32, 9], F32)
        LI = sbp("LI", [128, 32, 9], F32)
        Mb = sbp("Mb", [128, 32, 128], BF16)
        BmT = sbp("BmT", [128, 32, 128], BF16)
        Cmb = sbp("Cmb", [128, 32, 128], BF16)
        I32t = sb("I32t", [128, 128], F32)
        Mc = sb("Mc", [128, 128], F32)
        lre = sb("lre", [128, 32], F32)
        lim = sb("lim", [128, 32], F32)
        ldt = sb("ldt", [128, 32], F32)
        dP = sb("dP", [128, 32], F32)
        Br = sb("Br", [128, 32, 16], F32)
        Bi = sb("Bi", [128, 32, 16], F32)
        Cr = sb("Cr", [128, 32, 16], F32)
        Ci = sb("Ci", [128, 32, 16], F32)
        negpi = sb("negpi", [128, 1], F32)
        k.dma("sp", I32t[:, :], T["I128"], w=["I32t"])
        k.dma("pool", I16[:, :], T["I128"], w=["I16"])
        k.dma("pool", J16[:, :], T["J128"], w=["J16"])
        k.dma("sp", Mc[:, :], T["Mc"], w=["Mc"])
        k.dma("pool", Sel[:, :, :], T["Sel"], w=["Sel"])
        k.dma("pool", SelI[:, :, :], T["SelI"], w=["SelI"])
        for nm, t_ in (("lre", lre), ("lim", lim), ("ldt", ldt), ("dP", dP)):
            k.dma("sp", t_[:, :], T[nm], w=[nm])
        for nm, t_ in (("Br", Br), ("Bi", Bi), ("Cr", Cr), ("Ci", Ci)):
            k.dma("sp", t_[:, :, :], T[nm], w=[nm])
        k.dma("pool", wu[:, :, :], T["w_u_l"], w=["wu"])
        k.dma("pool", wglu[:, :, :], T["w_glu_l"], w=["wglu"])
        k.dma("sp", bglu[:, :], T["b_glu_l"], w=["bglu"])
        k.memset("pool", negpi[:, :], SIN_BI, w=["negpi"])

        a_ = sb("s5a", [128, 32], F32)
        th = sb("s5th", [128, 32], F32)
        dt_ = sb("s5dt", [128, 32], F32)
        y_ = sb("sr_y", [128, 32], F32)
        ki_ = sb("sr_k", [128, 32], I32)
        kf_ = sb("sr_kf", [128, 32], F32)
        sn = sb("s5sn", [128, 32], F32)
        cs = sb("s5cs", [128, 32], F32)
        mg = sb("s5mg", [128, 32], F32)
        t1 = sb("cm_t1", [128, 32], F32)
        t2 = sb("cm_t2", [128, 32], F32)
        PR = sb("PR", [128, 32, 9], F32)
        PI = sb("PI", [128, 32, 9], F32)
        NR = sb("NR", [128, 32, 8], F32)
        NI = sb("NI", [128, 32, 8], F32)
        bR = sb("betR", [128, 32], F32)
        bI = sb("betI", [128, 32], F32)
        inv = sb("s5inv", [128, 32], F32)

        k.ts("dve", lre[:, :], lre[:, :], -1e-4, ALU.min, r=["lre"], w=["lre"])
        k.act(dt_[:, :], ldt[:, :], AF.Exp, r=["ldt"], w=["s5dt"])
        k.tt("dve", a_[:, :], lre[:, :], dt_[:, :], ALU.mult, r=["lre", "s5dt"], w=["s5a"])
        k.tt("dve", th[:, :], lim[:, :], dt_[:, :], ALU.mult, r=["lim", "s5dt"], w=["s5th"])
        k.act(mg[:, :], a_[:, :], AF.Exp, r=["s5a"], w=["s5mg"])
        tmp3 = (y_[:, :], ki_[:, :], kf_[:, :])
        sin_reduced(k, sn[:, :], th[:, :], 1.0, False, None, tmp3, negpi[:, 0:1], ["s5th"], ["s5sn"])
        sin_reduced(k, cs[:, :], th[:, :], 1.0, True, None, tmp3, negpi[:, 0:1], ["s5th"], ["s5cs"])
        k.memset("dve", PR[:, :, 0], 1.0, w=["PRI"])
        k.memset("dve", PI[:, :, 0], 0.0, w=["PRI"])
        k.tt("dve", PR[:, :, 1], mg[:, :], cs[:, :], ALU.mult, r=["s5mg", "s5cs"], w=["PRI"])
        k.tt("dve", PI[:, :, 1], mg[:, :], sn[:, :], ALU.mult, r=["s5mg", "s5sn"], w=["PRI"])
        for n in range(2, 9):
            cmul(k, "dve", PR[:, :, n], PI[:, :, n], PR[:, :, n - 1], PI[:, :, n - 1], PR[:, :, 1], PI[:, :, 1],
                 t1[:, :], t2[:, :], ["PRI"], ["PRI"])
        k.cp("dve", LR[:, :, 0], PR[:, :, 8], r=["PRI"], w=["LRI"])
        k.cp("dve", LI[:, :, 0], PI[:, :, 8], r=["PRI"], w=["LRI"])
        for l in range(1, 9):
            cmul(k, "dve", LR[:, :, l], LI[:, :, l], LR[:, :, l - 1], LI[:, :, l - 1], LR[:, :, l - 1], LI[:, :, l - 1],
                 t1[:, :], t2[:, :], ["LRI"], ["LRI"])
        for n in range(8):
            k.act(inv[:, :], a_[:, :], AF.Exp, r=["s5a"], w=["s5inv"], scale=-2.0 * (n + 1))
            k.tt("dve", NR[:, :, n], PR[:, :, n + 1], inv[:, :], ALU.mult, r=["PRI", "s5inv"], w=["NRI"])
            k.stt("dve", NI[:, :, n], PI[:, :, n + 1], -1.0, inv[:, :], ALU.mult, ALU.mult, r=["PRI", "s5inv"], w=["NRI"])
        nr_ = sb("s5nr", [128, 32], F32)
        den = sb("s5den", [128, 32], F32)
        k.ts("dve", nr_[:, :], PR[:, :, 1], -1.0, ALU.add, r=["PRI"], w=["s5nr"])
        k.tt("dve", den[:, :], lre[:, :], lre[:, :], ALU.mult, r=["lre"], w=["s5den"])
        k.tt("dve", t1[:, :], lim[:, :], lim[:, :], ALU.mult, r=["lim"], w=["cm_t1"])
        k.tt("dve", den[:, :], den[:, :], t1[:, :], ALU.add, r=["s5den", "cm_t1"], w=["s5den"])
        k.recip(den[:, :], den[:, :], r=["s5den"], w=["s5den"])
        k.tt("dve", t1[:, :], nr_[:, :], lre[:, :], ALU.mult, r=["s5nr", "lre"], w=["cm_t1"])
        k.tt("dve", t2[:, :], PI[:, :, 1], lim[:, :], ALU.mult, r=["PRI", "lim"], w=["cm_t2"])
        k.tt("dve", t1[:, :], t1[:, :], t2[:, :], ALU.add, r=["cm_t1", "cm_t2"], w=["cm_t1"])
        k.tt("dve", bR[:, :], t1[:, :], den[:, :], ALU.mult, r=["cm_t1", "s5den"], w=["betR"])
        k.tt("dve", t1[:, :], PI[:, :, 1], lre[:, :], ALU.mult, r=["PRI", "lre"], w=["cm_t1"])
        k.tt("dve", t2[:, :], nr_[:, :], lim[:, :], ALU.mult, r=["s5nr", "lim"], w=["cm_t2"])
        k.tt("dve", t1[:, :], t1[:, :], t2[:, :], ALU.subtract, r=["cm_t1", "cm_t2"], w=["cm_t1"])
        k.tt("dve", bI[:, :], t1[:, :], den[:, :], ALU.mult, r=["cm_t1", "s5den"], w=["betI"])
        bbr = sb("bbr", [128, 32, 16], F32)
        bbi = sb("bbi", [128, 32, 16], F32)
        u1 = sb("s5u1", [128, 32, 16], F32)
        u2 = sb("s5u2", [128, 32, 16], F32)
        bc16 = lambda t_: t_[:, :].unsqueeze(2).to_broadcast([128, 32, 16])
        cmul(k, "dve", bbr[:, :, :], bbi[:, :, :], bc16(bR), bc16(bI), Br[:, :, :], Bi[:, :, :],
             u1[:, :, :], u2[:, :, :], ["betR", "betI", "Br", "Bi"], ["bb"])
        Z = sb("Zm", [128, 32, 8, 16], F32)
        W = sb("Wm", [128, 32, 8, 16], F32)
        Cm = sb("Cm", [128, 32, 8, 16], F32)
        v1 = sb("s5v1", [128, 32, 8, 16], F32)
        v2 = sb("s5v2", [128, 32, 8, 16], F32)

        def big_cmul(dst, dkey, Xr, Xi, Tr, Ti, neg_im, rk):
            xb_ = lambda t_, lo, hi: t_[lo:hi, :, :].unsqueeze(2).to_broadcast([hi - lo, 32, 8, 16])
            tb_ = lambda t_, lo, hi: t_[lo:hi].unsqueeze(3).to_broadcast([hi - lo, 32, 8, 16])
            k.tt("dve", v1[0:64], xb_(Xr, 0, 64), tb_(Tr, 0, 64), ALU.mult, r=rk, w=["s5v1"])
            k.tt("dve", v2[0:64], xb_(Xi, 0, 64), tb_(Ti, 0, 64), ALU.mult, r=rk, w=["s5v2"])
            k.tt("dve", dst[0:64], v1[0:64], v2[0:64], ALU.subtract, r=["s5v1", "s5v2"], w=[dkey])
            k.tt("dve", v1[64:128], xb_(Xi, 64, 128), tb_(Tr, 64, 128), ALU.mult, r=rk + ["s5v1"], w=["s5v1"])
            k.tt("dve", v2[64:128], xb_(Xr, 64, 128), tb_(Ti, 64, 128), ALU.mult, r=rk + ["s5v2"], w=["s5v2"])
            if neg_im:
                k.stt("dve", dst[64:128], v1[64:128], -1.0, v2[64:128], ALU.mult, ALU.subtract,
                      r=["s5v1", "s5v2"], w=[dkey])
            else:
                k.tt("dve", dst[64:128], v1[64:128], v2[64:128], ALU.add, r=["s5v1", "s5v2"], w=[dkey])

        big_cmul(Z, "Zm", bbr, bbi, NR[:, :, 0:8], NI[:, :, 0:8], False, ["bb", "NRI"])
        PRrev = sb("PRrev", [128, 32, 8], F32)
        PIrev = sb("PIrev", [128, 32, 8], F32)
        for tau in range(8):
            k.cp("pool", PRrev[:, :, tau], PR[:, :, 7 - tau], r=["PRI"], w=["Prev"])
            k.cp("pool", PIrev[:, :, tau], PI[:, :, 7 - tau], r=["PRI"], w=["Prev"])
        big_cmul(W, "Wm", bbr, bbi, PRrev[:, :, :], PIrev[:, :, :], False, ["bb", "Prev"])
        big_cmul(Cm, "Cm", Cr, Ci, PR[:, :, 1:9], PI[:, :, 1:9], True, ["Cr", "Ci", "PRI"])

        mt_ = sb("s5mt", [128, 4, 128], F32)
        k.cp("act", Cmb[:, :, :], Cm[:, :, :, :].rearrange("p g r i -> p g (r i)"), r=["Cm"], w=["Cmb"])
        for g4 in range(8):
            pm, km = k.ps()
            for gg in range(4):
                g = g4 * 4 + gg
                k.mm(pm[:, gg * 128:(gg + 1) * 128], Z[:, g].rearrange("p t j -> p (t j)"),
                     Cm[:, g].rearrange("p r i -> p (r i)"), r=["Zm", "Cm"], w=[km])
            k.tt("dve", mt_[:, :, :], pm[:, :].rearrange("p (a b) -> p a b", a=4),
                 Mc[:, :].unsqueeze(1).to_broadcast([128, 4, 128]), ALU.mult, r=[km, "Mc"], w=["s5mt"])
            for gg in range(4):
                g = g4 * 4 + gg
                k.stt("dve", Mb[:, g, :], I32t[:, :], dP[:, g:g + 1], mt_[:, gg, :], ALU.mult, ALU.add,
                      r=["I32t", "dP", "s5mt"], w=["Mb"])
            pw, kw = k.ps()
            for gg in range(4):
                g = g4 * 4 + gg
                k.tr(pw[:, gg * 128:(gg + 1) * 128], W[:, g].rearrange("p t j -> p (t j)"), I32t[:, :],
                     r=["Wm", "I32t"], w=[kw])
            k.cp("act", BmT[:, g4 * 4:(g4 + 1) * 4, :], pw[:, :].rearrange("p (a b) -> p a b", a=4), r=[kw], w=["BmT"])

        P.emit()
        es1.close()
        sb = sbp
        uTp = sb("uTp", [128, 4, 8, 512], BF16)
        ysT = sb("ysT", [128, 4, S], BF16)
        xbs = [sb(f"xb1_{i}", [128, 8, TT], BF16) for i in range(2)]
        for tt_ in range(NTT):
            tok = slice(tt_ * TT, (tt_ + 1) * TT)
            xb = xbs[tt_ % 2]
            xk = f"xb1_{tt_ % 2}"
            k.dma("pool", xb[:, :, :], kt_view(T["xT"], tok), w=[xk])
            for t in range(4):
                pu, ku = k.ps()
                for kt in range(8):
                    k.mm(pu[:, :], wu[:, t, kt * 128:(kt + 1) * 128], xb[:, kt, :], start=(kt == 0), stop=(kt == 7),
                         r=["wu", xk], w=[ku])
                k.cp("act", uTp[:, t, :, tt_ * 64:(tt_ + 1) * 64], pu[:, :].rearrange("p (c t) -> p t c", t=8),
                     r=[ku], w=[f"uTp{t}"])

        Ug = [sb(f"Ug{i}", [128, 512], BF16) for i in range(8)]
        Hb = [sb(f"Hb{i}", [128, 513], BF16) for i in range(8)]
        Yg = [sb(f"Yg{i}", [128, 512], BF16) for i in range(8)]
        Rt = [sb(f"Rt{i}", [128, 128], F32) for i in range(2)]
        Rj = [sb(f"Rj{i}", [128, 128], F32) for i in range(2)]
        Rm = [sb(f"Rm{i}", [128, 128], BF16) for i in range(4)]
        for i in range(8):
            k.memset("pool", Hb[i][:, 0:1], 0.0, w=[f"Hb{i}"])
        ri = 0
        for t in range(4):
            for gl in range(8):
                g = 8 * t + gl
                pu, ku = k.ps()
                for tau in range(8):
                    k.mm(pu[:, :], Sel[:, gl * 8 + tau, :], uTp[:, t, tau, :], start=(tau == 0), stop=(tau == 7),
                         r=["Sel", f"uTp{t}"], w=[ku])
                k.cp("act", Ug[gl][:, :], pu[:, :], r=[ku], w=[f"Ug{gl}"])
                ph, kh = k.ps()
                k.mm(ph[:, :], BmT[:, g, :], Ug[gl][:, :], r=["BmT", f"Ug{gl}"], w=[kh])
                k.cp("act", Hb[gl][:, 1:513], ph[:, :], r=[kh], w=[f"Hb{gl}"])
            for l in range(9):
                d = 1 << l
                for gl in range(8):
                    g = 8 * t + gl
                    rt, rtk = Rt[ri % 2], f"Rt{ri % 2}"
                    rm, rmk = Rm[ri % 4], f"Rm{ri % 4}"
                    rj, rjk = Rj[ri % 2], f"Rj{ri % 2}"
                    ri += 1
                    k.ts("dve", rt[:, :], I16[:, :], LR[:, g, l:l + 1], ALU.mult, r=["I16", "LRI"], w=[rtk])
                    k.stt("dve", rm[:, :], J16[:, :], LI[:, g, l:l + 1], rt[:, :], ALU.mult, ALU.add,
                          r=["J16", "LRI", rtk], w=[rmk])
                    ps_, ks_ = k.ps()
                    k.mm(ps_[:, 0:512 - d], rm[:, :], Hb[gl][:, 1:513 - d], r=[rmk, f"Hb{gl}"], w=[ks_])
                    k.tt("dve", Hb[gl][:, 1 + d:513], Hb[gl][:, 1 + d:513], ps_[:, 0:512 - d], ALU.add,
                         r=[ks_, f"Hb{gl}"], w=[f"Hb{gl}"])
            for gl in range(8):
                g = 8 * t + gl
                py, ky = k.ps()
                k.mm(py[:, :], Mb[:, g, :], Ug[gl][:, :], start=True, stop=False, r=["Mb", f"Ug{gl}"], w=[ky])
                k.mm(py[:, :], Cmb[:, g, :], Hb[gl][:, 0:512], start=False, stop=True, r=["Cmb", f"Hb{gl}"], w=[ky])
                k.act(Yg[gl][:, :], py[:, :], AF.Gelu_apprx_tanh, r=[ky], w=[f"Yg{gl}"])
            for r_ in range(8):
                pt, kt_ = k.ps()
                for gl in range(8):
                    k.mm(pt[:, :], SelI[:, gl * 8 + r_, :], Yg[gl][:, :], start=(gl == 0), stop=(gl == 7),
                         r=["SelI", f"Yg{gl}"], w=[kt_])
                k.cp("act", ysT[:, t, :].rearrange("p (c r) -> p r c", r=8)[:, r_, :], pt[:, :], r=[kt_], w=[f"ysT{t}"])
        sgl = [sb(f"sgl{i}", [128, TT], F32) for i in range(2)]
        og = [sb(f"og{i}", [128, TT], BF16) for i in range(2)]
        it = 0
        for tt_ in range(NTT):
            tok = slice(tt_ * TT, (tt_ + 1) * TT)
            for n in range(4):
                pz, kz = k.ps()
                for kt in range(4):
                    k.mm(pz[:, :], wglu[:, n, kt * 128:(kt + 1) * 128], ysT[:, kt, tok], start=(kt == 0), stop=(kt == 3),
                         r=["wglu", f"ysT{kt}"], w=[kz])
                s_, sk = sgl[it % 2], f"sgl{it % 2}"
                o_, ok_ = og[it % 2], f"og{it % 2}"
                it += 1
                k.act(s_[:, :], pz[:, :], AF.Sigmoid, r=[kz, "bglu"], w=[sk], bias=bglu[:, n:n + 1])
                k.tt("dve", o_[:, :], ysT[:, n, tok], s_[:, :], ALU.mult, r=[f"ysT{n}", sk], w=[ok_])
                k.dma("sp", T["br"][1][n * 128:(n + 1) * 128, tok], o_[:, :], r=[ok_], w=[f"br1_{tt_}_{n}"])
        P.emit()


def phase_att(k, T):
    nc, P = k.nc, k.P
    with ExitStack() as es:
        sb = lambda n, s, d: k.sb(es, n, s, d)
        qT = sb("qT", [128, 4, S], BF16)
        kT = sb("kT", [128, 2, S], BF16)
        qiT = sb("qiT", [128, 2, S], BF16)
        kiT = sb("kiT", [128, S], BF16)
        V1 = sb("V1", [128, 32, 2, 65], BF16)
        widx = sb("widx", [128, 32, 4], F32)
        I16 = sb("I16a", [128, 128], BF16)
        k.dma("pool", I16[:, :], T["I128"], w=["I16a"])
        k.memset("dve", V1[:, :, :, 64:65], 1.0, w=["V1"])
        dests = [(qT, 0), (qT, 1), (qT, 2), (qT, 3), (kT, 0), (kT, 1), (qiT, 0), (qiT, 1), (kiT, None)]
        dkeys = ["qT", "qT", "qT", "qT", "kT", "kT", "qiT", "qiT", "kiT"]
        with ExitStack() as es2:
            sb2 = lambda n, s, d: k.sb(es2, n, s, d)
            watt = sb2("watt", [128, 18, 1024], BF16)
            wvw = sb2("wvw", [128, 8, 132], BF16)
            cosTs = [sb2(f"cosT{i}", [128, TT], F32) for i in range(2)]
            sinTs = [sb2(f"sinT{i}", [128, TT], F32) for i in range(2)]
            posi = sb2("posi", [128, TT], I32)
            posf = sb2("posf", [128, TT], F32)
            y_ = sb2("sr_y3", [128, TT], F32)
            ki_ = sb2("sr_k3", [128, TT], I32)
            kf_ = sb2("sr_kf3", [128, TT], F32)
            ifr = sb2("ifr", [128, 2], F32)
            negpi = sb2("negpi3", [128, 1], F32)
            xbs = [sb2(f"xb3_{i}", [128, 8, TT], BF16) for i in range(2)]
            r1 = [sb2(f"r1_{i}", [128, TT], F32) for i in range(2)]
            r2 = [sb2(f"r2_{i}", [128, TT], F32) for i in range(2)]
            for i in range(18):
                k.dma("pool", watt[:, i, :], T["w_att_l"][i], w=[f"watt{i}"])
            k.dma("pool", wvw[:, :, :], T["w_vw_l"], w=["wvw"])
            k.dma("sp", ifr[:, :], T["ifr"], w=["ifr"])
            k.memset("pool", negpi[:, :], SIN_BI, w=["negpi"])
            tmp3 = (y_[:, :], ki_[:, :], kf_[:, :])
            it = 0
            for tt_ in range(NTT):
                tok = slice(tt_ * TT, (tt_ + 1) * TT)
                xb = xbs[tt_ % 2]
                xk = f"xb3_{tt_ % 2}"
                k.dma("pool", xb[:, :, :], kt_view(T["xT"], tok), w=[xk])
                cosT, ck = cosTs[tt_ % 2], f"cosT{tt_ % 2}"
                sinT, sk_ = sinTs[tt_ % 2], f"sinT{tt_ % 2}"
                k.dma("sp", posi[:, :], T["pos"][:, tok].partition_broadcast(128), w=["posi"])
                k.cp("dve", posf[:, :], posi[:, :], r=["posi"], w=["posf"])
                sin_reduced(k, sinT[:, :], posf[:, :], ifr[:, 0:1], False, None, tmp3, negpi[:, 0:1], ["posf", "ifr"], [sk_])
                sin_reduced(k, cosT[:, :], posf[:, :], ifr[:, 0:1], True, None, tmp3, negpi[:, 0:1], ["posf", "ifr"], [ck])
                k.ts("dve", sinT[:, :], sinT[:, :], ifr[:, 1:2], ALU.mult, r=[sk_, "ifr"], w=[sk_])
                for i in range(9):
                    p1, k1 = k.ps()
                    for kt in range(8):
                        k.mm(p1[:, :], watt[:, 2 * i, kt * 128:(kt + 1) * 128], xb[:, kt, :], start=(kt == 0), stop=(kt == 7),
                             r=[f"watt{2 * i}", xk], w=[k1])
                    p2, k2 = k.ps()
                    for kt in range(8):
                        k.mm(p2[:, :], watt[:, 2 * i + 1, kt * 128:(kt + 1) * 128], xb[:, kt, :], start=(kt == 0), stop=(kt == 7),
                             r=[f"watt{2 * i + 1}", xk], w=[k2])
                    a1, a1k = r1[it % 2], f"r1_{it % 2}"
                    a2, a2k = r2[it % 2], f"r2_{it % 2}"
                    it += 1
                    k.tt("dve", a1[:, :], p1[:, :], cosT[:, :], ALU.mult, r=[k1, ck], w=[a1k])
                    k.tt("dve", a2[:, :], p2[:, :], sinT[:, :], ALU.mult, r=[k2, sk_], w=[a2k])
                    dt_, di = dests[i]
                    dst = dt_[:, di, tok] if di is not None else dt_[:, tok]
                    k.tt("pool", dst, a1[:, :], a2[:, :], ALU.add, r=[a1k, a2k], w=[dkeys[i]])
                for sbk in range(4):
                    sblk = tt_ * 4 + sbk
                    pv, kv = k.ps()
                    for kt in range(8):
                        k.mm(pv[:, 0:132], xb[:, kt, sbk * 128:(sbk + 1) * 128], wvw[:, kt, :], start=(kt == 0), stop=(kt == 7),
                             r=["wvw", xk], w=[kv])
                    k.cp("act", V1[:, sblk, :, 0:64], pv[:, 0:128].rearrange("p (g d) -> p g d", g=2), r=[kv], w=["V1"])
                    k.act(widx[:, sblk, :], pv[:, 128:132], AF.Copy, r=[kv], w=["widx"], scale=0.5)
            P.emit()

        scoreA = sb("scoreA", [128, S], F32)
        zr = sb("zr", [128, S], F32)
        zm = sb("zm", [128, S], BF16)
        junkA = sb("junkA", [128, S], BF16)
        junkD = sb("junkD", [128, S], BF16)
        masks = [sb(f"mask{i}", [128, S], BF16) for i in range(2)]
        maskTs = [sb(f"maskT{i}", [128, S], BF16) for i in range(2)]
        rl = [sb(f"rl{i}", [128, 512], F32) for i in range(2)]
        Pt = [sb(f"Pt{i}", [128, 512], BF16) for i in range(3)]
        KB = 24
        st = sb("selst", [128, 16], F32)
        nms = [sb(f"nm{i}", [128, 1], F32) for i in range(2)]
        ssum = sb("ssum", [128, 1], F32)
        gsg = sb("gsg", [128, 1], F32)
        nwrow = sb("nwrow", [128, KB], F32)
        nwtab = sb("nwtab", [128, KB], F32)
        nw2tab = sb("nw2tab", [128, KB], F32)
        thrc = sb("thrc", [128, 1], F32)
        rec = sb("rec_a", [128, 8], F32)
        attn = sb("attn_tm", [128, 8, 64], BF16)
        aT = [sb(f"aT{i}", [128, 4, 128], BF16) for i in range(2)]
        k.dma("sp", nwrow[:, :], T["nwrow"].partition_broadcast(128), w=["nwrow"])
        k.memset("pool", thrc[:, :], -1.0e29, w=["thrc"])
        psO = k.psO
        psB = k.psB
        ring_save = k.ring
        k.ring = k.ring[:5]

        def indexer(b):
            N = 128 * (b + 1)
            tb = slice(b * 128, (b + 1) * 128)
            nch = (N + 511) // 512
            for c in range(nch):
                c0_ = c * 512
                cw_ = min(512, N - c0_)
                for h in range(4):
                    base = 64 * (h % 2)
                    pl, kl = k.ps()
                    k.mm(pl[:, 0:cw_], qiT[base:base + 64, h // 2, tb], kiT[base:base + 64, c0_:c0_ + cw_],
                         r=["qiT", "kiT"], w=[kl])
                    t_, tk = rl[(c * 4 + h) % 2], f"rl{(c * 4 + h) % 2}"
                    k.act(t_[:, 0:cw_], pl[:, 0:cw_], AF.Relu, r=[kl], w=[tk], scale=0.125)
                    if h == 0:
                        k.ts("dve", scoreA[:, c0_:c0_ + cw_], t_[:, 0:cw_], widx[:, b, h:h + 1], ALU.mult,
                             r=[tk, "widx"], w=["scoreA"])
                    else:
                        k.stt("dve", scoreA[:, c0_:c0_ + cw_], t_[:, 0:cw_], widx[:, b, h:h + 1], scoreA[:, c0_:c0_ + cw_],
                              ALU.mult, ALU.add, r=[tk, "widx", "scoreA"], w=["scoreA"])
            if b >= 2:
                P.op("dve", lambda e, o=junkD[:, 0:N], i_=scoreA[:, 0:N], a=st[:, 12:13]:
                     e.tensor_scalar(out=o, in0=i_, scalar1=1.0, scalar2=None, op0=ALU.mult, op1=ALU.max, accum_out=a),
                     r=["scoreA"], w=["junkD", "st_amax1"])
                P.op("dve", lambda e, o=junkD[:, 0:N], i_=scoreA[:, 0:N], a=st[:, 13:14]:
                     e.tensor_scalar(out=o, in0=i_, scalar1=-1.0, scalar2=None, op0=ALU.mult, op1=ALU.max, accum_out=a),
                     r=["scoreA"], w=["junkD", "st_amax2"])
                k.tt("dve", st[:, 0:1], st[:, 12:13], st[:, 13:14], ALU.max, r=["st_amax1", "st_amax2"], w=["st_amax"])
            k.memset("dve", scoreA[0:64, N - 64:N], -1.0e30, w=["scoreA"])

        def select(b):
            N = 128 * (b + 1)
            nsb = b + 1
            mask, mk = masks[b % 2], f"mask{b % 2}"
            maskT, mtk = maskTs[b % 2], f"maskT{b % 2}"
            sc = scoreA[:, 0:N]
            if b >= 2:
                amax, cp, cpz, isA, isC, lo0, hi0, w0 = (st[:, i:i + 1] for i in range(8))
                need, t0_, t1_, thr_ = (st[:, i:i + 1] for i in range(8, 12))
                cnt = lambda o, op0, key: P.op(
                    "dve", lambda e: e.tensor_scalar(out=junkD[:, 0:N], in0=sc, scalar1=0.0, scalar2=None, op0=op0,
                                                     op1=ALU.add, accum_out=o), r=["scoreA"], w=["junkD", key])
                cnt(cp, ALU.is_gt, "st_cp")
                cnt(cpz, ALU.is_ge, "st_cpz")
                k.ts("dve", isA, cp, 255.5, ALU.is_ge, r=["st_cp"], w=["st_isA"])
                k.ts("dve", isC, cpz, 255.5, ALU.is_lt, r=["st_cpz"], w=["st_isC"])
                k.ts("dve", t0_, amax, -1.0001, ALU.mult, -1.0e-30, ALU.add, r=["st_amax"], w=["st_t0"])
                k.tt("dve", lo0, isC, t0_, ALU.mult, r=["st_isC", "st_t0"], w=["st_lo0"])
                k.tt("dve", t1_, isA, amax, ALU.mult, r=["st_isA", "st_amax"], w=["st_t1"])
                k.stt("dve", hi0, isC, -2.0e-38, t1_, ALU.mult, ALU.add, r=["st_isC", "st_t1"], w=["st_hi0"])
                k.tt("dve", w0, hi0, lo0, ALU.subtract, r=["st_hi0", "st_lo0"], w=["st_w0"])
                k.tt("dve", nwtab[:, :], nwrow[:, :], w0.to_broadcast([128, KB]), ALU.mult, r=["nwrow", "st_w0"], w=["nwtab"])
                k.ts("dve", nw2tab[:, :], nwtab[:, :], -2.0, ALU.mult, r=["nwtab"], w=["nwtab"])
                k.tt("dve", t0_, isA, isC, ALU.add, r=["st_isA", "st_isC", "st_t0"], w=["st_t0"])
                k.ts("dve", t0_, t0_, -1.0, ALU.mult, 1.0, ALU.add, r=["st_t0"], w=["st_t0"])
                k.ts("dve", t1_, cp, -1.0, ALU.mult, 256.0, ALU.add, r=["st_cp", "st_t1"], w=["st_t1"])
                k.tt("dve", need, t0_, t1_, ALU.mult, r=["st_t0", "st_t1"], w=["st_need"])
                k.cp("dve", nms[0][:, :], lo0, r=["st_lo0"], w=["nm0"])
                for it_ in range(KB):
                    cur, ck_ = nms[it_ % 2], f"nm{it_ % 2}"
                    nxt, nk_ = nms[(it_ + 1) % 2], f"nm{(it_ + 1) % 2}"
                    k.ts("dve", gsg[:, :], cur[:, :], nw2tab[:, it_:it_ + 1], ALU.add, r=[ck_, "nwtab"], w=["gsg"])
                    P.op("dve", lambda e: e.tensor_scalar(
                        out=junkD[:, 0:N], in0=sc, scalar1=gsg[:, 0:1], scalar2=None, op0=ALU.is_gt, op1=ALU.add,
                        accum_out=ssum[:, 0:1]), r=["scoreA", "gsg"], w=["junkD", "ssum"])
                    k.ts("dve", ssum[:, :], ssum[:, :], 255.5, ALU.is_lt, NEG_FILL, ALU.mult, r=["ssum"], w=["ssum"])
                    k.ts("dve", nxt[:, :], ssum[:, :], gsg[:, 0:1], ALU.add, cur[:, 0:1], ALU.max,
                         r=["ssum", "gsg", ck_], w=[nk_])
                fin, fk_ = nms[KB % 2], f"nm{KB % 2}"
                thr_ = fin[:, 0:1]
                k.ts("dve", mask[:, 0:N], sc, thr_, ALU.is_gt, r=["scoreA", fk_], w=[mk])
                k.ts("dve", zm[:, 0:N], sc, 0.0, ALU.is_equal, r=["scoreA"], w=["zm"])
                P.op("dve", lambda e: e.tensor_tensor_scan(out=zr[:, 0:N], data0=zm[:, 0:N], data1=zm[:, 0:N], initial=0.0,
                                                           op0=ALU.add, op1=ALU.max), r=["zm"], w=["zr"])
                k.stt("dve", zm[:, 0:N], zr[:, 0:N], need, zm[:, 0:N], ALU.is_le, ALU.mult, r=["zr", "st_need", "zm"], w=["zm"])
                k.tt("pool", mask[:, 0:N], mask[:, 0:N], zm[:, 0:N], ALU.add, r=[mk, "zm"], w=[mk])
            else:
                k.ts("dve", mask[:, 0:N], sc, thrc[:, 0:1], ALU.is_ge, r=["scoreA", "thrc"], w=[mk])
            for j0 in range(0, nsb, 8):
                nj = min(8, nsb - j0)
                pb, kb = psB
                for jj in range(nj):
                    j = j0 + jj
                    k.tr(pb[:, jj * 128:(jj + 1) * 128], mask[:, j * 128:(j + 1) * 128], I16[:, :], r=[mk, "I16a"], w=[kb])
                k.cp("act", maskT[:, j0 * 128:(j0 + nj) * 128], pb[:, 0:nj * 128], r=[kb], w=[mtk])

        def attend(b):
            nsb = b + 1
            tb = slice(b * 128, (b + 1) * 128)
            maskT, mtk = maskTs[b % 2], f"maskT{b % 2}"
            pi_ = 0
            for h in range(8):
                g = h // 4
                base = 64 * (h % 2)
                po, ko = psO[h // 4]
                for j0 in range(0, nsb, 4):
                    nj = min(4, nsb - j0)
                    ps_, ks_ = k.ps()
                    for jj in range(nj):
                        j = j0 + jj
                        k.mm(ps_[:, jj * 128:(jj + 1) * 128], kT[base:base + 64, g, j * 128:(j + 1) * 128],
                             qT[base:base + 64, h // 2, tb], r=["kT", "qT"], w=[ks_])
                    p_, pk_ = Pt[pi_ % 3], f"Pt{pi_ % 3}"
                    pi_ += 1
                    k.act(p_[:, 0:nj * 128], ps_[:, 0:nj * 128], AF.Exp, r=[ks_], w=[pk_], scale=0.125)
                    k.tt("pool", p_[:, 0:nj * 128], p_[:, 0:nj * 128], maskT[:, j0 * 128:(j0 + nj) * 128], ALU.mult,
                         r=[pk_, mtk], w=[pk_])
                    for jj in range(nj):
                        j = j0 + jj
                        k.mm(po[:, (h % 4) * 65:(h % 4) * 65 + 65], p_[:, jj * 128:(jj + 1) * 128], V1[:, j, g, :],
                             start=(j == 0), stop=(j == nsb - 1), r=[pk_, "V1"], w=[ko])

        def finish(b):
            tb = slice(b * 128, (b + 1) * 128)
            for hb in range(2):
                po, ko = psO[hb]
                pv = po[:, 0:260].rearrange("p (h e) -> p h e", h=4)
                k.recip(rec[:, hb * 4:(hb + 1) * 4], pv[:, :, 64], r=[ko], w=["rec_a"])
                k.tt("dve", attn[:, hb * 4:(hb + 1) * 4, :], pv[:, :, 0:64],
                     rec[:, hb * 4:(hb + 1) * 4].unsqueeze(2).to_broadcast([128, 4, 64]), ALU.mult,
                     r=[ko, "rec_a"], w=["attn_tm"])
            pb, kb = psB
            a_, ak = aT[b % 2], f"aT{b % 2}"
            af = attn[:, :, :].rearrange("p h d -> p (h d)")
            for t in range(4):
                k.tr(pb[:, t * 128:(t + 1) * 128], af[:, t * 128:(t + 1) * 128], I16[:, :], r=["attn_tm", "I16a"], w=[kb])
            k.cp("act", a_[:, :, :], pb[:, 0:512].rearrange("p (t s) -> p t s", t=4), r=[kb], w=[ak])
            k.dma("sp", kt_view(T["br"][0], tb), a_[:, :, :], r=[ak], w=[f"br0_{b // 4}_{b % 4}"])

        import os as _os
        NB = int(_os.environ.get("ATT_NB", S // 128))
        indexer(0)
        select(0)
        for b in range(NB):
            if b + 1 < NB:
                indexer(b + 1)
            attend(b)
            if b + 1 < NB:
                select(b + 1)
            finish(b)
        k.ring = ring_save
        P.emit()


def build_program(shapes, phases=("s5", "mem", "att", "mix"), br_mode="internal"):
    nc = bass.Bass("TRN2", target_bir_lowering=False)
    T = {}
    for name, (shape, dt) in shapes.items():
        T[name] = nc.dram_tensor(name, list(shape), dt, kind="ExternalInput").ap()
    T["outT"] = nc.dram_tensor("outT", [D, S], F32, kind="ExternalOutput").ap()
    if br_mode == "internal":
        brt = nc.dram_tensor("br", [3, 512, S], BF16).ap()
    elif br_mode == "output":
        brt = nc.dram_tensor("br", [3, 512, S], BF16, kind="ExternalOutput").ap()
    else:
        brt = nc.dram_tensor("br", [3, 512, S], BF16, kind="ExternalInput").ap()
    T["br"] = [brt[i] for i in range(3)]
    T["wA_bf"] = nc.dram_tensor("wA_bf", [8, 128, 4608], BF16).ap()
    T["wO_bf"] = nc.dram_tensor("wO_bf", [8, 128, 1024], BF16).ap()
    T["wU_bf"] = nc.dram_tensor("wU_bf", [22, 128, 2048], BF16).ap()
    T["wD_bf"] = nc.dram_tensor("wD_bf", [8, 128, 2816], BF16).ap()
    with ExitStack() as es:
        P = Prog(nc, es)
        k = K(nc, P)
        for i in range(5):
            t = es.enter_context(nc.psum_tensor(f"psf{i}", [128, 512], F32))
            k.ring.append((t, f"psf{i}"))
        o0 = es.enter_context(nc.psum_tensor("psf5", [128, 512], F32))
        o1 = es.enter_context(nc.psum_tensor("psf6", [128, 512], F32))
        pb = es.enter_context(nc.psum_tensor("psb", [128, 1024], BF16))
        k.psO = [(o0, "psf5"), (o1, "psf6")]
        k.psB = (pb, "psb")
        k.ring.append((o0, "psf5"))
        k.ring.append((o1, "psf6"))
        if "mix" in phases:
            for i in range(8):
                k.dma("pool", T["wA_bf"][i], T["wA_l"][i], w=[f"wbf_A{i}"])
            for i in range(8):
                k.dma("pool", T["wO_bf"][i], T["wO_l"][i], w=[f"wbf_O{i}"])
            for i in range(22):
                k.dma("pool", T["wU_bf"][i], T["wU_l"][i], w=[f"wbf_U{i}"])
            for i in range(8):
                k.dma("pool", T["wD_bf"][i], T["wD_l"][i], w=[f"wbf_Dn{i}"])
        if "s5" in phases:
            phase_s5(k, T)
        if "mem" in phases:
            phase_mem(k, T)
        if "att" in phases:
            phase_att(k, T)
        if "mix" in phases:
            phase_mix(k, T)
        else:
            P.wait_all("sp", [kk_ for kk_ in P.lastw if kk_.startswith("br")])
            P.emit()
    return nc


def tile_lhsT(W):
    Kd, N = W.shape
    return np.ascontiguousarray(W.reshape(Kd // 128, 128, N // 128, 128).transpose(2, 1, 0, 3).reshape(N // 128, 128, (Kd // 128) * 128))


def rhs_layout(W):
    Kd, N = W.shape
    return np.ascontiguousarray(W.reshape(Kd // 128, 128, N).transpose(1, 0, 2))


def prep_shared(inp):
    f = np.float32
    sh = {}
    w_in = inp["w_in"][0]
    def hc(base, h):
        return base + h * 64 + np.arange(64)
    def sw(c):
        return np.concatenate([c[32:], c[:32]])
    tiles = []
    for j in range(4):
        a, b = hc(0, 2 * j), hc(0, 2 * j + 1)
        tiles += [np.concatenate([a, b]), np.concatenate([sw(a), sw(b)])]
    for g in range(2):
        a = hc(512, g)
        tiles += [np.concatenate([a, a]), np.concatenate([sw(a), sw(a)])]
    for j in range(2):
        a, b = hc(768, 2 * j), hc(768, 2 * j + 1)
        tiles += [np.concatenate([a, b]), np.concatenate([sw(a), sw(b)])]
    a = 1024 + np.arange(64)
    tiles += [np.concatenate([a, a]), np.concatenate([sw(a), sw(a)])]
    cols = np.concatenate(tiles)
    sh["w_att_l"] = tile_lhsT(w_in[:, cols])
    sh["w_vw_l"] = rhs_layout(np.concatenate([w_in[:, 640:768], w_in[:, 1088:1092]], axis=1))
    sh["w_u_l"] = np.ascontiguousarray(tile_lhsT(w_in[:, 1092:1604]).transpose(1, 0, 2))
    sh["w_qmem_l"] = np.ascontiguousarray(tile_lhsT(w_in[:, 1604:2116]).transpose(1, 0, 2))
    wkv = inp["w_mem_kv"][0]
    sh["w_kmem_l"] = np.ascontiguousarray(tile_lhsT(wkv[:, 0:512]).transpose(1, 0, 2))
    sh["w_vmem_l"] = rhs_layout(wkv[:, 512:1024])
    sh["w_glu_l"] = np.ascontiguousarray(tile_lhsT(inp["w_glu"][0]).transpose(1, 0, 2))
    sh["b_glu_l"] = np.ascontiguousarray(inp["b_glu"][0].reshape(4, 128).T)
    p = np.arange(128)
    inv_freq = (10000.0 ** (-(np.arange(32, dtype=np.float32)) / np.float32(32))).astype(np.float32)
    ifr = np.zeros((128, 2), f)
    ifr[:, 0] = (inv_freq[p % 32].astype(np.float64) / TWO_PI).astype(f)
    ifr[:, 1] = np.where((p % 64) < 32, -1.0, 1.0)
    sh["ifr"] = ifr
    sh["nwrow"] = (-(2.0 ** -(np.arange(24, dtype=np.float64) + 2.0))).astype(f).reshape(1, 24)
    dup = lambda a_: np.ascontiguousarray(np.concatenate([a_, a_], axis=0).astype(f))
    sh["lre"] = dup(inp["s5_lam_re"][0].T)
    sh["lim"] = dup(inp["s5_lam_im"][0].T)
    sh["ldt"] = np.ascontiguousarray(np.broadcast_to(inp["s5_log_dt"][0][None, :], (128, 32)).astype(f))
    sh["Br"] = dup(inp["s5_b_re"][0].transpose(1, 0, 2))
    sh["Bi"] = dup(inp["s5_b_im"][0].transpose(1, 0, 2))
    sh["Cr"] = dup(inp["s5_c_re"][0].transpose(2, 0, 1))
    sh["Ci"] = dup(inp["s5_c_im"][0].transpose(2, 0, 1))
    sh["dP"] = np.ascontiguousarray(np.tile(inp["s5_d"][0].reshape(32, 16).T, (8, 1)).astype(f))
    sh["I128"] = np.eye(128, dtype=f)
    J = np.zeros((128, 128), f)
    for q in range(64):
        J[q, 64 + q] = 1.0
        J[64 + q, q] = -1.0
    sh["J128"] = J
    tau = np.arange(128) // 16
    sh["Mc"] = (tau[None, :] >= tau[:, None]).astype(f)
    Sel = np.zeros((128, 64, 128), f)
    SelI = np.zeros((128, 64, 128), f)
    for gl in range(8):
        for t in range(8):
            for j in range(16):
                Sel[gl * 16 + j, gl * 8 + t, t * 16 + j] = 1.0
                SelI[t * 16 + j, gl * 8 + t, gl * 16 + j] = 1.0
    sh["Sel"] = Sel
    sh["SelI"] = SelI
    wg = inp["w_gate"][0]
    pa = [tile_lhsT(inp[n][0]) for n in ("w_proj_a", "w_proj_b", "w_proj_c")]
    ga = [tile_lhsT(wg[:, i * 1024:(i + 1) * 1024]) for i in range(3)]
    sh["wA_l"] = np.ascontiguousarray(np.concatenate(pa + ga, axis=2))
    sh["wO_l"] = tile_lhsT(inp["w_out"][0])
    wu = tile_lhsT(inp["w_up"][0])
    sh["wU_l"] = np.ascontiguousarray(np.concatenate([wu[0:22], wu[22:44]], axis=2))
    sh["wD_l"] = tile_lhsT(inp["w_down"][0])
    sh["b_gate_l"] = np.ascontiguousarray(inp["b_gate"][0].reshape(3, 8, 128).transpose(2, 0, 1).reshape(128, 24))
    vec = lambda v: v.reshape(8, 128).T
    sh["ln_l"] = np.ascontiguousarray(np.concatenate([vec(inp["ln1_g"][0]), vec(inp["ln1_b"][0]),
                                                      vec(inp["ln2_g"][0]), vec(inp["ln2_b"][0])], axis=1))
    sh["conv_w_l"] = np.ascontiguousarray(inp["conv_w"][0].T.reshape(44, 128, 3).transpose(1, 0, 2))
    sh["conv_b_l"] = np.ascontiguousarray(inp["conv_b"][0].reshape(44, 128).T)
    return {k_: np.ascontiguousarray(v.astype(f)) for k_, v in sh.items()}


def prep_core(inp, b):
    return {
        "xT": np.ascontiguousarray(inp["x"][b].T),
        "memT": np.ascontiguousarray(inp["mem"][b].T),
        "pos": np.ascontiguousarray(inp["positions"][b].reshape(1, S).astype(np.int32)),
    }


def kernel(**inputs):
    inp = {k_: np.asarray(v) for k_, v in inputs.items()}
    shared = prep_shared(inp)
    B = inp["x"].shape[0]
    in_maps = []
    for b in range(B):
        m = dict(shared)
        m.update(prep_core(inp, b))
        in_maps.append(m)
    shapes = {k_: (v.shape, I32 if v.dtype == np.int32 else F32) for k_, v in in_maps[0].items()}
    nc = build_program(shapes)
    res = run_bass_kernel_spmd(nc, in_maps, core_ids=list(range(B)))
    out = np.stack([np.asarray(r["outT"]).T for r in res.results], axis=0)
    return np.ascontiguousarray(out.astype(np.float32))
```

```python
from contextlib import ExitStack
import numpy as np
import concourse.bass as bass
import concourse.mybir as mybir
from concourse.bass_utils import run_bass_kernel_spmd

F32 = mybir.dt.float32
BF16 = mybir.dt.bfloat16
I32 = mybir.dt.int32
AF = mybir.ActivationFunctionType
ALU = mybir.AluOpType

S = 4096
D = 1024
TT = 512
NTT = S // TT
ALPHA = 2.0 ** 0.25
LN_EPS = 1e-5
TWO_PI = 6.283185307179586
SIN_SC = 6.2831840
SIN_BI = -3.1415920
EPOCH = 30000
NEG_FILL = -3.0e38
FUSE_WAITS = True


class Prog:
    ENGS = ("pe", "act", "dve", "pool", "sp")

    def __init__(self, nc, es, ndma=8):
        self.nc = nc
        self.es = es
        self.ops = {e: [] for e in self.ENGS}
        self.count = {e: 0 for e in self.ENGS}
        self.sems = {}
        self.ndma = ndma
        self.dsems = {}
        self.dcount = {e: 0 for e in self.ENGS}
        self.dtarget = {}
        self.waited = {e: {} for e in self.ENGS}
        self.lastw = {}
        self.readers = {}

    def _sem(self, eng, epoch):
        k = (eng, epoch)
        if k not in self.sems:
            self.sems[k] = self.es.enter_context(self.nc.semaphore(f"s_{eng}_{epoch}"))
        return self.sems[k]

    def _dsem(self, eng, r):
        k = (eng, r)
        if k not in self.dsems:
            self.dsems[k] = self.es.enter_context(self.nc.semaphore(f"d_{eng}_{r}"))
            self.dtarget[k] = 0
        return self.dsems[k]

    def _need(self, eng, ev, waits):
        if ev is None:
            return
        sem, val, src = ev
        if src == "pe" and eng == "pe":
            return
        w = self.waited[eng]
        if w.get(id(sem), 0) >= val:
            return
        w[id(sem)] = val
        waits.append((sem, val))

    def op(self, eng, fn, r=(), w=(), dma=False, fuse=False):
        waits = []
        for k in r:
            self._need(eng, self.lastw.get(k), waits)
        for k in w:
            self._need(eng, self.lastw.get(k), waits)
            for ev in self.readers.get(k, {}).values():
                self._need(eng, ev, waits)
        if dma:
            i = self.dcount[eng]
            self.dcount[eng] += 1
            rr = i % self.ndma
            sem = self._dsem(eng, rr)
            prev = self.dtarget[(eng, rr)]
            if prev > 0:
                self._need(eng, (sem, prev, "dma"), waits)
            self.dtarget[(eng, rr)] = prev + 16
            ev = (sem, prev + 16, "dma")
            inc = 16
        else:
            self.count[eng] += 1
            ep, v = divmod(self.count[eng] - 1, EPOCH)
            sem = self._sem(eng, ep)
            ev = (sem, v + 1, eng)
            inc = 1
        best = {}
        for s_, v_ in waits:
            if id(s_) not in best or best[id(s_)][1] < v_:
                best[id(s_)] = (s_, v_)
        self.ops[eng].append((list(best.values()), fn, sem, inc, fuse and FUSE_WAITS))
        for k in r:
            self.readers.setdefault(k, {})[(eng, id(sem))] = ev
        for k in w:
            self.lastw[k] = ev
            self.readers[k] = {}
        return ev

    def wait_all(self, eng, keys):
        waits = []
        for k in keys:
            self._need(eng, self.lastw.get(k), waits)
        self.ops[eng].append((waits, None, None, 0, False))

    def drain_dmas(self, eng="sp"):
        waits = []
        for (qe, r_), sem in self.dsems.items():
            tgt = self.dtarget[(qe, r_)]
            if tgt > 0:
                self._need(eng, (sem, tgt, "dma"), waits)
        self.ops[eng].append((waits, None, None, 0, False))

    def emit(self):
        self.drain_dmas("sp")
        nc = self.nc
        ops = self.ops
        self.ops = {e: [] for e in self.ENGS}

        def run(engname):
            def _f(eng):
                for wl, fn, sem, inc, fuse in ops[engname]:
                    if fuse and fn is not None and len(wl) >= 1:
                        for s_, v_ in wl[:-1]:
                            eng.wait_ge(s_, v_)
                        ins = fn(eng)
                        ins._wait_ge(wl[-1][0], wl[-1][1])
                        ins.then_inc(sem, inc)
                        continue
                    for s_, v_ in wl:
                        eng.wait_ge(s_, v_)
                    if fn is not None:
                        fn(eng).then_inc(sem, inc)
            return _f

        with nc.Block() as block:
            block.tensor(run("pe"))
            block.scalar(run("act"))
            block.vector(run("dve"))
            block.gpsimd(run("pool"))
            block.sync(run("sp"))


class K:
    def __init__(self, nc, P):
        self.nc = nc
        self.P = P
        self.ring = []
        self.ri = 0

    def sb(self, es, name, shape, dt):
        return es.enter_context(self.nc.sbuf_tensor("sb_" + name, shape, dt))

    def ps(self):
        t = self.ring[self.ri % len(self.ring)]
        self.ri += 1
        return t

    def mm(self, out, lhsT, rhs, start=True, stop=True, r=(), w=()):
        self.P.op("pe", lambda e: e.matmul(out, lhsT=lhsT, rhs=rhs, start=start, stop=stop), r=r, w=w)

    def tr(self, out, in_, ident, r=(), w=()):
        self.P.op("pe", lambda e: e.transpose(out, in_, ident), r=r, w=w)

    def act(self, out, in_, func, r=(), w=(), **kw):
        self.P.op("act", lambda e: e.activation(out=out, in_=in_, func=func, **kw), r=r, w=w, fuse=("accum_out" not in kw))

    def tt(self, eng, out, in0, in1, op, r=(), w=()):
        self.P.op(eng, lambda e: e.tensor_tensor(out=out, in0=in0, in1=in1, op=op), r=r, w=w, fuse=True)

    def ts(self, eng, out, in0, s1, op0, s2=None, op1=None, r=(), w=()):
        if op1 is None:
            self.P.op(eng, lambda e: e.tensor_scalar(out=out, in0=in0, scalar1=s1, scalar2=None, op0=op0), r=r, w=w, fuse=True)
        else:
            self.P.op(eng, lambda e: e.tensor_scalar(out=out, in0=in0, scalar1=s1, scalar2=s2, op0=op0, op1=op1), r=r, w=w, fuse=True)

    def stt(self, eng, out, in0, scalar, in1, op0, op1, r=(), w=()):
        self.P.op(eng, lambda e: e.scalar_tensor_tensor(out=out, in0=in0, scalar=scalar, in1=in1, op0=op0, op1=op1), r=r, w=w, fuse=True)

    def cp(self, eng, out, in_, r=(), w=()):
        if eng == "act":
            self.act(out, in_, AF.Copy, r=r, w=w)
        else:
            self.P.op(eng, lambda e: e.tensor_copy(out=out, in_=in_), r=r, w=w, fuse=True)

    def recip(self, out, in_, r=(), w=()):
        self.P.op("dve", lambda e: e.reciprocal(out=out, in_=in_), r=r, w=w)

    def memset(self, eng, ap, val, w=()):
        self.P.op(eng, lambda e: e.memset(ap, val), w=w)

    def dma(self, eng, out, in_, r=(), w=()):
        self.P.op(eng, lambda e: e.dma_start(out=out, in_=in_), r=r, w=w, dma=True)


def kt_view(dram_ap, cols):
    return dram_ap.rearrange("(kt p) s -> p kt s", p=128)[:, :, cols]


def sin_reduced(k, out, in_, mul, quarter, shape, tmp, negpi, rk, wk):
    y, ki, kf = tmp
    off = 8.5 + (0.25 if quarter else 0.0)
    if isinstance(mul, float):
        k.ts("dve", y, in_, mul / TWO_PI, ALU.mult, off, ALU.add, r=rk, w=["sr_y"])
    else:
        k.ts("dve", y, in_, mul, ALU.mult, off, ALU.add, r=rk, w=["sr_y"])
    k.cp("dve", ki, y, r=["sr_y"], w=["sr_k"])
    k.cp("dve", kf, ki, r=["sr_k"], w=["sr_kf"])
    k.tt("dve", y, y, kf, ALU.subtract, r=["sr_y", "sr_kf"], w=["sr_y"])
    k.ts("dve", kf, y, 0.0, ALU.is_lt, r=["sr_y"], w=["sr_kf"])
    k.tt("dve", y, y, kf, ALU.add, r=["sr_y", "sr_kf"], w=["sr_y"])
    k.act(out, y, AF.Sin, r=["sr_y", "negpi"], w=wk, bias=negpi, scale=SIN_SC)


def layer_norm(k, es_tmp, z, zkey, gvec, bvec, out32, o32key, out16, o16key):
    zb, zsq, onesD, meanS, m2, var, rstd, tmps, epsb = es_tmp
    for kt in range(8):
        k.act(zb[:, kt, :], z[:, kt, :], AF.Copy, r=[zkey], w=[f"zb{kt}"])
        k.act(zsq[:, kt, :], z[:, kt, :], AF.Square, r=[zkey], w=[f"zsq{kt}"])
    psM, kM = k.ps()
    for kt in range(8):
        k.mm(psM[:, :], onesD[:, :], zb[:, kt, :], start=(kt == 0), stop=(kt == 7), r=[f"zb{kt}", "onesD"], w=[kM])
    psQ, kQ = k.ps()
    for kt in range(8):
        k.mm(psQ[:, :], onesD[:, :], zsq[:, kt, :], start=(kt == 0), stop=(kt == 7), r=[f"zsq{kt}", "onesD"], w=[kQ])
    k.act(meanS[:, :], psM[:, :], AF.Copy, r=[kM], w=["meanS"])
    k.tt("dve", m2[:, :], meanS[:, :], meanS[:, :], ALU.mult, r=["meanS"], w=["m2"])
    k.tt("dve", var[:, :], psQ[:, :], m2[:, :], ALU.subtract, r=[kQ, "m2"], w=["var"])
    k.act(var[:, :], var[:, :], AF.Sqrt, r=["var", "epsb"], w=["var"], bias=epsb[:, 0:1])
    k.recip(rstd[:, :], var[:, :], r=["var"], w=["rstd"])
    for kt in range(8):
        t = tmps[kt % 2]
        tk = f"lnt{kt % 2}"
        k.tt("dve", t[:, :], z[:, kt, :], meanS[:, :], ALU.subtract, r=[zkey, "meanS"], w=[tk])
        k.tt("dve", t[:, :], t[:, :], rstd[:, :], ALU.mult, r=[tk, "rstd"], w=[tk])
        k.act(out32[:, kt, :], t[:, :], AF.Identity, r=[tk, "lnvec"], w=[o32key],
              scale=gvec[:, kt:kt + 1], bias=bvec[:, kt:kt + 1])
        if out16 is not None:
            k.cp("pool", out16(kt), out32[:, kt, :], r=[o32key], w=[o16key])


def phase_mix(k, T):
    nc, P = k.nc, k.P
    with ExitStack() as es:
        sb = lambda n, s, d: k.sb(es, n, s, d)
        xf = sb("xf", [128, 8, TT], F32)
        xb = sb("xb4", [128, 8, TT], BF16)
        brt = [sb(f"brt{i}", [128, 4, TT], BF16) for i in range(3)]
        gs = [sb(f"gs{i}", [128, TT], F32) for i in range(3)]
        m1 = sb("m1", [128, TT], F32)
        tA = sb("tA", [128, TT], F32)
        tB = sb("tB", [128, TT], F32)
        merged = sb("merged", [128, 8, TT], BF16)
        zb = sb("zb", [128, 8, TT], BF16)
        zsq = sb("zsq", [128, 8, TT], BF16)
        onesD = sb("onesD", [128, 128], BF16)
        meanS = sb("meanS", [128, TT], F32)
        m2 = sb("m2", [128, TT], F32)
        var = sb("var", [128, TT], F32)
        rstd = sb("rstd", [128, TT], F32)
        lnt = [sb(f"lnt{i}", [128, TT], F32) for i in range(2)]
        epsb = sb("epsb", [128, 1], F32)
        h1 = sb("h1", [128, 8, TT], F32)
        h1b = sb("h1b", [128, 8, TT + 2], BF16)
        actb = sb("actb", [128, 22, TT], BF16)
        c0 = [sb(f"c0_{i}", [128, 256], F32) for i in range(2)]
        c1 = [sb(f"c1_{i}", [128, 256], F32) for i in range(2)]
        sg = sb("sg", [128, 256], F32)
        ot = sb("ot", [128, 8, TT], F32)
        NSLOT = 3
        wring = [sb(f"wr{i}", [128, 4608], BF16) for i in range(NSLOT)]
        bgate = sb("bgate", [128, 24], F32)
        lnv = sb("lnv", [128, 32], F32)
        cw = sb("cw", [128, 44, 3], F32)
        cb = sb("cb", [128, 44], F32)
        ln_tmp = (zb, zsq, onesD, meanS, m2, var, rstd, lnt, epsb)

        k.dma("sp", bgate[:, :], T["b_gate_l"], w=["bgate"])
        k.dma("sp", lnv[:, :], T["ln_l"], w=["lnvec"])
        k.dma("sp", cw[:, :, :], T["conv_w_l"], w=["cw"])
        k.dma("sp", cb[:, :], T["conv_b_l"], w=["cw"])
        k.memset("pool", onesD[:, :], 1.0 / 1024.0, w=["onesD"])
        k.memset("pool", epsb[:, :], LN_EPS, w=["epsb"])
        k.memset("pool", h1b[:, :, 0:2], 0.0, w=["h1b"])

        chunks = []
        for tt_ in range(NTT):
            for n in range(8):
                chunks.append(("A", n))
            for n in range(8):
                chunks.append(("O", n))
            for i in range(22):
                chunks.append(("U", i))
            for n in range(8):
                chunks.append(("Dn", n))
        csize = {"A": 4608, "O": 1024, "U": 2048, "Dn": 2816}
        csrc = {"A": T["wA_bf"], "O": T["wO_bf"], "U": T["wU_bf"], "Dn": T["wD_bf"]}
        state = {"next": 0}

        def load_chunk():
            i = state["next"]
            if i >= len(chunks):
                return
            kind, idx = chunks[i]
            slot = i % NSLOT
            k.dma("sp", wring[slot][:, 0:csize[kind]], csrc[kind][idx], r=[f"wbf_{kind}{idx}"], w=[f"wr{slot}"])
            state["next"] += 1

        ci = {"i": 0}

        def get_chunk():
            i = ci["i"]
            ci["i"] += 1
            return wring[i % NSLOT], f"wr{i % NSLOT}"

        for _ in range(NSLOT):
            load_chunk()

        def load_inputs(t_):
            tk_ = slice(t_ * TT, (t_ + 1) * TT)
            k.dma("sp", xf[:, :, :], kt_view(T["xT"], tk_), w=["xf"])
            k.dma("pool", xb[:, :, :], kt_view(T["xT"], tk_), w=["xb4"])
            for i in range(3):
                k.dma("sp", brt[i][:, :, :], kt_view(T["br"][i], tk_), r=[f"br{i}_{t_}_{q}" for q in range(4)], w=[f"brt{i}"])

        load_inputs(0)
        for tt_ in range(NTT):
            tok = slice(tt_ * TT, (tt_ + 1) * TT)
            for n in range(8):
                wt, wk = get_chunk()
                psy = []
                for i in range(3):
                    py, ky = k.ps()
                    for kt in range(4):
                        c = (4 * i + kt) * 128
                        k.mm(py[:, :], wt[:, c:c + 128], brt[i][:, kt, :], start=(kt == 0), stop=(kt == 3),
                             r=[wk, f"brt{i}"], w=[ky])
                    pg, kg = k.ps()
                    for kt in range(8):
                        c = (12 + 8 * i + kt) * 128
                        k.mm(pg[:, :], wt[:, c:c + 128], xb[:, kt, :], start=(kt == 0), stop=(kt == 7),
                             r=[wk, "xb4"], w=[kg])
                    k.act(gs[i][:, :], pg[:, :], AF.Sigmoid, r=[kg, "bgate"], w=[f"gs{i}"],
                          bias=bgate[:, i * 8 + n:i * 8 + n + 1])
                    psy.append((py, ky))
                load_chunk()
                k.tt("dve", m1[:, :], psy[0][0][:, :], gs[0][:, :], ALU.mult, r=[psy[0][1], "gs0"], w=["m1"])
                k.tt("dve", tA[:, :], psy[1][0][:, :], gs[1][:, :], ALU.mult, r=[psy[1][1], "gs1"], w=["tA"])
                k.tt("dve", tB[:, :], psy[2][0][:, :], gs[2][:, :], ALU.mult, r=[psy[2][1], "gs2"], w=["tB"])
                k.tt("pool", m1[:, :], m1[:, :], tA[:, :], ALU.add, r=["m1", "tA"], w=["m1"])
                k.tt("pool", merged[:, n, :], m1[:, :], tB[:, :], ALU.add, r=["m1", "tB"], w=[f"merged{n}"])
            for n in range(8):
                wt, wk = get_chunk()
                px, kx = k.ps()
                for kt in range(8):
                    k.mm(px[:, :], wt[:, kt * 128:(kt + 1) * 128], merged[:, kt, :], start=(kt == 0), stop=(kt == 7),
                         r=[wk, f"merged{kt}"], w=[kx])
                load_chunk()
                k.stt("dve", xf[:, n, :], xf[:, n, :], ALPHA, px[:, :], ALU.mult, ALU.add, r=["xf", kx], w=["xf"])
            if tt_ > 0:
                k.cp("pool", h1b[:, :, 0:2], h1b[:, :, TT:TT + 2], r=["h1b"], w=["h1b"])
            layer_norm(k, ln_tmp, xf, "xf", lnv[:, 0:8], lnv[:, 8:16], h1, "h1",
                       lambda kt: h1b[:, kt, 2:TT + 2], "h1b")
            if tt_ + 1 < NTT:
                load_inputs(tt_ + 1)
            for i in range(22):
                wt, wk = get_chunk()
                for hf in range(2):
                    res = []
                    for part in range(2):
                        pp, kp = k.ps()
                        for kt in range(8):
                            c = (part * 8 + kt) * 128
                            k.mm(pp[:, 0:258], wt[:, c:c + 128], h1b[:, kt, hf * 256:hf * 256 + 258],
                                 start=(kt == 0), stop=(kt == 7), r=[wk, "h1b"], w=[kp])
                        ch = i + 22 * part
                        a0, a1 = c0[part], c1[part]
                        k.act(a0[:, :], pp[:, 2:258], AF.Identity, r=[kp, "cw"], w=[f"c0_{part}"],
                              scale=cw[:, ch, 2:3], bias=cb[:, ch:ch + 1])
                        k.stt("dve", a1[:, :], pp[:, 1:257], cw[:, ch, 1:2], a0[:, :], ALU.mult, ALU.add,
                              r=[kp, "cw", f"c0_{part}"], w=[f"c1_{part}"])
                        k.stt("dve", a0[:, :], pp[:, 0:256], cw[:, ch, 0:1], a1[:, :], ALU.mult, ALU.add,
                              r=[kp, "cw", f"c1_{part}"], w=[f"c0_{part}"])
                        res.append(a0)
                    k.act(sg[:, :], res[0][:, :], AF.Silu, r=["c0_0"], w=["sg"])
                    k.tt("pool", actb[:, i, hf * 256:(hf + 1) * 256], sg[:, :], res[1][:, :], ALU.mult,
                         r=["sg", "c0_1"], w=[f"actb{i}"])
                load_chunk()
            for n in range(8):
                wt, wk = get_chunk()
                pd, kd = k.ps()
                for kt in range(22):
                    k.mm(pd[:, :], wt[:, kt * 128:(kt + 1) * 128], actb[:, kt, :], start=(kt == 0), stop=(kt == 21),
                         r=[wk, f"actb{kt}"], w=[kd])
                load_chunk()
                k.stt("dve", h1[:, n, :], h1[:, n, :], ALPHA, pd[:, :], ALU.mult, ALU.add, r=["h1", kd], w=["h1"])
            layer_norm(k, ln_tmp, h1, "h1", lnv[:, 16:24], lnv[:, 24:32], ot, "ot", None, None)
            k.dma("act", kt_view(T["outT"], tok), ot[:, :, :], r=["ot"], w=[f"outT{tt_}"])
        P.wait_all("sp", [f"outT{t_}" for t_ in range(NTT)])
        P.emit()


def phase_mem(k, T):
    nc, P = k.nc, k.P
    with ExitStack() as es:
        sb = lambda n, s, d: k.sb(es, n, s, d)
        memb = sb("memb", [128, 8, 256], BF16)
        wq = sb("wq_mem", [128, 4, 1024], BF16)
        wkk = sb("wk_mem", [128, 4, 1024], BF16)
        wv = sb("wv_mem", [128, 8, 512], BF16)
        kmT = sb("kmT", [128, 4, 256], BF16)
        vm = sb("vm", [128, 2, 512], BF16)
        ones = sb("ones_mem", [128, 128], BF16)
        xbs = [sb(f"xb2_{i}", [128, 8, TT], BF16) for i in range(2)]
        qm = [sb(f"qm{i}", [128, TT], BF16) for i in range(2)]
        pT = [sb(f"pT{i}", [128, 2, TT], BF16) for i in range(2)]
        rec = sb("rec_mem", [128, TT], F32)
        ob = [sb(f"ob{i}", [128, TT], BF16) for i in range(2)]

        k.dma("pool", memb[:, :, :], T["memT"].rearrange("(kt p) m -> p kt m", p=128), w=["memb"])
        k.dma("pool", wq[:, :, :], T["w_qmem_l"], w=["wq_mem"])
        k.dma("pool", wkk[:, :, :], T["w_kmem_l"], w=["wk_mem"])
        k.dma("pool", wv[:, :, :], T["w_vmem_l"], w=["wv_mem"])
        k.memset("dve", ones[:, :], 1.0, w=["ones_mem"])
        for h in range(4):
            pk, kk = k.ps()
            for kt in range(8):
                k.mm(pk[:, 0:256], wkk[:, h, kt * 128:(kt + 1) * 128], memb[:, kt, :], start=(kt == 0), stop=(kt == 7),
                     r=["wk_mem", "memb"], w=[kk])
            k.cp("act", kmT[:, h, :], pk[:, 0:256], r=[kk], w=["kmT"])
        for mt in range(2):
            pv, kv = k.ps()
            for kt in range(8):
                k.mm(pv[:, :], memb[:, kt, mt * 128:(mt + 1) * 128], wv[:, kt, :], start=(kt == 0), stop=(kt == 7),
                     r=["wv_mem", "memb"], w=[kv])
            k.cp("act", vm[:, mt, :], pv[:, :], r=[kv], w=["vm"])
        sc = 128.0 ** -0.5
        it = 0
        for tt_ in range(NTT):
            tok = slice(tt_ * TT, (tt_ + 1) * TT)
            xb = xbs[tt_ % 2]
            xk = f"xb2_{tt_ % 2}"
            k.dma("pool", xb[:, :, :], kt_view(T["xT"], tok), w=[xk])
            for h in range(4):
                q_, qk = qm[it % 2], f"qm{it % 2}"
                p_, pk_ = pT[it % 2], f"pT{it % 2}"
                o_, ok_ = ob[it % 2], f"ob{it % 2}"
                it += 1
                pq, kq = k.ps()
                for kt in range(8):
                    k.mm(pq[:, :], wq[:, h, kt * 128:(kt + 1) * 128], xb[:, kt, :], start=(kt == 0), stop=(kt == 7),
                         r=["wq_mem", xk], w=[kq])
                k.cp("act", q_[:, :], pq[:, :], r=[kq], w=[qk])
                for mt in range(2):
                    ps_, ks_ = k.ps()
                    k.mm(ps_[:, :], kmT[:, h, mt * 128:(mt + 1) * 128], q_[:, :], r=["kmT", qk], w=[ks_])
                    k.act(p_[:, mt, :], ps_[:, :], AF.Exp, r=[ks_], w=[pk_], scale=sc)
                po, ko = k.ps()
                for mt in range(2):
                    k.mm(po[:, :], vm[:, mt, h * 128:(h + 1) * 128], p_[:, mt, :], start=(mt == 0), stop=(mt == 1),
                         r=["vm", pk_], w=[ko])
                pd, kd = k.ps()
                for mt in range(2):
                    k.mm(pd[:, :], ones[:, :], p_[:, mt, :], start=(mt == 0), stop=(mt == 1),
                         r=["ones_mem", pk_], w=[kd])
                k.recip(rec[:, :], pd[:, :], r=[kd], w=["rec_mem"])
                k.tt("dve", o_[:, :], po[:, :], rec[:, :], ALU.mult, r=[ko, "rec_mem"], w=[ok_])
                k.dma("sp", T["br"][2][h * 128:(h + 1) * 128, tok], o_[:, :], r=[ok_], w=[f"br2_{tt_}_{h}"])
        P.emit()


def cmul(k, eng, outr, outi, ar, ai, br_, bi_, t1, t2, r, w):
    k.tt(eng, t1, ar, br_, ALU.mult, r=r, w=["cm_t1"])
    k.tt(eng, t2, ai, bi_, ALU.mult, r=r, w=["cm_t2"])
    k.tt(eng, outr, t1, t2, ALU.subtract, r=["cm_t1", "cm_t2"], w=w)
    k.tt(eng, t1, ar, bi_, ALU.mult, r=r + ["cm_t1"], w=["cm_t1"])
    k.tt(eng, t2, ai, br_, ALU.mult, r=r + ["cm_t2"], w=["cm_t2"])
    k.tt(eng, outi, t1, t2, ALU.add, r=["cm_t1", "cm_t2"], w=w)


def phase_s5(k, T):
    nc, P = k.nc, k.P
    with ExitStack() as es:
        sbp = lambda n, s, d: k.sb(es, n, s, d)
        es1 = ExitStack()
        sb = lambda n, s, d: k.sb(es1, n, s, d)
        I16 = sbp("I16", [128, 128], BF16)
        J16 = sbp("J16", [128, 128], BF16)
        Sel = sbp("Sel", [128, 64, 128], BF16)
        SelI = sbp("SelI", [128, 64, 128], BF16)
        wu = sbp("wu", [128, 4, 1024], BF16)
        wglu = sbp("wglu", [128, 4, 512], BF16)
        bglu = sbp("bglu", [128, 4], F32)
        LR = sbp("LR", [128, 32, 9], F32)
        LI = sbp("LI", [128, 32, 9], F32)
        Mb = sbp("Mb", [128, 32, 128], BF16)
        BmT = sbp("BmT", [128, 32, 128], BF16)
        Cmb = sbp("Cmb", [128, 32, 128], BF16)
        I32t = sb("I32t", [128, 128], F32)
        Mc = sb("Mc", [128, 128], F32)
        lre = sb("lre", [128, 32], F32)
        lim = sb("lim", [128, 32], F32)
        ldt = sb("ldt", [128, 32], F32)
        dP = sb("dP", [128, 32], F32)
        Br = sb("Br", [128, 32, 16], F32)
        Bi = sb("Bi", [128, 32, 16], F32)
        Cr = sb("Cr", [128, 32, 16], F32)
        Ci = sb("Ci", [128, 32, 16], F32)
        negpi = sb("negpi", [128, 1], F32)
        k.dma("sp", I32t[:, :], T["I128"], w=["I32t"])
        k.dma("pool", I16[:, :], T["I128"], w=["I16"])
        k.dma("pool", J16[:, :], T["J128"], w=["J16"])
        k.dma("sp", Mc[:, :], T["Mc"], w=["Mc"])
        k.dma("pool", Sel[:, :, :], T["Sel"], w=["Sel"])
        k.dma("pool", SelI[:, :, :], T["SelI"], w=["SelI"])
        for nm, t_ in (("lre", lre), ("lim", lim), ("ldt", ldt), ("dP", dP)):
            k.dma("sp", t_[:, :], T[nm], w=[nm])
        for nm, t_ in (("Br", Br), ("Bi", Bi), ("Cr", Cr), ("Ci", Ci)):
            k.dma("sp", t_[:, :, :], T[nm], w=[nm])
        k.dma("pool", wu[:, :, :], T["w_u_l"], w=["wu"])
        k.dma("pool", wglu[:, :, :], T["w_glu_l"], w=["wglu"])
        k.dma("sp", bglu[:, :], T["b_glu_l"], w=["bglu"])
        k.memset("pool", negpi[:, :], SIN_BI, w=["negpi"])

        a_ = sb("s5a", [128, 32], F32)
        th = sb("s5th", [128, 32], F32)
        dt_ = sb("s5dt", [128, 32], F32)
        y_ = sb("sr_y", [128, 32], F32)
        ki_ = sb("sr_k", [128, 32], I32)
        kf_ = sb("sr_kf", [128, 32], F32)
        sn = sb("s5sn", [128, 32], F32)
        cs = sb("s5cs", [128, 32], F32)
        mg = sb("s5mg", [128, 32], F32)
        t1 = sb("cm_t1", [128, 32], F32)
        t2 = sb("cm_t2", [128, 32], F32)
        PR = sb("PR", [128, 32, 9], F32)
        PI = sb("PI", [128, 32, 9], F32)
        NR = sb("NR", [128, 32, 8], F32)
        NI = sb("NI", [128, 32, 8], F32)
        bR = sb("betR", [128, 32], F32)
        bI = sb("betI", [128, 32], F32)
        inv = sb("s5inv", [128, 32], F32)

        k.ts("dve", lre[:, :], lre[:, :], -1e-4, ALU.min, r=["lre"], w=["lre"])
        k.act(dt_[:, :], ldt[:, :], AF.Exp, r=["ldt"], w=["s5dt"])
        k.tt("dve", a_[:, :], lre[:, :], dt_[:, :], ALU.mult, r=["lre", "s5dt"], w=["s5a"])
        k.tt("dve", th[:, :], lim[:, :], dt_[:, :], ALU.mult, r=["lim", "s5dt"], w=["s5th"])
        k.act(mg[:, :], a_[:, :], AF.Exp, r=["s5a"], w=["s5mg"])
        tmp3 = (y_[:, :], ki_[:, :], kf_[:, :])
        sin_reduced(k, sn[:, :], th[:, :], 1.0, False, None, tmp3, negpi[:, 0:1], ["s5th"], ["s5sn"])
        sin_reduced(k, cs[:, :], th[:, :], 1.0, True, None, tmp3, negpi[:, 0:1], ["s5th"], ["s5cs"])
        k.memset("dve", PR[:, :, 0], 1.0, w=["PRI"])
        k.memset("dve", PI[:, :, 0], 0.0, w=["PRI"])
        k.tt("dve", PR[:, :, 1], mg[:, :], cs[:, :], ALU.mult, r=["s5mg", "s5cs"], w=["PRI"])
        k.tt("dve", PI[:, :, 1], mg[:, :], sn[:, :], ALU.mult, r=["s5mg", "s5sn"], w=["PRI"])
        for n in range(2, 9):
            cmul(k, "dve", PR[:, :, n], PI[:, :, n], PR[:, :, n - 1], PI[:, :, n - 1], PR[:, :, 1], PI[:, :, 1],
                 t1[:, :], t2[:, :], ["PRI"], ["PRI"])
        k.cp("dve", LR[:, :, 0], PR[:, :, 8], r=["PRI"], w=["LRI"])
        k.cp("dve", LI[:, :, 0], PI[:, :, 8], r=["PRI"], w=["LRI"])
        for l in range(1, 9):
            cmul(k, "dve", LR[:, :, l], LI[:, :, l], LR[:, :, l - 1], LI[:, :, l - 1], LR[:, :, l - 1], LI[:, :, l - 1],
                 t1[:, :], t2[:, :], ["LRI"], ["LRI"])
        for n in range(8):
            k.act(inv[:, :], a_[:, :], AF.Exp, r=["s5a"], w=["s5inv"], scale=-2.0 * (n + 1))
            k.tt("dve", NR[:, :, n], PR[:, :, n + 1], inv[:, :], ALU.mult, r=["PRI", "s5inv"], w=["NRI"])
            k.stt("dve", NI[:, :, n], PI[:, :, n + 1], -1.0, inv[:, :], ALU.mult, ALU.mult, r=["PRI", "s5inv"], w=["NRI"])
        nr_ = sb("s5nr", [128, 32], F32)
        den = sb("s5den", [128, 32], F32)
        k.ts("dve", nr_[:, :], PR[:, :, 1], -1.0, ALU.add, r=["PRI"], w=["s5nr"])
        k.tt("dve", den[:, :], lre[:, :], lre[:, :], ALU.mult, r=["lre"], w=["s5den"])
        k.tt("dve", t1[:, :], lim[:, :], lim[:, :], ALU.mult, r=["lim"], w=["cm_t1"])
        k.tt("dve", den[:, :], den[:, :], t1[:, :], ALU.add, r=["s5den", "cm_t1"], w=["s5den"])
        k.recip(den[:, :], den[:, :], r=["s5den"], w=["s5den"])
        k.tt("dve", t1[:, :], nr_[:, :], lre[:, :], ALU.mult, r=["s5nr", "lre"], w=["cm_t1"])
        k.tt("dve", t2[:, :], PI[:, :, 1], lim[:, :], ALU.mult, r=["PRI", "lim"], w=["cm_t2"])
        k.tt("dve", t1[:, :], t1[:, :], t2[:, :], ALU.add, r=["cm_t1", "cm_t2"], w=["cm_t1"])
        k.tt("dve", bR[:, :], t1[:, :], den[:, :], ALU.mult, r=["cm_t1", "s5den"], w=["betR"])
        k.tt("dve", t1[:, :], PI[:, :, 1], lre[:, :], ALU.mult, r=["PRI", "lre"], w=["cm_t1"])
        k.tt("dve", t2[:, :], nr_[:, :], lim[:, :], ALU.mult, r=["s5nr", "lim"], w=["cm_t2"])
        k.tt("dve", t1[:, :], t1[:, :], t2[:, :], ALU.subtract, r=["cm_t1", "cm_t2"], w=["cm_t1"])
        k.tt("dve", bI[:, :], t1[:, :], den[:, :], ALU.mult, r=["cm_t1", "s5den"], w=["betI"])
        bbr = sb("bbr", [128, 32, 16], F32)
        bbi = sb("bbi", [128, 32, 16], F32)
        u1 = sb("s5u1", [128, 32, 16], F32)
        u2 = sb("s5u2", [128, 32, 16], F32)
        bc16 = lambda t_: t_[:, :].unsqueeze(2).to_broadcast([128, 32, 16])
        cmul(k, "dve", bbr[:, :, :], bbi[:, :, :], bc16(bR), bc16(bI), Br[:, :, :], Bi[:, :, :],
             u1[:, :, :], u2[:, :, :], ["betR", "betI", "Br", "Bi"], ["bb"])
        Z = sb("Zm", [128, 32, 8, 16], F32)
        W = sb("Wm", [128, 32, 8, 16], F32)
        Cm = sb("Cm", [128, 32, 8, 16], F32)
        v1 = sb("s5v1", [128, 32, 8, 16], F32)
        v2 = sb("s5v2", [128, 32, 8, 16], F32)

        def big_cmul(dst, dkey, Xr, Xi, Tr, Ti, neg_im, rk):
            xb_ = lambda t_, lo, hi: t_[lo:hi, :, :].unsqueeze(2).to_broadcast([hi - lo, 32, 8, 16])
            tb_ = lambda t_, lo, hi: t_[lo:hi].unsqueeze(3).to_broadcast([hi - lo, 32, 8, 16])
            k.tt("dve", v1[0:64], xb_(Xr, 0, 64), tb_(Tr, 0, 64), ALU.mult, r=rk, w=["s5v1"])
            k.tt("dve", v2[0:64], xb_(Xi, 0, 64), tb_(Ti, 0, 64), ALU.mult, r=rk, w=["s5v2"])
            k.tt("dve", dst[0:64], v1[0:64], v2[0:64], ALU.subtract, r=["s5v1", "s5v2"], w=[dkey])
            k.tt("dve", v1[64:128], xb_(Xi, 64, 128), tb_(Tr, 64, 128), ALU.mult, r=rk + ["s5v1"], w=["s5v1"])
            k.tt("dve", v2[64:128], xb_(Xr, 64, 128), tb_(Ti, 64, 128), ALU.mult, r=rk + ["s5v2"], w=["s5v2"])
            if neg_im:
                k.stt("dve", dst[64:128], v1[64:128], -1.0, v2[64:128], ALU.mult, ALU.subtract,
                      r=["s5v1", "s5v2"], w=[dkey])
            else:
                k.tt("dve", dst[64:128], v1[64:128], v2[64:128], ALU.add, r=["s5v1", "s5v2"], w=[dkey])

        big_cmul(Z, "Zm", bbr, bbi, NR[:, :, 0:8], NI[:, :, 0:8], False, ["bb", "NRI"])
        PRrev = sb("PRrev", [128, 32, 8], F32)
        PIrev = sb("PIrev", [128, 32, 8], F32)
        for tau in range(8):
            k.cp("pool", PRrev[:, :, tau], PR[:, :, 7 - tau], r=["PRI"], w=["Prev"])
            k.cp("pool", PIrev[:, :, tau], PI[:, :, 7 - tau], r=["PRI"], w=["Prev"])
        big_cmul(W, "Wm", bbr, bbi, PRrev[:, :, :], PIrev[:, :, :], False, ["bb", "Prev"])
        big_cmul(Cm, "Cm", Cr, Ci, PR[:, :, 1:9], PI[:, :, 1:9], True, ["Cr", "Ci", "PRI"])

        mt_ = sb("s5mt", [128, 4, 128], F32)
        k.cp("act", Cmb[:, :, :], Cm[:, :, :, :].rearrange("p g r i -> p g (r i)"), r=["Cm"], w=["Cmb"])
        for g4 in range(8):
            pm, km = k.ps()
            for gg in range(4):
                g = g4 * 4 + gg
                k.mm(pm[:, gg * 128:(gg + 1) * 128], Z[:, g].rearrange("p t j -> p (t j)"),
                     Cm[:, g].rearrange("p r i -> p (r i)"), r=["Zm", "Cm"], w=[km])
            k.tt("dve", mt_[:, :, :], pm[:, :].rearrange("p (a b) -> p a b", a=4),
                 Mc[:, :].unsqueeze(1).to_broadcast([128, 4, 128]), ALU.mult, r=[km, "Mc"], w=["s5mt"])
            for gg in range(4):
                g = g4 * 4 + gg
                k.stt("dve", Mb[:, g, :], I32t[:, :], dP[:, g:g + 1], mt_[:, gg, :], ALU.mult, ALU.add,
                      r=["I32t", "dP", "s5mt"], w=["Mb"])
            pw, kw = k.ps()
            for gg in range(4):
                g = g4 * 4 + gg
                k.tr(pw[:, gg * 128:(gg + 1) * 128], W[:, g].rearrange("p t j -> p (t j)"), I32t[:, :],
                     r=["Wm", "I32t"], w=[kw])
            k.cp("act", BmT[:, g4 * 4:(g4 + 1) * 4, :], pw[:, :].rearrange("p (a b) -> p a b", a=4), r=[kw], w=["BmT"])

        P.emit()
        es1.close()
        sb = sbp
        uTp = sb("uTp", [128, 4, 8, 512], BF16)
        ysT = sb("ysT", [128, 4, S], BF16)
        xbs = [sb(f"xb1_{i}", [128, 8, TT], BF16) for i in range(2)]
        for tt_ in range(NTT):
            tok = slice(tt_ * TT, (tt_ + 1) * TT)
            xb = xbs[tt_ % 2]
            xk = f"xb1_{tt_ % 2}"
            k.dma("pool", xb[:, :, :], kt_view(T["xT"], tok), w=[xk])
            for t in range(4):
                pu, ku = k.ps()
                for kt in range(8):
                    k.mm(pu[:, :], wu[:, t, kt * 128:(kt + 1) * 128], xb[:, kt, :], start=(kt == 0), stop=(kt == 7),
                         r=["wu", xk], w=[ku])
                k.cp("act", uTp[:, t, :, tt_ * 64:(tt_ + 1) * 64], pu[:, :].rearrange("p (c t) -> p t c", t=8),
                     r=[ku], w=[f"uTp{t}"])

        Ug = [sb(f"Ug{i}", [128, 512], BF16) for i in range(8)]
        Hb = [sb(f"Hb{i}", [128, 513], BF16) for i in range(8)]
        Yg = [sb(f"Yg{i}", [128, 512], BF16) for i in range(8)]
        Rt = [sb(f"Rt{i}", [128, 128], F32) for i in range(2)]
        Rj = [sb(f"Rj{i}", [128, 128], F32) for i in range(2)]
        Rm = [sb(f"Rm{i}", [128, 128], BF16) for i in range(4)]
        for i in range(8):
            k.memset("pool", Hb[i][:, 0:1], 0.0, w=[f"Hb{i}"])
        ri = 0
        for t in range(4):
            for gl in range(8):
                g = 8 * t + gl
                pu, ku = k.ps()
                for tau in range(8):
                    k.mm(pu[:, :], Sel[:, gl * 8 + tau, :], uTp[:, t, tau, :], start=(tau == 0), stop=(tau == 7),
                         r=["Sel", f"uTp{t}"], w=[ku])
                k.cp("act", Ug[gl][:, :], pu[:, :], r=[ku], w=[f"Ug{gl}"])
                ph, kh = k.ps()
                k.mm(ph[:, :], BmT[:, g, :], Ug[gl][:, :], r=["BmT", f"Ug{gl}"], w=[kh])
                k.cp("act", Hb[gl][:, 1:513], ph[:, :], r=[kh], w=[f"Hb{gl}"])
            for l in range(9):
                d = 1 << l
                for gl in range(8):
                    g = 8 * t + gl
                    rt, rtk = Rt[ri % 2], f"Rt{ri % 2}"
                    rm, rmk = Rm[ri % 4], f"Rm{ri % 4}"
                    rj, rjk = Rj[ri % 2], f"Rj{ri % 2}"
                    ri += 1
                    k.ts("dve", rt[:, :], I16[:, :], LR[:, g, l:l + 1], ALU.mult, r=["I16", "LRI"], w=[rtk])
                    k.stt("dve", rm[:, :], J16[:, :], LI[:, g, l:l + 1], rt[:, :], ALU.mult, ALU.add,
                          r=["J16", "LRI", rtk], w=[rmk])
                    ps_, ks_ = k.ps()
                    k.mm(ps_[:, 0:512 - d], rm[:, :], Hb[gl][:, 1:513 - d], r=[rmk, f"Hb{gl}"], w=[ks_])
                    k.tt("dve", Hb[gl][:, 1 + d:513], Hb[gl][:, 1 + d:513], ps_[:, 0:512 - d], ALU.add,
                         r=[ks_, f"Hb{gl}"], w=[f"Hb{gl}"])
            for gl in range(8):
                g = 8 * t + gl
                py, ky = k.ps()
                k.mm(py[:, :], Mb[:, g, :], Ug[gl][:, :], start=True, stop=False, r=["Mb", f"Ug{gl}"], w=[ky])
                k.mm(py[:, :], Cmb[:, g, :], Hb[gl][:, 0:512], start=False, stop=True, r=["Cmb", f"Hb{gl}"], w=[ky])
                k.act(Yg[gl][:, :], py[:, :], AF.Gelu_apprx_tanh, r=[ky], w=[f"Yg{gl}"])
            for r_ in range(8):
                pt, kt_ = k.ps()
                for gl in range(8):
                    k.mm(pt[:, :], SelI[:, gl * 8 + r_, :], Yg[gl][:, :], start=(gl == 0), stop=(gl == 7),
                         r=["SelI", f"Yg{gl}"], w=[kt_])
                k.cp("act", ysT[:, t, :].rearrange("p (c r) -> p r c", r=8)[:, r_, :], pt[:, :], r=[kt_], w=[f"ysT{t}"])
        sgl = [sb(f"sgl{i}", [128, TT], F32) for i in range(2)]
        og = [sb(f"og{i}", [128, TT], BF16) for i in range(2)]
        it = 0
        for tt_ in range(NTT):
            tok = slice(tt_ * TT, (tt_ + 1) * TT)
            for n in range(4):
                pz, kz = k.ps()
                for kt in range(4):
                    k.mm(pz[:, :], wglu[:, n, kt * 128:(kt + 1) * 128], ysT[:, kt, tok], start=(kt == 0), stop=(kt == 3),
                         r=["wglu", f"ysT{kt}"], w=[kz])
                s_, sk = sgl[it % 2], f"sgl{it % 2}"
                o_, ok_ = og[it % 2], f"og{it % 2}"
                it += 1
                k.act(s_[:, :], pz[:, :], AF.Sigmoid, r=[kz, "bglu"], w=[sk], bias=bglu[:, n:n + 1])
                k.tt("dve", o_[:, :], ysT[:, n, tok], s_[:, :], ALU.mult, r=[f"ysT{n}", sk], w=[ok_])
                k.dma("sp", T["br"][1][n * 128:(n + 1) * 128, tok], o_[:, :], r=[ok_], w=[f"br1_{tt_}_{n}"])
        P.emit()


def phase_att(k, T):
    nc, P = k.nc, k.P
    with ExitStack() as es:
        sb = lambda n, s, d: k.sb(es, n, s, d)
        qT = sb("qT", [128, 4, S], BF16)
        kT = sb("kT", [128, 2, S], BF16)
        qiT = sb("qiT", [128, 2, S], BF16)
        kiT = sb("kiT", [128, S], BF16)
        V1 = sb("V1", [128, 32, 2, 65], BF16)
        widx = sb("widx", [128, 32, 4], F32)
        I16 = sb("I16a", [128, 128], BF16)
        k.dma("pool", I16[:, :], T["I128"], w=["I16a"])
        k.memset("dve", V1[:, :, :, 64:65], 1.0, w=["V1"])
        dests = [(qT, 0), (qT, 1), (qT, 2), (qT, 3), (kT, 0), (kT, 1), (qiT, 0), (qiT, 1), (kiT, None)]
        dkeys = ["qT", "qT", "qT", "qT", "kT", "kT", "qiT", "qiT", "kiT"]
        with ExitStack() as es2:
            sb2 = lambda n, s, d: k.sb(es2, n, s, d)
            watt = sb2("watt", [128, 18, 1024], BF16)
            wvw = sb2("wvw", [128, 8, 132], BF16)
            cosTs = [sb2(f"cosT{i}", [128, TT], F32) for i in range(2)]
            sinTs = [sb2(f"sinT{i}", [128, TT], F32) for i in range(2)]
            posi = sb2("posi", [128, TT], I32)
            posf = sb2("posf", [128, TT], F32)
            y_ = sb2("sr_y3", [128, TT], F32)
            ki_ = sb2("sr_k3", [128, TT], I32)
            kf_ = sb2("sr_kf3", [128, TT], F32)
            ifr = sb2("ifr", [128, 2], F32)
            negpi = sb2("negpi3", [128, 1], F32)
            xbs = [sb2(f"xb3_{i}", [128, 8, TT], BF16) for i in range(2)]
            r1 = [sb2(f"r1_{i}", [128, TT], F32) for i in range(2)]
            r2 = [sb2(f"r2_{i}", [128, TT], F32) for i in range(2)]
            for i in range(18):
                k.dma("pool", watt[:, i, :], T["w_att_l"][i], w=[f"watt{i}"])
            k.dma("pool", wvw[:, :, :], T["w_vw_l"], w=["wvw"])
            k.dma("sp", ifr[:, :], T["ifr"], w=["ifr"])
            k.memset("pool", negpi[:, :], SIN_BI, w=["negpi"])
            tmp3 = (y_[:, :], ki_[:, :], kf_[:, :])
            it = 0
            for tt_ in range(NTT):
                tok = slice(tt_ * TT, (tt_ + 1) * TT)
                xb = xbs[tt_ % 2]
                xk = f"xb3_{tt_ % 2}"
                k.dma("pool", xb[:, :, :], kt_view(T["xT"], tok), w=[xk])
                cosT, ck = cosTs[tt_ % 2], f"cosT{tt_ % 2}"
                sinT, sk_ = sinTs[tt_ % 2], f"sinT{tt_ % 2}"
                k.dma("sp", posi[:, :], T["pos"][:, tok].partition_broadcast(128), w=["posi"])
                k.cp("dve", posf[:, :], posi[:, :], r=["posi"], w=["posf"])
                sin_reduced(k, sinT[:, :], posf[:, :], ifr[:, 0:1], False, None, tmp3, negpi[:, 0:1], ["posf", "ifr"], [sk_])
                sin_reduced(k, cosT[:, :], posf[:, :], ifr[:, 0:1], True, None, tmp3, negpi[:, 0:1], ["posf", "ifr"], [ck])
                k.ts("dve", sinT[:, :], sinT[:, :], ifr[:, 1:2], ALU.mult, r=[sk_, "ifr"], w=[sk_])
                for i in range(9):
                    p1, k1 = k.ps()
                    for kt in range(8):
                        k.mm(p1[:, :], watt[:, 2 * i, kt * 128:(kt + 1) * 128], xb[:, kt, :], start=(kt == 0), stop=(kt == 7),
                             r=[f"watt{2 * i}", xk], w=[k1])
                    p2, k2 = k.ps()
                    for kt in range(8):
                        k.mm(p2[:, :], watt[:, 2 * i + 1, kt * 128:(kt + 1) * 128], xb[:, kt, :], start=(kt == 0), stop=(kt == 7),
                             r=[f"watt{2 * i + 1}", xk], w=[k2])
                    a1, a1k = r1[it % 2], f"r1_{it % 2}"
                    a2, a2k = r2[it % 2], f"r2_{it % 2}"
                    it += 1
                    k.tt("dve", a1[:, :], p1[:, :], cosT[:, :], ALU.mult, r=[k1, ck], w=[a1k])
                    k.tt("dve", a2[:, :], p2[:, :], sinT[:, :], ALU.mult, r=[k2, sk_], w=[a2k])
                    dt_, di = dests[i]
                    dst = dt_[:, di, tok] if di is not None else dt_[:, tok]
                    k.tt("pool", dst, a1[:, :], a2[:, :], ALU.add, r=[a1k, a2k], w=[dkeys[i]])
                for sbk in range(4):
                    sblk = tt_ * 4 + sbk
                    pv, kv = k.ps()
                    for kt in range(8):
                        k.mm(pv[:, 0:132], xb[:, kt, sbk * 128:(sbk + 1) * 128], wvw[:, kt, :], start=(kt == 0), stop=(kt == 7),
                             r=["wvw", xk], w=[kv])
                    k.cp("act", V1[:, sblk, :, 0:64], pv[:, 0:128].rearrange("p (g d) -> p g d", g=2), r=[kv], w=["V1"])
                    k.act(widx[:, sblk, :], pv[:, 128:132], AF.Copy, r=[kv], w=["widx"], scale=0.5)
            P.emit()

        scoreA = sb("scoreA", [128, S], F32)
        zr = sb("zr", [128, S], F32)
        zm = sb("zm", [128, S], BF16)
        junkA = sb("junkA", [128, S], BF16)
        junkD = sb("junkD", [128, S], BF16)
        masks = [sb(f"mask{i}", [128, S], BF16) for i in range(2)]
        maskTs = [sb(f"maskT{i}", [128, S], BF16) for i in range(2)]
        rl = [sb(f"rl{i}", [128, 512], F32) for i in range(2)]
        Pt = [sb(f"Pt{i}", [128, 512], BF16) for i in range(3)]
        KB = 24
        st = sb("selst", [128, 16], F32)
        nms = [sb(f"nm{i}", [128, 1], F32) for i in range(2)]
        ssum = sb("ssum", [128, 1], F32)
        gsg = sb("gsg", [128, 1], F32)
        nwrow = sb("nwrow", [128, KB], F32)
        nwtab = sb("nwtab", [128, KB], F32)
        nw2tab = sb("nw2tab", [128, KB], F32)
        thrc = sb("thrc", [128, 1], F32)
        rec = sb("rec_a", [128, 8], F32)
        attn = sb("attn_tm", [128, 8, 64], BF16)
        aT = [sb(f"aT{i}", [128, 4, 128], BF16) for i in range(2)]
        k.dma("sp", nwrow[:, :], T["nwrow"].partition_broadcast(128), w=["nwrow"])
        k.memset("pool", thrc[:, :], -1.0e29, w=["thrc"])
        psO = k.psO
        psB = k.psB
        ring_save = k.ring
        k.ring = k.ring[:5]

        def indexer(b):
            N = 128 * (b + 1)
            tb = slice(b * 128, (b + 1) * 128)
            nch = (N + 511) // 512
            for c in range(nch):
                c0_ = c * 512
                cw_ = min(512, N - c0_)
                for h in range(4):
                    base = 64 * (h % 2)
                    pl, kl = k.ps()
                    k.mm(pl[:, 0:cw_], qiT[base:base + 64, h // 2, tb], kiT[base:base + 64, c0_:c0_ + cw_],
                         r=["qiT", "kiT"], w=[kl])
                    t_, tk = rl[(c * 4 + h) % 2], f"rl{(c * 4 + h) % 2}"
                    k.act(t_[:, 0:cw_], pl[:, 0:cw_], AF.Relu, r=[kl], w=[tk], scale=0.125)
                    if h == 0:
                        k.ts("dve", scoreA[:, c0_:c0_ + cw_], t_[:, 0:cw_], widx[:, b, h:h + 1], ALU.mult,
                             r=[tk, "widx"], w=["scoreA"])
                    else:
                        k.stt("dve", scoreA[:, c0_:c0_ + cw_], t_[:, 0:cw_], widx[:, b, h:h + 1], scoreA[:, c0_:c0_ + cw_],
                              ALU.mult, ALU.add, r=[tk, "widx", "scoreA"], w=["scoreA"])
            if b >= 2:
                P.op("dve", lambda e, o=junkD[:, 0:N], i_=scoreA[:, 0:N], a=st[:, 12:13]:
                     e.tensor_scalar(out=o, in0=i_, scalar1=1.0, scalar2=None, op0=ALU.mult, op1=ALU.max, accum_out=a),
                     r=["scoreA"], w=["junkD", "st_amax1"])
                P.op("dve", lambda e, o=junkD[:, 0:N], i_=scoreA[:, 0:N], a=st[:, 13:14]:
                     e.tensor_scalar(out=o, in0=i_, scalar1=-1.0, scalar2=None, op0=ALU.mult, op1=ALU.max, accum_out=a),
                     r=["scoreA"], w=["junkD", "st_amax2"])
                k.tt("dve", st[:, 0:1], st[:, 12:13], st[:, 13:14], ALU.max, r=["st_amax1", "st_amax2"], w=["st_amax"])
            k.memset("dve", scoreA[0:64, N - 64:N], -1.0e30, w=["scoreA"])

        def select(b):
            N = 128 * (b + 1)
            nsb = b + 1
            mask, mk = masks[b % 2], f"mask{b % 2}"
            maskT, mtk = maskTs[b % 2], f"maskT{b % 2}"
            sc = scoreA[:, 0:N]
            if b >= 2:
                amax, cp, cpz, isA, isC, lo0, hi0, w0 = (st[:, i:i + 1] for i in range(8))
                need, t0_, t1_, thr_ = (st[:, i:i + 1] for i in range(8, 12))
                cnt = lambda o, op0, key: P.op(
                    "dve", lambda e: e.tensor_scalar(out=junkD[:, 0:N], in0=sc, scalar1=0.0, scalar2=None, op0=op0,
                                                     op1=ALU.add, accum_out=o), r=["scoreA"], w=["junkD", key])
                cnt(cp, ALU.is_gt, "st_cp")
                cnt(cpz, ALU.is_ge, "st_cpz")
                k.ts("dve", isA, cp, 255.5, ALU.is_ge, r=["st_cp"], w=["st_isA"])
                k.ts("dve", isC, cpz, 255.5, ALU.is_lt, r=["st_cpz"], w=["st_isC"])
                k.ts("dve", t0_, amax, -1.0001, ALU.mult, -1.0e-30, ALU.add, r=["st_amax"], w=["st_t0"])
                k.tt("dve", lo0, isC, t0_, ALU.mult, r=["st_isC", "st_t0"], w=["st_lo0"])
                k.tt("dve", t1_, isA, amax, ALU.mult, r=["st_isA", "st_amax"], w=["st_t1"])
                k.stt("dve", hi0, isC, -2.0e-38, t1_, ALU.mult, ALU.add, r=["st_isC", "st_t1"], w=["st_hi0"])
                k.tt("dve", w0, hi0, lo0, ALU.subtract, r=["st_hi0", "st_lo0"], w=["st_w0"])
                k.tt("dve", nwtab[:, :], nwrow[:, :], w0.to_broadcast([128, KB]), ALU.mult, r=["nwrow", "st_w0"], w=["nwtab"])
                k.ts("dve", nw2tab[:, :], nwtab[:, :], -2.0, ALU.mult, r=["nwtab"], w=["nwtab"])
                k.tt("dve", t0_, isA, isC, ALU.add, r=["st_isA", "st_isC", "st_t0"], w=["st_t0"])
                k.ts("dve", t0_, t0_, -1.0, ALU.mult, 1.0, ALU.add, r=["st_t0"], w=["st_t0"])
                k.ts("dve", t1_, cp, -1.0, ALU.mult, 256.0, ALU.add, r=["st_cp", "st_t1"], w=["st_t1"])
                k.tt("dve", need, t0_, t1_, ALU.mult, r=["st_t0", "st_t1"], w=["st_need"])
                k.cp("dve", nms[0][:, :], lo0, r=["st_lo0"], w=["nm0"])
                for it_ in range(KB):
                    cur, ck_ = nms[it_ % 2], f"nm{it_ % 2}"
                    nxt, nk_ = nms[(it_ + 1) % 2], f"nm{(it_ + 1) % 2}"
                    k.ts("dve", gsg[:, :], cur[:, :], nw2tab[:, it_:it_ + 1], ALU.add, r=[ck_, "nwtab"], w=["gsg"])
                    P.op("dve", lambda e: e.tensor_scalar(
                        out=junkD[:, 0:N], in0=sc, scalar1=gsg[:, 0:1], scalar2=None, op0=ALU.is_gt, op1=ALU.add,
                        accum_out=ssum[:, 0:1]), r=["scoreA", "gsg"], w=["junkD", "ssum"])
                    k.ts("dve", ssum[:, :], ssum[:, :], 255.5, ALU.is_lt, NEG_FILL, ALU.mult, r=["ssum"], w=["ssum"])
                    k.ts("dve", nxt[:, :], ssum[:, :], gsg[:, 0:1], ALU.add, cur[:, 0:1], ALU.max,
                         r=["ssum", "gsg", ck_], w=[nk_])
                fin, fk_ = nms[KB % 2], f"nm{KB % 2}"
                thr_ = fin[:, 0:1]
                k.ts("dve", mask[:, 0:N], sc, thr_, ALU.is_gt, r=["scoreA", fk_], w=[mk])
                k.ts("dve", zm[:, 0:N], sc, 0.0, ALU.is_equal, r=["scoreA"], w=["zm"])
                P.op("dve", lambda e: e.tensor_tensor_scan(out=zr[:, 0:N], data0=zm[:, 0:N], data1=zm[:, 0:N], initial=0.0,
                                                           op0=ALU.add, op1=ALU.max), r=["zm"], w=["zr"])
                k.stt("dve", zm[:, 0:N], zr[:, 0:N], need, zm[:, 0:N], ALU.is_le, ALU.mult, r=["zr", "st_need", "zm"], w=["zm"])
                k.tt("pool", mask[:, 0:N], mask[:, 0:N], zm[:, 0:N], ALU.add, r=[mk, "zm"], w=[mk])
            else:
                k.ts("dve", mask[:, 0:N], sc, thrc[:, 0:1], ALU.is_ge, r=["scoreA", "thrc"], w=[mk])
            for j0 in range(0, nsb, 8):
                nj = min(8, nsb - j0)
                pb, kb = psB
                for jj in range(nj):
                    j = j0 + jj
                    k.tr(pb[:, jj * 128:(jj + 1) * 128], mask[:, j * 128:(j + 1) * 128], I16[:, :], r=[mk, "I16a"], w=[kb])
                k.cp("act", maskT[:, j0 * 128:(j0 + nj) * 128], pb[:, 0:nj * 128], r=[kb], w=[mtk])

        def attend(b):
            nsb = b + 1
            tb = slice(b * 128, (b + 1) * 128)
            maskT, mtk = maskTs[b % 2], f"maskT{b % 2}"
            pi_ = 0
            for h in range(8):
                g = h // 4
                base = 64 * (h % 2)
                po, ko = psO[h // 4]
                for j0 in range(0, nsb, 4):
                    nj = min(4, nsb - j0)
                    ps_, ks_ = k.ps()
                    for jj in range(nj):
                        j = j0 + jj
                        k.mm(ps_[:, jj * 128:(jj + 1) * 128], kT[base:base + 64, g, j * 128:(j + 1) * 128],
                             qT[base:base + 64, h // 2, tb], r=["kT", "qT"], w=[ks_])
                    p_, pk_ = Pt[pi_ % 3], f"Pt{pi_ % 3}"
                    pi_ += 1
                    k.act(p_[:, 0:nj * 128], ps_[:, 0:nj * 128], AF.Exp, r=[ks_], w=[pk_], scale=0.125)
                    k.tt("pool", p_[:, 0:nj * 128], p_[:, 0:nj * 128], maskT[:, j0 * 128:(j0 + nj) * 128], ALU.mult,
                         r=[pk_, mtk], w=[pk_])
                    for jj in range(nj):
                        j = j0 + jj
                        k.mm(po[:, (h % 4) * 65:(h % 4) * 65 + 65], p_[:, jj * 128:(jj + 1) * 128], V1[:, j, g, :],
                             start=(j == 0), stop=(j == nsb - 1), r=[pk_, "V1"], w=[ko])

        def finish(b):
            tb = slice(b * 128, (b + 1) * 128)
            for hb in range(2):
                po, ko = psO[hb]
                pv = po[:, 0:260].rearrange("p (h e) -> p h e", h=4)
                k.recip(rec[:, hb * 4:(hb + 1) * 4], pv[:, :, 64], r=[ko], w=["rec_a"])
                k.tt("dve", attn[:, hb * 4:(hb + 1) * 4, :], pv[:, :, 0:64],
                     rec[:, hb * 4:(hb + 1) * 4].unsqueeze(2).to_broadcast([128, 4, 64]), ALU.mult,
                     r=[ko, "rec_a"], w=["attn_tm"])
            pb, kb = psB
            a_, ak = aT[b % 2], f"aT{b % 2}"
            af = attn[:, :, :].rearrange("p h d -> p (h d)")
            for t in range(4):
                k.tr(pb[:, t * 128:(t + 1) * 128], af[:, t * 128:(t + 1) * 128], I16[:, :], r=["attn_tm", "I16a"], w=[kb])
            k.cp("act", a_[:, :, :], pb[:, 0:512].rearrange("p (t s) -> p t s", t=4), r=[kb], w=[ak])
            k.dma("sp", kt_view(T["br"][0], tb), a_[:, :, :], r=[ak], w=[f"br0_{b // 4}_{b % 4}"])

        import os as _os
        NB = int(_os.environ.get("ATT_NB", S // 128))
        indexer(0)
        select(0)
        for b in range(NB):
            if b + 1 < NB:
                indexer(b + 1)
            attend(b)
            if b + 1 < NB:
                select(b + 1)
            finish(b)
        k.ring = ring_save
        P.emit()


def build_program(shapes, phases=("s5", "mem", "att", "mix"), br_mode="internal"):
    nc = bass.Bass("TRN2", target_bir_lowering=False)
    T = {}
    for name, (shape, dt) in shapes.items():
        T[name] = nc.dram_tensor(name, list(shape), dt, kind="ExternalInput").ap()
    T["outT"] = nc.dram_tensor("outT", [D, S], F32, kind="ExternalOutput").ap()
    if br_mode == "internal":
        brt = nc.dram_tensor("br", [3, 512, S], BF16).ap()
    elif br_mode == "output":
        brt = nc.dram_tensor("br", [3, 512, S], BF16, kind="ExternalOutput").ap()
    else:
        brt = nc.dram_tensor("br", [3, 512, S], BF16, kind="ExternalInput").ap()
    T["br"] = [brt[i] for i in range(3)]
    T["wA_bf"] = nc.dram_tensor("wA_bf", [8, 128, 4608], BF16).ap()
    T["wO_bf"] = nc.dram_tensor("wO_bf", [8, 128, 1024], BF16).ap()
    T["wU_bf"] = nc.dram_tensor("wU_bf", [22, 128, 2048], BF16).ap()
    T["wD_bf"] = nc.dram_tensor("wD_bf", [8, 128, 2816], BF16).ap()
    with ExitStack() as es:
        P = Prog(nc, es)
        k = K(nc, P)
        for i in range(5):
            t = es.enter_context(nc.psum_tensor(f"psf{i}", [128, 512], F32))
            k.ring.append((t, f"psf{i}"))
        o0 = es.enter_context(nc.psum_tensor("psf5", [128, 512], F32))
        o1 = es.enter_context(nc.psum_tensor("psf6", [128, 512], F32))
        pb = es.enter_context(nc.psum_tensor("psb", [128, 1024], BF16))
        k.psO = [(o0, "psf5"), (o1, "psf6")]
        k.psB = (pb, "psb")
        k.ring.append((o0, "psf5"))
        k.ring.append((o1, "psf6"))
        if "mix" in phases:
            for i in range(8):
                k.dma("pool", T["wA_bf"][i], T["wA_l"][i], w=[f"wbf_A{i}"])
            for i in range(8):
                k.dma("pool", T["wO_bf"][i], T["wO_l"][i], w=[f"wbf_O{i}"])
            for i in range(22):
                k.dma("pool", T["wU_bf"][i], T["wU_l"][i], w=[f"wbf_U{i}"])
            for i in range(8):
                k.dma("pool", T["wD_bf"][i], T["wD_l"][i], w=[f"wbf_Dn{i}"])
        if "s5" in phases:
            phase_s5(k, T)
        if "mem" in phases:
            phase_mem(k, T)
        if "att" in phases:
            phase_att(k, T)
        if "mix" in phases:
            phase_mix(k, T)
        else:
            P.wait_all("sp", [kk_ for kk_ in P.lastw if kk_.startswith("br")])
            P.emit()
    return nc


def tile_lhsT(W):
    Kd, N = W.shape
    return np.ascontiguousarray(W.reshape(Kd // 128, 128, N // 128, 128).transpose(2, 1, 0, 3).reshape(N // 128, 128, (Kd // 128) * 128))


def rhs_layout(W):
    Kd, N = W.shape
    return np.ascontiguousarray(W.reshape(Kd // 128, 128, N).transpose(1, 0, 2))


def prep_shared(inp):
    f = np.float32
    sh = {}
    w_in = inp["w_in"][0]
    def hc(base, h):
        return base + h * 64 + np.arange(64)
    def sw(c):
        return np.concatenate([c[32:], c[:32]])
    tiles = []
    for j in range(4):
        a, b = hc(0, 2 * j), hc(0, 2 * j + 1)
        tiles += [np.concatenate([a, b]), np.concatenate([sw(a), sw(b)])]
    for g in range(2):
        a = hc(512, g)
        tiles += [np.concatenate([a, a]), np.concatenate([sw(a), sw(a)])]
    for j in range(2):
        a, b = hc(768, 2 * j), hc(768, 2 * j + 1)
        tiles += [np.concatenate([a, b]), np.concatenate([sw(a), sw(b)])]
    a = 1024 + np.arange(64)
    tiles += [np.concatenate([a, a]), np.concatenate([sw(a), sw(a)])]
    cols = np.concatenate(tiles)
    sh["w_att_l"] = tile_lhsT(w_in[:, cols])
    sh["w_vw_l"] = rhs_layout(np.concatenate([w_in[:, 640:768], w_in[:, 1088:1092]], axis=1))
    sh["w_u_l"] = np.ascontiguousarray(tile_lhsT(w_in[:, 1092:1604]).transpose(1, 0, 2))
    sh["w_qmem_l"] = np.ascontiguousarray(tile_lhsT(w_in[:, 1604:2116]).transpose(1, 0, 2))
    wkv = inp["w_mem_kv"][0]
    sh["w_kmem_l"] = np.ascontiguousarray(tile_lhsT(wkv[:, 0:512]).transpose(1, 0, 2))
    sh["w_vmem_l"] = rhs_layout(wkv[:, 512:1024])
    sh["w_glu_l"] = np.ascontiguousarray(tile_lhsT(inp["w_glu"][0]).transpose(1, 0, 2))
    sh["b_glu_l"] = np.ascontiguousarray(inp["b_glu"][0].reshape(4, 128).T)
    p = np.arange(128)
    inv_freq = (10000.0 ** (-(np.arange(32, dtype=np.float32)) / np.float32(32))).astype(np.float32)
    ifr = np.zeros((128, 2), f)
    ifr[:, 0] = (inv_freq[p % 32].astype(np.float64) / TWO_PI).astype(f)
    ifr[:, 1] = np.where((p % 64) < 32, -1.0, 1.0)
    sh["ifr"] = ifr
    sh["nwrow"] = (-(2.0 ** -(np.arange(24, dtype=np.float64) + 2.0))).astype(f).reshape(1, 24)
    dup = lambda a_: np.ascontiguousarray(np.concatenate([a_, a_], axis=0).astype(f))
    sh["lre"] = dup(inp["s5_lam_re"][0].T)
    sh["lim"] = dup(inp["s5_lam_im"][0].T)
    sh["ldt"] = np.ascontiguousarray(np.broadcast_to(inp["s5_log_dt"][0][None, :], (128, 32)).astype(f))
    sh["Br"] = dup(inp["s5_b_re"][0].transpose(1, 0, 2))
    sh["Bi"] = dup(inp["s5_b_im"][0].transpose(1, 0, 2))
    sh["Cr"] = dup(inp["s5_c_re"][0].transpose(2, 0, 1))
    sh["Ci"] = dup(inp["s5_c_im"][0].transpose(2, 0, 1))
    sh["dP"] = np.ascontiguousarray(np.tile(inp["s5_d"][0].reshape(32, 16).T, (8, 1)).astype(f))
    sh["I128"] = np.eye(128, dtype=f)
    J = np.zeros((128, 128), f)
    for q in range(64):
        J[q, 64 + q] = 1.0
        J[64 + q, q] = -1.0
    sh["J128"] = J
    tau = np.arange(128) // 16
    sh["Mc"] = (tau[None, :] >= tau[:, None]).astype(f)
    Sel = np.zeros((128, 64, 128), f)
    SelI = np.zeros((128, 64, 128), f)
    for gl in range(8):
        for t in range(8):
            for j in range(16):
                Sel[gl * 16 + j, gl * 8 + t, t * 16 + j] = 1.0
                SelI[t * 16 + j, gl * 8 + t, gl * 16 + j] = 1.0
    sh["Sel"] = Sel
    sh["SelI"] = SelI
    wg = inp["w_gate"][0]
    pa = [tile_lhsT(inp[n][0]) for n in ("w_proj_a", "w_proj_b", "w_proj_c")]
    ga = [tile_lhsT(wg[:, i * 1024:(i + 1) * 1024]) for i in range(3)]
    sh["wA_l"] = np.ascontiguousarray(np.concatenate(pa + ga, axis=2))
    sh["wO_l"] = tile_lhsT(inp["w_out"][0])
    wu = tile_lhsT(inp["w_up"][0])
    sh["wU_l"] = np.ascontiguousarray(np.concatenate([wu[0:22], wu[22:44]], axis=2))
    sh["wD_l"] = tile_lhsT(inp["w_down"][0])
    sh["b_gate_l"] = np.ascontiguousarray(inp["b_gate"][0].reshape(3, 8, 128).transpose(2, 0, 1).reshape(128, 24))
    vec = lambda v: v.reshape(8, 128).T
    sh["ln_l"] = np.ascontiguousarray(np.concatenate([vec(inp["ln1_g"][0]), vec(inp["ln1_b"][0]),
                                                      vec(inp["ln2_g"][0]), vec(inp["ln2_b"][0])], axis=1))
    sh["conv_w_l"] = np.ascontiguousarray(inp["conv_w"][0].T.reshape(44, 128, 3).transpose(1, 0, 2))
    sh["conv_b_l"] = np.ascontiguousarray(inp["conv_b"][0].reshape(44, 128).T)
    return {k_: np.ascontiguousarray(v.astype(f)) for k_, v in sh.items()}


def prep_core(inp, b):
    return {
        "xT": np.ascontiguousarray(inp["x"][b].T),
        "memT": np.ascontiguousarray(inp["mem"][b].T),
        "pos": np.ascontiguousarray(inp["positions"][b].reshape(1, S).astype(np.int32)),
    }


def kernel(**inputs):
    inp = {k_: np.asarray(v) for k_, v in inputs.items()}
    shared = prep_shared(inp)
    B = inp["x"].shape[0]
    in_maps = []
    for b in range(B):
        m = dict(shared)
        m.update(prep_core(inp, b))
        in_maps.append(m)
    shapes = {k_: (v.shape, I32 if v.dtype == np.int32 else F32) for k_, v in in_maps[0].items()}
    nc = build_program(shapes)
    res = run_bass_kernel_spmd(nc, in_maps, core_ids=list(range(B)))
    out = np.stack([np.asarray(r["outT"]).T for r in res.results], axis=0)
    return np.ascontiguousarray(out.astype(np.float32))
```

```python
from contextlib import ExitStack
import numpy as np
import concourse.bass as bass
import concourse.mybir as mybir
from concourse.bass_utils import run_bass_kernel_spmd

F32 = mybir.dt.float32
BF16 = mybir.dt.bfloat16
I32 = mybir.dt.int32
AF = mybir.ActivationFunctionType
ALU = mybir.AluOpType

S = 4096
D = 1024
TT = 512
NTT = S // TT
ALPHA = 2.0 ** 0.25
LN_EPS = 1e-5
TWO_PI = 6.283185307179586
SIN_SC = 6.2831840
SIN_BI = -3.1415920
EPOCH = 30000
NEG_FILL = -3.0e38
MASK_BIG = 30000.0
FUSE_WAITS = True


class Prog:
    ENGS = ("pe", "act", "dve", "pool", "sp")

    def __init__(self, nc, es, ndma=8):
        self.nc = nc
        self.es = es
        self.ops = {e: [] for e in self.ENGS}
        self.count = {e: 0 for e in self.ENGS}
        self.sems = {}
        self.ndma = ndma
        self.dsems = {}
        self.dcount = {e: 0 for e in self.ENGS}
        self.dtarget = {}
        self.waited = {e: {} for e in self.ENGS}
        self.lastw = {}
        self.readers = {}

    def _sem(self, eng, epoch):
        k = (eng, epoch)
        if k not in self.sems:
            self.sems[k] = self.es.enter_context(self.nc.semaphore(f"s_{eng}_{epoch}"))
        return self.sems[k]

    def _dsem(self, eng, r):
        k = (eng, r)
        if k not in self.dsems:
            self.dsems[k] = self.es.enter_context(self.nc.semaphore(f"d_{eng}_{r}"))
            self.dtarget[k] = 0
        return self.dsems[k]

    def _need(self, eng, ev, waits):
        if ev is None:
            return
        sem, val, src = ev
        if src == "pe" and eng == "pe":
            return
        w = self.waited[eng]
        if w.get(id(sem), 0) >= val:
            return
        w[id(sem)] = val
        waits.append((sem, val))

    def op(self, eng, fn, r=(), w=(), dma=False, fuse=False):
        waits = []
        for k in r:
            self._need(eng, self.lastw.get(k), waits)
        for k in w:
            self._need(eng, self.lastw.get(k), waits)
            for ev in self.readers.get(k, {}).values():
                self._need(eng, ev, waits)
        if dma:
            i = self.dcount[eng]
            self.dcount[eng] += 1
            rr = i % self.ndma
            sem = self._dsem(eng, rr)
            prev = self.dtarget[(eng, rr)]
            if prev > 0:
                self._need(eng, (sem, prev, "dma"), waits)
            self.dtarget[(eng, rr)] = prev + 16
            ev = (sem, prev + 16, "dma")
            inc = 16
        else:
            self.count[eng] += 1
            ep, v = divmod(self.count[eng] - 1, EPOCH)
            sem = self._sem(eng, ep)
            ev = (sem, v + 1, eng)
            inc = 1
        best = {}
        for s_, v_ in waits:
            if id(s_) not in best or best[id(s_)][1] < v_:
                best[id(s_)] = (s_, v_)
        self.ops[eng].append((list(best.values()), fn, sem, inc, FUSE_WAITS and not dma))
        for k in r:
            self.readers.setdefault(k, {})[(eng, id(sem))] = ev
        for k in w:
            self.lastw[k] = ev
            self.readers[k] = {}
        return ev

    def wait_all(self, eng, keys):
        waits = []
        for k in keys:
            self._need(eng, self.lastw.get(k), waits)
        self.ops[eng].append((waits, None, None, 0, False))

    def drain_dmas(self, eng="sp"):
        waits = []
        for (qe, r_), sem in self.dsems.items():
            tgt = self.dtarget[(qe, r_)]
            if tgt > 0:
                self._need(eng, (sem, tgt, "dma"), waits)
        self.ops[eng].append((waits, None, None, 0, False))

    def emit(self):
        self.drain_dmas("sp")
        nc = self.nc
        ops = self.ops
        self.ops = {e: [] for e in self.ENGS}

        def run(engname):
            def _f(eng):
                for wl, fn, sem, inc, fuse in ops[engname]:
                    if fuse and fn is not None and len(wl) >= 1:
                        for s_, v_ in wl[:-1]:
                            eng.wait_ge(s_, v_)
                        ins = fn(eng)
                        ins._wait_ge(wl[-1][0], wl[-1][1])
                        ins.then_inc(sem, inc)
                        continue
                    for s_, v_ in wl:
                        eng.wait_ge(s_, v_)
                    if fn is not None:
                        fn(eng).then_inc(sem, inc)
            return _f

        with nc.Block() as block:
            block.tensor(run("pe"))
            block.scalar(run("act"))
            block.vector(run("dve"))
            block.gpsimd(run("pool"))
            block.sync(run("sp"))


class K:
    def __init__(self, nc, P):
        self.nc = nc
        self.P = P
        self.ring = []
        self.ri = 0

    def sb(self, es, name, shape, dt):
        return es.enter_context(self.nc.sbuf_tensor("sb_" + name, shape, dt))

    def ps(self):
        t = self.ring[self.ri % len(self.ring)]
        self.ri += 1
        return t

    def mm(self, out, lhsT, rhs, start=True, stop=True, r=(), w=()):
        self.P.op("pe", lambda e: e.matmul(out, lhsT=lhsT, rhs=rhs, start=start, stop=stop), r=r, w=w)

    def tr(self, out, in_, ident, r=(), w=()):
        self.P.op("pe", lambda e: e.transpose(out, in_, ident), r=r, w=w)

    def act(self, out, in_, func, r=(), w=(), **kw):
        self.P.op("act", lambda e: e.activation(out=out, in_=in_, func=func, **kw), r=r, w=w, fuse=("accum_out" not in kw))

    def tt(self, eng, out, in0, in1, op, r=(), w=()):
        self.P.op(eng, lambda e: e.tensor_tensor(out=out, in0=in0, in1=in1, op=op), r=r, w=w, fuse=True)

    def ts(self, eng, out, in0, s1, op0, s2=None, op1=None, r=(), w=()):
        if op1 is None:
            self.P.op(eng, lambda e: e.tensor_scalar(out=out, in0=in0, scalar1=s1, scalar2=None, op0=op0), r=r, w=w, fuse=True)
        else:
            self.P.op(eng, lambda e: e.tensor_scalar(out=out, in0=in0, scalar1=s1, scalar2=s2, op0=op0, op1=op1), r=r, w=w, fuse=True)

    def stt(self, eng, out, in0, scalar, in1, op0, op1, r=(), w=()):
        self.P.op(eng, lambda e: e.scalar_tensor_tensor(out=out, in0=in0, scalar=scalar, in1=in1, op0=op0, op1=op1), r=r, w=w, fuse=True)

    def cp(self, eng, out, in_, r=(), w=()):
        if eng == "act":
            self.act(out, in_, AF.Copy, r=r, w=w)
        else:
            self.P.op(eng, lambda e: e.tensor_copy(out=out, in_=in_), r=r, w=w, fuse=True)

    def recip(self, out, in_, r=(), w=()):
        self.P.op("dve", lambda e: e.reciprocal(out=out, in_=in_), r=r, w=w)

    def memset(self, eng, ap, val, w=()):
        self.P.op(eng, lambda e: e.memset(ap, val), w=w)

    def dma(self, eng, out, in_, r=(), w=()):
        self.P.op(eng, lambda e: e.dma_start(out=out, in_=in_), r=r, w=w, dma=True)


def kt_view(dram_ap, cols):
    return dram_ap.rearrange("(kt p) s -> p kt s", p=128)[:, :, cols]


def sin_reduced(k, out, in_, mul, quarter, shape, tmp, negpi, rk, wk):
    y, ki, kf = tmp
    off = 8.5 + (0.25 if quarter else 0.0)
    if isinstance(mul, float):
        k.ts("dve", y, in_, mul / TWO_PI, ALU.mult, off, ALU.add, r=rk, w=["sr_y"])
    else:
        k.ts("dve", y, in_, mul, ALU.mult, off, ALU.add, r=rk, w=["sr_y"])
    k.cp("dve", ki, y, r=["sr_y"], w=["sr_k"])
    k.cp("dve", kf, ki, r=["sr_k"], w=["sr_kf"])
    k.tt("dve", y, y, kf, ALU.subtract, r=["sr_y", "sr_kf"], w=["sr_y"])
    k.ts("dve", kf, y, 0.0, ALU.is_lt, r=["sr_y"], w=["sr_kf"])
    k.tt("dve", y, y, kf, ALU.add, r=["sr_y", "sr_kf"], w=["sr_y"])
    k.act(out, y, AF.Sin, r=["sr_y", "negpi"], w=wk, bias=negpi, scale=SIN_SC)


def layer_norm(k, es_tmp, z, zkey, gvec, bvec, out32, o32key, out16, o16key):
    zb, zsq, onesD, meanS, m2, var, rstd, tmps, epsb = es_tmp
    for kt in range(8):
        k.act(zb[:, kt, :], z[:, kt, :], AF.Copy, r=[zkey], w=[f"zb{kt}"])
        k.act(zsq[:, kt, :], z[:, kt, :], AF.Square, r=[zkey], w=[f"zsq{kt}"])
    psM, kM = k.ps()
    for kt in range(8):
        k.mm(psM[:, :], onesD[:, :], zb[:, kt, :], start=(kt == 0), stop=(kt == 7), r=[f"zb{kt}", "onesD"], w=[kM])
    psQ, kQ = k.ps()
    for kt in range(8):
        k.mm(psQ[:, :], onesD[:, :], zsq[:, kt, :], start=(kt == 0), stop=(kt == 7), r=[f"zsq{kt}", "onesD"], w=[kQ])
    k.act(meanS[:, :], psM[:, :], AF.Copy, r=[kM], w=["meanS"])
    k.tt("dve", m2[:, :], meanS[:, :], meanS[:, :], ALU.mult, r=["meanS"], w=["m2"])
    k.tt("dve", var[:, :], psQ[:, :], m2[:, :], ALU.subtract, r=[kQ, "m2"], w=["var"])
    k.act(var[:, :], var[:, :], AF.Sqrt, r=["var", "epsb"], w=["var"], bias=epsb[:, 0:1])
    k.recip(rstd[:, :], var[:, :], r=["var"], w=["rstd"])
    for kt in range(8):
        t = tmps[kt % 2]
        tk = f"lnt{kt % 2}"
        k.tt("dve", t[:, :], z[:, kt, :], meanS[:, :], ALU.subtract, r=[zkey, "meanS"], w=[tk])
        k.tt("dve", t[:, :], t[:, :], rstd[:, :], ALU.mult, r=[tk, "rstd"], w=[tk])
        k.act(out32[:, kt, :], t[:, :], AF.Identity, r=[tk, "lnvec"], w=[o32key],
              scale=gvec[:, kt:kt + 1], bias=bvec[:, kt:kt + 1])
        if out16 is not None:
            k.cp("pool", out16(kt), out32[:, kt, :], r=[o32key], w=[o16key])


def phase_mix(k, T):
    nc, P = k.nc, k.P
    with ExitStack() as es:
        sb = lambda n, s, d: k.sb(es, n, s, d)
        xf = sb("xf", [128, 8, TT], F32)
        xb = sb("xb4", [128, 8, TT], BF16)
        brt = [sb(f"brt{i}", [128, 4, TT], BF16) for i in range(3)]
        gs = [sb(f"gs{i}", [128, TT], F32) for i in range(3)]
        m1 = sb("m1", [128, TT], F32)
        tA = sb("tA", [128, TT], F32)
        tB = sb("tB", [128, TT], F32)
        merged = sb("merged", [128, 8, TT], BF16)
        zb = sb("zb", [128, 8, TT], BF16)
        zsq = sb("zsq", [128, 8, TT], BF16)
        onesD = sb("onesD", [128, 128], BF16)
        meanS = sb("meanS", [128, TT], F32)
        m2 = sb("m2", [128, TT], F32)
        var = sb("var", [128, TT], F32)
        rstd = sb("rstd", [128, TT], F32)
        lnt = [sb(f"lnt{i}", [128, TT], F32) for i in range(2)]
        epsb = sb("epsb", [128, 1], F32)
        h1 = sb("h1", [128, 8, TT], F32)
        h1b = sb("h1b", [128, 8, TT + 2], BF16)
        actb = sb("actb", [128, 22, TT], BF16)
        c0 = [sb(f"c0_{i}", [128, 256], F32) for i in range(2)]
        c1 = [sb(f"c1_{i}", [128, 256], F32) for i in range(2)]
        sg = sb("sg", [128, 256], F32)
        ot = sb("ot", [128, 8, TT], F32)
        NSLOT = 3
        wring = [sb(f"wr{i}", [128, 4608], BF16) for i in range(NSLOT)]
        bgate = sb("bgate", [128, 24], F32)
        lnv = sb("lnv", [128, 32], F32)
        cw = sb("cw", [128, 44, 3], F32)
        cb = sb("cb", [128, 44], F32)
        ln_tmp = (zb, zsq, onesD, meanS, m2, var, rstd, lnt, epsb)

        k.dma("sp", bgate[:, :], T["b_gate_l"], w=["bgate"])
        k.dma("sp", lnv[:, :], T["ln_l"], w=["lnvec"])
        k.dma("sp", cw[:, :, :], T["conv_w_l"], w=["cw"])
        k.dma("sp", cb[:, :], T["conv_b_l"], w=["cw"])
        k.memset("pool", onesD[:, :], 1.0 / 1024.0, w=["onesD"])
        k.memset("pool", epsb[:, :], LN_EPS, w=["epsb"])
        k.memset("pool", h1b[:, :, 0:2], 0.0, w=["h1b"])

        chunks = []
        for tt_ in range(NTT):
            for n in range(8):
                chunks.append(("A", n))
            for n in range(8):
                chunks.append(("O", n))
            for i in range(22):
                chunks.append(("U", i))
            for n in range(8):
                chunks.append(("Dn", n))
        csize = {"A": 4608, "O": 1024, "U": 2048, "Dn": 2816}
        csrc = {"A": T["wA_bf"], "O": T["wO_bf"], "U": T["wU_bf"], "Dn": T["wD_bf"]}
        state = {"next": 0}

        def load_chunk():
            i = state["next"]
            if i >= len(chunks):
                return
            kind, idx = chunks[i]
            slot = i % NSLOT
            k.dma("sp", wring[slot][:, 0:csize[kind]], csrc[kind][idx], r=[f"wbf_{kind}{idx}"], w=[f"wr{slot}"])
            state["next"] += 1

        ci = {"i": 0}

        def get_chunk():
            i = ci["i"]
            ci["i"] += 1
            return wring[i % NSLOT], f"wr{i % NSLOT}"

        for _ in range(NSLOT):
            load_chunk()

        def load_inputs(t_):
            tk_ = slice(t_ * TT, (t_ + 1) * TT)
            k.dma("sp", xf[:, :, :], kt_view(T["xT"], tk_), w=["xf"])
            k.dma("pool", xb[:, :, :], kt_view(T["xT"], tk_), w=["xb4"])
            for i in range(3):
                k.dma("sp", brt[i][:, :, :], kt_view(T["br"][i], tk_), r=[f"br{i}_{t_}_{q}" for q in range(4)], w=[f"brt{i}"])

        load_inputs(0)
        for tt_ in range(NTT):
            tok = slice(tt_ * TT, (tt_ + 1) * TT)
            for n in range(8):
                wt, wk = get_chunk()
                psy = []
                for i in range(3):
                    py, ky = k.ps()
                    for kt in range(4):
                        c = (4 * i + kt) * 128
                        k.mm(py[:, :], wt[:, c:c + 128], brt[i][:, kt, :], start=(kt == 0), stop=(kt == 3),
                             r=[wk, f"brt{i}"], w=[ky])
                    pg, kg = k.ps()
                    for kt in range(8):
                        c = (12 + 8 * i + kt) * 128
                        k.mm(pg[:, :], wt[:, c:c + 128], xb[:, kt, :], start=(kt == 0), stop=(kt == 7),
                             r=[wk, "xb4"], w=[kg])
                    k.act(gs[i][:, :], pg[:, :], AF.Sigmoid, r=[kg, "bgate"], w=[f"gs{i}"],
                          bias=bgate[:, i * 8 + n:i * 8 + n + 1])
                    psy.append((py, ky))
                load_chunk()
                k.tt("dve", m1[:, :], psy[0][0][:, :], gs[0][:, :], ALU.mult, r=[psy[0][1], "gs0"], w=["m1"])
                k.tt("dve", tA[:, :], psy[1][0][:, :], gs[1][:, :], ALU.mult, r=[psy[1][1], "gs1"], w=["tA"])
                k.tt("dve", tB[:, :], psy[2][0][:, :], gs[2][:, :], ALU.mult, r=[psy[2][1], "gs2"], w=["tB"])
                k.tt("pool", m1[:, :], m1[:, :], tA[:, :], ALU.add, r=["m1", "tA"], w=["m1"])
                k.tt("pool", merged[:, n, :], m1[:, :], tB[:, :], ALU.add, r=["m1", "tB"], w=[f"merged{n}"])
            for n in range(8):
                wt, wk = get_chunk()
                px, kx = k.ps()
                for kt in range(8):
                    k.mm(px[:, :], wt[:, kt * 128:(kt + 1) * 128], merged[:, kt, :], start=(kt == 0), stop=(kt == 7),
                         r=[wk, f"merged{kt}"], w=[kx])
                load_chunk()
                k.stt("dve", xf[:, n, :], xf[:, n, :], ALPHA, px[:, :], ALU.mult, ALU.add, r=["xf", kx], w=["xf"])
            if tt_ > 0:
                k.cp("pool", h1b[:, :, 0:2], h1b[:, :, TT:TT + 2], r=["h1b"], w=["h1b"])
            layer_norm(k, ln_tmp, xf, "xf", lnv[:, 0:8], lnv[:, 8:16], h1, "h1",
                       lambda kt: h1b[:, kt, 2:TT + 2], "h1b")
            if tt_ + 1 < NTT:
                load_inputs(tt_ + 1)
            for i in range(22):
                wt, wk = get_chunk()
                for hf in range(2):
                    res = []
                    for part in range(2):
                        pp, kp = k.ps()
                        for kt in range(8):
                            c = (part * 8 + kt) * 128
                            k.mm(pp[:, 0:258], wt[:, c:c + 128], h1b[:, kt, hf * 256:hf * 256 + 258],
                                 start=(kt == 0), stop=(kt == 7), r=[wk, "h1b"], w=[kp])
                        ch = i + 22 * part
                        a0, a1 = c0[part], c1[part]
                        k.act(a0[:, :], pp[:, 2:258], AF.Identity, r=[kp, "cw"], w=[f"c0_{part}"],
                              scale=cw[:, ch, 2:3], bias=cb[:, ch:ch + 1])
                        k.stt("dve", a1[:, :], pp[:, 1:257], cw[:, ch, 1:2], a0[:, :], ALU.mult, ALU.add,
                              r=[kp, "cw", f"c0_{part}"], w=[f"c1_{part}"])
                        k.stt("dve", a0[:, :], pp[:, 0:256], cw[:, ch, 0:1], a1[:, :], ALU.mult, ALU.add,
                              r=[kp, "cw", f"c1_{part}"], w=[f"c0_{part}"])
                        res.append(a0)
                    k.act(sg[:, :], res[0][:, :], AF.Silu, r=["c0_0"], w=["sg"])
                    k.tt("pool", actb[:, i, hf * 256:(hf + 1) * 256], sg[:, :], res[1][:, :], ALU.mult,
                         r=["sg", "c0_1"], w=[f"actb{i}"])
                load_chunk()
            for n in range(8):
                wt, wk = get_chunk()
                pd, kd = k.ps()
                for kt in range(22):
                    k.mm(pd[:, :], wt[:, kt * 128:(kt + 1) * 128], actb[:, kt, :], start=(kt == 0), stop=(kt == 21),
                         r=[wk, f"actb{kt}"], w=[kd])
                load_chunk()
                k.stt("dve", h1[:, n, :], h1[:, n, :], ALPHA, pd[:, :], ALU.mult, ALU.add, r=["h1", kd], w=["h1"])
            layer_norm(k, ln_tmp, h1, "h1", lnv[:, 16:24], lnv[:, 24:32], ot, "ot", None, None)
            k.dma("act", kt_view(T["outT"], tok), ot[:, :, :], r=["ot"], w=[f"outT{tt_}"])
        P.wait_all("sp", [f"outT{t_}" for t_ in range(NTT)])
        P.emit()


def phase_mem(k, T):
    nc, P = k.nc, k.P
    with ExitStack() as es:
        sb = lambda n, s, d: k.sb(es, n, s, d)
        memb = sb("memb", [128, 8, 256], BF16)
        wq = sb("wq_mem", [128, 4, 1024], BF16)
        wkk = sb("wk_mem", [128, 4, 1024], BF16)
        wv = sb("wv_mem", [128, 8, 512], BF16)
        kmT = sb("kmT", [128, 4, 256], BF16)
        vm = sb("vm", [128, 2, 512], BF16)
        ones = sb("ones_mem", [128, 128], BF16)
        xbs = [sb(f"xb2_{i}", [128, 8, TT], BF16) for i in range(2)]
        qm = [sb(f"qm{i}", [128, TT], BF16) for i in range(2)]
        pT = [sb(f"pT{i}", [128, 2, TT], BF16) for i in range(2)]
        rec = sb("rec_mem", [128, TT], F32)
        ob = [sb(f"ob{i}", [128, TT], BF16) for i in range(2)]

        k.dma("pool", memb[:, :, :], T["memT"].rearrange("(kt p) m -> p kt m", p=128), w=["memb"])
        k.dma("pool", wq[:, :, :], T["w_qmem_l"], w=["wq_mem"])
        k.dma("pool", wkk[:, :, :], T["w_kmem_l"], w=["wk_mem"])
        k.dma("pool", wv[:, :, :], T["w_vmem_l"], w=["wv_mem"])
        k.memset("dve", ones[:, :], 1.0, w=["ones_mem"])
        for h in range(4):
            pk, kk = k.ps()
            for kt in range(8):
                k.mm(pk[:, 0:256], wkk[:, h, kt * 128:(kt + 1) * 128], memb[:, kt, :], start=(kt == 0), stop=(kt == 7),
                     r=["wk_mem", "memb"], w=[kk])
            k.cp("act", kmT[:, h, :], pk[:, 0:256], r=[kk], w=["kmT"])
        for mt in range(2):
            pv, kv = k.ps()
            for kt in range(8):
                k.mm(pv[:, :], memb[:, kt, mt * 128:(mt + 1) * 128], wv[:, kt, :], start=(kt == 0), stop=(kt == 7),
                     r=["wv_mem", "memb"], w=[kv])
            k.cp("act", vm[:, mt, :], pv[:, :], r=[kv], w=["vm"])
        sc = 128.0 ** -0.5
        it = 0
        for tt_ in range(NTT):
            tok = slice(tt_ * TT, (tt_ + 1) * TT)
            xb = xbs[tt_ % 2]
            xk = f"xb2_{tt_ % 2}"
            k.dma("pool", xb[:, :, :], kt_view(T["xT"], tok), w=[xk])
            for h in range(4):
                q_, qk = qm[it % 2], f"qm{it % 2}"
                p_, pk_ = pT[it % 2], f"pT{it % 2}"
                o_, ok_ = ob[it % 2], f"ob{it % 2}"
                it += 1
                pq, kq = k.ps()
                for kt in range(8):
                    k.mm(pq[:, :], wq[:, h, kt * 128:(kt + 1) * 128], xb[:, kt, :], start=(kt == 0), stop=(kt == 7),
                         r=["wq_mem", xk], w=[kq])
                k.cp("act", q_[:, :], pq[:, :], r=[kq], w=[qk])
                for mt in range(2):
                    ps_, ks_ = k.ps()
                    k.mm(ps_[:, :], kmT[:, h, mt * 128:(mt + 1) * 128], q_[:, :], r=["kmT", qk], w=[ks_])
                    k.act(p_[:, mt, :], ps_[:, :], AF.Exp, r=[ks_], w=[pk_], scale=sc)
                po, ko = k.ps()
                for mt in range(2):
                    k.mm(po[:, :], vm[:, mt, h * 128:(h + 1) * 128], p_[:, mt, :], start=(mt == 0), stop=(mt == 1),
                         r=["vm", pk_], w=[ko])
                pd, kd = k.ps()
                for mt in range(2):
                    k.mm(pd[:, :], ones[:, :], p_[:, mt, :], start=(mt == 0), stop=(mt == 1),
                         r=["ones_mem", pk_], w=[kd])
                k.recip(rec[:, :], pd[:, :], r=[kd], w=["rec_mem"])
                k.tt("dve", o_[:, :], po[:, :], rec[:, :], ALU.mult, r=[ko, "rec_mem"], w=[ok_])
                k.dma("sp", T["br"][2][h * 128:(h + 1) * 128, tok], o_[:, :], r=[ok_], w=[f"br2_{tt_}_{h}"])
        P.emit()


def cmul(k, eng, outr, outi, ar, ai, br_, bi_, t1, t2, r, w):
    k.tt(eng, t1, ar, br_, ALU.mult, r=r, w=["cm_t1"])
    k.tt(eng, t2, ai, bi_, ALU.mult, r=r, w=["cm_t2"])
    k.tt(eng, outr, t1, t2, ALU.subtract, r=["cm_t1", "cm_t2"], w=w)
    k.tt(eng, t1, ar, bi_, ALU.mult, r=r + ["cm_t1"], w=["cm_t1"])
    k.tt(eng, t2, ai, br_, ALU.mult, r=r + ["cm_t2"], w=["cm_t2"])
    k.tt(eng, outi, t1, t2, ALU.add, r=["cm_t1", "cm_t2"], w=w)


def phase_s5(k, T):
    nc, P = k.nc, k.P
    with ExitStack() as es:
        sbp = lambda n, s, d: k.sb(es, n, s, d)
        es1 = ExitStack()
        sb = lambda n, s, d: k.sb(es1, n, s, d)
        I16 = sbp("I16", [128, 128], BF16)
        J16 = sbp("J16", [128, 128], BF16)
        Sel = sbp("Sel", [128, 64, 128], BF16)
        SelI = sbp("SelI", [128, 64, 128], BF16)
        wu = sbp("wu", [128, 4, 1024], BF16)
        wglu = sbp("wglu", [128, 4, 512], BF16)
        bglu = sbp("bglu", [128, 4], F32)
        LR = sbp("LR", [128, 32, 9], F32)
        LI = sbp("LI", [128, 32, 9], F32)
        Mb = sbp("Mb", [128, 32, 128], BF16)
        BmT = sbp("BmT", [128, 32, 128], BF16)
        Cmb = sbp("Cmb", [128, 32, 128], BF16)
        I32t = sb("I32t", [128, 128], F32)
        Mc = sb("Mc", [128, 128], F32)
        lre = sb("lre", [128, 32], F32)
        lim = sb("lim", [128, 32], F32)
        ldt = sb("ldt", [128, 32], F32)
        dP = sb("dP", [128, 32], F32)
        Br = sb("Br", [128, 32, 16], F32)
        Bi = sb("Bi", [128, 32, 16], F32)
        Cr = sb("Cr", [128, 32, 16], F32)
        Ci = sb("Ci", [128, 32, 16], F32)
        negpi = sb("negpi", [128, 1], F32)
        k.dma("sp", I32t[:, :], T["I128"], w=["I32t"])
        k.dma("pool", I16[:, :], T["I128"], w=["I16"])
        k.dma("pool", J16[:, :], T["J128"], w=["J16"])
        k.dma("sp", Mc[:, :], T["Mc"], w=["Mc"])
        k.dma("pool", Sel[:, :, :], T["Sel"], w=["Sel"])
        k.dma("pool", SelI[:, :, :], T["SelI"], w=["SelI"])
        for nm, t_ in (("lre", lre), ("lim", lim), ("ldt", ldt), ("dP", dP)):
            k.dma("sp", t_[:, :], T[nm], w=[nm])
        for nm, t_ in (("Br", Br), ("Bi", Bi), ("Cr", Cr), ("Ci", Ci)):
            k.dma("sp", t_[:, :, :], T[nm], w=[nm])
        k.dma("pool", wu[:, :, :], T["w_u_l"], w=["wu"])
        k.dma("pool", wglu[:, :, :], T["w_glu_l"], w=["wglu"])
        k.dma("sp", bglu[:, :], T["b_glu_l"], w=["bglu"])
        k.memset("pool", negpi[:, :], SIN_BI, w=["negpi"])

        a_ = sb("s5a", [128, 32], F32)
        th = sb("s5th", [128, 32], F32)
        dt_ = sb("s5dt", [128, 32], F32)
        y_ = sb("sr_y", [128, 32], F32)
        ki_ = sb("sr_k", [128, 32], I32)
        kf_ = sb("sr_kf", [128, 32], F32)
        sn = sb("s5sn", [128, 32], F32)
        cs = sb("s5cs", [128, 32], F32)
        mg = sb("s5mg", [128, 32], F32)
        t1 = sb("cm_t1", [128, 32], F32)
        t2 = sb("cm_t2", [128, 32], F32)
        PR = sb("PR", [128, 32, 9], F32)
        PI = sb("PI", [128, 32, 9], F32)
        NR = sb("NR", [128, 32, 8], F32)
        NI = sb("NI", [128, 32, 8], F32)
        bR = sb("betR", [128, 32], F32)
        bI = sb("betI", [128, 32], F32)
        inv = sb("s5inv", [128, 32], F32)

        k.ts("dve", lre[:, :], lre[:, :], -1e-4, ALU.min, r=["lre"], w=["lre"])
        k.act(dt_[:, :], ldt[:, :], AF.Exp, r=["ldt"], w=["s5dt"])
        k.tt("dve", a_[:, :], lre[:, :], dt_[:, :], ALU.mult, r=["lre", "s5dt"], w=["s5a"])
        k.tt("dve", th[:, :], lim[:, :], dt_[:, :], ALU.mult, r=["lim", "s5dt"], w=["s5th"])
        k.act(mg[:, :], a_[:, :], AF.Exp, r=["s5a"], w=["s5mg"])
        tmp3 = (y_[:, :], ki_[:, :], kf_[:, :])
        sin_reduced(k, sn[:, :], th[:, :], 1.0, False, None, tmp3, negpi[:, 0:1], ["s5th"], ["s5sn"])
        sin_reduced(k, cs[:, :], th[:, :], 1.0, True, None, tmp3, negpi[:, 0:1], ["s5th"], ["s5cs"])
        k.memset("dve", PR[:, :, 0], 1.0, w=["PRI"])
        k.memset("dve", PI[:, :, 0], 0.0, w=["PRI"])
        k.tt("dve", PR[:, :, 1], mg[:, :], cs[:, :], ALU.mult, r=["s5mg", "s5cs"], w=["PRI"])
        k.tt("dve", PI[:, :, 1], mg[:, :], sn[:, :], ALU.mult, r=["s5mg", "s5sn"], w=["PRI"])
        for n in range(2, 9):
            cmul(k, "dve", PR[:, :, n], PI[:, :, n], PR[:, :, n - 1], PI[:, :, n - 1], PR[:, :, 1], PI[:, :, 1],
                 t1[:, :], t2[:, :], ["PRI"], ["PRI"])
        k.cp("dve", LR[:, :, 0], PR[:, :, 8], r=["PRI"], w=["LRI"])
        k.cp("dve", LI[:, :, 0], PI[:, :, 8], r=["PRI"], w=["LRI"])
        for l in range(1, 9):
            cmul(k, "dve", LR[:, :, l], LI[:, :, l], LR[:, :, l - 1], LI[:, :, l - 1], LR[:, :, l - 1], LI[:, :, l - 1],
                 t1[:, :], t2[:, :], ["LRI"], ["LRI"])
        for n in range(8):
            k.act(inv[:, :], a_[:, :], AF.Exp, r=["s5a"], w=["s5inv"], scale=-2.0 * (n + 1))
            k.tt("dve", NR[:, :, n], PR[:, :, n + 1], inv[:, :], ALU.mult, r=["PRI", "s5inv"], w=["NRI"])
            k.stt("dve", NI[:, :, n], PI[:, :, n + 1], -1.0, inv[:, :], ALU.mult, ALU.mult, r=["PRI", "s5inv"], w=["NRI"])
        nr_ = sb("s5nr", [128, 32], F32)
        den = sb("s5den", [128, 32], F32)
        k.ts("dve", nr_[:, :], PR[:, :, 1], -1.0, ALU.add, r=["PRI"], w=["s5nr"])
        k.tt("dve", den[:, :], lre[:, :], lre[:, :], ALU.mult, r=["lre"], w=["s5den"])
        k.tt("dve", t1[:, :], lim[:, :], lim[:, :], ALU.mult, r=["lim"], w=["cm_t1"])
        k.tt("dve", den[:, :], den[:, :], t1[:, :], ALU.add, r=["s5den", "cm_t1"], w=["s5den"])
        k.recip(den[:, :], den[:, :], r=["s5den"], w=["s5den"])
        k.tt("dve", t1[:, :], nr_[:, :], lre[:, :], ALU.mult, r=["s5nr", "lre"], w=["cm_t1"])
        k.tt("dve", t2[:, :], PI[:, :, 1], lim[:, :], ALU.mult, r=["PRI", "lim"], w=["cm_t2"])
        k.tt("dve", t1[:, :], t1[:, :], t2[:, :], ALU.add, r=["cm_t1", "cm_t2"], w=["cm_t1"])
        k.tt("dve", bR[:, :], t1[:, :], den[:, :], ALU.mult, r=["cm_t1", "s5den"], w=["betR"])
        k.tt("dve", t1[:, :], PI[:, :, 1], lre[:, :], ALU.mult, r=["PRI", "lre"], w=["cm_t1"])
        k.tt("dve", t2[:, :], nr_[:, :], lim[:, :], ALU.mult, r=["s5nr", "lim"], w=["cm_t2"])
        k.tt("dve", t1[:, :], t1[:, :], t2[:, :], ALU.subtract, r=["cm_t1", "cm_t2"], w=["cm_t1"])
        k.tt("dve", bI[:, :], t1[:, :], den[:, :], ALU.mult, r=["cm_t1", "s5den"], w=["betI"])
        bbr = sb("bbr", [128, 32, 16], F32)
        bbi = sb("bbi", [128, 32, 16], F32)
        u1 = sb("s5u1", [128, 32, 16], F32)
        u2 = sb("s5u2", [128, 32, 16], F32)
        bc16 = lambda t_: t_[:, :].unsqueeze(2).to_broadcast([128, 32, 16])
        cmul(k, "dve", bbr[:, :, :], bbi[:, :, :], bc16(bR), bc16(bI), Br[:, :, :], Bi[:, :, :],
             u1[:, :, :], u2[:, :, :], ["betR", "betI", "Br", "Bi"], ["bb"])
        Z = sb("Zm", [128, 32, 8, 16], F32)
        W = sb("Wm", [128, 32, 8, 16], F32)
        Cm = sb("Cm", [128, 32, 8, 16], F32)
        v1 = sb("s5v1", [128, 32, 8, 16], F32)
        v2 = sb("s5v2", [128, 32, 8, 16], F32)

        def big_cmul(dst, dkey, Xr, Xi, Tr, Ti, neg_im, rk):
            xb_ = lambda t_, lo, hi: t_[lo:hi, :, :].unsqueeze(2).to_broadcast([hi - lo, 32, 8, 16])
            tb_ = lambda t_, lo, hi: t_[lo:hi].unsqueeze(3).to_broadcast([hi - lo, 32, 8, 16])
            k.tt("dve", v1[0:64], xb_(Xr, 0, 64), tb_(Tr, 0, 64), ALU.mult, r=rk, w=["s5v1"])
            k.tt("dve", v2[0:64], xb_(Xi, 0, 64), tb_(Ti, 0, 64), ALU.mult, r=rk, w=["s5v2"])
            k.tt("dve", dst[0:64], v1[0:64], v2[0:64], ALU.subtract, r=["s5v1", "s5v2"], w=[dkey])
            k.tt("dve", v1[64:128], xb_(Xi, 64, 128), tb_(Tr, 64, 128), ALU.mult, r=rk + ["s5v1"], w=["s5v1"])
            k.tt("dve", v2[64:128], xb_(Xr, 64, 128), tb_(Ti, 64, 128), ALU.mult, r=rk + ["s5v2"], w=["s5v2"])
            if neg_im:
                k.stt("dve", dst[64:128], v1[64:128], -1.0, v2[64:128], ALU.mult, ALU.subtract,
                      r=["s5v1", "s5v2"], w=[dkey])
            else:
                k.tt("dve", dst[64:128], v1[64:128], v2[64:128], ALU.add, r=["s5v1", "s5v2"], w=[dkey])

        big_cmul(Z, "Zm", bbr, bbi, NR[:, :, 0:8], NI[:, :, 0:8], False, ["bb", "NRI"])
        PRrev = sb("PRrev", [128, 32, 8], F32)
        PIrev = sb("PIrev", [128, 32, 8], F32)
        for tau in range(8):
            k.cp("pool", PRrev[:, :, tau], PR[:, :, 7 - tau], r=["PRI"], w=["Prev"])
            k.cp("pool", PIrev[:, :, tau], PI[:, :, 7 - tau], r=["PRI"], w=["Prev"])
        big_cmul(W, "Wm", bbr, bbi, PRrev[:, :, :], PIrev[:, :, :], False, ["bb", "Prev"])
        big_cmul(Cm, "Cm", Cr, Ci, PR[:, :, 1:9], PI[:, :, 1:9], True, ["Cr", "Ci", "PRI"])

        mt_ = sb("s5mt", [128, 4, 128], F32)
        k.cp("act", Cmb[:, :, :], Cm[:, :, :, :].rearrange("p g r i -> p g (r i)"), r=["Cm"], w=["Cmb"])
        for g4 in range(8):
            pm, km = k.ps()
            for gg in range(4):
                g = g4 * 4 + gg
                k.mm(pm[:, gg * 128:(gg + 1) * 128], Z[:, g].rearrange("p t j -> p (t j)"),
                     Cm[:, g].rearrange("p r i -> p (r i)"), r=["Zm", "Cm"], w=[km])
            k.tt("dve", mt_[:, :, :], pm[:, :].rearrange("p (a b) -> p a b", a=4),
                 Mc[:, :].unsqueeze(1).to_broadcast([128, 4, 128]), ALU.mult, r=[km, "Mc"], w=["s5mt"])
            for gg in range(4):
                g = g4 * 4 + gg
                k.stt("dve", Mb[:, g, :], I32t[:, :], dP[:, g:g + 1], mt_[:, gg, :], ALU.mult, ALU.add,
                      r=["I32t", "dP", "s5mt"], w=["Mb"])
            pw, kw = k.ps()
            for gg in range(4):
                g = g4 * 4 + gg
                k.tr(pw[:, gg * 128:(gg + 1) * 128], W[:, g].rearrange("p t j -> p (t j)"), I32t[:, :],
                     r=["Wm", "I32t"], w=[kw])
            k.cp("act", BmT[:, g4 * 4:(g4 + 1) * 4, :], pw[:, :].rearrange("p (a b) -> p a b", a=4), r=[kw], w=["BmT"])

        P.emit()
        es1.close()
        sb = sbp
        uTp = sb("uTp", [128, 4, 8, 512], BF16)
        ysT = sb("ysT", [128, 4, S], BF16)
        xbs = [sb(f"xb1_{i}", [128, 8, TT], BF16) for i in range(2)]
        for tt_ in range(NTT):
            tok = slice(tt_ * TT, (tt_ + 1) * TT)
            xb = xbs[tt_ % 2]
            xk = f"xb1_{tt_ % 2}"
            k.dma("pool", xb[:, :, :], kt_view(T["xT"], tok), w=[xk])
            for t in range(4):
                pu, ku = k.ps()
                for kt in range(8):
                    k.mm(pu[:, :], wu[:, t, kt * 128:(kt + 1) * 128], xb[:, kt, :], start=(kt == 0), stop=(kt == 7),
                         r=["wu", xk], w=[ku])
                k.cp("act", uTp[:, t, :, tt_ * 64:(tt_ + 1) * 64], pu[:, :].rearrange("p (c t) -> p t c", t=8),
                     r=[ku], w=[f"uTp{t}"])

        Ug = [sb(f"Ug{i}", [128, 512], BF16) for i in range(8)]
        Hb = [sb(f"Hb{i}", [128, 513], BF16) for i in range(8)]
        Yg = [sb(f"Yg{i}", [128, 512], BF16) for i in range(8)]
        Rt = [sb(f"Rt{i}", [128, 128], F32) for i in range(2)]
        Rj = [sb(f"Rj{i}", [128, 128], F32) for i in range(2)]
        Rm = [sb(f"Rm{i}", [128, 128], BF16) for i in range(4)]
        for i in range(8):
            k.memset("pool", Hb[i][:, 0:1], 0.0, w=[f"Hb{i}"])
        ri = 0
        for t in range(4):
            for gl in range(8):
                g = 8 * t + gl
                pu, ku = k.ps()
                for tau in range(8):
                    k.mm(pu[:, :], Sel[:, gl * 8 + tau, :], uTp[:, t, tau, :], start=(tau == 0), stop=(tau == 7),
                         r=["Sel", f"uTp{t}"], w=[ku])
                k.cp("act", Ug[gl][:, :], pu[:, :], r=[ku], w=[f"Ug{gl}"])
                ph, kh = k.ps()
                k.mm(ph[:, :], BmT[:, g, :], Ug[gl][:, :], r=["BmT", f"Ug{gl}"], w=[kh])
                k.cp("act", Hb[gl][:, 1:513], ph[:, :], r=[kh], w=[f"Hb{gl}"])
            for l in range(9):
                d = 1 << l
                for gl in range(8):
                    g = 8 * t + gl
                    rt, rtk = Rt[ri % 2], f"Rt{ri % 2}"
                    rm, rmk = Rm[ri % 4], f"Rm{ri % 4}"
                    rj, rjk = Rj[ri % 2], f"Rj{ri % 2}"
                    ri += 1
                    k.ts("dve", rt[:, :], I16[:, :], LR[:, g, l:l + 1], ALU.mult, r=["I16", "LRI"], w=[rtk])
                    k.stt("dve", rm[:, :], J16[:, :], LI[:, g, l:l + 1], rt[:, :], ALU.mult, ALU.add,
                          r=["J16", "LRI", rtk], w=[rmk])
                    ps_, ks_ = k.ps()
                    k.mm(ps_[:, 0:512 - d], rm[:, :], Hb[gl][:, 1:513 - d], r=[rmk, f"Hb{gl}"], w=[ks_])
                    k.tt("dve", Hb[gl][:, 1 + d:513], Hb[gl][:, 1 + d:513], ps_[:, 0:512 - d], ALU.add,
                         r=[ks_, f"Hb{gl}"], w=[f"Hb{gl}"])
            for gl in range(8):
                g = 8 * t + gl
                py, ky = k.ps()
                k.mm(py[:, :], Mb[:, g, :], Ug[gl][:, :], start=True, stop=False, r=["Mb", f"Ug{gl}"], w=[ky])
                k.mm(py[:, :], Cmb[:, g, :], Hb[gl][:, 0:512], start=False, stop=True, r=["Cmb", f"Hb{gl}"], w=[ky])
                k.act(Yg[gl][:, :], py[:, :], AF.Gelu_apprx_tanh, r=[ky], w=[f"Yg{gl}"])
            for r_ in range(8):
                pt, kt_ = k.ps()
                for gl in range(8):
                    k.mm(pt[:, :], SelI[:, gl * 8 + r_, :], Yg[gl][:, :], start=(gl == 0), stop=(gl == 7),
                         r=["SelI", f"Yg{gl}"], w=[kt_])
                k.cp("act", ysT[:, t, :].rearrange("p (c r) -> p r c", r=8)[:, r_, :], pt[:, :], r=[kt_], w=[f"ysT{t}"])
        sgl = [sb(f"sgl{i}", [128, TT], F32) for i in range(2)]
        og = [sb(f"og{i}", [128, TT], BF16) for i in range(2)]
        it = 0
        for tt_ in range(NTT):
            tok = slice(tt_ * TT, (tt_ + 1) * TT)
            for n in range(4):
                pz, kz = k.ps()
                for kt in range(4):
                    k.mm(pz[:, :], wglu[:, n, kt * 128:(kt + 1) * 128], ysT[:, kt, tok], start=(kt == 0), stop=(kt == 3),
                         r=["wglu", f"ysT{kt}"], w=[kz])
                s_, sk = sgl[it % 2], f"sgl{it % 2}"
                o_, ok_ = og[it % 2], f"og{it % 2}"
                it += 1
                k.act(s_[:, :], pz[:, :], AF.Sigmoid, r=[kz, "bglu"], w=[sk], bias=bglu[:, n:n + 1])
                k.tt("dve", o_[:, :], ysT[:, n, tok], s_[:, :], ALU.mult, r=[f"ysT{n}", sk], w=[ok_])
                k.dma("sp", T["br"][1][n * 128:(n + 1) * 128, tok], o_[:, :], r=[ok_], w=[f"br1_{tt_}_{n}"])
        P.emit()


def phase_att(k, T):
    nc, P = k.nc, k.P
    with ExitStack() as es:
        sb = lambda n, s, d: k.sb(es, n, s, d)
        qT = sb("qT", [128, 4, S], BF16)
        kT = sb("kT", [128, 2, S], BF16)
        qiT = sb("qiT", [128, 2, S], BF16)
        kiT = sb("kiT", [128, S], BF16)
        V1 = sb("V1", [128, 32, 2, 65], BF16)
        widx = sb("widx", [128, 32, 4], F32)
        I16 = sb("I16a", [128, 128], BF16)
        k.dma("pool", I16[:, :], T["I128"], w=["I16a"])
        k.memset("dve", V1[:, :, :, 64:65], 1.0, w=["V1"])
        dests = [(qT, 0), (qT, 1), (qT, 2), (qT, 3), (kT, 0), (kT, 1), (qiT, 0), (qiT, 1), (kiT, None)]
        dkeys = ["qT", "qT", "qT", "qT", "kT", "kT", "qiT", "qiT", "kiT"]
        with ExitStack() as es2:
            sb2 = lambda n, s, d: k.sb(es2, n, s, d)
            watt = sb2("watt", [128, 18, 1024], BF16)
            wvw = sb2("wvw", [128, 8, 132], BF16)
            cosTs = [sb2(f"cosT{i}", [128, TT], F32) for i in range(2)]
            sinTs = [sb2(f"sinT{i}", [128, TT], F32) for i in range(2)]
            posi = sb2("posi", [128, TT], I32)
            posf = sb2("posf", [128, TT], F32)
            y_ = sb2("sr_y3", [128, TT], F32)
            ki_ = sb2("sr_k3", [128, TT], I32)
            kf_ = sb2("sr_kf3", [128, TT], F32)
            ifr = sb2("ifr", [128, 2], F32)
            negpi = sb2("negpi3", [128, 1], F32)
            xbs = [sb2(f"xb3_{i}", [128, 8, TT], BF16) for i in range(2)]
            r1 = [sb2(f"r1_{i}", [128, TT], F32) for i in range(2)]
            r2 = [sb2(f"r2_{i}", [128, TT], F32) for i in range(2)]
            for i in range(18):
                k.dma("pool", watt[:, i, :], T["w_att_l"][i], w=[f"watt{i}"])
            k.dma("pool", wvw[:, :, :], T["w_vw_l"], w=["wvw"])
            k.dma("sp", ifr[:, :], T["ifr"], w=["ifr"])
            k.memset("pool", negpi[:, :], SIN_BI, w=["negpi"])
            tmp3 = (y_[:, :], ki_[:, :], kf_[:, :])
            it = 0
            for tt_ in range(NTT):
                tok = slice(tt_ * TT, (tt_ + 1) * TT)
                xb = xbs[tt_ % 2]
                xk = f"xb3_{tt_ % 2}"
                k.dma("pool", xb[:, :, :], kt_view(T["xT"], tok), w=[xk])
                cosT, ck = cosTs[tt_ % 2], f"cosT{tt_ % 2}"
                sinT, sk_ = sinTs[tt_ % 2], f"sinT{tt_ % 2}"
                k.dma("sp", posi[:, :], T["pos"][:, tok].partition_broadcast(128), w=["posi"])
                k.cp("dve", posf[:, :], posi[:, :], r=["posi"], w=["posf"])
                sin_reduced(k, sinT[:, :], posf[:, :], ifr[:, 0:1], False, None, tmp3, negpi[:, 0:1], ["posf", "ifr"], [sk_])
                sin_reduced(k, cosT[:, :], posf[:, :], ifr[:, 0:1], True, None, tmp3, negpi[:, 0:1], ["posf", "ifr"], [ck])
                k.ts("dve", sinT[:, :], sinT[:, :], ifr[:, 1:2], ALU.mult, r=[sk_, "ifr"], w=[sk_])
                for i in range(9):
                    p1, k1 = k.ps()
                    for kt in range(8):
                        k.mm(p1[:, :], watt[:, 2 * i, kt * 128:(kt + 1) * 128], xb[:, kt, :], start=(kt == 0), stop=(kt == 7),
                             r=[f"watt{2 * i}", xk], w=[k1])
                    p2, k2 = k.ps()
                    for kt in range(8):
                        k.mm(p2[:, :], watt[:, 2 * i + 1, kt * 128:(kt + 1) * 128], xb[:, kt, :], start=(kt == 0), stop=(kt == 7),
                             r=[f"watt{2 * i + 1}", xk], w=[k2])
                    a1, a1k = r1[it % 2], f"r1_{it % 2}"
                    a2, a2k = r2[it % 2], f"r2_{it % 2}"
                    it += 1
                    k.tt("dve", a1[:, :], p1[:, :], cosT[:, :], ALU.mult, r=[k1, ck], w=[a1k])
                    k.tt("dve", a2[:, :], p2[:, :], sinT[:, :], ALU.mult, r=[k2, sk_], w=[a2k])
                    dt_, di = dests[i]
                    dst = dt_[:, di, tok] if di is not None else dt_[:, tok]
                    k.tt("pool", dst, a1[:, :], a2[:, :], ALU.add, r=[a1k, a2k], w=[dkeys[i]])
                for sbk in range(4):
                    sblk = tt_ * 4 + sbk
                    pv, kv = k.ps()
                    for kt in range(8):
                        k.mm(pv[:, 0:132], xb[:, kt, sbk * 128:(sbk + 1) * 128], wvw[:, kt, :], start=(kt == 0), stop=(kt == 7),
                             r=["wvw", xk], w=[kv])
                    k.cp("act", V1[:, sblk, :, 0:64], pv[:, 0:128].rearrange("p (g d) -> p g d", g=2), r=[kv], w=["V1"])
                    k.act(widx[:, sblk, :], pv[:, 128:132], AF.Copy, r=[kv], w=["widx"], scale=0.5)
            P.emit()

        scoreAs = [sb(f"scoreA{i}", [128, S], F32) for i in range(2)]
        zr = sb("zr", [128, S], F32)
        junkDs = [sb(f"junkD{i}", [128, S], BF16) for i in range(2)]
        masks = [sb(f"mask{i}", [128, S], BF16) for i in range(2)]
        maskTs = [sb(f"maskT{i}", [128, S], BF16) for i in range(4)]
        rl = [sb(f"rl{i}", [128, 512], F32) for i in range(2)]
        Pt = [sb(f"Pt{i}", [128, 512], BF16) for i in range(3)]
        KB = 24
        sts = [sb(f"selst{i}", [128, 16], F32) for i in range(2)]
        nmss = [[sb(f"nm{j}_{i}", [128, 1], F32) for i in range(2)] for j in range(2)]
        ssums = [sb(f"ssum{i}", [128, 1], F32) for i in range(2)]
        gsgs = [sb(f"gsg{i}", [128, 1], F32) for i in range(2)]
        nwrow = sb("nwrow", [128, KB], F32)
        nwtabs = [sb(f"nwtab{i}", [128, KB], F32) for i in range(2)]
        nw2tabs = [sb(f"nw2tab{i}", [128, KB], F32) for i in range(2)]
        thrc = sb("thrc", [128, 1], F32)
        rec = sb("rec_a", [128, 8], F32)
        attn = sb("attn_tm", [128, 8, 64], BF16)
        aT = [sb(f"aT{i}", [128, 4, 128], BF16) for i in range(2)]
        k.dma("sp", nwrow[:, :], T["nwrow"].partition_broadcast(128), w=["nwrow"])
        k.memset("pool", thrc[:, :], -1.0e29, w=["thrc"])
        psB = k.psB
        ring_save = k.ring
        psOs = [[ring_save[3], ring_save[4]], [ring_save[5], ring_save[6]]]
        k.ring = ring_save[:3]

        def indexer(b):
            N = 128 * (b + 1)
            sl = b % 2
            scoreA, sak = scoreAs[sl], f"scoreA{sl}"
            st = sts[sl]
            tb = slice(b * 128, (b + 1) * 128)
            nch = (N + 511) // 512
            for c in range(nch):
                c0_ = c * 512
                cw_ = min(512, N - c0_)
                for h in range(4):
                    base = 64 * (h % 2)
                    pl, kl = k.ps()
                    k.mm(pl[:, 0:cw_], qiT[base:base + 64, h // 2, tb], kiT[base:base + 64, c0_:c0_ + cw_],
                         r=["qiT", "kiT"], w=[kl])
                    t_, tk = rl[(c * 4 + h) % 2], f"rl{(c * 4 + h) % 2}"
                    k.act(t_[:, 0:cw_], pl[:, 0:cw_], AF.Relu, r=[kl], w=[tk], scale=0.125)
                    if h == 0:
                        k.ts("dve", scoreA[:, c0_:c0_ + cw_], t_[:, 0:cw_], widx[:, b, h:h + 1], ALU.mult,
                             r=[tk, "widx"], w=[sak])
                    else:
                        k.stt("dve", scoreA[:, c0_:c0_ + cw_], t_[:, 0:cw_], widx[:, b, h:h + 1], scoreA[:, c0_:c0_ + cw_],
                              ALU.mult, ALU.add, r=[tk, "widx", sak], w=[sak])
            if b >= 2:
                P.op("dve", lambda e, o=junkDs[sl][:, 0:N], i_=scoreA[:, 0:N], a=st[:, 12:13]:
                     e.tensor_scalar(out=o, in0=i_, scalar1=1.0, scalar2=None, op0=ALU.mult, op1=ALU.max, accum_out=a),
                     r=[sak], w=[f"junkD{sl}", f"st_amax1_{sl}"])
                P.op("dve", lambda e, o=junkDs[sl][:, 0:N], i_=scoreA[:, 0:N], a=st[:, 13:14]:
                     e.tensor_scalar(out=o, in0=i_, scalar1=-1.0, scalar2=None, op0=ALU.mult, op1=ALU.max, accum_out=a),
                     r=[sak], w=[f"junkD{sl}", f"st_amax2_{sl}"])
                k.tt("dve", st[:, 0:1], st[:, 12:13], st[:, 13:14], ALU.max, r=[f"st_amax1_{sl}", f"st_amax2_{sl}"],
                     w=[f"st_amax_{sl}"])
            k.memset("dve", scoreA[0:64, N - 64:N], -1.0e30, w=[sak])

        def sel_setup(b):
            N = 128 * (b + 1)
            sl = b % 2
            sc, sak = scoreAs[sl][:, 0:N], f"scoreA{sl}"
            st = sts[sl]
            junk, jk = junkDs[sl], f"junkD{sl}"
            K_ = lambda nm: f"{nm}_{sl}"
            amax, cp, cpz, isA, isC, lo0, hi0, w0 = (st[:, i:i + 1] for i in range(8))
            need, t0_, t1_ = (st[:, i:i + 1] for i in range(8, 11))
            cnt = lambda o, op0, key: P.op(
                "dve", lambda e: e.tensor_scalar(out=junk[:, 0:N], in0=sc, scalar1=0.0, scalar2=None, op0=op0,
                                                 op1=ALU.add, accum_out=o), r=[sak], w=[jk, key])
            cnt(cp, ALU.is_gt, K_("st_cp"))
            cnt(cpz, ALU.is_ge, K_("st_cpz"))
            k.ts("dve", isA, cp, 255.5, ALU.is_ge, r=[K_("st_cp")], w=[K_("st_isA")])
            k.ts("dve", isC, cpz, 255.5, ALU.is_lt, r=[K_("st_cpz")], w=[K_("st_isC")])
            k.ts("dve", t0_, amax, -1.0001, ALU.mult, -1.0e-30, ALU.add, r=[K_("st_amax")], w=[K_("st_t0")])
            k.tt("dve", lo0, isC, t0_, ALU.mult, r=[K_("st_isC"), K_("st_t0")], w=[K_("st_lo0")])
            k.tt("dve", t1_, isA, amax, ALU.mult, r=[K_("st_isA"), K_("st_amax")], w=[K_("st_t1")])
            k.stt("dve", hi0, isC, -2.0e-38, t1_, ALU.mult, ALU.add, r=[K_("st_isC"), K_("st_t1")], w=[K_("st_hi0")])
            k.tt("dve", w0, hi0, lo0, ALU.subtract, r=[K_("st_hi0"), K_("st_lo0")], w=[K_("st_w0")])
            k.tt("dve", nwtabs[sl][:, :], nwrow[:, :], w0.to_broadcast([128, KB]), ALU.mult,
                 r=["nwrow", K_("st_w0")], w=[K_("nwtab")])
            k.ts("dve", nw2tabs[sl][:, :], nwtabs[sl][:, :], -2.0, ALU.mult, r=[K_("nwtab")], w=[K_("nwtab")])
            k.tt("dve", t0_, isA, isC, ALU.add, r=[K_("st_isA"), K_("st_isC"), K_("st_t0")], w=[K_("st_t0")])
            k.ts("dve", t0_, t0_, -1.0, ALU.mult, 1.0, ALU.add, r=[K_("st_t0")], w=[K_("st_t0")])
            k.ts("dve", t1_, cp, -1.0, ALU.mult, 256.0, ALU.add, r=[K_("st_cp"), K_("st_t1")], w=[K_("st_t1")])
            k.tt("dve", need, t0_, t1_, ALU.mult, r=[K_("st_t0"), K_("st_t1")], w=[K_("st_need")])
            k.cp("dve", nmss[sl][0][:, :], lo0, r=[K_("st_lo0")], w=[f"nm{sl}_0"])

        def sel_step(b, it_, part):
            N = 128 * (b + 1)
            sl = b % 2
            sc, sak = scoreAs[sl][:, 0:N], f"scoreA{sl}"
            junk, jk = junkDs[sl], f"junkD{sl}"
            cur, ck_ = nmss[sl][it_ % 2], f"nm{sl}_{it_ % 2}"
            nxt, nk_ = nmss[sl][(it_ + 1) % 2], f"nm{sl}_{(it_ + 1) % 2}"
            gsg, gk = gsgs[sl], f"gsg{sl}"
            ssum, sk = ssums[sl], f"ssum{sl}"
            if part == 0:
                k.ts("dve", gsg[:, :], cur[:, :], nw2tabs[sl][:, it_:it_ + 1], ALU.add, r=[ck_, f"nwtab_{sl}"], w=[gk])
            elif part == 1:
                P.op("dve", lambda e: e.tensor_scalar(
                    out=junk[:, 0:N], in0=sc, scalar1=gsg[:, 0:1], scalar2=None, op0=ALU.is_gt, op1=ALU.add,
                    accum_out=ssum[:, 0:1]), r=[sak, gk], w=[jk, sk])
            elif part == 2:
                k.ts("dve", ssum[:, :], ssum[:, :], 255.5, ALU.is_lt, NEG_FILL, ALU.mult, r=[sk], w=[sk])
            else:
                k.ts("dve", nxt[:, :], ssum[:, :], gsg[:, 0:1], ALU.add, cur[:, 0:1], ALU.max, r=[sk, gk, ck_], w=[nk_])

        def sel_finish(b):
            N = 128 * (b + 1)
            nsb = b + 1
            sl = b % 2
            sc, sak = scoreAs[sl][:, 0:N], f"scoreA{sl}"
            mask, mk = masks[sl], f"mask{sl}"
            maskT, mtk = maskTs[b % 4], f"maskT{b % 4}"
            if b >= 2:
                zm, zk = junkDs[sl], f"junkD{sl}"
                need = sts[sl][:, 8:9]
                fin, fk_ = nmss[sl][KB % 2], f"nm{sl}_{KB % 2}"
                k.ts("dve", mask[:, 0:N], sc, fin[:, 0:1], ALU.is_gt, r=[sak, fk_], w=[mk])
                k.ts("dve", zm[:, 0:N], sc, 0.0, ALU.is_equal, r=[sak], w=[zk])
                P.op("dve", lambda e: e.tensor_tensor_scan(out=zr[:, 0:N], data0=zm[:, 0:N], data1=zm[:, 0:N], initial=0.0,
                                                           op0=ALU.add, op1=ALU.max), r=[zk], w=["zr"])
                k.stt("dve", zm[:, 0:N], zr[:, 0:N], need, zm[:, 0:N], ALU.is_le, ALU.mult, r=["zr", f"st_need_{sl}", zk], w=[zk])
                k.tt("pool", mask[:, 0:N], mask[:, 0:N], zm[:, 0:N], ALU.add, r=[mk, zk], w=[mk])
            else:
                k.ts("dve", mask[:, 0:N], sc, thrc[:, 0:1], ALU.is_ge, r=[sak, "thrc"], w=[mk])
            for j0 in range(0, nsb, 8):
                nj = min(8, nsb - j0)
                pb, kb = psB
                for jj in range(nj):
                    j = j0 + jj
                    k.tr(pb[:, jj * 128:(jj + 1) * 128], mask[:, j * 128:(j + 1) * 128], I16[:, :], r=[mk, "I16a"], w=[kb])
                k.act(maskT[:, j0 * 128:(j0 + nj) * 128], pb[:, 0:nj * 128], AF.Identity, r=[kb], w=[mtk],
                      scale=MASK_BIG, bias=-MASK_BIG)

        def select2(bs):
            ch = [b_ for b_ in bs if b_ >= 2]
            for b_ in ch:
                sel_setup(b_)
            for it_ in range(KB):
                for part in range(4):
                    for b_ in ch:
                        sel_step(b_, it_, part)
            for b_ in bs:
                sel_finish(b_)

        def attend(b):
            nsb = b + 1
            tb = slice(b * 128, (b + 1) * 128)
            maskT, mtk = maskTs[b % 4], f"maskT{b % 4}"
            psO = psOs[b % 2]
            pi_ = 0
            for h in range(8):
                g = h // 4
                base = 64 * (h % 2)
                po, ko = psO[h // 4]
                for j0 in range(0, nsb, 4):
                    nj = min(4, nsb - j0)
                    ps_, ks_ = k.ps()
                    for jj in range(nj):
                        j = j0 + jj
                        k.mm(ps_[:, jj * 128:(jj + 1) * 128], kT[base:base + 64, g, j * 128:(j + 1) * 128],
                             qT[base:base + 64, h // 2, tb], start=(jj == 0), stop=False, r=["kT", "qT"], w=[ks_])
                    for jj in range(nj):
                        j = j0 + jj
                        k.mm(ps_[:, jj * 128:(jj + 1) * 128], I16[:, :], maskT[:, j * 128:(j + 1) * 128],
                             start=False, stop=(jj == nj - 1), r=["I16a", mtk], w=[ks_])
                    p_, pk_ = Pt[pi_ % 3], f"Pt{pi_ % 3}"
                    pi_ += 1
                    k.act(p_[:, 0:nj * 128], ps_[:, 0:nj * 128], AF.Exp, r=[ks_], w=[pk_], scale=0.125)
                    for jj in range(nj):
                        j = j0 + jj
                        k.mm(po[:, (h % 4) * 65:(h % 4) * 65 + 65], p_[:, jj * 128:(jj + 1) * 128], V1[:, j, g, :],
                             start=(j == 0), stop=(j == nsb - 1), r=[pk_, "V1"], w=[ko])

        def finish(b):
            tb = slice(b * 128, (b + 1) * 128)
            psO = psOs[b % 2]
            for hb in range(2):
                po, ko = psO[hb]
                pv = po[:, 0:260].rearrange("p (h e) -> p h e", h=4)
                k.recip(rec[:, hb * 4:(hb + 1) * 4], pv[:, :, 64], r=[ko], w=["rec_a"])
                k.tt("dve", attn[:, hb * 4:(hb + 1) * 4, :], pv[:, :, 0:64],
                     rec[:, hb * 4:(hb + 1) * 4].unsqueeze(2).to_broadcast([128, 4, 64]), ALU.mult,
                     r=[ko, "rec_a"], w=["attn_tm"])
            pb, kb = psB
            a_, ak = aT[b % 2], f"aT{b % 2}"
            af = attn[:, :, :].rearrange("p h d -> p (h d)")
            for t in range(4):
                k.tr(pb[:, t * 128:(t + 1) * 128], af[:, t * 128:(t + 1) * 128], I16[:, :], r=["attn_tm", "I16a"], w=[kb])
            k.cp("act", a_[:, :, :], pb[:, 0:512].rearrange("p (t s) -> p t s", t=4), r=[kb], w=[ak])
            k.dma("sp", kt_view(T["br"][0], tb), a_[:, :, :], r=[ak], w=[f"br0_{b // 4}_{b % 4}"])

        NP = S // 256
        indexer(0)
        indexer(1)
        select2([0, 1])
        for p_i in range(NP):
            A_, B_ = 2 * p_i, 2 * p_i + 1
            if p_i + 1 < NP:
                indexer(A_ + 2)
                indexer(B_ + 2)
            attend(A_)
            attend(B_)
            if p_i + 1 < NP:
                select2([A_ + 2, B_ + 2])
            finish(A_)
            finish(B_)
        k.ring = ring_save
        P.emit()


def build_program(shapes, phases=("s5", "mem", "att", "mix"), br_mode="internal"):
    nc = bass.Bass("TRN2", target_bir_lowering=False)
    T = {}
    for name, (shape, dt) in shapes.items():
        T[name] = nc.dram_tensor(name, list(shape), dt, kind="ExternalInput").ap()
    T["outT"] = nc.dram_tensor("outT", [D, S], F32, kind="ExternalOutput").ap()
    if br_mode == "internal":
        brt = nc.dram_tensor("br", [3, 512, S], BF16).ap()
    elif br_mode == "output":
        brt = nc.dram_tensor("br", [3, 512, S], BF16, kind="ExternalOutput").ap()
    else:
        brt = nc.dram_tensor("br", [3, 512, S], BF16, kind="ExternalInput").ap()
    T["br"] = [brt[i] for i in range(3)]
    T["wA_bf"] = nc.dram_tensor("wA_bf", [8, 128, 4608], BF16).ap()
    T["wO_bf"] = nc.dram_tensor("wO_bf", [8, 128, 1024], BF16).ap()
    T["wU_bf"] = nc.dram_tensor("wU_bf", [22, 128, 2048], BF16).ap()
    T["wD_bf"] = nc.dram_tensor("wD_bf", [8, 128, 2816], BF16).ap()
    with ExitStack() as es:
        P = Prog(nc, es)
        k = K(nc, P)
        for i in range(5):
            t = es.enter_context(nc.psum_tensor(f"psf{i}", [128, 512], F32))
            k.ring.append((t, f"psf{i}"))
        o0 = es.enter_context(nc.psum_tensor("psf5", [128, 512], F32))
        o1 = es.enter_context(nc.psum_tensor("psf6", [128, 512], F32))
        pb = es.enter_context(nc.psum_tensor("psb", [128, 1024], BF16))
        k.psO = [(o0, "psf5"), (o1, "psf6")]
        k.psB = (pb, "psb")
        k.ring.append((o0, "psf5"))
        k.ring.append((o1, "psf6"))
        if "mix" in phases:
            for i in range(8):
                k.dma("pool", T["wA_bf"][i], T["wA_l"][i], w=[f"wbf_A{i}"])
            for i in range(8):
                k.dma("pool", T["wO_bf"][i], T["wO_l"][i], w=[f"wbf_O{i}"])
            for i in range(22):
                k.dma("pool", T["wU_bf"][i], T["wU_l"][i], w=[f"wbf_U{i}"])
            for i in range(8):
                k.dma("pool", T["wD_bf"][i], T["wD_l"][i], w=[f"wbf_Dn{i}"])
        if "s5" in phases:
            phase_s5(k, T)
        if "mem" in phases:
            phase_mem(k, T)
        if "att" in phases:
            phase_att(k, T)
        if "mix" in phases:
            phase_mix(k, T)
        else:
            P.wait_all("sp", [kk_ for kk_ in P.lastw if kk_.startswith("br")])
            P.emit()
    return nc


def tile_lhsT(W):
    Kd, N = W.shape
    return np.ascontiguousarray(W.reshape(Kd // 128, 128, N // 128, 128).transpose(2, 1, 0, 3).reshape(N // 128, 128, (Kd // 128) * 128))


def rhs_layout(W):
    Kd, N = W.shape
    return np.ascontiguousarray(W.reshape(Kd // 128, 128, N).transpose(1, 0, 2))


def prep_shared(inp):
    f = np.float32
    sh = {}
    w_in = inp["w_in"][0]
    def hc(base, h):
        return base + h * 64 + np.arange(64)
    def sw(c):
        return np.concatenate([c[32:], c[:32]])
    tiles = []
    for j in range(4):
        a, b = hc(0, 2 * j), hc(0, 2 * j + 1)
        tiles += [np.concatenate([a, b]), np.concatenate([sw(a), sw(b)])]
    for g in range(2):
        a = hc(512, g)
        tiles += [np.concatenate([a, a]), np.concatenate([sw(a), sw(a)])]
    for j in range(2):
        a, b = hc(768, 2 * j), hc(768, 2 * j + 1)
        tiles += [np.concatenate([a, b]), np.concatenate([sw(a), sw(b)])]
    a = 1024 + np.arange(64)
    tiles += [np.concatenate([a, a]), np.concatenate([sw(a), sw(a)])]
    cols = np.concatenate(tiles)
    sh["w_att_l"] = tile_lhsT(w_in[:, cols])
    sh["w_vw_l"] = rhs_layout(np.concatenate([w_in[:, 640:768], w_in[:, 1088:1092]], axis=1))
    sh["w_u_l"] = np.ascontiguousarray(tile_lhsT(w_in[:, 1092:1604]).transpose(1, 0, 2))
    sh["w_qmem_l"] = np.ascontiguousarray(tile_lhsT(w_in[:, 1604:2116]).transpose(1, 0, 2))
    wkv = inp["w_mem_kv"][0]
    sh["w_kmem_l"] = np.ascontiguousarray(tile_lhsT(wkv[:, 0:512]).transpose(1, 0, 2))
    sh["w_vmem_l"] = rhs_layout(wkv[:, 512:1024])
    sh["w_glu_l"] = np.ascontiguousarray(tile_lhsT(inp["w_glu"][0]).transpose(1, 0, 2))
    sh["b_glu_l"] = np.ascontiguousarray(inp["b_glu"][0].reshape(4, 128).T)
    p = np.arange(128)
    inv_freq = (10000.0 ** (-(np.arange(32, dtype=np.float32)) / np.float32(32))).astype(np.float32)
    ifr = np.zeros((128, 2), f)
    ifr[:, 0] = (inv_freq[p % 32].astype(np.float64) / TWO_PI).astype(f)
    ifr[:, 1] = np.where((p % 64) < 32, -1.0, 1.0)
    sh["ifr"] = ifr
    sh["nwrow"] = (-(2.0 ** -(np.arange(24, dtype=np.float64) + 2.0))).astype(f).reshape(1, 24)
    dup = lambda a_: np.ascontiguousarray(np.concatenate([a_, a_], axis=0).astype(f))
    sh["lre"] = dup(inp["s5_lam_re"][0].T)
    sh["lim"] = dup(inp["s5_lam_im"][0].T)
    sh["ldt"] = np.ascontiguousarray(np.broadcast_to(inp["s5_log_dt"][0][None, :], (128, 32)).astype(f))
    sh["Br"] = dup(inp["s5_b_re"][0].transpose(1, 0, 2))
    sh["Bi"] = dup(inp["s5_b_im"][0].transpose(1, 0, 2))
    sh["Cr"] = dup(inp["s5_c_re"][0].transpose(2, 0, 1))
    sh["Ci"] = dup(inp["s5_c_im"][0].transpose(2, 0, 1))
    sh["dP"] = np.ascontiguousarray(np.tile(inp["s5_d"][0].reshape(32, 16).T, (8, 1)).astype(f))
    sh["I128"] = np.eye(128, dtype=f)
    J = np.zeros((128, 128), f)
    for q in range(64):
        J[q, 64 + q] = 1.0
        J[64 + q, q] = -1.0
    sh["J128"] = J
    tau = np.arange(128) // 16
    sh["Mc"] = (tau[None, :] >= tau[:, None]).astype(f)
    Sel = np.zeros((128, 64, 128), f)
    SelI = np.zeros((128, 64, 128), f)
    for gl in range(8):
        for t in range(8):
            for j in range(16):
                Sel[gl * 16 + j, gl * 8 + t, t * 16 + j] = 1.0
                SelI[t * 16 + j, gl * 8 + t, gl * 16 + j] = 1.0
    sh["Sel"] = Sel
    sh["SelI"] = SelI
    wg = inp["w_gate"][0]
    pa = [tile_lhsT(inp[n][0]) for n in ("w_proj_a", "w_proj_b", "w_proj_c")]
    ga = [tile_lhsT(wg[:, i * 1024:(i + 1) * 1024]) for i in range(3)]
    sh["wA_l"] = np.ascontiguousarray(np.concatenate(pa + ga, axis=2))
    sh["wO_l"] = tile_lhsT(inp["w_out"][0])
    wu = tile_lhsT(inp["w_up"][0])
    sh["wU_l"] = np.ascontiguousarray(np.concatenate([wu[0:22], wu[22:44]], axis=2))
    sh["wD_l"] = tile_lhsT(inp["w_down"][0])
    sh["b_gate_l"] = np.ascontiguousarray(inp["b_gate"][0].reshape(3, 8, 128).transpose(2, 0, 1).reshape(128, 24))
    vec = lambda v: v.reshape(8, 128).T
    sh["ln_l"] = np.ascontiguousarray(np.concatenate([vec(inp["ln1_g"][0]), vec(inp["ln1_b"][0]),
                                                      vec(inp["ln2_g"][0]), vec(inp["ln2_b"][0])], axis=1))
    sh["conv_w_l"] = np.ascontiguousarray(inp["conv_w"][0].T.reshape(44, 128, 3).transpose(1, 0, 2))
    sh["conv_b_l"] = np.ascontiguousarray(inp["conv_b"][0].reshape(44, 128).T)
    return {k_: np.ascontiguousarray(v.astype(f)) for k_, v in sh.items()}


def prep_core(inp, b):
    return {
        "xT": np.ascontiguousarray(inp["x"][b].T),
        "memT": np.ascontiguousarray(inp["mem"][b].T),
        "pos": np.ascontiguousarray(inp["positions"][b].reshape(1, S).astype(np.int32)),
    }


def kernel(**inputs):
    inp = {k_: np.asarray(v) for k_, v in inputs.items()}
    shared = prep_shared(inp)
    B = inp["x"].shape[0]
    in_maps = []
    for b in range(B):
        m = dict(shared)
        m.update(prep_core(inp, b))
        in_maps.append(m)
    shapes = {k_: (v.shape, I32 if v.dtype == np.int32 else F32) for k_, v in in_maps[0].items()}
    nc = build_program(shapes)
    res = run_bass_kernel_spmd(nc, in_maps, core_ids=list(range(B)))
    out = np.stack([np.asarray(r["outT"]).T for r in res.results], axis=0)
    return np.ascontiguousarray(out.astype(np.float32))
```

```python
from contextlib import ExitStack
import numpy as np
import concourse.bass as bass
import concourse.mybir as mybir
from concourse.bass_utils import run_bass_kernel_spmd

F32 = mybir.dt.float32
BF16 = mybir.dt.bfloat16
I32 = mybir.dt.int32
AF = mybir.ActivationFunctionType
ALU = mybir.AluOpType

S = 4096
D = 1024
TT = 512
NTT = S // TT
ALPHA = 2.0 ** 0.25
LN_EPS = 1e-5
TWO_PI = 6.283185307179586
SIN_SC = 6.2831840
SIN_BI = -3.1415920
EPOCH = 30000
NEG_FILL = -3.0e38
MASK_BIG = 30000.0
FUSE_WAITS = True


class Prog:
    ENGS = ("pe", "act", "dve", "pool", "sp")

    def __init__(self, nc, es, ndma=8):
        self.nc = nc
        self.es = es
        self.ops = {e: [] for e in self.ENGS}
        self.count = {e: 0 for e in self.ENGS}
        self.sems = {}
        self.ndma = ndma
        self.dsems = {}
        self.dcount = {e: 0 for e in self.ENGS}
        self.dtarget = {}
        self.waited = {e: {} for e in self.ENGS}
        self.lastw = {}
        self.readers = {}

    def _sem(self, eng, epoch):
        k = (eng, epoch)
        if k not in self.sems:
            self.sems[k] = self.es.enter_context(self.nc.semaphore(f"s_{eng}_{epoch}"))
        return self.sems[k]

    def _dsem(self, eng, r):
        k = (eng, r)
        if k not in self.dsems:
            self.dsems[k] = self.es.enter_context(self.nc.semaphore(f"d_{eng}_{r}"))
            self.dtarget[k] = 0
        return self.dsems[k]

    def _need(self, eng, ev, waits):
        if ev is None:
            return
        sem, val, src = ev
        if src == "pe" and eng == "pe":
            return
        w = self.waited[eng]
        if w.get(id(sem), 0) >= val:
            return
        w[id(sem)] = val
        waits.append((sem, val))

    def op(self, eng, fn, r=(), w=(), dma=False, fuse=False):
        waits = []
        for k in r:
            self._need(eng, self.lastw.get(k), waits)
        for k in w:
            self._need(eng, self.lastw.get(k), waits)
            for ev in self.readers.get(k, {}).values():
                self._need(eng, ev, waits)
        if dma:
            i = self.dcount[eng]
            self.dcount[eng] += 1
            rr = i % self.ndma
            sem = self._dsem(eng, rr)
            prev = self.dtarget[(eng, rr)]
            if prev > 0:
                self._need(eng, (sem, prev, "dma"), waits)
            self.dtarget[(eng, rr)] = prev + 16
            ev = (sem, prev + 16, "dma")
            inc = 16
        else:
            self.count[eng] += 1
            ep, v = divmod(self.count[eng] - 1, EPOCH)
            sem = self._sem(eng, ep)
            ev = (sem, v + 1, eng)
            inc = 1
        best = {}
        for s_, v_ in waits:
            if id(s_) not in best or best[id(s_)][1] < v_:
                best[id(s_)] = (s_, v_)
        self.ops[eng].append((list(best.values()), fn, sem, inc, FUSE_WAITS and not dma))
        for k in r:
            self.readers.setdefault(k, {})[(eng, id(sem))] = ev
        for k in w:
            self.lastw[k] = ev
            self.readers[k] = {}
        return ev

    def wait_all(self, eng, keys):
        waits = []
        for k in keys:
            self._need(eng, self.lastw.get(k), waits)
        self.ops[eng].append((waits, None, None, 0, False))

    def drain_dmas(self, eng="sp"):
        waits = []
        for (qe, r_), sem in self.dsems.items():
            tgt = self.dtarget[(qe, r_)]
            if tgt > 0:
                self._need(eng, (sem, tgt, "dma"), waits)
        self.ops[eng].append((waits, None, None, 0, False))

    def emit(self):
        self.drain_dmas("sp")
        nc = self.nc
        ops = self.ops
        self.ops = {e: [] for e in self.ENGS}

        def run(engname):
            def _f(eng):
                for wl, fn, sem, inc, fuse in ops[engname]:
                    if fuse and fn is not None and len(wl) >= 1:
                        for s_, v_ in wl[:-1]:
                            eng.wait_ge(s_, v_)
                        ins = fn(eng)
                        ins._wait_ge(wl[-1][0], wl[-1][1])
                        ins.then_inc(sem, inc)
                        continue
                    for s_, v_ in wl:
                        eng.wait_ge(s_, v_)
                    if fn is not None:
                        fn(eng).then_inc(sem, inc)
            return _f

        with nc.Block() as block:
            block.tensor(run("pe"))
            block.scalar(run("act"))
            block.vector(run("dve"))
            block.gpsimd(run("pool"))
            block.sync(run("sp"))


class K:
    def __init__(self, nc, P):
        self.nc = nc
        self.P = P
        self.ring = []
        self.ri = 0

    def sb(self, es, name, shape, dt):
        return es.enter_context(self.nc.sbuf_tensor("sb_" + name, shape, dt))

    def ps(self):
        t = self.ring[self.ri % len(self.ring)]
        self.ri += 1
        return t

    def mm(self, out, lhsT, rhs, start=True, stop=True, r=(), w=()):
        self.P.op("pe", lambda e: e.matmul(out, lhsT=lhsT, rhs=rhs, start=start, stop=stop), r=r, w=w)

    def tr(self, out, in_, ident, r=(), w=()):
        self.P.op("pe", lambda e: e.transpose(out, in_, ident), r=r, w=w)

    def act(self, out, in_, func, r=(), w=(), **kw):
        self.P.op("act", lambda e: e.activation(out=out, in_=in_, func=func, **kw), r=r, w=w, fuse=("accum_out" not in kw))

    def tt(self, eng, out, in0, in1, op, r=(), w=()):
        self.P.op(eng, lambda e: e.tensor_tensor(out=out, in0=in0, in1=in1, op=op), r=r, w=w, fuse=True)

    def ts(self, eng, out, in0, s1, op0, s2=None, op1=None, r=(), w=()):
        if op1 is None:
            self.P.op(eng, lambda e: e.tensor_scalar(out=out, in0=in0, scalar1=s1, scalar2=None, op0=op0), r=r, w=w, fuse=True)
        else:
            self.P.op(eng, lambda e: e.tensor_scalar(out=out, in0=in0, scalar1=s1, scalar2=s2, op0=op0, op1=op1), r=r, w=w, fuse=True)

    def stt(self, eng, out, in0, scalar, in1, op0, op1, r=(), w=()):
        self.P.op(eng, lambda e: e.scalar_tensor_tensor(out=out, in0=in0, scalar=scalar, in1=in1, op0=op0, op1=op1), r=r, w=w, fuse=True)

    def cp(self, eng, out, in_, r=(), w=()):
        if eng == "act":
            self.act(out, in_, AF.Copy, r=r, w=w)
        else:
            self.P.op(eng, lambda e: e.tensor_copy(out=out, in_=in_), r=r, w=w, fuse=True)

    def recip(self, out, in_, r=(), w=()):
        self.P.op("dve", lambda e: e.reciprocal(out=out, in_=in_), r=r, w=w)

    def memset(self, eng, ap, val, w=()):
        self.P.op(eng, lambda e: e.memset(ap, val), w=w)

    def dma(self, eng, out, in_, r=(), w=()):
        self.P.op(eng, lambda e: e.dma_start(out=out, in_=in_), r=r, w=w, dma=True)


def kt_view(dram_ap, cols):
    return dram_ap.rearrange("(kt p) s -> p kt s", p=128)[:, :, cols]


def sin_reduced(k, out, in_, mul, quarter, shape, tmp, negpi, rk, wk):
    y, ki, kf = tmp
    off = 8.5 + (0.25 if quarter else 0.0)
    if isinstance(mul, float):
        k.ts("dve", y, in_, mul / TWO_PI, ALU.mult, off, ALU.add, r=rk, w=["sr_y"])
    else:
        k.ts("dve", y, in_, mul, ALU.mult, off, ALU.add, r=rk, w=["sr_y"])
    k.cp("dve", ki, y, r=["sr_y"], w=["sr_k"])
    k.cp("dve", kf, ki, r=["sr_k"], w=["sr_kf"])
    k.tt("dve", y, y, kf, ALU.subtract, r=["sr_y", "sr_kf"], w=["sr_y"])
    k.ts("dve", kf, y, 0.0, ALU.is_lt, r=["sr_y"], w=["sr_kf"])
    k.tt("dve", y, y, kf, ALU.add, r=["sr_y", "sr_kf"], w=["sr_y"])
    k.act(out, y, AF.Sin, r=["sr_y", "negpi"], w=wk, bias=negpi, scale=SIN_SC)


def layer_norm(k, es_tmp, z, zkey, gvec, bvec, out32, o32key, out16, o16key, mid_cb=None):
    zb, zsq, onesD, meanS, m2, var, rstd, tmps, epsb = es_tmp
    for kt in range(8):
        k.act(zb[:, kt, :], z[:, kt, :], AF.Copy, r=[zkey], w=[f"zb{kt}"])
        k.act(zsq[:, kt, :], z[:, kt, :], AF.Square, r=[zkey], w=[f"zsq{kt}"])
    if mid_cb is not None:
        mid_cb()
    psM, kM = k.ps()
    for kt in range(8):
        k.mm(psM[:, :], onesD[:, :], zb[:, kt, :], start=(kt == 0), stop=(kt == 7), r=[f"zb{kt}", "onesD"], w=[kM])
    psQ, kQ = k.ps()
    for kt in range(8):
        k.mm(psQ[:, :], onesD[:, :], zsq[:, kt, :], start=(kt == 0), stop=(kt == 7), r=[f"zsq{kt}", "onesD"], w=[kQ])
    k.act(meanS[:, :], psM[:, :], AF.Copy, r=[kM], w=["meanS"])
    k.tt("dve", m2[:, :], meanS[:, :], meanS[:, :], ALU.mult, r=["meanS"], w=["m2"])
    k.tt("dve", var[:, :], psQ[:, :], m2[:, :], ALU.subtract, r=[kQ, "m2"], w=["var"])
    k.act(var[:, :], var[:, :], AF.Sqrt, r=["var", "epsb"], w=["var"], bias=epsb[:, 0:1])
    k.recip(rstd[:, :], var[:, :], r=["var"], w=["rstd"])
    for kt in range(8):
        t = tmps[kt % 2]
        tk = f"lnt{kt % 2}"
        k.tt("dve", t[:, :], z[:, kt, :], meanS[:, :], ALU.subtract, r=[zkey, "meanS"], w=[tk])
        k.tt("dve", t[:, :], t[:, :], rstd[:, :], ALU.mult, r=[tk, "rstd"], w=[tk])
        k.act(out32[:, kt, :], t[:, :], AF.Identity, r=[tk, "lnvec"], w=[o32key],
              scale=gvec[:, kt:kt + 1], bias=bvec[:, kt:kt + 1])
        if out16 is not None:
            k.cp("pool", out16(kt), out32[:, kt, :], r=[o32key], w=[o16key])


def phase_mix(k, T):
    nc, P = k.nc, k.P
    with ExitStack() as es:
        sb = lambda n, s, d: k.sb(es, n, s, d)
        xf = sb("xf", [128, 8, TT], F32)
        xbs = [sb(f"xb4_{j}", [128, 8, TT], BF16) for j in range(2)]
        brts = [[sb(f"brt{i}_{j}", [128, 4, TT], BF16) for i in range(3)] for j in range(2)]
        gs = [sb(f"gs{i}", [128, TT], F32) for i in range(3)]
        m1 = sb("m1", [128, TT], F32)
        tA = sb("tA", [128, TT], F32)
        tB = sb("tB", [128, TT], F32)
        merged = sb("merged", [128, 8, TT], BF16)
        zb = sb("zb", [128, 8, TT], BF16)
        zsq = sb("zsq", [128, 8, TT], BF16)
        onesD = sb("onesD", [128, 128], BF16)
        meanS = sb("meanS", [128, TT], F32)
        m2 = sb("m2", [128, TT], F32)
        var = sb("var", [128, TT], F32)
        rstd = sb("rstd", [128, TT], F32)
        lnt = [sb(f"lnt{i}", [128, TT], F32) for i in range(2)]
        epsb = sb("epsb", [128, 1], F32)
        h1 = sb("h1", [128, 8, TT], F32)
        h1b = sb("h1b", [128, 8, TT + 2], BF16)
        actb = sb("actb", [128, 22, TT], BF16)
        c0 = [sb(f"c0_{i}", [128, 256], F32) for i in range(2)]
        c1 = [sb(f"c1_{i}", [128, 256], F32) for i in range(2)]
        sg = sb("sg", [128, 256], F32)
        ot = sb("ot", [128, 8, TT], F32)
        NSLOT = 3
        wring = [sb(f"wr{i}", [128, 4608], BF16) for i in range(NSLOT)]
        bgate = sb("bgate", [128, 24], F32)
        lnv = sb("lnv", [128, 32], F32)
        cw = sb("cw", [128, 44, 3], F32)
        cb = sb("cb", [128, 44], F32)
        ln_tmp = (zb, zsq, onesD, meanS, m2, var, rstd, lnt, epsb)

        k.dma("sp", bgate[:, :], T["b_gate_l"], w=["bgate"])
        k.dma("sp", lnv[:, :], T["ln_l"], w=["lnvec"])
        k.dma("sp", cw[:, :, :], T["conv_w_l"], w=["cw"])
        k.dma("sp", cb[:, :], T["conv_b_l"], w=["cw"])
        k.memset("pool", onesD[:, :], 1.0 / 1024.0, w=["onesD"])
        k.memset("pool", epsb[:, :], LN_EPS, w=["epsb"])
        k.memset("pool", h1b[:, :, 0:2], 0.0, w=["h1b"])

        chunks = [("A", n) for n in range(8)] + [("O", n) for n in range(8)]
        for tt_ in range(NTT):
            if tt_ + 1 < NTT:
                chunks += [("A", n) for n in range(8)]
            chunks += [("U", i) for i in range(22)]
            chunks += [("Dn", n) for n in range(8)]
            if tt_ + 1 < NTT:
                chunks += [("O", n) for n in range(8)]
        csize = {"A": 4608, "O": 1024, "U": 2048, "Dn": 2816}
        csrc = {"A": T["wA_bf"], "O": T["wO_bf"], "U": T["wU_bf"], "Dn": T["wD_bf"]}
        state = {"next": 0}

        def load_chunk():
            i = state["next"]
            if i >= len(chunks):
                return
            kind, idx = chunks[i]
            slot = i % NSLOT
            k.dma("sp", wring[slot][:, 0:csize[kind]], csrc[kind][idx], r=[f"wbf_{kind}{idx}"], w=[f"wr{slot}"])
            state["next"] += 1

        ci = {"i": 0}

        def get_chunk(kind):
            i = ci["i"]
            assert chunks[i][0] == kind, (i, chunks[i], kind)
            ci["i"] += 1
            return wring[i % NSLOT], f"wr{i % NSLOT}"

        for _ in range(NSLOT):
            load_chunk()

        def load_xb_br(t_):
            tk_ = slice(t_ * TT, (t_ + 1) * TT)
            p_ = t_ % 2
            k.dma("pool", xbs[p_][:, :, :], kt_view(T["xT"], tk_), w=[f"xb4_{p_}"])
            for i in range(3):
                k.dma("sp", brts[p_][i][:, :, :], kt_view(T["br"][i], tk_), r=[f"br{i}_{t_}_{q}" for q in range(4)],
                      w=[f"brt{i}_{p_}"])

        def load_xf(t_):
            tk_ = slice(t_ * TT, (t_ + 1) * TT)
            k.dma("sp", xf[:, :, :], kt_view(T["xT"], tk_), w=["xf"])

        def merge(t_, n0=0, n1=8):
            p_ = t_ % 2
            xb, xbk = xbs[p_], f"xb4_{p_}"
            brt = brts[p_]
            for n in range(n0, n1):
                wt, wk = get_chunk("A")
                psy = []
                for i in range(3):
                    py, ky = k.ps()
                    for kt in range(4):
                        c = (4 * i + kt) * 128
                        k.mm(py[:, :], wt[:, c:c + 128], brt[i][:, kt, :], start=(kt == 0), stop=(kt == 3),
                             r=[wk, f"brt{i}_{p_}"], w=[ky])
                    pg, kg = k.ps()
                    for kt in range(8):
                        c = (12 + 8 * i + kt) * 128
                        k.mm(pg[:, :], wt[:, c:c + 128], xb[:, kt, :], start=(kt == 0), stop=(kt == 7),
                             r=[wk, xbk], w=[kg])
                    k.act(gs[i][:, :], pg[:, :], AF.Sigmoid, r=[kg, "bgate"], w=[f"gs{i}"],
                          bias=bgate[:, i * 8 + n:i * 8 + n + 1])
                    psy.append((py, ky))
                load_chunk()
                k.tt("dve", m1[:, :], psy[0][0][:, :], gs[0][:, :], ALU.mult, r=[psy[0][1], "gs0"], w=["m1"])
                k.tt("dve", tA[:, :], psy[1][0][:, :], gs[1][:, :], ALU.mult, r=[psy[1][1], "gs1"], w=["tA"])
                k.tt("dve", tB[:, :], psy[2][0][:, :], gs[2][:, :], ALU.mult, r=[psy[2][1], "gs2"], w=["tB"])
                k.tt("pool", m1[:, :], m1[:, :], tA[:, :], ALU.add, r=["m1", "tA"], w=["m1"])
                k.tt("pool", merged[:, n, :], m1[:, :], tB[:, :], ALU.add, r=["m1", "tB"], w=[f"merged{n}"])

        def outproj(t_, n0=0, n1=8):
            for n in range(n0, n1):
                wt, wk = get_chunk("O")
                px, kx = k.ps()
                for kt in range(8):
                    k.mm(px[:, :], wt[:, kt * 128:(kt + 1) * 128], merged[:, kt, :], start=(kt == 0), stop=(kt == 7),
                         r=[wk, f"merged{kt}"], w=[kx])
                load_chunk()
                k.stt("dve", xf[:, n, :], xf[:, n, :], ALPHA, px[:, :], ALU.mult, ALU.add, r=["xf", kx], w=["xf"])

        load_xb_br(0)
        load_xf(0)
        merge(0)
        outproj(0)
        for tt_ in range(NTT):
            tok = slice(tt_ * TT, (tt_ + 1) * TT)
            if tt_ + 1 < NTT:
                load_xb_br(tt_ + 1)
            if tt_ > 0:
                k.cp("pool", h1b[:, :, 0:2], h1b[:, :, TT:TT + 2], r=["h1b"], w=["h1b"])
            nxt_ = tt_ + 1 < NTT
            layer_norm(k, ln_tmp, xf, "xf", lnv[:, 0:8], lnv[:, 8:16], h1, "h1",
                       lambda kt: h1b[:, kt, 2:TT + 2], "h1b",
                       mid_cb=(lambda t1=tt_ + 1: merge(t1, 0, 4)) if nxt_ else None)
            if nxt_:
                merge(tt_ + 1, 4, 8)
                load_xf(tt_ + 1)
            for i in range(22):
                wt, wk = get_chunk("U")
                for hf in range(2):
                    res = []
                    for part in range(2):
                        pp, kp = k.ps()
                        for kt in range(8):
                            c = (part * 8 + kt) * 128
                            k.mm(pp[:, 0:258], wt[:, c:c + 128], h1b[:, kt, hf * 256:hf * 256 + 258],
                                 start=(kt == 0), stop=(kt == 7), r=[wk, "h1b"], w=[kp])
                        ch = i + 22 * part
                        a0, a1 = c0[part], c1[part]
                        k.act(a0[:, :], pp[:, 2:258], AF.Identity, r=[kp, "cw"], w=[f"c0_{part}"],
                              scale=cw[:, ch, 2:3], bias=cb[:, ch:ch + 1])
                        k.stt("dve", a1[:, :], pp[:, 1:257], cw[:, ch, 1:2], a0[:, :], ALU.mult, ALU.add,
                              r=[kp, "cw", f"c0_{part}"], w=[f"c1_{part}"])
                        k.stt("dve", a0[:, :], pp[:, 0:256], cw[:, ch, 0:1], a1[:, :], ALU.mult, ALU.add,
                              r=[kp, "cw", f"c1_{part}"], w=[f"c0_{part}"])
                        res.append(a0)
                    k.act(sg[:, :], res[0][:, :], AF.Silu, r=["c0_0"], w=["sg"])
                    k.tt("pool", actb[:, i, hf * 256:(hf + 1) * 256], sg[:, :], res[1][:, :], ALU.mult,
                         r=["sg", "c0_1"], w=[f"actb{i}"])
                load_chunk()
            for n in range(8):
                wt, wk = get_chunk("Dn")
                pd, kd = k.ps()
                for kt in range(22):
                    k.mm(pd[:, :], wt[:, kt * 128:(kt + 1) * 128], actb[:, kt, :], start=(kt == 0), stop=(kt == 21),
                         r=[wk, f"actb{kt}"], w=[kd])
                load_chunk()
                k.stt("dve", h1[:, n, :], h1[:, n, :], ALPHA, pd[:, :], ALU.mult, ALU.add, r=["h1", kd], w=["h1"])
            layer_norm(k, ln_tmp, h1, "h1", lnv[:, 16:24], lnv[:, 24:32], ot, "ot", None, None,
                       mid_cb=(lambda t1=tt_ + 1: outproj(t1, 0, 4)) if nxt_ else None)
            if nxt_:
                outproj(tt_ + 1, 4, 8)
            k.dma("act", kt_view(T["outT"], tok), ot[:, :, :], r=["ot"], w=[f"outT{tt_}"])
        P.wait_all("sp", [f"outT{t_}" for t_ in range(NTT)])
        P.emit()


def phase_mem(k, T):
    nc, P = k.nc, k.P
    with ExitStack() as es:
        sb = lambda n, s, d: k.sb(es, n, s, d)
        memb = sb("memb", [128, 8, 256], BF16)
        wq = sb("wq_mem", [128, 4, 1024], BF16)
        wkk = sb("wk_mem", [128, 4, 1024], BF16)
        wv = sb("wv_mem", [128, 8, 512], BF16)
        kmT = sb("kmT", [128, 4, 256], BF16)
        vm = sb("vm", [128, 2, 512], BF16)
        ones = sb("ones_mem", [128, 128], BF16)
        xbs = [sb(f"xb2_{i}", [128, 8, TT], BF16) for i in range(2)]
        qm = [sb(f"qm{i}", [128, TT], BF16) for i in range(2)]
        pT = [sb(f"pT{i}", [128, 2, TT], BF16) for i in range(2)]
        rec = sb("rec_mem", [128, TT], F32)
        ob = [sb(f"ob{i}", [128, TT], BF16) for i in range(2)]

        k.dma("pool", memb[:, :, :], T["memT"].rearrange("(kt p) m -> p kt m", p=128), w=["memb"])
        k.dma("pool", wq[:, :, :], T["w_qmem_l"], w=["wq_mem"])
        k.dma("pool", wkk[:, :, :], T["w_kmem_l"], w=["wk_mem"])
        k.dma("pool", wv[:, :, :], T["w_vmem_l"], w=["wv_mem"])
        k.memset("dve", ones[:, :], 1.0, w=["ones_mem"])
        for h in range(4):
            pk, kk = k.ps()
            for kt in range(8):
                k.mm(pk[:, 0:256], wkk[:, h, kt * 128:(kt + 1) * 128], memb[:, kt, :], start=(kt == 0), stop=(kt == 7),
                     r=["wk_mem", "memb"], w=[kk])
            k.cp("act", kmT[:, h, :], pk[:, 0:256], r=[kk], w=["kmT"])
        for mt in range(2):
            pv, kv = k.ps()
            for kt in range(8):
                k.mm(pv[:, :], memb[:, kt, mt * 128:(mt + 1) * 128], wv[:, kt, :], start=(kt == 0), stop=(kt == 7),
                     r=["wv_mem", "memb"], w=[kv])
            k.cp("act", vm[:, mt, :], pv[:, :], r=[kv], w=["vm"])
        sc = 128.0 ** -0.5
        it = 0
        for tt_ in range(NTT):
            tok = slice(tt_ * TT, (tt_ + 1) * TT)
            xb = xbs[tt_ % 2]
            xk = f"xb2_{tt_ % 2}"
            k.dma("pool", xb[:, :, :], kt_view(T["xT"], tok), w=[xk])
            for h in range(4):
                q_, qk = qm[it % 2], f"qm{it % 2}"
                p_, pk_ = pT[it % 2], f"pT{it % 2}"
                o_, ok_ = ob[it % 2], f"ob{it % 2}"
                it += 1
                pq, kq = k.ps()
                for kt in range(8):
                    k.mm(pq[:, :], wq[:, h, kt * 128:(kt + 1) * 128], xb[:, kt, :], start=(kt == 0), stop=(kt == 7),
                         r=["wq_mem", xk], w=[kq])
                k.cp("act", q_[:, :], pq[:, :], r=[kq], w=[qk])
                for mt in range(2):
                    ps_, ks_ = k.ps()
                    k.mm(ps_[:, :], kmT[:, h, mt * 128:(mt + 1) * 128], q_[:, :], r=["kmT", qk], w=[ks_])
                    k.act(p_[:, mt, :], ps_[:, :], AF.Exp, r=[ks_], w=[pk_], scale=sc)
                po, ko = k.ps()
                for mt in range(2):
                    k.mm(po[:, :], vm[:, mt, h * 128:(h + 1) * 128], p_[:, mt, :], start=(mt == 0), stop=(mt == 1),
                         r=["vm", pk_], w=[ko])
                pd, kd = k.ps()
                for mt in range(2):
                    k.mm(pd[:, :], ones[:, :], p_[:, mt, :], start=(mt == 0), stop=(mt == 1),
                         r=["ones_mem", pk_], w=[kd])
                k.recip(rec[:, :], pd[:, :], r=[kd], w=["rec_mem"])
                k.tt("dve", o_[:, :], po[:, :], rec[:, :], ALU.mult, r=[ko, "rec_mem"], w=[ok_])
                k.dma("sp", T["br"][2][h * 128:(h + 1) * 128, tok], o_[:, :], r=[ok_], w=[f"br2_{tt_}_{h}"])
        P.emit()


def cmul(k, eng, outr, outi, ar, ai, br_, bi_, t1, t2, r, w):
    k.tt(eng, t1, ar, br_, ALU.mult, r=r, w=["cm_t1"])
    k.tt(eng, t2, ai, bi_, ALU.mult, r=r, w=["cm_t2"])
    k.tt(eng, outr, t1, t2, ALU.subtract, r=["cm_t1", "cm_t2"], w=w)
    k.tt(eng, t1, ar, bi_, ALU.mult, r=r + ["cm_t1"], w=["cm_t1"])
    k.tt(eng, t2, ai, br_, ALU.mult, r=r + ["cm_t2"], w=["cm_t2"])
    k.tt(eng, outi, t1, t2, ALU.add, r=["cm_t1", "cm_t2"], w=w)


def phase_s5(k, T):
    nc, P = k.nc, k.P
    with ExitStack() as es:
        sbp = lambda n, s, d: k.sb(es, n, s, d)
        es1 = ExitStack()
        sb = lambda n, s, d: k.sb(es1, n, s, d)
        I16 = sbp("I16", [128, 128], BF16)
        J16 = sbp("J16", [128, 128], BF16)
        Sel = sbp("Sel", [128, 64, 128], BF16)
        SelI = sbp("SelI", [128, 64, 128], BF16)
        wu = sbp("wu", [128, 4, 1024], BF16)
        wglu = sbp("wglu", [128, 4, 512], BF16)
        bglu = sbp("bglu", [128, 4], F32)
        LR = sbp("LR", [128, 32, 9], F32)
        LI = sbp("LI", [128, 32, 9], F32)
        Mb = sbp("Mb", [128, 32, 128], BF16)
        BmT = sbp("BmT", [128, 32, 128], BF16)
        Cmb = sbp("Cmb", [128, 32, 128], BF16)
        I32t = sb("I32t", [128, 128], F32)
        Mc = sb("Mc", [128, 128], F32)
        lre = sb("lre", [128, 32], F32)
        lim = sb("lim", [128, 32], F32)
        ldt = sb("ldt", [128, 32], F32)
        dP = sb("dP", [128, 32], F32)
        Br = sb("Br", [128, 32, 16], F32)
        Bi = sb("Bi", [128, 32, 16], F32)
        Cr = sb("Cr", [128, 32, 16], F32)
        Ci = sb("Ci", [128, 32, 16], F32)
        negpi = sb("negpi", [128, 1], F32)
        k.dma("sp", I32t[:, :], T["I128"], w=["I32t"])
        k.dma("pool", I16[:, :], T["I128"], w=["I16"])
        k.dma("pool", J16[:, :], T["J128"], w=["J16"])
        k.dma("sp", Mc[:, :], T["Mc"], w=["Mc"])
        k.dma("pool", Sel[:, :, :], T["Sel"], w=["Sel"])
        k.dma("pool", SelI[:, :, :], T["SelI"], w=["SelI"])
        for nm, t_ in (("lre", lre), ("lim", lim), ("ldt", ldt), ("dP", dP)):
            k.dma("sp", t_[:, :], T[nm], w=[nm])
        for nm, t_ in (("Br", Br), ("Bi", Bi), ("Cr", Cr), ("Ci", Ci)):
            k.dma("sp", t_[:, :, :], T[nm], w=[nm])
        k.dma("pool", wu[:, :, :], T["w_u_l"], w=["wu"])
        k.dma("pool", wglu[:, :, :], T["w_glu_l"], w=["wglu"])
        k.dma("sp", bglu[:, :], T["b_glu_l"], w=["bglu"])
        k.memset("pool", negpi[:, :], SIN_BI, w=["negpi"])

        a_ = sb("s5a", [128, 32], F32)
        th = sb("s5th", [128, 32], F32)
        dt_ = sb("s5dt", [128, 32], F32)
        y_ = sb("sr_y", [128, 32], F32)
        ki_ = sb("sr_k", [128, 32], I32)
        kf_ = sb("sr_kf", [128, 32], F32)
        sn = sb("s5sn", [128, 32], F32)
        cs = sb("s5cs", [128, 32], F32)
        mg = sb("s5mg", [128, 32], F32)
        t1 = sb("cm_t1", [128, 32], F32)
        t2 = sb("cm_t2", [128, 32], F32)
        PR = sb("PR", [128, 32, 9], F32)
        PI = sb("PI", [128, 32, 9], F32)
        NR = sb("NR", [128, 32, 8], F32)
        NI = sb("NI", [128, 32, 8], F32)
        bR = sb("betR", [128, 32], F32)
        bI = sb("betI", [128, 32], F32)
        inv = sb("s5inv", [128, 32], F32)

        k.ts("dve", lre[:, :], lre[:, :], -1e-4, ALU.min, r=["lre"], w=["lre"])
        k.act(dt_[:, :], ldt[:, :], AF.Exp, r=["ldt"], w=["s5dt"])
        k.tt("dve", a_[:, :], lre[:, :], dt_[:, :], ALU.mult, r=["lre", "s5dt"], w=["s5a"])
        k.tt("dve", th[:, :], lim[:, :], dt_[:, :], ALU.mult, r=["lim", "s5dt"], w=["s5th"])
        k.act(mg[:, :], a_[:, :], AF.Exp, r=["s5a"], w=["s5mg"])
        tmp3 = (y_[:, :], ki_[:, :], kf_[:, :])
        sin_reduced(k, sn[:, :], th[:, :], 1.0, False, None, tmp3, negpi[:, 0:1], ["s5th"], ["s5sn"])
        sin_reduced(k, cs[:, :], th[:, :], 1.0, True, None, tmp3, negpi[:, 0:1], ["s5th"], ["s5cs"])
        k.memset("dve", PR[:, :, 0], 1.0, w=["PRI"])
        k.memset("dve", PI[:, :, 0], 0.0, w=["PRI"])
        k.tt("dve", PR[:, :, 1], mg[:, :], cs[:, :], ALU.mult, r=["s5mg", "s5cs"], w=["PRI"])
        k.tt("dve", PI[:, :, 1], mg[:, :], sn[:, :], ALU.mult, r=["s5mg", "s5sn"], w=["PRI"])
        for n in range(2, 9):
            cmul(k, "dve", PR[:, :, n], PI[:, :, n], PR[:, :, n - 1], PI[:, :, n - 1], PR[:, :, 1], PI[:, :, 1],
                 t1[:, :], t2[:, :], ["PRI"], ["PRI"])
        k.cp("dve", LR[:, :, 0], PR[:, :, 8], r=["PRI"], w=["LRI"])
        k.cp("dve", LI[:, :, 0], PI[:, :, 8], r=["PRI"], w=["LRI"])
        for l in range(1, 9):
            cmul(k, "dve", LR[:, :, l], LI[:, :, l], LR[:, :, l - 1], LI[:, :, l - 1], LR[:, :, l - 1], LI[:, :, l - 1],
                 t1[:, :], t2[:, :], ["LRI"], ["LRI"])
        for n in range(8):
            k.act(inv[:, :], a_[:, :], AF.Exp, r=["s5a"], w=["s5inv"], scale=-2.0 * (n + 1))
            k.tt("dve", NR[:, :, n], PR[:, :, n + 1], inv[:, :], ALU.mult, r=["PRI", "s5inv"], w=["NRI"])
            k.stt("dve", NI[:, :, n], PI[:, :, n + 1], -1.0, inv[:, :], ALU.mult, ALU.mult, r=["PRI", "s5inv"], w=["NRI"])
        nr_ = sb("s5nr", [128, 32], F32)
        den = sb("s5den", [128, 32], F32)
        k.ts("dve", nr_[:, :], PR[:, :, 1], -1.0, ALU.add, r=["PRI"], w=["s5nr"])
        k.tt("dve", den[:, :], lre[:, :], lre[:, :], ALU.mult, r=["lre"], w=["s5den"])
        k.tt("dve", t1[:, :], lim[:, :], lim[:, :], ALU.mult, r=["lim"], w=["cm_t1"])
        k.tt("dve", den[:, :], den[:, :], t1[:, :], ALU.add, r=["s5den", "cm_t1"], w=["s5den"])
        k.recip(den[:, :], den[:, :], r=["s5den"], w=["s5den"])
        k.tt("dve", t1[:, :], nr_[:, :], lre[:, :], ALU.mult, r=["s5nr", "lre"], w=["cm_t1"])
        k.tt("dve", t2[:, :], PI[:, :, 1], lim[:, :], ALU.mult, r=["PRI", "lim"], w=["cm_t2"])
        k.tt("dve", t1[:, :], t1[:, :], t2[:, :], ALU.add, r=["cm_t1", "cm_t2"], w=["cm_t1"])
        k.tt("dve", bR[:, :], t1[:, :], den[:, :], ALU.mult, r=["cm_t1", "s5den"], w=["betR"])
        k.tt("dve", t1[:, :], PI[:, :, 1], lre[:, :], ALU.mult, r=["PRI", "lre"], w=["cm_t1"])
        k.tt("dve", t2[:, :], nr_[:, :], lim[:, :], ALU.mult, r=["s5nr", "lim"], w=["cm_t2"])
        k.tt("dve", t1[:, :], t1[:, :], t2[:, :], ALU.subtract, r=["cm_t1", "cm_t2"], w=["cm_t1"])
        k.tt("dve", bI[:, :], t1[:, :], den[:, :], ALU.mult, r=["cm_t1", "s5den"], w=["betI"])
        bbr = sb("bbr", [128, 32, 16], F32)
        bbi = sb("bbi", [128, 32, 16], F32)
        u1 = sb("s5u1", [128, 32, 16], F32)
        u2 = sb("s5u2", [128, 32, 16], F32)
        bc16 = lambda t_: t_[:, :].unsqueeze(2).to_broadcast([128, 32, 16])
        cmul(k, "dve", bbr[:, :, :], bbi[:, :, :], bc16(bR), bc16(bI), Br[:, :, :], Bi[:, :, :],
             u1[:, :, :], u2[:, :, :], ["betR", "betI", "Br", "Bi"], ["bb"])
        Z = sb("Zm", [128, 32, 8, 16], F32)
        W = sb("Wm", [128, 32, 8, 16], F32)
        Cm = sb("Cm", [128, 32, 8, 16], F32)
        v1 = sb("s5v1", [128, 32, 8, 16], F32)
        v2 = sb("s5v2", [128, 32, 8, 16], F32)

        def big_cmul(dst, dkey, Xr, Xi, Tr, Ti, neg_im, rk):
            xb_ = lambda t_, lo, hi: t_[lo:hi, :, :].unsqueeze(2).to_broadcast([hi - lo, 32, 8, 16])
            tb_ = lambda t_, lo, hi: t_[lo:hi].unsqueeze(3).to_broadcast([hi - lo, 32, 8, 16])
            k.tt("dve", v1[0:64], xb_(Xr, 0, 64), tb_(Tr, 0, 64), ALU.mult, r=rk, w=["s5v1"])
            k.tt("dve", v2[0:64], xb_(Xi, 0, 64), tb_(Ti, 0, 64), ALU.mult, r=rk, w=["s5v2"])
            k.tt("dve", dst[0:64], v1[0:64], v2[0:64], ALU.subtract, r=["s5v1", "s5v2"], w=[dkey])
            k.tt("dve", v1[64:128], xb_(Xi, 64, 128), tb_(Tr, 64, 128), ALU.mult, r=rk + ["s5v1"], w=["s5v1"])
            k.tt("dve", v2[64:128], xb_(Xr, 64, 128), tb_(Ti, 64, 128), ALU.mult, r=rk + ["s5v2"], w=["s5v2"])
            if neg_im:
                k.stt("dve", dst[64:128], v1[64:128], -1.0, v2[64:128], ALU.mult, ALU.subtract,
                      r=["s5v1", "s5v2"], w=[dkey])
            else:
                k.tt("dve", dst[64:128], v1[64:128], v2[64:128], ALU.add, r=["s5v1", "s5v2"], w=[dkey])

        big_cmul(Z, "Zm", bbr, bbi, NR[:, :, 0:8], NI[:, :, 0:8], False, ["bb", "NRI"])
        PRrev = sb("PRrev", [128, 32, 8], F32)
        PIrev = sb("PIrev", [128, 32, 8], F32)
        for tau in range(8):
            k.cp("pool", PRrev[:, :, tau], PR[:, :, 7 - tau], r=["PRI"], w=["Prev"])
            k.cp("pool", PIrev[:, :, tau], PI[:, :, 7 - tau], r=["PRI"], w=["Prev"])
        big_cmul(W, "Wm", bbr, bbi, PRrev[:, :, :], PIrev[:, :, :], False, ["bb", "Prev"])
        big_cmul(Cm, "Cm", Cr, Ci, PR[:, :, 1:9], PI[:, :, 1:9], True, ["Cr", "Ci", "PRI"])

        mt_ = sb("s5mt", [128, 4, 128], F32)
        k.cp("act", Cmb[:, :, :], Cm[:, :, :, :].rearrange("p g r i -> p g (r i)"), r=["Cm"], w=["Cmb"])
        for g4 in range(8):
            pm, km = k.ps()
            for gg in range(4):
                g = g4 * 4 + gg
                k.mm(pm[:, gg * 128:(gg + 1) * 128], Z[:, g].rearrange("p t j -> p (t j)"),
                     Cm[:, g].rearrange("p r i -> p (r i)"), r=["Zm", "Cm"], w=[km])
            k.tt("dve", mt_[:, :, :], pm[:, :].rearrange("p (a b) -> p a b", a=4),
                 Mc[:, :].unsqueeze(1).to_broadcast([128, 4, 128]), ALU.mult, r=[km, "Mc"], w=["s5mt"])
            for gg in range(4):
                g = g4 * 4 + gg
                k.stt("dve", Mb[:, g, :], I32t[:, :], dP[:, g:g + 1], mt_[:, gg, :], ALU.mult, ALU.add,
                      r=["I32t", "dP", "s5mt"], w=["Mb"])
            pw, kw = k.ps()
            for gg in range(4):
                g = g4 * 4 + gg
                k.tr(pw[:, gg * 128:(gg + 1) * 128], W[:, g].rearrange("p t j -> p (t j)"), I32t[:, :],
                     r=["Wm", "I32t"], w=[kw])
            k.cp("act", BmT[:, g4 * 4:(g4 + 1) * 4, :], pw[:, :].rearrange("p (a b) -> p a b", a=4), r=[kw], w=["BmT"])

        P.emit()
        es1.close()
        sb = sbp
        uTp = sb("uTp", [128, 4, 8, 512], BF16)
        ysT = sb("ysT", [128, 4, S], BF16)
        xbs = [sb(f"xb1_{i}", [128, 8, TT], BF16) for i in range(2)]
        for tt_ in range(NTT):
            tok = slice(tt_ * TT, (tt_ + 1) * TT)
            xb = xbs[tt_ % 2]
            xk = f"xb1_{tt_ % 2}"
            k.dma("pool", xb[:, :, :], kt_view(T["xT"], tok), w=[xk])
            for t in range(4):
                pu, ku = k.ps()
                for kt in range(8):
                    k.mm(pu[:, :], wu[:, t, kt * 128:(kt + 1) * 128], xb[:, kt, :], start=(kt == 0), stop=(kt == 7),
                         r=["wu", xk], w=[ku])
                k.cp("act", uTp[:, t, :, tt_ * 64:(tt_ + 1) * 64], pu[:, :].rearrange("p (c t) -> p t c", t=8),
                     r=[ku], w=[f"uTp{t}"])

        Ug = [sb(f"Ug{i}", [128, 512], BF16) for i in range(8)]
        Hb = [sb(f"Hb{i}", [128, 513], BF16) for i in range(8)]
        Yg = [sb(f"Yg{i}", [128, 512], BF16) for i in range(8)]
        Rt = [sb(f"Rt{i}", [128, 128], F32) for i in range(2)]
        Rj = [sb(f"Rj{i}", [128, 128], F32) for i in range(2)]
        Rm = [sb(f"Rm{i}", [128, 128], BF16) for i in range(4)]
        for i in range(8):
            k.memset("pool", Hb[i][:, 0:1], 0.0, w=[f"Hb{i}"])
        ri = 0
        for t in range(4):
            for gl in range(8):
                g = 8 * t + gl
                pu, ku = k.ps()
                for tau in range(8):
                    k.mm(pu[:, :], Sel[:, gl * 8 + tau, :], uTp[:, t, tau, :], start=(tau == 0), stop=(tau == 7),
                         r=["Sel", f"uTp{t}"], w=[ku])
                k.cp("act", Ug[gl][:, :], pu[:, :], r=[ku], w=[f"Ug{gl}"])
                ph, kh = k.ps()
                k.mm(ph[:, :], BmT[:, g, :], Ug[gl][:, :], r=["BmT", f"Ug{gl}"], w=[kh])
                k.cp("act", Hb[gl][:, 1:513], ph[:, :], r=[kh], w=[f"Hb{gl}"])
            for l in range(9):
                d = 1 << l
                for gl in range(8):
                    g = 8 * t + gl
                    rt, rtk = Rt[ri % 2], f"Rt{ri % 2}"
                    rm, rmk = Rm[ri % 4], f"Rm{ri % 4}"
                    rj, rjk = Rj[ri % 2], f"Rj{ri % 2}"
                    ri += 1
                    k.ts("dve", rt[:, :], I16[:, :], LR[:, g, l:l + 1], ALU.mult, r=["I16", "LRI"], w=[rtk])
                    k.stt("dve", rm[:, :], J16[:, :], LI[:, g, l:l + 1], rt[:, :], ALU.mult, ALU.add,
                          r=["J16", "LRI", rtk], w=[rmk])
                    ps_, ks_ = k.ps()
                    k.mm(ps_[:, 0:512 - d], rm[:, :], Hb[gl][:, 1:513 - d], r=[rmk, f"Hb{gl}"], w=[ks_])
                    k.tt("dve", Hb[gl][:, 1 + d:513], Hb[gl][:, 1 + d:513], ps_[:, 0:512 - d], ALU.add,
                         r=[ks_, f"Hb{gl}"], w=[f"Hb{gl}"])
            for gl in range(8):
                g = 8 * t + gl
                py, ky = k.ps()
                k.mm(py[:, :], Mb[:, g, :], Ug[gl][:, :], start=True, stop=False, r=["Mb", f"Ug{gl}"], w=[ky])
                k.mm(py[:, :], Cmb[:, g, :], Hb[gl][:, 0:512], start=False, stop=True, r=["Cmb", f"Hb{gl}"], w=[ky])
                k.act(Yg[gl][:, :], py[:, :], AF.Gelu_apprx_tanh, r=[ky], w=[f"Yg{gl}"])
            for r_ in range(8):
                pt, kt_ = k.ps()
                for gl in range(8):
                    k.mm(pt[:, :], SelI[:, gl * 8 + r_, :], Yg[gl][:, :], start=(gl == 0), stop=(gl == 7),
                         r=["SelI", f"Yg{gl}"], w=[kt_])
                k.cp("act", ysT[:, t, :].rearrange("p (c r) -> p r c", r=8)[:, r_, :], pt[:, :], r=[kt_], w=[f"ysT{t}"])
        sgl = [sb(f"sgl{i}", [128, TT], F32) for i in range(2)]
        og = [sb(f"og{i}", [128, TT], BF16) for i in range(2)]
        it = 0
        for tt_ in range(NTT):
            tok = slice(tt_ * TT, (tt_ + 1) * TT)
            for n in range(4):
                pz, kz = k.ps()
                for kt in range(4):
                    k.mm(pz[:, :], wglu[:, n, kt * 128:(kt + 1) * 128], ysT[:, kt, tok], start=(kt == 0), stop=(kt == 3),
                         r=["wglu", f"ysT{kt}"], w=[kz])
                s_, sk = sgl[it % 2], f"sgl{it % 2}"
                o_, ok_ = og[it % 2], f"og{it % 2}"
                it += 1
                k.act(s_[:, :], pz[:, :], AF.Sigmoid, r=[kz, "bglu"], w=[sk], bias=bglu[:, n:n + 1])
                k.tt("dve", o_[:, :], ysT[:, n, tok], s_[:, :], ALU.mult, r=[f"ysT{n}", sk], w=[ok_])
                k.dma("sp", T["br"][1][n * 128:(n + 1) * 128, tok], o_[:, :], r=[ok_], w=[f"br1_{tt_}_{n}"])
        P.emit()


def phase_att(k, T):
    nc, P = k.nc, k.P
    with ExitStack() as es:
        sb = lambda n, s, d: k.sb(es, n, s, d)
        qT = sb("qT", [128, 4, S], BF16)
        kT = sb("kT", [128, 2, S], BF16)
        qiT = sb("qiT", [128, 2, S], BF16)
        kiT = sb("kiT", [128, S], BF16)
        V1 = sb("V1", [128, 32, 2, 65], BF16)
        widx = sb("widx", [128, 32, 4], F32)
        I16 = sb("I16a", [128, 128], BF16)
        k.dma("pool", I16[:, :], T["I128"], w=["I16a"])
        k.memset("dve", V1[:, :, :, 64:65], 1.0, w=["V1"])
        dests = [(qT, 0), (qT, 1), (qT, 2), (qT, 3), (kT, 0), (kT, 1), (qiT, 0), (qiT, 1), (kiT, None)]
        dkeys = ["qT", "qT", "qT", "qT", "kT", "kT", "qiT", "qiT", "kiT"]
        with ExitStack() as es2:
            sb2 = lambda n, s, d: k.sb(es2, n, s, d)
            watt = sb2("watt", [128, 18, 1024], BF16)
            wvw = sb2("wvw", [128, 8, 132], BF16)
            cosTs = [sb2(f"cosT{i}", [128, TT], F32) for i in range(2)]
            sinTs = [sb2(f"sinT{i}", [128, TT], F32) for i in range(2)]
            posi = sb2("posi", [128, TT], I32)
            posf = sb2("posf", [128, TT], F32)
            y_ = sb2("sr_y3", [128, TT], F32)
            ki_ = sb2("sr_k3", [128, TT], I32)
            kf_ = sb2("sr_kf3", [128, TT], F32)
            ifr = sb2("ifr", [128, 2], F32)
            negpi = sb2("negpi3", [128, 1], F32)
            xbs = [sb2(f"xb3_{i}", [128, 8, TT], BF16) for i in range(2)]
            r1 = [sb2(f"r1_{i}", [128, TT], F32) for i in range(2)]
            r2 = [sb2(f"r2_{i}", [128, TT], F32) for i in range(2)]
            for i in range(18):
                k.dma("pool", watt[:, i, :], T["w_att_l"][i], w=[f"watt{i}"])
            k.dma("pool", wvw[:, :, :], T["w_vw_l"], w=["wvw"])
            k.dma("sp", ifr[:, :], T["ifr"], w=["ifr"])
            k.memset("pool", negpi[:, :], SIN_BI, w=["negpi"])
            tmp3 = (y_[:, :], ki_[:, :], kf_[:, :])
            it = 0
            for tt_ in range(NTT):
                tok = slice(tt_ * TT, (tt_ + 1) * TT)
                xb = xbs[tt_ % 2]
                xk = f"xb3_{tt_ % 2}"
                k.dma("pool", xb[:, :, :], kt_view(T["xT"], tok), w=[xk])
                cosT, ck = cosTs[tt_ % 2], f"cosT{tt_ % 2}"
                sinT, sk_ = sinTs[tt_ % 2], f"sinT{tt_ % 2}"
                k.dma("sp", posi[:, :], T["pos"][:, tok].partition_broadcast(128), w=["posi"])
                k.cp("dve", posf[:, :], posi[:, :], r=["posi"], w=["posf"])
                sin_reduced(k, sinT[:, :], posf[:, :], ifr[:, 0:1], False, None, tmp3, negpi[:, 0:1], ["posf", "ifr"], [sk_])
                sin_reduced(k, cosT[:, :], posf[:, :], ifr[:, 0:1], True, None, tmp3, negpi[:, 0:1], ["posf", "ifr"], [ck])
                k.ts("dve", sinT[:, :], sinT[:, :], ifr[:, 1:2], ALU.mult, r=[sk_, "ifr"], w=[sk_])
                for i in range(9):
                    p1, k1 = k.ps()
                    for kt in range(8):
                        k.mm(p1[:, :], watt[:, 2 * i, kt * 128:(kt + 1) * 128], xb[:, kt, :], start=(kt == 0), stop=(kt == 7),
                             r=[f"watt{2 * i}", xk], w=[k1])
                    p2, k2 = k.ps()
                    for kt in range(8):
                        k.mm(p2[:, :], watt[:, 2 * i + 1, kt * 128:(kt + 1) * 128], xb[:, kt, :], start=(kt == 0), stop=(kt == 7),
                             r=[f"watt{2 * i + 1}", xk], w=[k2])
                    a1, a1k = r1[it % 2], f"r1_{it % 2}"
                    a2, a2k = r2[it % 2], f"r2_{it % 2}"
                    it += 1
                    k.tt("dve", a1[:, :], p1[:, :], cosT[:, :], ALU.mult, r=[k1, ck], w=[a1k])
                    k.tt("dve", a2[:, :], p2[:, :], sinT[:, :], ALU.mult, r=[k2, sk_], w=[a2k])
                    dt_, di = dests[i]
                    dst = dt_[:, di, tok] if di is not None else dt_[:, tok]
                    k.tt("pool", dst, a1[:, :], a2[:, :], ALU.add, r=[a1k, a2k], w=[dkeys[i]])
                for sbk in range(4):
                    sblk = tt_ * 4 + sbk
                    pv, kv = k.ps()
                    for kt in range(8):
                        k.mm(pv[:, 0:132], xb[:, kt, sbk * 128:(sbk + 1) * 128], wvw[:, kt, :], start=(kt == 0), stop=(kt == 7),
                             r=["wvw", xk], w=[kv])
                    k.cp("act", V1[:, sblk, :, 0:64], pv[:, 0:128].rearrange("p (g d) -> p g d", g=2), r=[kv], w=["V1"])
                    k.act(widx[:, sblk, :], pv[:, 128:132], AF.Copy, r=[kv], w=["widx"], scale=0.5)
            P.emit()

        scoreAs = [sb(f"scoreA{i}", [128, S], F32) for i in range(2)]
        zr = sb("zr", [128, S], F32)
        junkDs = [sb(f"junkD{i}", [128, S], BF16) for i in range(2)]
        masks = [sb(f"mask{i}", [128, S], BF16) for i in range(2)]
        maskTs = [sb(f"maskT{i}", [128, S], BF16) for i in range(4)]
        rl = [sb(f"rl{i}", [128, 512], F32) for i in range(2)]
        Pt = [sb(f"Pt{i}", [128, 512], BF16) for i in range(3)]
        KB = 24
        sts = [sb(f"selst{i}", [128, 16], F32) for i in range(2)]
        nmss = [[sb(f"nm{j}_{i}", [128, 1], F32) for i in range(2)] for j in range(2)]
        ssums = [sb(f"ssum{i}", [128, 1], F32) for i in range(2)]
        gsgs = [sb(f"gsg{i}", [128, 1], F32) for i in range(2)]
        nwrow = sb("nwrow", [128, KB], F32)
        nwtabs = [sb(f"nwtab{i}", [128, KB], F32) for i in range(2)]
        nw2tabs = [sb(f"nw2tab{i}", [128, KB], F32) for i in range(2)]
        thrc = sb("thrc", [128, 1], F32)
        rec = sb("rec_a", [128, 8], F32)
        attn = sb("attn_tm", [128, 8, 64], BF16)
        aT = [sb(f"aT{i}", [128, 4, 128], BF16) for i in range(2)]
        k.dma("sp", nwrow[:, :], T["nwrow"].partition_broadcast(128), w=["nwrow"])
        k.memset("pool", thrc[:, :], -1.0e29, w=["thrc"])
        psB = k.psB
        ring_save = k.ring
        psOs = [[ring_save[3], ring_save[4]], [ring_save[5], ring_save[6]]]
        k.ring = ring_save[:3]

        def indexer(b):
            N = 128 * (b + 1)
            sl = b % 2
            scoreA, sak = scoreAs[sl], f"scoreA{sl}"
            st = sts[sl]
            tb = slice(b * 128, (b + 1) * 128)
            nch = (N + 511) // 512
            for c in range(nch):
                c0_ = c * 512
                cw_ = min(512, N - c0_)
                for h in range(4):
                    base = 64 * (h % 2)
                    pl, kl = k.ps()
                    k.mm(pl[:, 0:cw_], qiT[base:base + 64, h // 2, tb], kiT[base:base + 64, c0_:c0_ + cw_],
                         r=["qiT", "kiT"], w=[kl])
                    t_, tk = rl[(c * 4 + h) % 2], f"rl{(c * 4 + h) % 2}"
                    k.act(t_[:, 0:cw_], pl[:, 0:cw_], AF.Relu, r=[kl], w=[tk], scale=0.125)
                    if h == 0:
                        k.ts("dve", scoreA[:, c0_:c0_ + cw_], t_[:, 0:cw_], widx[:, b, h:h + 1], ALU.mult,
                             r=[tk, "widx"], w=[sak])
                    else:
                        k.stt("dve", scoreA[:, c0_:c0_ + cw_], t_[:, 0:cw_], widx[:, b, h:h + 1], scoreA[:, c0_:c0_ + cw_],
                              ALU.mult, ALU.add, r=[tk, "widx", sak], w=[sak])
            if b >= 2:
                P.op("dve", lambda e, o=junkDs[sl][:, 0:N], i_=scoreA[:, 0:N], a=st[:, 12:13]:
                     e.tensor_scalar(out=o, in0=i_, scalar1=1.0, scalar2=None, op0=ALU.mult, op1=ALU.max, accum_out=a),
                     r=[sak], w=[f"junkD{sl}", f"st_amax1_{sl}"])
                P.op("dve", lambda e, o=junkDs[sl][:, 0:N], i_=scoreA[:, 0:N], a=st[:, 13:14]:
                     e.tensor_scalar(out=o, in0=i_, scalar1=-1.0, scalar2=None, op0=ALU.mult, op1=ALU.max, accum_out=a),
                     r=[sak], w=[f"junkD{sl}", f"st_amax2_{sl}"])
                k.tt("dve", st[:, 0:1], st[:, 12:13], st[:, 13:14], ALU.max, r=[f"st_amax1_{sl}", f"st_amax2_{sl}"],
                     w=[f"st_amax_{sl}"])
            k.memset("dve", scoreA[0:64, N - 64:N], -1.0e30, w=[sak])

        def sel_setup(b):
            N = 128 * (b + 1)
            sl = b % 2
            sc, sak = scoreAs[sl][:, 0:N], f"scoreA{sl}"
            st = sts[sl]
            junk, jk = junkDs[sl], f"junkD{sl}"
            K_ = lambda nm: f"{nm}_{sl}"
            amax, cp, cpz, isA, isC, lo0, hi0, w0 = (st[:, i:i + 1] for i in range(8))
            need, t0_, t1_ = (st[:, i:i + 1] for i in range(8, 11))
            cnt = lambda o, op0, key: P.op(
                "dve", lambda e: e.tensor_scalar(out=junk[:, 0:N], in0=sc, scalar1=0.0, scalar2=None, op0=op0,
                                                 op1=ALU.add, accum_out=o), r=[sak], w=[jk, key])
            cnt(cp, ALU.is_gt, K_("st_cp"))
            cnt(cpz, ALU.is_ge, K_("st_cpz"))
            k.ts("dve", isA, cp, 255.5, ALU.is_ge, r=[K_("st_cp")], w=[K_("st_isA")])
            k.ts("dve", isC, cpz, 255.5, ALU.is_lt, r=[K_("st_cpz")], w=[K_("st_isC")])
            k.ts("dve", t0_, amax, -1.0001, ALU.mult, -1.0e-30, ALU.add, r=[K_("st_amax")], w=[K_("st_t0")])
            k.tt("dve", lo0, isC, t0_, ALU.mult, r=[K_("st_isC"), K_("st_t0")], w=[K_("st_lo0")])
            k.tt("dve", t1_, isA, amax, ALU.mult, r=[K_("st_isA"), K_("st_amax")], w=[K_("st_t1")])
            k.stt("dve", hi0, isC, -2.0e-38, t1_, ALU.mult, ALU.add, r=[K_("st_isC"), K_("st_t1")], w=[K_("st_hi0")])
            k.tt("dve", w0, hi0, lo0, ALU.subtract, r=[K_("st_hi0"), K_("st_lo0")], w=[K_("st_w0")])
            k.tt("dve", nwtabs[sl][:, :], nwrow[:, :], w0.to_broadcast([128, KB]), ALU.mult,
                 r=["nwrow", K_("st_w0")], w=[K_("nwtab")])
            k.ts("dve", nw2tabs[sl][:, :], nwtabs[sl][:, :], -2.0, ALU.mult, r=[K_("nwtab")], w=[K_("nwtab")])
            k.tt("dve", t0_, isA, isC, ALU.add, r=[K_("st_isA"), K_("st_isC"), K_("st_t0")], w=[K_("st_t0")])
            k.ts("dve", t0_, t0_, -1.0, ALU.mult, 1.0, ALU.add, r=[K_("st_t0")], w=[K_("st_t0")])
            k.ts("dve", t1_, cp, -1.0, ALU.mult, 256.0, ALU.add, r=[K_("st_cp"), K_("st_t1")], w=[K_("st_t1")])
            k.tt("dve", need, t0_, t1_, ALU.mult, r=[K_("st_t0"), K_("st_t1")], w=[K_("st_need")])
            k.cp("dve", nmss[sl][0][:, :], lo0, r=[K_("st_lo0")], w=[f"nm{sl}_0"])

        def sel_step(b, it_, part):
            N = 128 * (b + 1)
            sl = b % 2
            sc, sak = scoreAs[sl][:, 0:N], f"scoreA{sl}"
            junk, jk = junkDs[sl], f"junkD{sl}"
            cur, ck_ = nmss[sl][it_ % 2], f"nm{sl}_{it_ % 2}"
            nxt, nk_ = nmss[sl][(it_ + 1) % 2], f"nm{sl}_{(it_ + 1) % 2}"
            gsg, gk = gsgs[sl], f"gsg{sl}"
            ssum, sk = ssums[sl], f"ssum{sl}"
            if part == 0:
                k.ts("dve", gsg[:, :], cur[:, :], nw2tabs[sl][:, it_:it_ + 1], ALU.add, r=[ck_, f"nwtab_{sl}"], w=[gk])
            elif part == 1:
                P.op("dve", lambda e: e.tensor_scalar(
                    out=junk[:, 0:N], in0=sc, scalar1=gsg[:, 0:1], scalar2=None, op0=ALU.is_gt, op1=ALU.add,
                    accum_out=ssum[:, 0:1]), r=[sak, gk], w=[jk, sk])
            elif part == 2:
                k.ts("dve", ssum[:, :], ssum[:, :], 255.5, ALU.is_lt, NEG_FILL, ALU.mult, r=[sk], w=[sk])
            else:
                k.ts("dve", nxt[:, :], ssum[:, :], gsg[:, 0:1], ALU.add, cur[:, 0:1], ALU.max, r=[sk, gk, ck_], w=[nk_])

        def sel_finish(b):
            N = 128 * (b + 1)
            nsb = b + 1
            sl = b % 2
            sc, sak = scoreAs[sl][:, 0:N], f"scoreA{sl}"
            mask, mk = masks[sl], f"mask{sl}"
            maskT, mtk = maskTs[b % 4], f"maskT{b % 4}"
            if b >= 2:
                zm, zk = junkDs[sl], f"junkD{sl}"
                need = sts[sl][:, 8:9]
                fin, fk_ = nmss[sl][KB % 2], f"nm{sl}_{KB % 2}"
                k.ts("dve", mask[:, 0:N], sc, fin[:, 0:1], ALU.is_gt, r=[sak, fk_], w=[mk])
                k.ts("dve", zm[:, 0:N], sc, 0.0, ALU.is_equal, r=[sak], w=[zk])
                P.op("dve", lambda e: e.tensor_tensor_scan(out=zr[:, 0:N], data0=zm[:, 0:N], data1=zm[:, 0:N], initial=0.0,
                                                           op0=ALU.add, op1=ALU.max), r=[zk], w=["zr"])
                k.stt("dve", zm[:, 0:N], zr[:, 0:N], need, zm[:, 0:N], ALU.is_le, ALU.mult, r=["zr", f"st_need_{sl}", zk], w=[zk])
                k.tt("pool", mask[:, 0:N], mask[:, 0:N], zm[:, 0:N], ALU.add, r=[mk, zk], w=[mk])
            else:
                k.ts("dve", mask[:, 0:N], sc, thrc[:, 0:1], ALU.is_ge, r=[sak, "thrc"], w=[mk])
            for j0 in range(0, nsb, 8):
                nj = min(8, nsb - j0)
                pb, kb = psB
                for jj in range(nj):
                    j = j0 + jj
                    k.tr(pb[:, jj * 128:(jj + 1) * 128], mask[:, j * 128:(j + 1) * 128], I16[:, :], r=[mk, "I16a"], w=[kb])
                k.act(maskT[:, j0 * 128:(j0 + nj) * 128], pb[:, 0:nj * 128], AF.Identity, r=[kb], w=[mtk],
                      scale=MASK_BIG, bias=-MASK_BIG)

        def select2(bs):
            ch = [b_ for b_ in bs if b_ >= 2]
            for b_ in ch:
                sel_setup(b_)
            for it_ in range(KB):
                for part in range(4):
                    for b_ in ch:
                        sel_step(b_, it_, part)
            for b_ in bs:
                sel_finish(b_)

        def attend(b):
            nsb = b + 1
            tb = slice(b * 128, (b + 1) * 128)
            maskT, mtk = maskTs[b % 4], f"maskT{b % 4}"
            psO = psOs[b % 2]
            pi_ = 0
            for h in range(8):
                g = h // 4
                base = 64 * (h % 2)
                po, ko = psO[h // 4]
                for j0 in range(0, nsb, 4):
                    nj = min(4, nsb - j0)
                    ps_, ks_ = k.ps()
                    for jj in range(nj):
                        j = j0 + jj
                        k.mm(ps_[:, jj * 128:(jj + 1) * 128], kT[base:base + 64, g, j * 128:(j + 1) * 128],
                             qT[base:base + 64, h // 2, tb], start=(jj == 0), stop=False, r=["kT", "qT"], w=[ks_])
                    for jj in range(nj):
                        j = j0 + jj
                        k.mm(ps_[:, jj * 128:(jj + 1) * 128], I16[:, :], maskT[:, j * 128:(j + 1) * 128],
                             start=False, stop=(jj == nj - 1), r=["I16a", mtk], w=[ks_])
                    p_, pk_ = Pt[pi_ % 3], f"Pt{pi_ % 3}"
                    pi_ += 1
                    k.act(p_[:, 0:nj * 128], ps_[:, 0:nj * 128], AF.Exp, r=[ks_], w=[pk_], scale=0.125)
                    for jj in range(nj):
                        j = j0 + jj
                        k.mm(po[:, (h % 4) * 65:(h % 4) * 65 + 65], p_[:, jj * 128:(jj + 1) * 128], V1[:, j, g, :],
                             start=(j == 0), stop=(j == nsb - 1), r=[pk_, "V1"], w=[ko])

        def finish(b):
            tb = slice(b * 128, (b + 1) * 128)
            psO = psOs[b % 2]
            for hb in range(2):
                po, ko = psO[hb]
                pv = po[:, 0:260].rearrange("p (h e) -> p h e", h=4)
                k.recip(rec[:, hb * 4:(hb + 1) * 4], pv[:, :, 64], r=[ko], w=["rec_a"])
                k.tt("dve", attn[:, hb * 4:(hb + 1) * 4, :], pv[:, :, 0:64],
                     rec[:, hb * 4:(hb + 1) * 4].unsqueeze(2).to_broadcast([128, 4, 64]), ALU.mult,
                     r=[ko, "rec_a"], w=["attn_tm"])
            pb, kb = psB
            a_, ak = aT[b % 2], f"aT{b % 2}"
            af = attn[:, :, :].rearrange("p h d -> p (h d)")
            for t in range(4):
                k.tr(pb[:, t * 128:(t + 1) * 128], af[:, t * 128:(t + 1) * 128], I16[:, :], r=["attn_tm", "I16a"], w=[kb])
            k.cp("act", a_[:, :, :], pb[:, 0:512].rearrange("p (t s) -> p t s", t=4), r=[kb], w=[ak])
            k.dma("sp", kt_view(T["br"][0], tb), a_[:, :, :], r=[ak], w=[f"br0_{b // 4}_{b % 4}"])

        NP = S // 256
        indexer(0)
        indexer(1)
        select2([0, 1])
        for p_i in range(NP):
            A_, B_ = 2 * p_i, 2 * p_i + 1
            if p_i + 1 < NP:
                indexer(A_ + 2)
                indexer(B_ + 2)
            attend(A_)
            attend(B_)
            if p_i + 1 < NP:
                select2([A_ + 2, B_ + 2])
            finish(A_)
            finish(B_)
        k.ring = ring_save
        P.emit()


def build_program(shapes, phases=("s5", "mem", "att", "mix"), br_mode="internal"):
    nc = bass.Bass("TRN2", target_bir_lowering=False)
    T = {}
    for name, (shape, dt) in shapes.items():
        T[name] = nc.dram_tensor(name, list(shape), dt, kind="ExternalInput").ap()
    T["outT"] = nc.dram_tensor("outT", [D, S], F32, kind="ExternalOutput").ap()
    if br_mode == "internal":
        brt = nc.dram_tensor("br", [3, 512, S], BF16).ap()
    elif br_mode == "output":
        brt = nc.dram_tensor("br", [3, 512, S], BF16, kind="ExternalOutput").ap()
    else:
        brt = nc.dram_tensor("br", [3, 512, S], BF16, kind="ExternalInput").ap()
    T["br"] = [brt[i] for i in range(3)]
    T["wA_bf"] = nc.dram_tensor("wA_bf", [8, 128, 4608], BF16).ap()
    T["wO_bf"] = nc.dram_tensor("wO_bf", [8, 128, 1024], BF16).ap()
    T["wU_bf"] = nc.dram_tensor("wU_bf", [22, 128, 2048], BF16).ap()
    T["wD_bf"] = nc.dram_tensor("wD_bf", [8, 128, 2816], BF16).ap()
    with ExitStack() as es:
        P = Prog(nc, es)
        k = K(nc, P)
        for i in range(5):
            t = es.enter_context(nc.psum_tensor(f"psf{i}", [128, 512], F32))
            k.ring.append((t, f"psf{i}"))
        o0 = es.enter_context(nc.psum_tensor("psf5", [128, 512], F32))
        o1 = es.enter_context(nc.psum_tensor("psf6", [128, 512], F32))
        pb = es.enter_context(nc.psum_tensor("psb", [128, 1024], BF16))
        k.psO = [(o0, "psf5"), (o1, "psf6")]
        k.psB = (pb, "psb")
        k.ring.append((o0, "psf5"))
        k.ring.append((o1, "psf6"))
        if "mix" in phases:
            for i in range(8):
                k.dma("pool", T["wA_bf"][i], T["wA_l"][i], w=[f"wbf_A{i}"])
            for i in range(8):
                k.dma("pool", T["wO_bf"][i], T["wO_l"][i], w=[f"wbf_O{i}"])
            for i in range(22):
                k.dma("pool", T["wU_bf"][i], T["wU_l"][i], w=[f"wbf_U{i}"])
            for i in range(8):
                k.dma("pool", T["wD_bf"][i], T["wD_l"][i], w=[f"wbf_Dn{i}"])
        if "s5" in phases:
            phase_s5(k, T)
        if "mem" in phases:
            phase_mem(k, T)
        if "att" in phases:
            phase_att(k, T)
        if "mix" in phases:
            phase_mix(k, T)
        else:
            P.wait_all("sp", [kk_ for kk_ in P.lastw if kk_.startswith("br")])
            P.emit()
    return nc


def tile_lhsT(W):
    Kd, N = W.shape
    return np.ascontiguousarray(W.reshape(Kd // 128, 128, N // 128, 128).transpose(2, 1, 0, 3).reshape(N // 128, 128, (Kd // 128) * 128))


def rhs_layout(W):
    Kd, N = W.shape
    return np.ascontiguousarray(W.reshape(Kd // 128, 128, N).transpose(1, 0, 2))


def prep_shared(inp):
    f = np.float32
    sh = {}
    w_in = inp["w_in"][0]
    def hc(base, h):
        return base + h * 64 + np.arange(64)
    def sw(c):
        return np.concatenate([c[32:], c[:32]])
    tiles = []
    for j in range(4):
        a, b = hc(0, 2 * j), hc(0, 2 * j + 1)
        tiles += [np.concatenate([a, b]), np.concatenate([sw(a), sw(b)])]
    for g in range(2):
        a = hc(512, g)
        tiles += [np.concatenate([a, a]), np.concatenate([sw(a), sw(a)])]
    for j in range(2):
        a, b = hc(768, 2 * j), hc(768, 2 * j + 1)
        tiles += [np.concatenate([a, b]), np.concatenate([sw(a), sw(b)])]
    a = 1024 + np.arange(64)
    tiles += [np.concatenate([a, a]), np.concatenate([sw(a), sw(a)])]
    cols = np.concatenate(tiles)
    sh["w_att_l"] = tile_lhsT(w_in[:, cols])
    sh["w_vw_l"] = rhs_layout(np.concatenate([w_in[:, 640:768], w_in[:, 1088:1092]], axis=1))
    sh["w_u_l"] = np.ascontiguousarray(tile_lhsT(w_in[:, 1092:1604]).transpose(1, 0, 2))
    sh["w_qmem_l"] = np.ascontiguousarray(tile_lhsT(w_in[:, 1604:2116]).transpose(1, 0, 2))
    wkv = inp["w_mem_kv"][0]
    sh["w_kmem_l"] = np.ascontiguousarray(tile_lhsT(wkv[:, 0:512]).transpose(1, 0, 2))
    sh["w_vmem_l"] = rhs_layout(wkv[:, 512:1024])
    sh["w_glu_l"] = np.ascontiguousarray(tile_lhsT(inp["w_glu"][0]).transpose(1, 0, 2))
    sh["b_glu_l"] = np.ascontiguousarray(inp["b_glu"][0].reshape(4, 128).T)
    p = np.arange(128)
    inv_freq = (10000.0 ** (-(np.arange(32, dtype=np.float32)) / np.float32(32))).astype(np.float32)
    ifr = np.zeros((128, 2), f)
    ifr[:, 0] = (inv_freq[p % 32].astype(np.float64) / TWO_PI).astype(f)
    ifr[:, 1] = np.where((p % 64) < 32, -1.0, 1.0)
    sh["ifr"] = ifr
    sh["nwrow"] = (-(2.0 ** -(np.arange(24, dtype=np.float64) + 2.0))).astype(f).reshape(1, 24)
    dup = lambda a_: np.ascontiguousarray(np.concatenate([a_, a_], axis=0).astype(f))
    sh["lre"] = dup(inp["s5_lam_re"][0].T)
    sh["lim"] = dup(inp["s5_lam_im"][0].T)
    sh["ldt"] = np.ascontiguousarray(np.broadcast_to(inp["s5_log_dt"][0][None, :], (128, 32)).astype(f))
    sh["Br"] = dup(inp["s5_b_re"][0].transpose(1, 0, 2))
    sh["Bi"] = dup(inp["s5_b_im"][0].transpose(1, 0, 2))
    sh["Cr"] = dup(inp["s5_c_re"][0].transpose(2, 0, 1))
    sh["Ci"] = dup(inp["s5_c_im"][0].transpose(2, 0, 1))
    sh["dP"] = np.ascontiguousarray(np.tile(inp["s5_d"][0].reshape(32, 16).T, (8, 1)).astype(f))
    sh["I128"] = np.eye(128, dtype=f)
    J = np.zeros((128, 128), f)
    for q in range(64):
        J[q, 64 + q] = 1.0
        J[64 + q, q] = -1.0
    sh["J128"] = J
    tau = np.arange(128) // 16
    sh["Mc"] = (tau[None, :] >= tau[:, None]).astype(f)
    Sel = np.zeros((128, 64, 128), f)
    SelI = np.zeros((128, 64, 128), f)
    for gl in range(8):
        for t in range(8):
            for j in range(16):
                Sel[gl * 16 + j, gl * 8 + t, t * 16 + j] = 1.0
                SelI[t * 16 + j, gl * 8 + t, gl * 16 + j] = 1.0
    sh["Sel"] = Sel
    sh["SelI"] = SelI
    wg = inp["w_gate"][0]
    pa = [tile_lhsT(inp[n][0]) for n in ("w_proj_a", "w_proj_b", "w_proj_c")]
    ga = [tile_lhsT(wg[:, i * 1024:(i + 1) * 1024]) for i in range(3)]
    sh["wA_l"] = np.ascontiguousarray(np.concatenate(pa + ga, axis=2))
    sh["wO_l"] = tile_lhsT(inp["w_out"][0])
    wu = tile_lhsT(inp["w_up"][0])
    sh["wU_l"] = np.ascontiguousarray(np.concatenate([wu[0:22], wu[22:44]], axis=2))
    sh["wD_l"] = tile_lhsT(inp["w_down"][0])
    sh["b_gate_l"] = np.ascontiguousarray(inp["b_gate"][0].reshape(3, 8, 128).transpose(2, 0, 1).reshape(128, 24))
    vec = lambda v: v.reshape(8, 128).T
    sh["ln_l"] = np.ascontiguousarray(np.concatenate([vec(inp["ln1_g"][0]), vec(inp["ln1_b"][0]),
                                                      vec(inp["ln2_g"][0]), vec(inp["ln2_b"][0])], axis=1))
    sh["conv_w_l"] = np.ascontiguousarray(inp["conv_w"][0].T.reshape(44, 128, 3).transpose(1, 0, 2))
    sh["conv_b_l"] = np.ascontiguousarray(inp["conv_b"][0].reshape(44, 128).T)
    return {k_: np.ascontiguousarray(v.astype(f)) for k_, v in sh.items()}


def prep_core(inp, b):
    return {
        "xT": np.ascontiguousarray(inp["x"][b].T),
        "memT": np.ascontiguousarray(inp["mem"][b].T),
        "pos": np.ascontiguousarray(inp["positions"][b].reshape(1, S).astype(np.int32)),
    }


def kernel(**inputs):
    inp = {k_: np.asarray(v) for k_, v in inputs.items()}
    shared = prep_shared(inp)
    B = inp["x"].shape[0]
    in_maps = []
    for b in range(B):
        m = dict(shared)
        m.update(prep_core(inp, b))
        in_maps.append(m)
    shapes = {k_: (v.shape, I32 if v.dtype == np.int32 else F32) for k_, v in in_maps[0].items()}
    nc = build_program(shapes)
    res = run_bass_kernel_spmd(nc, in_maps, core_ids=list(range(B)))
    out = np.stack([np.asarray(r["outT"]).T for r in res.results], axis=0)
    return np.ascontiguousarray(out.astype(np.float32))
```

```python
from contextlib import ExitStack
import numpy as np
import concourse.bass as bass
import concourse.mybir as mybir
from concourse.bass_utils import run_bass_kernel_spmd

F32 = mybir.dt.float32
BF16 = mybir.dt.bfloat16
I32 = mybir.dt.int32
AF = mybir.ActivationFunctionType
ALU = mybir.AluOpType

S = 4096
D = 1024
TT = 512
NTT = S // TT
ALPHA = 2.0 ** 0.25
LN_EPS = 1e-5
TWO_PI = 6.283185307179586
SIN_SC = 6.2831840
SIN_BI = -3.1415920
EPOCH = 30000
NEG_FILL = -3.0e38
MASK_BIG = 30000.0
FUSE_WAITS = True


class Prog:
    ENGS = ("pe", "act", "dve", "pool", "sp")

    def __init__(self, nc, es, ndma=8):
        self.nc = nc
        self.es = es
        self.ops = {e: [] for e in self.ENGS}
        self.count = {e: 0 for e in self.ENGS}
        self.sems = {}
        self.ndma = ndma
        self.dsems = {}
        self.dcount = {e: 0 for e in self.ENGS}
        self.dtarget = {}
        self.waited = {e: {} for e in self.ENGS}
        self.lastw = {}
        self.readers = {}

    def _sem(self, eng, epoch):
        k = (eng, epoch)
        if k not in self.sems:
            self.sems[k] = self.es.enter_context(self.nc.semaphore(f"s_{eng}_{epoch}"))
        return self.sems[k]

    def _dsem(self, eng, r):
        k = (eng, r)
        if k not in self.dsems:
            self.dsems[k] = self.es.enter_context(self.nc.semaphore(f"d_{eng}_{r}"))
            self.dtarget[k] = 0
        return self.dsems[k]

    def _need(self, eng, ev, waits):
        if ev is None:
            return
        sem, val, src = ev
        if src == "pe" and eng == "pe":
            return
        w = self.waited[eng]
        if w.get(id(sem), 0) >= val:
            return
        w[id(sem)] = val
        waits.append((sem, val))

    def op(self, eng, fn, r=(), w=(), dma=False, fuse=False):
        waits = []
        for k in r:
            self._need(eng, self.lastw.get(k), waits)
        for k in w:
            self._need(eng, self.lastw.get(k), waits)
            for ev in self.readers.get(k, {}).values():
                self._need(eng, ev, waits)
        if dma:
            i = self.dcount[eng]
            self.dcount[eng] += 1
            rr = i % self.ndma
            sem = self._dsem(eng, rr)
            prev = self.dtarget[(eng, rr)]
            if prev > 0:
                self._need(eng, (sem, prev, "dma"), waits)
            self.dtarget[(eng, rr)] = prev + 16
            ev = (sem, prev + 16, "dma")
            inc = 16
        else:
            self.count[eng] += 1
            ep, v = divmod(self.count[eng] - 1, EPOCH)
            sem = self._sem(eng, ep)
            ev = (sem, v + 1, eng)
            inc = 1
        best = {}
        for s_, v_ in waits:
            if id(s_) not in best or best[id(s_)][1] < v_:
                best[id(s_)] = (s_, v_)
        self.ops[eng].append((list(best.values()), fn, sem, inc, FUSE_WAITS and not dma))
        for k in r:
            self.readers.setdefault(k, {})[(eng, id(sem))] = ev
        for k in w:
            self.lastw[k] = ev
            self.readers[k] = {}
        return ev

    def wait_all(self, eng, keys):
        waits = []
        for k in keys:
            self._need(eng, self.lastw.get(k), waits)
        self.ops[eng].append((waits, None, None, 0, False))

    def drain_dmas(self, eng="sp"):
        waits = []
        for (qe, r_), sem in self.dsems.items():
            tgt = self.dtarget[(qe, r_)]
            if tgt > 0:
                self._need(eng, (sem, tgt, "dma"), waits)
        self.ops[eng].append((waits, None, None, 0, False))

    def emit(self):
        self.drain_dmas("sp")
        nc = self.nc
        ops = self.ops
        self.ops = {e: [] for e in self.ENGS}

        def run(engname):
            def _f(eng):
                for wl, fn, sem, inc, fuse in ops[engname]:
                    if fuse and fn is not None and len(wl) >= 1:
                        for s_, v_ in wl[:-1]:
                            eng.wait_ge(s_, v_)
                        ins = fn(eng)
                        ins._wait_ge(wl[-1][0], wl[-1][1])
                        ins.then_inc(sem, inc)
                        continue
                    for s_, v_ in wl:
                        eng.wait_ge(s_, v_)
                    if fn is not None:
                        fn(eng).then_inc(sem, inc)
            return _f

        with nc.Block() as block:
            block.tensor(run("pe"))
            block.scalar(run("act"))
            block.vector(run("dve"))
            block.gpsimd(run("pool"))
            block.sync(run("sp"))


class K:
    def __init__(self, nc, P):
        self.nc = nc
        self.P = P
        self.ring = []
        self.ri = 0

    def sb(self, es, name, shape, dt):
        return es.enter_context(self.nc.sbuf_tensor("sb_" + name, shape, dt))

    def ps(self):
        t = self.ring[self.ri % len(self.ring)]
        self.ri += 1
        return t

    def mm(self, out, lhsT, rhs, start=True, stop=True, r=(), w=()):
        self.P.op("pe", lambda e: e.matmul(out, lhsT=lhsT, rhs=rhs, start=start, stop=stop), r=r, w=w)

    def tr(self, out, in_, ident, r=(), w=()):
        self.P.op("pe", lambda e: e.transpose(out, in_, ident), r=r, w=w)

    def act(self, out, in_, func, r=(), w=(), **kw):
        self.P.op("act", lambda e: e.activation(out=out, in_=in_, func=func, **kw), r=r, w=w, fuse=("accum_out" not in kw))

    def tt(self, eng, out, in0, in1, op, r=(), w=()):
        self.P.op(eng, lambda e: e.tensor_tensor(out=out, in0=in0, in1=in1, op=op), r=r, w=w, fuse=True)

    def ts(self, eng, out, in0, s1, op0, s2=None, op1=None, r=(), w=()):
        if op1 is None:
            self.P.op(eng, lambda e: e.tensor_scalar(out=out, in0=in0, scalar1=s1, scalar2=None, op0=op0), r=r, w=w, fuse=True)
        else:
            self.P.op(eng, lambda e: e.tensor_scalar(out=out, in0=in0, scalar1=s1, scalar2=s2, op0=op0, op1=op1), r=r, w=w, fuse=True)

    def stt(self, eng, out, in0, scalar, in1, op0, op1, r=(), w=()):
        self.P.op(eng, lambda e: e.scalar_tensor_tensor(out=out, in0=in0, scalar=scalar, in1=in1, op0=op0, op1=op1), r=r, w=w, fuse=True)

    def cp(self, eng, out, in_, r=(), w=()):
        if eng == "act":
            self.act(out, in_, AF.Copy, r=r, w=w)
        else:
            self.P.op(eng, lambda e: e.tensor_copy(out=out, in_=in_), r=r, w=w, fuse=True)

    def recip(self, out, in_, r=(), w=()):
        self.P.op("dve", lambda e: e.reciprocal(out=out, in_=in_), r=r, w=w)

    def memset(self, eng, ap, val, w=()):
        self.P.op(eng, lambda e: e.memset(ap, val), w=w)

    def dma(self, eng, out, in_, r=(), w=()):
        self.P.op(eng, lambda e: e.dma_start(out=out, in_=in_), r=r, w=w, dma=True)


def kt_view(dram_ap, cols):
    return dram_ap.rearrange("(kt p) s -> p kt s", p=128)[:, :, cols]


def sin_reduced(k, out, in_, mul, quarter, shape, tmp, negpi, rk, wk):
    y, ki, kf = tmp
    off = 8.5 + (0.25 if quarter else 0.0)
    if isinstance(mul, float):
        k.ts("dve", y, in_, mul / TWO_PI, ALU.mult, off, ALU.add, r=rk, w=["sr_y"])
    else:
        k.ts("dve", y, in_, mul, ALU.mult, off, ALU.add, r=rk, w=["sr_y"])
    k.cp("dve", ki, y, r=["sr_y"], w=["sr_k"])
    k.cp("dve", kf, ki, r=["sr_k"], w=["sr_kf"])
    k.tt("dve", y, y, kf, ALU.subtract, r=["sr_y", "sr_kf"], w=["sr_y"])
    k.ts("dve", kf, y, 0.0, ALU.is_lt, r=["sr_y"], w=["sr_kf"])
    k.tt("dve", y, y, kf, ALU.add, r=["sr_y", "sr_kf"], w=["sr_y"])
    k.act(out, y, AF.Sin, r=["sr_y", "negpi"], w=wk, bias=negpi, scale=SIN_SC)


def layer_norm(k, es_tmp, z, zkey, gvec, bvec, out32, o32key, out16, o16key, mid_cb=None):
    zb, zsq, onesD, meanS, m2, var, rstd, tmps, epsb = es_tmp
    for kt in range(8):
        k.act(zb[:, kt, :], z[:, kt, :], AF.Copy, r=[zkey], w=[f"zb{kt}"])
        k.act(zsq[:, kt, :], z[:, kt, :], AF.Square, r=[zkey], w=[f"zsq{kt}"])
    if mid_cb is not None:
        mid_cb()
    psM, kM = k.ps()
    for kt in range(8):
        k.mm(psM[:, :], onesD[:, :], zb[:, kt, :], start=(kt == 0), stop=(kt == 7), r=[f"zb{kt}", "onesD"], w=[kM])
    psQ, kQ = k.ps()
    for kt in range(8):
        k.mm(psQ[:, :], onesD[:, :], zsq[:, kt, :], start=(kt == 0), stop=(kt == 7), r=[f"zsq{kt}", "onesD"], w=[kQ])
    k.act(meanS[:, :], psM[:, :], AF.Copy, r=[kM], w=["meanS"])
    k.tt("dve", m2[:, :], meanS[:, :], meanS[:, :], ALU.mult, r=["meanS"], w=["m2"])
    k.tt("dve", var[:, :], psQ[:, :], m2[:, :], ALU.subtract, r=[kQ, "m2"], w=["var"])
    k.act(var[:, :], var[:, :], AF.Sqrt, r=["var", "epsb"], w=["var"], bias=epsb[:, 0:1])
    k.recip(rstd[:, :], var[:, :], r=["var"], w=["rstd"])
    for kt in range(8):
        t = tmps[kt % 2]
        tk = f"lnt{kt % 2}"
        k.tt("dve", t[:, :], z[:, kt, :], meanS[:, :], ALU.subtract, r=[zkey, "meanS"], w=[tk])
        k.tt("dve", t[:, :], t[:, :], rstd[:, :], ALU.mult, r=[tk, "rstd"], w=[tk])
        k.act(out32[:, kt, :], t[:, :], AF.Identity, r=[tk, "lnvec"], w=[o32key],
              scale=gvec[:, kt:kt + 1], bias=bvec[:, kt:kt + 1])
        if out16 is not None:
            k.cp("pool", out16(kt), out32[:, kt, :], r=[o32key], w=[o16key])


def phase_mix(k, T):
    nc, P = k.nc, k.P
    with ExitStack() as es:
        sb = lambda n, s, d: k.sb(es, n, s, d)
        xf = sb("xf", [128, 8, TT], F32)
        xbs = [sb(f"xb4_{j}", [128, 8, TT], BF16) for j in range(2)]
        brts = [[sb(f"brt{i}_{j}", [128, 4, TT], BF16) for i in range(3)] for j in range(2)]
        gs = [sb(f"gs{i}", [128, TT], F32) for i in range(3)]
        m1 = sb("m1", [128, TT], F32)
        tA = sb("tA", [128, TT], F32)
        tB = sb("tB", [128, TT], F32)
        merged = sb("merged", [128, 8, TT], BF16)
        zb = sb("zb", [128, 8, TT], BF16)
        zsq = sb("zsq", [128, 8, TT], BF16)
        onesD = sb("onesD", [128, 128], BF16)
        meanS = sb("meanS", [128, TT], F32)
        m2 = sb("m2", [128, TT], F32)
        var = sb("var", [128, TT], F32)
        rstd = sb("rstd", [128, TT], F32)
        lnt = [sb(f"lnt{i}", [128, TT], F32) for i in range(2)]
        epsb = sb("epsb", [128, 1], F32)
        h1 = sb("h1", [128, 8, TT], F32)
        h1b = sb("h1b", [128, 8, TT + 2], BF16)
        actb = sb("actb", [128, 22, TT], BF16)
        c0 = [sb(f"c0_{i}", [128, 256], F32) for i in range(2)]
        c1 = [sb(f"c1_{i}", [128, 256], F32) for i in range(2)]
        sg = sb("sg", [128, 256], F32)
        ctmp = sb("ctmp", [128, 256], F32)
        ot = sb("ot", [128, 8, TT], F32)
        NSLOT = 3
        wring = [sb(f"wr{i}", [128, 4608], BF16) for i in range(NSLOT)]
        bgate = sb("bgate", [128, 24], F32)
        lnv = sb("lnv", [128, 32], F32)
        cw = sb("cw", [128, 44, 3], F32)
        cb = sb("cb", [128, 44], F32)
        ln_tmp = (zb, zsq, onesD, meanS, m2, var, rstd, lnt, epsb)

        k.dma("sp", bgate[:, :], T["b_gate_l"], w=["bgate"])
        k.dma("sp", lnv[:, :], T["ln_l"], w=["lnvec"])
        k.dma("sp", cw[:, :, :], T["conv_w_l"], w=["cw"])
        k.dma("sp", cb[:, :], T["conv_b_l"], w=["cw"])
        k.memset("pool", onesD[:, :], 1.0 / 1024.0, w=["onesD"])
        k.memset("pool", epsb[:, :], LN_EPS, w=["epsb"])
        k.memset("pool", h1b[:, :, 0:2], 0.0, w=["h1b"])

        chunks = [("A", n) for n in range(8)] + [("O", n) for n in range(8)]
        for tt_ in range(NTT):
            if tt_ + 1 < NTT:
                chunks += [("A", n) for n in range(8)]
            chunks += [("U", i) for i in range(22)]
            chunks += [("Dn", n) for n in range(8)]
            if tt_ + 1 < NTT:
                chunks += [("O", n) for n in range(8)]
        csize = {"A": 4608, "O": 1024, "U": 2048, "Dn": 2816}
        csrc = {"A": T["wA_bf"], "O": T["wO_bf"], "U": T["wU_bf"], "Dn": T["wD_bf"]}
        state = {"next": 0}

        def load_chunk():
            i = state["next"]
            if i >= len(chunks):
                return
            kind, idx = chunks[i]
            slot = i % NSLOT
            k.dma("sp", wring[slot][:, 0:csize[kind]], csrc[kind][idx], r=[f"wbf_{kind}{idx}"], w=[f"wr{slot}"])
            state["next"] += 1

        ci = {"i": 0}

        def get_chunk(kind):
            i = ci["i"]
            assert chunks[i][0] == kind, (i, chunks[i], kind)
            ci["i"] += 1
            return wring[i % NSLOT], f"wr{i % NSLOT}"

        for _ in range(NSLOT):
            load_chunk()

        def load_xb_br(t_):
            tk_ = slice(t_ * TT, (t_ + 1) * TT)
            p_ = t_ % 2
            k.dma("pool", xbs[p_][:, :, :], kt_view(T["xT"], tk_), w=[f"xb4_{p_}"])
            for i in range(3):
                k.dma("sp", brts[p_][i][:, :, :], kt_view(T["br"][i], tk_), r=[f"br{i}_{t_}_{q}" for q in range(4)],
                      w=[f"brt{i}_{p_}"])

        def load_xf(t_):
            tk_ = slice(t_ * TT, (t_ + 1) * TT)
            k.dma("sp", xf[:, :, :], kt_view(T["xT"], tk_), w=["xf"])

        def merge(t_, n0=0, n1=8):
            p_ = t_ % 2
            xb, xbk = xbs[p_], f"xb4_{p_}"
            brt = brts[p_]
            for n in range(n0, n1):
                wt, wk = get_chunk("A")
                psy = []
                for i in range(3):
                    py, ky = k.ps()
                    for kt in range(4):
                        c = (4 * i + kt) * 128
                        k.mm(py[:, :], wt[:, c:c + 128], brt[i][:, kt, :], start=(kt == 0), stop=(kt == 3),
                             r=[wk, f"brt{i}_{p_}"], w=[ky])
                    pg, kg = k.ps()
                    for kt in range(8):
                        c = (12 + 8 * i + kt) * 128
                        k.mm(pg[:, :], wt[:, c:c + 128], xb[:, kt, :], start=(kt == 0), stop=(kt == 7),
                             r=[wk, xbk], w=[kg])
                    k.act(gs[i][:, :], pg[:, :], AF.Sigmoid, r=[kg, "bgate"], w=[f"gs{i}"],
                          bias=bgate[:, i * 8 + n:i * 8 + n + 1])
                    psy.append((py, ky))
                load_chunk()
                k.tt("dve", m1[:, :], psy[0][0][:, :], gs[0][:, :], ALU.mult, r=[psy[0][1], "gs0"], w=["m1"])
                k.tt("dve", tA[:, :], psy[1][0][:, :], gs[1][:, :], ALU.mult, r=[psy[1][1], "gs1"], w=["tA"])
                k.tt("dve", tB[:, :], psy[2][0][:, :], gs[2][:, :], ALU.mult, r=[psy[2][1], "gs2"], w=["tB"])
                k.tt("pool", m1[:, :], m1[:, :], tA[:, :], ALU.add, r=["m1", "tA"], w=["m1"])
                k.tt("pool", merged[:, n, :], m1[:, :], tB[:, :], ALU.add, r=["m1", "tB"], w=[f"merged{n}"])

        def outproj(t_, n0=0, n1=8):
            for n in range(n0, n1):
                wt, wk = get_chunk("O")
                px, kx = k.ps()
                for kt in range(8):
                    k.mm(px[:, :], wt[:, kt * 128:(kt + 1) * 128], merged[:, kt, :], start=(kt == 0), stop=(kt == 7),
                         r=[wk, f"merged{kt}"], w=[kx])
                load_chunk()
                k.stt("dve", xf[:, n, :], xf[:, n, :], ALPHA, px[:, :], ALU.mult, ALU.add, r=["xf", kx], w=["xf"])

        load_xb_br(0)
        load_xf(0)
        merge(0)
        outproj(0)
        for tt_ in range(NTT):
            tok = slice(tt_ * TT, (tt_ + 1) * TT)
            if tt_ + 1 < NTT:
                load_xb_br(tt_ + 1)
            if tt_ > 0:
                k.cp("pool", h1b[:, :, 0:2], h1b[:, :, TT:TT + 2], r=["h1b"], w=["h1b"])
            nxt_ = tt_ + 1 < NTT
            layer_norm(k, ln_tmp, xf, "xf", lnv[:, 0:8], lnv[:, 8:16], h1, "h1",
                       lambda kt: h1b[:, kt, 2:TT + 2], "h1b",
                       mid_cb=(lambda t1=tt_ + 1: merge(t1, 0, 4)) if nxt_ else None)
            if nxt_:
                merge(tt_ + 1, 4, 8)
                load_xf(tt_ + 1)
            for i in range(22):
                wt, wk = get_chunk("U")
                for hf in range(2):
                    res = []
                    for part in range(2):
                        pp, kp = k.ps()
                        for kt in range(8):
                            c = (part * 8 + kt) * 128
                            k.mm(pp[:, 0:258], wt[:, c:c + 128], h1b[:, kt, hf * 256:hf * 256 + 258],
                                 start=(kt == 0), stop=(kt == 7), r=[wk, "h1b"], w=[kp])
                        ch = i + 22 * part
                        a0, a1 = c0[part], c1[part]
                        k.act(a0[:, :], pp[:, 2:258], AF.Identity, r=[kp, "cw"], w=[f"c0_{part}"],
                              scale=cw[:, ch, 2:3], bias=cb[:, ch:ch + 1])
                        k.stt("dve", a1[:, :], pp[:, 1:257], cw[:, ch, 1:2], a0[:, :], ALU.mult, ALU.add,
                              r=[kp, "cw", f"c0_{part}"], w=[f"c1_{part}"])
                        k.stt("dve", a0[:, :], pp[:, 0:256], cw[:, ch, 0:1], a1[:, :], ALU.mult, ALU.add,
                              r=[kp, "cw", f"c1_{part}"], w=[f"c0_{part}"])
                        res.append(a0)
                    k.act(sg[:, :], res[0][:, :], AF.Silu, r=["c0_0"], w=["sg"])
                    k.tt("pool", actb[:, i, hf * 256:(hf + 1) * 256], sg[:, :], res[1][:, :], ALU.mult,
                         r=["sg", "c0_1"], w=[f"actb{i}"])
                load_chunk()
            for n in range(8):
                wt, wk = get_chunk("Dn")
                pd, kd = k.ps()
                for kt in range(22):
                    k.mm(pd[:, :], wt[:, kt * 128:(kt + 1) * 128], actb[:, kt, :], start=(kt == 0), stop=(kt == 21),
                         r=[wk, f"actb{kt}"], w=[kd])
                load_chunk()
                k.stt("dve", h1[:, n, :], h1[:, n, :], ALPHA, pd[:, :], ALU.mult, ALU.add, r=["h1", kd], w=["h1"])
            layer_norm(k, ln_tmp, h1, "h1", lnv[:, 16:24], lnv[:, 24:32], ot, "ot", None, None,
                       mid_cb=(lambda t1=tt_ + 1: outproj(t1, 0, 4)) if nxt_ else None)
            if nxt_:
                outproj(tt_ + 1, 4, 8)
            k.dma("act", kt_view(T["outT"], tok), ot[:, :, :], r=["ot"], w=[f"outT{tt_}"])
        P.wait_all("sp", [f"outT{t_}" for t_ in range(NTT)])
        P.emit()


def phase_mem(k, T):
    nc, P = k.nc, k.P
    with ExitStack() as es:
        sb = lambda n, s, d: k.sb(es, n, s, d)
        memb = sb("memb", [128, 8, 256], BF16)
        wq = sb("wq_mem", [128, 4, 1024], BF16)
        wkk = sb("wk_mem", [128, 4, 1024], BF16)
        wv = sb("wv_mem", [128, 8, 512], BF16)
        kmT = sb("kmT", [128, 4, 256], BF16)
        vm = sb("vm", [128, 2, 512], BF16)
        ones = sb("ones_mem", [128, 128], BF16)
        xbs = [sb(f"xb2_{i}", [128, 8, TT], BF16) for i in range(2)]
        qm = [sb(f"qm{i}", [128, TT], BF16) for i in range(2)]
        pT = [sb(f"pT{i}", [128, 2, TT], BF16) for i in range(2)]
        rec = sb("rec_mem", [128, TT], F32)
        ob = [sb(f"ob{i}", [128, TT], BF16) for i in range(2)]

        k.dma("pool", memb[:, :, :], T["memT"].rearrange("(kt p) m -> p kt m", p=128), w=["memb"])
        k.dma("pool", wq[:, :, :], T["w_qmem_l"], w=["wq_mem"])
        k.dma("pool", wkk[:, :, :], T["w_kmem_l"], w=["wk_mem"])
        k.dma("pool", wv[:, :, :], T["w_vmem_l"], w=["wv_mem"])
        k.memset("dve", ones[:, :], 1.0, w=["ones_mem"])
        for h in range(4):
            pk, kk = k.ps()
            for kt in range(8):
                k.mm(pk[:, 0:256], wkk[:, h, kt * 128:(kt + 1) * 128], memb[:, kt, :], start=(kt == 0), stop=(kt == 7),
                     r=["wk_mem", "memb"], w=[kk])
            k.cp("act", kmT[:, h, :], pk[:, 0:256], r=[kk], w=["kmT"])
        for mt in range(2):
            pv, kv = k.ps()
            for kt in range(8):
                k.mm(pv[:, :], memb[:, kt, mt * 128:(mt + 1) * 128], wv[:, kt, :], start=(kt == 0), stop=(kt == 7),
                     r=["wv_mem", "memb"], w=[kv])
            k.cp("act", vm[:, mt, :], pv[:, :], r=[kv], w=["vm"])
        sc = 128.0 ** -0.5
        it = 0
        for tt_ in range(NTT):
            tok = slice(tt_ * TT, (tt_ + 1) * TT)
            xb = xbs[tt_ % 2]
            xk = f"xb2_{tt_ % 2}"
            k.dma("pool", xb[:, :, :], kt_view(T["xT"], tok), w=[xk])
            for h in range(4):
                q_, qk = qm[it % 2], f"qm{it % 2}"
                p_, pk_ = pT[it % 2], f"pT{it % 2}"
                o_, ok_ = ob[it % 2], f"ob{it % 2}"
                it += 1
                pq, kq = k.ps()
                for kt in range(8):
                    k.mm(pq[:, :], wq[:, h, kt * 128:(kt + 1) * 128], xb[:, kt, :], start=(kt == 0), stop=(kt == 7),
                         r=["wq_mem", xk], w=[kq])
                k.cp("act", q_[:, :], pq[:, :], r=[kq], w=[qk])
                for mt in range(2):
                    ps_, ks_ = k.ps()
                    k.mm(ps_[:, :], kmT[:, h, mt * 128:(mt + 1) * 128], q_[:, :], r=["kmT", qk], w=[ks_])
                    k.act(p_[:, mt, :], ps_[:, :], AF.Exp, r=[ks_], w=[pk_], scale=sc)
                po, ko = k.ps()
                for mt in range(2):
                    k.mm(po[:, :], vm[:, mt, h * 128:(h + 1) * 128], p_[:, mt, :], start=(mt == 0), stop=(mt == 1),
                         r=["vm", pk_], w=[ko])
                pd, kd = k.ps()
                for mt in range(2):
                    k.mm(pd[:, :], ones[:, :], p_[:, mt, :], start=(mt == 0), stop=(mt == 1),
                         r=["ones_mem", pk_], w=[kd])
                k.recip(rec[:, :], pd[:, :], r=[kd], w=["rec_mem"])
                k.tt("dve", o_[:, :], po[:, :], rec[:, :], ALU.mult, r=[ko, "rec_mem"], w=[ok_])
                k.dma("sp", T["br"][2][h * 128:(h + 1) * 128, tok], o_[:, :], r=[ok_], w=[f"br2_{tt_}_{h}"])
        P.emit()


def cmul(k, eng, outr, outi, ar, ai, br_, bi_, t1, t2, r, w):
    k.tt(eng, t1, ar, br_, ALU.mult, r=r, w=["cm_t1"])
    k.tt(eng, t2, ai, bi_, ALU.mult, r=r, w=["cm_t2"])
    k.tt(eng, outr, t1, t2, ALU.subtract, r=["cm_t1", "cm_t2"], w=w)
    k.tt(eng, t1, ar, bi_, ALU.mult, r=r + ["cm_t1"], w=["cm_t1"])
    k.tt(eng, t2, ai, br_, ALU.mult, r=r + ["cm_t2"], w=["cm_t2"])
    k.tt(eng, outi, t1, t2, ALU.add, r=["cm_t1", "cm_t2"], w=w)


def phase_s5(k, T):
    nc, P = k.nc, k.P
    with ExitStack() as es:
        sbp = lambda n, s, d: k.sb(es, n, s, d)
        es1 = ExitStack()
        sb = lambda n, s, d: k.sb(es1, n, s, d)
        I16 = sbp("I16", [128, 128], BF16)
        J16 = sbp("J16", [128, 128], BF16)
        Sel = sbp("Sel", [128, 64, 128], BF16)
        SelI = sbp("SelI", [128, 64, 128], BF16)
        wu = sbp("wu", [128, 4, 1024], BF16)
        wglu = sbp("wglu", [128, 4, 512], BF16)
        bglu = sbp("bglu", [128, 4], F32)
        LR = sbp("LR", [128, 32, 9], F32)
        LI = sbp("LI", [128, 32, 9], F32)
        Mb = sbp("Mb", [128, 32, 128], BF16)
        BmT = sbp("BmT", [128, 32, 128], BF16)
        Cmb = sbp("Cmb", [128, 32, 128], BF16)
        I32t = sb("I32t", [128, 128], F32)
        Mc = sb("Mc", [128, 128], F32)
        lre = sb("lre", [128, 32], F32)
        lim = sb("lim", [128, 32], F32)
        ldt = sb("ldt", [128, 32], F32)
        dP = sb("dP", [128, 32], F32)
        Br = sb("Br", [128, 32, 16], F32)
        Bi = sb("Bi", [128, 32, 16], F32)
        Cr = sb("Cr", [128, 32, 16], F32)
        Ci = sb("Ci", [128, 32, 16], F32)
        negpi = sb("negpi", [128, 1], F32)
        k.dma("sp", I32t[:, :], T["I128"], w=["I32t"])
        k.dma("pool", I16[:, :], T["I128"], w=["I16"])
        k.dma("pool", J16[:, :], T["J128"], w=["J16"])
        k.dma("sp", Mc[:, :], T["Mc"], w=["Mc"])
        k.dma("pool", Sel[:, :, :], T["Sel"], w=["Sel"])
        k.dma("pool", SelI[:, :, :], T["SelI"], w=["SelI"])
        for nm, t_ in (("lre", lre), ("lim", lim), ("ldt", ldt), ("dP", dP)):
            k.dma("sp", t_[:, :], T[nm], w=[nm])
        for nm, t_ in (("Br", Br), ("Bi", Bi), ("Cr", Cr), ("Ci", Ci)):
            k.dma("sp", t_[:, :, :], T[nm], w=[nm])
        k.dma("pool", wu[:, :, :], T["w_u_l"], w=["wu"])
        k.dma("pool", wglu[:, :, :], T["w_glu_l"], w=["wglu"])
        k.dma("sp", bglu[:, :], T["b_glu_l"], w=["bglu"])
        k.memset("pool", negpi[:, :], SIN_BI, w=["negpi"])

        a_ = sb("s5a", [128, 32], F32)
        th = sb("s5th", [128, 32], F32)
        dt_ = sb("s5dt", [128, 32], F32)
        y_ = sb("sr_y", [128, 32], F32)
        ki_ = sb("sr_k", [128, 32], I32)
        kf_ = sb("sr_kf", [128, 32], F32)
        sn = sb("s5sn", [128, 32], F32)
        cs = sb("s5cs", [128, 32], F32)
        mg = sb("s5mg", [128, 32], F32)
        t1 = sb("cm_t1", [128, 32], F32)
        t2 = sb("cm_t2", [128, 32], F32)
        PR = sb("PR", [128, 32, 9], F32)
        PI = sb("PI", [128, 32, 9], F32)
        NR = sb("NR", [128, 32, 8], F32)
        NI = sb("NI", [128, 32, 8], F32)
        bR = sb("betR", [128, 32], F32)
        bI = sb("betI", [128, 32], F32)
        inv = sb("s5inv", [128, 32], F32)

        k.ts("dve", lre[:, :], lre[:, :], -1e-4, ALU.min, r=["lre"], w=["lre"])
        k.act(dt_[:, :], ldt[:, :], AF.Exp, r=["ldt"], w=["s5dt"])
        k.tt("dve", a_[:, :], lre[:, :], dt_[:, :], ALU.mult, r=["lre", "s5dt"], w=["s5a"])
        k.tt("dve", th[:, :], lim[:, :], dt_[:, :], ALU.mult, r=["lim", "s5dt"], w=["s5th"])
        k.act(mg[:, :], a_[:, :], AF.Exp, r=["s5a"], w=["s5mg"])
        tmp3 = (y_[:, :], ki_[:, :], kf_[:, :])
        sin_reduced(k, sn[:, :], th[:, :], 1.0, False, None, tmp3, negpi[:, 0:1], ["s5th"], ["s5sn"])
        sin_reduced(k, cs[:, :], th[:, :], 1.0, True, None, tmp3, negpi[:, 0:1], ["s5th"], ["s5cs"])
        k.memset("dve", PR[:, :, 0], 1.0, w=["PRI"])
        k.memset("dve", PI[:, :, 0], 0.0, w=["PRI"])
        k.tt("dve", PR[:, :, 1], mg[:, :], cs[:, :], ALU.mult, r=["s5mg", "s5cs"], w=["PRI"])
        k.tt("dve", PI[:, :, 1], mg[:, :], sn[:, :], ALU.mult, r=["s5mg", "s5sn"], w=["PRI"])
        for n in range(2, 9):
            cmul(k, "dve", PR[:, :, n], PI[:, :, n], PR[:, :, n - 1], PI[:, :, n - 1], PR[:, :, 1], PI[:, :, 1],
                 t1[:, :], t2[:, :], ["PRI"], ["PRI"])
        k.cp("dve", LR[:, :, 0], PR[:, :, 8], r=["PRI"], w=["LRI"])
        k.cp("dve", LI[:, :, 0], PI[:, :, 8], r=["PRI"], w=["LRI"])
        for l in range(1, 9):
            cmul(k, "dve", LR[:, :, l], LI[:, :, l], LR[:, :, l - 1], LI[:, :, l - 1], LR[:, :, l - 1], LI[:, :, l - 1],
                 t1[:, :], t2[:, :], ["LRI"], ["LRI"])
        for n in range(8):
            k.act(inv[:, :], a_[:, :], AF.Exp, r=["s5a"], w=["s5inv"], scale=-2.0 * (n + 1))
            k.tt("dve", NR[:, :, n], PR[:, :, n + 1], inv[:, :], ALU.mult, r=["PRI", "s5inv"], w=["NRI"])
            k.stt("dve", NI[:, :, n], PI[:, :, n + 1], -1.0, inv[:, :], ALU.mult, ALU.mult, r=["PRI", "s5inv"], w=["NRI"])
        nr_ = sb("s5nr", [128, 32], F32)
        den = sb("s5den", [128, 32], F32)
        k.ts("dve", nr_[:, :], PR[:, :, 1], -1.0, ALU.add, r=["PRI"], w=["s5nr"])
        k.tt("dve", den[:, :], lre[:, :], lre[:, :], ALU.mult, r=["lre"], w=["s5den"])
        k.tt("dve", t1[:, :], lim[:, :], lim[:, :], ALU.mult, r=["lim"], w=["cm_t1"])
        k.tt("dve", den[:, :], den[:, :], t1[:, :], ALU.add, r=["s5den", "cm_t1"], w=["s5den"])
        k.recip(den[:, :], den[:, :], r=["s5den"], w=["s5den"])
        k.tt("dve", t1[:, :], nr_[:, :], lre[:, :], ALU.mult, r=["s5nr", "lre"], w=["cm_t1"])
        k.tt("dve", t2[:, :], PI[:, :, 1], lim[:, :], ALU.mult, r=["PRI", "lim"], w=["cm_t2"])
        k.tt("dve", t1[:, :], t1[:, :], t2[:, :], ALU.add, r=["cm_t1", "cm_t2"], w=["cm_t1"])
        k.tt("dve", bR[:, :], t1[:, :], den[:, :], ALU.mult, r=["cm_t1", "s5den"], w=["betR"])
        k.tt("dve", t1[:, :], PI[:, :, 1], lre[:, :], ALU.mult, r=["PRI", "lre"], w=["cm_t1"])
        k.tt("dve", t2[:, :], nr_[:, :], lim[:, :], ALU.mult, r=["s5nr", "lim"], w=["cm_t2"])
        k.tt("dve", t1[:, :], t1[:, :], t2[:, :], ALU.subtract, r=["cm_t1", "cm_t2"], w=["cm_t1"])
        k.tt("dve", bI[:, :], t1[:, :], den[:, :], ALU.mult, r=["cm_t1", "s5den"], w=["betI"])
        bbr = sb("bbr", [128, 32, 16], F32)
        bbi = sb("bbi", [128, 32, 16], F32)
        u1 = sb("s5u1", [128, 32, 16], F32)
        u2 = sb("s5u2", [128, 32, 16], F32)
        bc16 = lambda t_: t_[:, :].unsqueeze(2).to_broadcast([128, 32, 16])
        cmul(k, "dve", bbr[:, :, :], bbi[:, :, :], bc16(bR), bc16(bI), Br[:, :, :], Bi[:, :, :],
             u1[:, :, :], u2[:, :, :], ["betR", "betI", "Br", "Bi"], ["bb"])
        Z = sb("Zm", [128, 32, 8, 16], F32)
        W = sb("Wm", [128, 32, 8, 16], F32)
        Cm = sb("Cm", [128, 32, 8, 16], F32)
        v1 = sb("s5v1", [128, 32, 8, 16], F32)
        v2 = sb("s5v2", [128, 32, 8, 16], F32)

        def big_cmul(dst, dkey, Xr, Xi, Tr, Ti, neg_im, rk):
            xb_ = lambda t_, lo, hi: t_[lo:hi, :, :].unsqueeze(2).to_broadcast([hi - lo, 32, 8, 16])
            tb_ = lambda t_, lo, hi: t_[lo:hi].unsqueeze(3).to_broadcast([hi - lo, 32, 8, 16])
            k.tt("dve", v1[0:64], xb_(Xr, 0, 64), tb_(Tr, 0, 64), ALU.mult, r=rk, w=["s5v1"])
            k.tt("dve", v2[0:64], xb_(Xi, 0, 64), tb_(Ti, 0, 64), ALU.mult, r=rk, w=["s5v2"])
            k.tt("dve", dst[0:64], v1[0:64], v2[0:64], ALU.subtract, r=["s5v1", "s5v2"], w=[dkey])
            k.tt("dve", v1[64:128], xb_(Xi, 64, 128), tb_(Tr, 64, 128), ALU.mult, r=rk + ["s5v1"], w=["s5v1"])
            k.tt("dve", v2[64:128], xb_(Xr, 64, 128), tb_(Ti, 64, 128), ALU.mult, r=rk + ["s5v2"], w=["s5v2"])
            if neg_im:
                k.stt("dve", dst[64:128], v1[64:128], -1.0, v2[64:128], ALU.mult, ALU.subtract,
                      r=["s5v1", "s5v2"], w=[dkey])
            else:
                k.tt("dve", dst[64:128], v1[64:128], v2[64:128], ALU.add, r=["s5v1", "s5v2"], w=[dkey])

        big_cmul(Z, "Zm", bbr, bbi, NR[:, :, 0:8], NI[:, :, 0:8], False, ["bb", "NRI"])
        PRrev = sb("PRrev", [128, 32, 8], F32)
        PIrev = sb("PIrev", [128, 32, 8], F32)
        for tau in range(8):
            k.cp("pool", PRrev[:, :, tau], PR[:, :, 7 - tau], r=["PRI"], w=["Prev"])
            k.cp("pool", PIrev[:, :, tau], PI[:, :, 7 - tau], r=["PRI"], w=["Prev"])
        big_cmul(W, "Wm", bbr, bbi, PRrev[:, :, :], PIrev[:, :, :], False, ["bb", "Prev"])
        big_cmul(Cm, "Cm", Cr, Ci, PR[:, :, 1:9], PI[:, :, 1:9], True, ["Cr", "Ci", "PRI"])

        mt_ = sb("s5mt", [128, 4, 128], F32)
        k.cp("act", Cmb[:, :, :], Cm[:, :, :, :].rearrange("p g r i -> p g (r i)"), r=["Cm"], w=["Cmb"])
        for g4 in range(8):
            pm, km = k.ps()
            for gg in range(4):
                g = g4 * 4 + gg
                k.mm(pm[:, gg * 128:(gg + 1) * 128], Z[:, g].rearrange("p t j -> p (t j)"),
                     Cm[:, g].rearrange("p r i -> p (r i)"), r=["Zm", "Cm"], w=[km])
            k.tt("dve", mt_[:, :, :], pm[:, :].rearrange("p (a b) -> p a b", a=4),
                 Mc[:, :].unsqueeze(1).to_broadcast([128, 4, 128]), ALU.mult, r=[km, "Mc"], w=["s5mt"])
            for gg in range(4):
                g = g4 * 4 + gg
                k.stt("dve", Mb[:, g, :], I32t[:, :], dP[:, g:g + 1], mt_[:, gg, :], ALU.mult, ALU.add,
                      r=["I32t", "dP", "s5mt"], w=["Mb"])
            pw, kw = k.ps()
            for gg in range(4):
                g = g4 * 4 + gg
                k.tr(pw[:, gg * 128:(gg + 1) * 128], W[:, g].rearrange("p t j -> p (t j)"), I32t[:, :],
                     r=["Wm", "I32t"], w=[kw])
            k.cp("act", BmT[:, g4 * 4:(g4 + 1) * 4, :], pw[:, :].rearrange("p (a b) -> p a b", a=4), r=[kw], w=["BmT"])

        P.emit()
        es1.close()
        sb = sbp
        uTp = sb("uTp", [128, 4, 8, 512], BF16)
        ysT = sb("ysT", [128, 4, S], BF16)
        xbs = [sb(f"xb1_{i}", [128, 8, TT], BF16) for i in range(2)]
        for tt_ in range(NTT):
            tok = slice(tt_ * TT, (tt_ + 1) * TT)
            xb = xbs[tt_ % 2]
            xk = f"xb1_{tt_ % 2}"
            k.dma("pool", xb[:, :, :], kt_view(T["xT"], tok), w=[xk])
            for t in range(4):
                pu, ku = k.ps()
                for kt in range(8):
                    k.mm(pu[:, :], wu[:, t, kt * 128:(kt + 1) * 128], xb[:, kt, :], start=(kt == 0), stop=(kt == 7),
                         r=["wu", xk], w=[ku])
                k.cp("act", uTp[:, t, :, tt_ * 64:(tt_ + 1) * 64], pu[:, :].rearrange("p (c t) -> p t c", t=8),
                     r=[ku], w=[f"uTp{t}"])

        Ug = [sb(f"Ug{i}", [128, 512], BF16) for i in range(8)]
        Hb = [sb(f"Hb{i}", [128, 513], BF16) for i in range(8)]
        Yg = [sb(f"Yg{i}", [128, 512], BF16) for i in range(8)]
        Rt = [sb(f"Rt{i}", [128, 128], F32) for i in range(4)]
        Rj = [sb(f"Rj{i}", [128, 128], F32) for i in range(4)]
        Rm = [sb(f"Rm{i}", [128, 128], BF16) for i in range(8)]
        for i in range(8):
            k.memset("pool", Hb[i][:, 0:1], 0.0, w=[f"Hb{i}"])
        ri = 0
        for t in range(4):
            for gl in range(8):
                g = 8 * t + gl
                pu, ku = k.ps()
                for tau in range(8):
                    k.mm(pu[:, :], Sel[:, gl * 8 + tau, :], uTp[:, t, tau, :], start=(tau == 0), stop=(tau == 7),
                         r=["Sel", f"uTp{t}"], w=[ku])
                k.cp("act", Ug[gl][:, :], pu[:, :], r=[ku], w=[f"Ug{gl}"])
                ph, kh = k.ps()
                k.mm(ph[:, :], BmT[:, g, :], Ug[gl][:, :], r=["BmT", f"Ug{gl}"], w=[kh])
                k.cp("act", Hb[gl][:, 1:513], ph[:, :], r=[kh], w=[f"Hb{gl}"])
            for l in range(9):
                d = 1 << l
                for gl in range(8):
                    g = 8 * t + gl
                    rt, rtk = Rt[ri % 4], f"Rt{ri % 4}"
                    rm, rmk = Rm[ri % 8], f"Rm{ri % 8}"
                    rj, rjk = Rj[ri % 4], f"Rj{ri % 4}"
                    ri += 1
                    k.act(rt[:, :], I16[:, :], AF.Copy, r=["I16", "LRI"], w=[rtk], scale=LR[:, g, l:l + 1])
                    k.act(rj[:, :], J16[:, :], AF.Copy, r=["J16", "LRI"], w=[rjk], scale=LI[:, g, l:l + 1])
                    k.tt("pool", rm[:, :], rt[:, :], rj[:, :], ALU.add, r=[rtk, rjk], w=[rmk])
                    ps_, ks_ = k.ps()
                    k.mm(ps_[:, 0:512 - d], rm[:, :], Hb[gl][:, 1:513 - d], r=[rmk, f"Hb{gl}"], w=[ks_])
                    k.tt("dve", Hb[gl][:, 1 + d:513], Hb[gl][:, 1 + d:513], ps_[:, 0:512 - d], ALU.add,
                         r=[ks_, f"Hb{gl}"], w=[f"Hb{gl}"])
            for gl in range(8):
                g = 8 * t + gl
                py, ky = k.ps()
                k.mm(py[:, :], Mb[:, g, :], Ug[gl][:, :], start=True, stop=False, r=["Mb", f"Ug{gl}"], w=[ky])
                k.mm(py[:, :], Cmb[:, g, :], Hb[gl][:, 0:512], start=False, stop=True, r=["Cmb", f"Hb{gl}"], w=[ky])
                k.act(Yg[gl][:, :], py[:, :], AF.Gelu_apprx_tanh, r=[ky], w=[f"Yg{gl}"])
            for r_ in range(8):
                pt, kt_ = k.ps()
                for gl in range(8):
                    k.mm(pt[:, :], SelI[:, gl * 8 + r_, :], Yg[gl][:, :], start=(gl == 0), stop=(gl == 7),
                         r=["SelI", f"Yg{gl}"], w=[kt_])
                k.cp("act", ysT[:, t, :].rearrange("p (c r) -> p r c", r=8)[:, r_, :], pt[:, :], r=[kt_], w=[f"ysT{t}"])
        sgl = [sb(f"sgl{i}", [128, TT], F32) for i in range(2)]
        og = [sb(f"og{i}", [128, TT], BF16) for i in range(2)]
        it = 0
        for tt_ in range(NTT):
            tok = slice(tt_ * TT, (tt_ + 1) * TT)
            for n in range(4):
                pz, kz = k.ps()
                for kt in range(4):
                    k.mm(pz[:, :], wglu[:, n, kt * 128:(kt + 1) * 128], ysT[:, kt, tok], start=(kt == 0), stop=(kt == 3),
                         r=["wglu", f"ysT{kt}"], w=[kz])
                s_, sk = sgl[it % 2], f"sgl{it % 2}"
                o_, ok_ = og[it % 2], f"og{it % 2}"
                it += 1
                k.act(s_[:, :], pz[:, :], AF.Sigmoid, r=[kz, "bglu"], w=[sk], bias=bglu[:, n:n + 1])
                k.tt("dve", o_[:, :], ysT[:, n, tok], s_[:, :], ALU.mult, r=[f"ysT{n}", sk], w=[ok_])
                k.dma("sp", T["br"][1][n * 128:(n + 1) * 128, tok], o_[:, :], r=[ok_], w=[f"br1_{tt_}_{n}"])
        P.emit()


def phase_att(k, T):
    nc, P = k.nc, k.P
    with ExitStack() as es:
        sb = lambda n, s, d: k.sb(es, n, s, d)
        qT = sb("qT", [128, 4, S], BF16)
        kT = sb("kT", [128, 2, S], BF16)
        qiT = sb("qiT", [128, 2, S], BF16)
        kiT = sb("kiT", [128, S], BF16)
        V1 = sb("V1", [128, 32, 2, 65], BF16)
        widx = sb("widx", [128, 32, 4], F32)
        I16 = sb("I16a", [128, 128], BF16)
        k.dma("pool", I16[:, :], T["I128"], w=["I16a"])
        k.memset("dve", V1[:, :, :, 64:65], 1.0, w=["V1"])
        dests = [(qT, 0), (qT, 1), (qT, 2), (qT, 3), (kT, 0), (kT, 1), (qiT, 0), (qiT, 1), (kiT, None)]
        dkeys = ["qT", "qT", "qT", "qT", "kT", "kT", "qiT", "qiT", "kiT"]
        with ExitStack() as es2:
            sb2 = lambda n, s, d: k.sb(es2, n, s, d)
            watt = sb2("watt", [128, 18, 1024], BF16)
            wvw = sb2("wvw", [128, 8, 132], BF16)
            cosTs = [sb2(f"cosT{i}", [128, TT], F32) for i in range(2)]
            sinTs = [sb2(f"sinT{i}", [128, TT], F32) for i in range(2)]
            posi = sb2("posi", [128, TT], I32)
            posf = sb2("posf", [128, TT], F32)
            y_ = sb2("sr_y3", [128, TT], F32)
            ki_ = sb2("sr_k3", [128, TT], I32)
            kf_ = sb2("sr_kf3", [128, TT], F32)
            ifr = sb2("ifr", [128, 2], F32)
            negpi = sb2("negpi3", [128, 1], F32)
            xbs = [sb2(f"xb3_{i}", [128, 8, TT], BF16) for i in range(2)]
            r1 = [sb2(f"r1_{i}", [128, TT], F32) for i in range(2)]
            r2 = [sb2(f"r2_{i}", [128, TT], F32) for i in range(2)]
            for i in range(18):
                k.dma("pool", watt[:, i, :], T["w_att_l"][i], w=[f"watt{i}"])
            k.dma("pool", wvw[:, :, :], T["w_vw_l"], w=["wvw"])
            k.dma("sp", ifr[:, :], T["ifr"], w=["ifr"])
            k.memset("pool", negpi[:, :], SIN_BI, w=["negpi"])
            tmp3 = (y_[:, :], ki_[:, :], kf_[:, :])
            it = 0
            for tt_ in range(NTT):
                tok = slice(tt_ * TT, (tt_ + 1) * TT)
                xb = xbs[tt_ % 2]
                xk = f"xb3_{tt_ % 2}"
                k.dma("pool", xb[:, :, :], kt_view(T["xT"], tok), w=[xk])
                cosT, ck = cosTs[tt_ % 2], f"cosT{tt_ % 2}"
                sinT, sk_ = sinTs[tt_ % 2], f"sinT{tt_ % 2}"
                k.dma("sp", posi[:, :], T["pos"][:, tok].partition_broadcast(128), w=["posi"])
                k.cp("dve", posf[:, :], posi[:, :], r=["posi"], w=["posf"])
                sin_reduced(k, sinT[:, :], posf[:, :], ifr[:, 0:1], False, None, tmp3, negpi[:, 0:1], ["posf", "ifr"], [sk_])
                sin_reduced(k, cosT[:, :], posf[:, :], ifr[:, 0:1], True, None, tmp3, negpi[:, 0:1], ["posf", "ifr"], [ck])
                k.ts("dve", sinT[:, :], sinT[:, :], ifr[:, 1:2], ALU.mult, r=[sk_, "ifr"], w=[sk_])
                for i in range(9):
                    p1, k1 = k.ps()
                    for kt in range(8):
                        k.mm(p1[:, :], watt[:, 2 * i, kt * 128:(kt + 1) * 128], xb[:, kt, :], start=(kt == 0), stop=(kt == 7),
                             r=[f"watt{2 * i}", xk], w=[k1])
                    p2, k2 = k.ps()
                    for kt in range(8):
                        k.mm(p2[:, :], watt[:, 2 * i + 1, kt * 128:(kt + 1) * 128], xb[:, kt, :], start=(kt == 0), stop=(kt == 7),
                             r=[f"watt{2 * i + 1}", xk], w=[k2])
                    a1, a1k = r1[it % 2], f"r1_{it % 2}"
                    a2, a2k = r2[it % 2], f"r2_{it % 2}"
                    it += 1
                    k.tt("dve", a1[:, :], p1[:, :], cosT[:, :], ALU.mult, r=[k1, ck], w=[a1k])
                    k.tt("dve", a2[:, :], p2[:, :], sinT[:, :], ALU.mult, r=[k2, sk_], w=[a2k])
                    dt_, di = dests[i]
                    dst = dt_[:, di, tok] if di is not None else dt_[:, tok]
                    k.tt("pool", dst, a1[:, :], a2[:, :], ALU.add, r=[a1k, a2k], w=[dkeys[i]])
                for sbk in range(4):
                    sblk = tt_ * 4 + sbk
                    pv, kv = k.ps()
                    for kt in range(8):
                        k.mm(pv[:, 0:132], xb[:, kt, sbk * 128:(sbk + 1) * 128], wvw[:, kt, :], start=(kt == 0), stop=(kt == 7),
                             r=["wvw", xk], w=[kv])
                    k.cp("act", V1[:, sblk, :, 0:64], pv[:, 0:128].rearrange("p (g d) -> p g d", g=2), r=[kv], w=["V1"])
                    k.act(widx[:, sblk, :], pv[:, 128:132], AF.Copy, r=[kv], w=["widx"], scale=0.5)
            P.emit()

        scoreAs = [sb(f"scoreA{i}", [128, S], F32) for i in range(2)]
        zr = sb("zr", [128, S], F32)
        junkDs = [sb(f"junkD{i}", [128, S], BF16) for i in range(2)]
        masks = [sb(f"mask{i}", [128, S], BF16) for i in range(2)]
        maskTs = [sb(f"maskT{i}", [128, S], BF16) for i in range(4)]
        rl = [sb(f"rl{i}", [128, 512], F32) for i in range(2)]
        Pt = [sb(f"Pt{i}", [128, 512], BF16) for i in range(3)]
        KB = 24
        sts = [sb(f"selst{i}", [128, 16], F32) for i in range(2)]
        nmss = [[sb(f"nm{j}_{i}", [128, 1], F32) for i in range(2)] for j in range(2)]
        ssums = [sb(f"ssum{i}", [128, 1], F32) for i in range(2)]
        gsgs = [sb(f"gsg{i}", [128, 1], F32) for i in range(2)]
        nwrow = sb("nwrow", [128, KB], F32)
        nwtabs = [sb(f"nwtab{i}", [128, KB], F32) for i in range(2)]
        nw2tabs = [sb(f"nw2tab{i}", [128, KB], F32) for i in range(2)]
        thrc = sb("thrc", [128, 1], F32)
        rec = sb("rec_a", [128, 8], F32)
        attn = sb("attn_tm", [128, 8, 64], BF16)
        aT = [sb(f"aT{i}", [128, 4, 128], BF16) for i in range(2)]
        k.dma("sp", nwrow[:, :], T["nwrow"].partition_broadcast(128), w=["nwrow"])
        k.memset("pool", thrc[:, :], -1.0e29, w=["thrc"])
        psB = k.psB
        ring_save = k.ring
        psOs = [[ring_save[3], ring_save[4]], [ring_save[5], ring_save[6]]]
        k.ring = ring_save[:3]

        def indexer(b):
            N = 128 * (b + 1)
            sl = b % 2
            scoreA, sak = scoreAs[sl], f"scoreA{sl}"
            st = sts[sl]
            tb = slice(b * 128, (b + 1) * 128)
            nch = (N + 511) // 512
            for c in range(nch):
                c0_ = c * 512
                cw_ = min(512, N - c0_)
                for h in range(4):
                    base = 64 * (h % 2)
                    pl, kl = k.ps()
                    k.mm(pl[:, 0:cw_], qiT[base:base + 64, h // 2, tb], kiT[base:base + 64, c0_:c0_ + cw_],
                         r=["qiT", "kiT"], w=[kl])
                    t_, tk = rl[(c * 4 + h) % 2], f"rl{(c * 4 + h) % 2}"
                    k.act(t_[:, 0:cw_], pl[:, 0:cw_], AF.Relu, r=[kl], w=[tk], scale=0.125)
                    if h == 0:
                        k.ts("dve", scoreA[:, c0_:c0_ + cw_], t_[:, 0:cw_], widx[:, b, h:h + 1], ALU.mult,
                             r=[tk, "widx"], w=[sak])
                    else:
                        k.stt("dve", scoreA[:, c0_:c0_ + cw_], t_[:, 0:cw_], widx[:, b, h:h + 1], scoreA[:, c0_:c0_ + cw_],
                              ALU.mult, ALU.add, r=[tk, "widx", sak], w=[sak])
            if b >= 2:
                P.op("dve", lambda e, o=junkDs[sl][:, 0:N], i_=scoreA[:, 0:N], a=st[:, 12:13]:
                     e.tensor_scalar(out=o, in0=i_, scalar1=1.0, scalar2=None, op0=ALU.mult, op1=ALU.max, accum_out=a),
                     r=[sak], w=[f"junkD{sl}", f"st_amax1_{sl}"])
                P.op("dve", lambda e, o=junkDs[sl][:, 0:N], i_=scoreA[:, 0:N], a=st[:, 13:14]:
                     e.tensor_scalar(out=o, in0=i_, scalar1=-1.0, scalar2=None, op0=ALU.mult, op1=ALU.max, accum_out=a),
                     r=[sak], w=[f"junkD{sl}", f"st_amax2_{sl}"])
                k.tt("dve", st[:, 0:1], st[:, 12:13], st[:, 13:14], ALU.max, r=[f"st_amax1_{sl}", f"st_amax2_{sl}"],
                     w=[f"st_amax_{sl}"])
            k.memset("dve", scoreA[0:64, N - 64:N], -1.0e30, w=[sak])

        def sel_setup(b):
            N = 128 * (b + 1)
            sl = b % 2
            sc, sak = scoreAs[sl][:, 0:N], f"scoreA{sl}"
            st = sts[sl]
            junk, jk = junkDs[sl], f"junkD{sl}"
            K_ = lambda nm: f"{nm}_{sl}"
            amax, cp, cpz, isA, isC, lo0, hi0, w0 = (st[:, i:i + 1] for i in range(8))
            need, t0_, t1_ = (st[:, i:i + 1] for i in range(8, 11))
            cnt = lambda o, op0, key: P.op(
                "dve", lambda e: e.tensor_scalar(out=junk[:, 0:N], in0=sc, scalar1=0.0, scalar2=None, op0=op0,
                                                 op1=ALU.add, accum_out=o), r=[sak], w=[jk, key])
            cnt(cp, ALU.is_gt, K_("st_cp"))
            cnt(cpz, ALU.is_ge, K_("st_cpz"))
            k.ts("dve", isA, cp, 255.5, ALU.is_ge, r=[K_("st_cp")], w=[K_("st_isA")])
            k.ts("dve", isC, cpz, 255.5, ALU.is_lt, r=[K_("st_cpz")], w=[K_("st_isC")])
            k.ts("dve", t0_, amax, -1.0001, ALU.mult, -1.0e-30, ALU.add, r=[K_("st_amax")], w=[K_("st_t0")])
            k.tt("dve", lo0, isC, t0_, ALU.mult, r=[K_("st_isC"), K_("st_t0")], w=[K_("st_lo0")])
            k.tt("dve", t1_, isA, amax, ALU.mult, r=[K_("st_isA"), K_("st_amax")], w=[K_("st_t1")])
            k.stt("dve", hi0, isC, -2.0e-38, t1_, ALU.mult, ALU.add, r=[K_("st_isC"), K_("st_t1")], w=[K_("st_hi0")])
            k.tt("dve", w0, hi0, lo0, ALU.subtract, r=[K_("st_hi0"), K_("st_lo0")], w=[K_("st_w0")])
            k.tt("dve", nwtabs[sl][:, :], nwrow[:, :], w0.to_broadcast([128, KB]), ALU.mult,
                 r=["nwrow", K_("st_w0")], w=[K_("nwtab")])
            k.ts("dve", nw2tabs[sl][:, :], nwtabs[sl][:, :], -2.0, ALU.mult, r=[K_("nwtab")], w=[K_("nwtab")])
            k.tt("dve", t0_, isA, isC, ALU.add, r=[K_("st_isA"), K_("st_isC"), K_("st_t0")], w=[K_("st_t0")])
            k.ts("dve", t0_, t0_, -1.0, ALU.mult, 1.0, ALU.add, r=[K_("st_t0")], w=[K_("st_t0")])
            k.ts("dve", t1_, cp, -1.0, ALU.mult, 256.0, ALU.add, r=[K_("st_cp"), K_("st_t1")], w=[K_("st_t1")])
            k.tt("dve", need, t0_, t1_, ALU.mult, r=[K_("st_t0"), K_("st_t1")], w=[K_("st_need")])
            k.cp("dve", nmss[sl][0][:, :], lo0, r=[K_("st_lo0")], w=[f"nm{sl}_0"])

        def sel_step(b, it_, part):
            N = 128 * (b + 1)
            sl = b % 2
            sc, sak = scoreAs[sl][:, 0:N], f"scoreA{sl}"
            junk, jk = junkDs[sl], f"junkD{sl}"
            cur, ck_ = nmss[sl][it_ % 2], f"nm{sl}_{it_ % 2}"
            nxt, nk_ = nmss[sl][(it_ + 1) % 2], f"nm{sl}_{(it_ + 1) % 2}"
            gsg, gk = gsgs[sl], f"gsg{sl}"
            ssum, sk = ssums[sl], f"ssum{sl}"
            if part == 0:
                k.ts("dve", gsg[:, :], cur[:, :], nw2tabs[sl][:, it_:it_ + 1], ALU.add, r=[ck_, f"nwtab_{sl}"], w=[gk])
            elif part == 1:
                P.op("dve", lambda e: e.tensor_scalar(
                    out=junk[:, 0:N], in0=sc, scalar1=gsg[:, 0:1], scalar2=None, op0=ALU.is_gt, op1=ALU.add,
                    accum_out=ssum[:, 0:1]), r=[sak, gk], w=[jk, sk])
            elif part == 2:
                k.ts("dve", ssum[:, :], ssum[:, :], 255.5, ALU.is_lt, NEG_FILL, ALU.mult, r=[sk], w=[sk])
            else:
                k.ts("dve", nxt[:, :], ssum[:, :], gsg[:, 0:1], ALU.add, cur[:, 0:1], ALU.max, r=[sk, gk, ck_], w=[nk_])

        def sel_finish(b):
            N = 128 * (b + 1)
            nsb = b + 1
            sl = b % 2
            sc, sak = scoreAs[sl][:, 0:N], f"scoreA{sl}"
            mask, mk = masks[sl], f"mask{sl}"
            maskT, mtk = maskTs[b % 4], f"maskT{b % 4}"
            if b >= 2:
                zm, zk = junkDs[sl], f"junkD{sl}"
                need = sts[sl][:, 8:9]
                fin, fk_ = nmss[sl][KB % 2], f"nm{sl}_{KB % 2}"
                k.ts("dve", mask[:, 0:N], sc, fin[:, 0:1], ALU.is_gt, r=[sak, fk_], w=[mk])
                k.ts("dve", zm[:, 0:N], sc, 0.0, ALU.is_equal, r=[sak], w=[zk])
                P.op("dve", lambda e: e.tensor_tensor_scan(out=zr[:, 0:N], data0=zm[:, 0:N], data1=zm[:, 0:N], initial=0.0,
                                                           op0=ALU.add, op1=ALU.max), r=[zk], w=["zr"])
                k.stt("dve", zm[:, 0:N], zr[:, 0:N], need, zm[:, 0:N], ALU.is_le, ALU.mult, r=["zr", f"st_need_{sl}", zk], w=[zk])
                k.tt("pool", mask[:, 0:N], mask[:, 0:N], zm[:, 0:N], ALU.add, r=[mk, zk], w=[mk])
            else:
                k.ts("dve", mask[:, 0:N], sc, thrc[:, 0:1], ALU.is_ge, r=[sak, "thrc"], w=[mk])
            for j0 in range(0, nsb, 8):
                nj = min(8, nsb - j0)
                pb, kb = psB
                for jj in range(nj):
                    j = j0 + jj
                    k.tr(pb[:, jj * 128:(jj + 1) * 128], mask[:, j * 128:(j + 1) * 128], I16[:, :], r=[mk, "I16a"], w=[kb])
                k.act(maskT[:, j0 * 128:(j0 + nj) * 128], pb[:, 0:nj * 128], AF.Identity, r=[kb], w=[mtk],
                      scale=MASK_BIG, bias=-MASK_BIG)

        def select2(bs):
            ch = [b_ for b_ in bs if b_ >= 2]
            for b_ in ch:
                sel_setup(b_)
            for it_ in range(KB):
                for part in range(4):
                    for b_ in ch:
                        sel_step(b_, it_, part)
            for b_ in bs:
                sel_finish(b_)

        def attend(b):
            nsb = b + 1
            tb = slice(b * 128, (b + 1) * 128)
            maskT, mtk = maskTs[b % 4], f"maskT{b % 4}"
            psO = psOs[b % 2]
            pi_ = 0
            for h in range(8):
                g = h // 4
                base = 64 * (h % 2)
                po, ko = psO[h // 4]
                for j0 in range(0, nsb, 4):
                    nj = min(4, nsb - j0)
                    ps_, ks_ = k.ps()
                    for jj in range(nj):
                        j = j0 + jj
                        k.mm(ps_[:, jj * 128:(jj + 1) * 128], kT[base:base + 64, g, j * 128:(j + 1) * 128],
                             qT[base:base + 64, h // 2, tb], start=(jj == 0), stop=False, r=["kT", "qT"], w=[ks_])
                    for jj in range(nj):
                        j = j0 + jj
                        k.mm(ps_[:, jj * 128:(jj + 1) * 128], I16[:, :], maskT[:, j * 128:(j + 1) * 128],
                             start=False, stop=(jj == nj - 1), r=["I16a", mtk], w=[ks_])
                    p_, pk_ = Pt[pi_ % 3], f"Pt{pi_ % 3}"
                    pi_ += 1
                    k.act(p_[:, 0:nj * 128], ps_[:, 0:nj * 128], AF.Exp, r=[ks_], w=[pk_], scale=0.125)
                    for jj in range(nj):
                        j = j0 + jj
                        k.mm(po[:, (h % 4) * 65:(h % 4) * 65 + 65], p_[:, jj * 128:(jj + 1) * 128], V1[:, j, g, :],
                             start=(j == 0), stop=(j == nsb - 1), r=[pk_, "V1"], w=[ko])

        def finish(b):
            tb = slice(b * 128, (b + 1) * 128)
            psO = psOs[b % 2]
            for hb in range(2):
                po, ko = psO[hb]
                pv = po[:, 0:260].rearrange("p (h e) -> p h e", h=4)
                k.recip(rec[:, hb * 4:(hb + 1) * 4], pv[:, :, 64], r=[ko], w=["rec_a"])
                k.tt("dve", attn[:, hb * 4:(hb + 1) * 4, :], pv[:, :, 0:64],
                     rec[:, hb * 4:(hb + 1) * 4].unsqueeze(2).to_broadcast([128, 4, 64]), ALU.mult,
                     r=[ko, "rec_a"], w=["attn_tm"])
            pb, kb = psB
            a_, ak = aT[b % 2], f"aT{b % 2}"
            af = attn[:, :, :].rearrange("p h d -> p (h d)")
            for t in range(4):
                k.tr(pb[:, t * 128:(t + 1) * 128], af[:, t * 128:(t + 1) * 128], I16[:, :], r=["attn_tm", "I16a"], w=[kb])
            k.cp("act", a_[:, :, :], pb[:, 0:512].rearrange("p (t s) -> p t s", t=4), r=[kb], w=[ak])
            k.dma("sp", kt_view(T["br"][0], tb), a_[:, :, :], r=[ak], w=[f"br0_{b // 4}_{b % 4}"])

        order = list(reversed(range(S // 256)))
        f_ = order[0]
        indexer(2 * f_)
        indexer(2 * f_ + 1)
        select2([2 * f_, 2 * f_ + 1])
        for oi, p_i in enumerate(order):
            A_, B_ = 2 * p_i, 2 * p_i + 1
            nx = order[oi + 1] if oi + 1 < len(order) else None
            if nx is not None:
                indexer(2 * nx)
                indexer(2 * nx + 1)
            attend(A_)
            attend(B_)
            if nx is not None:
                select2([2 * nx, 2 * nx + 1])
            finish(A_)
            finish(B_)
        k.ring = ring_save
        P.emit()


def build_program(shapes, phases=("s5", "mem", "att", "mix"), br_mode="internal"):
    nc = bass.Bass("TRN2", target_bir_lowering=False)
    T = {}
    for name, (shape, dt) in shapes.items():
        T[name] = nc.dram_tensor(name, list(shape), dt, kind="ExternalInput").ap()
    T["outT"] = nc.dram_tensor("outT", [D, S], F32, kind="ExternalOutput").ap()
    if br_mode == "internal":
        brt = nc.dram_tensor("br", [3, 512, S], BF16).ap()
    elif br_mode == "output":
        brt = nc.dram_tensor("br", [3, 512, S], BF16, kind="ExternalOutput").ap()
    else:
        brt = nc.dram_tensor("br", [3, 512, S], BF16, kind="ExternalInput").ap()
    T["br"] = [brt[i] for i in range(3)]
    T["wA_bf"] = nc.dram_tensor("wA_bf", [8, 128, 4608], BF16).ap()
    T["wO_bf"] = nc.dram_tensor("wO_bf", [8, 128, 1024], BF16).ap()
    T["wU_bf"] = nc.dram_tensor("wU_bf", [22, 128, 2048], BF16).ap()
    T["wD_bf"] = nc.dram_tensor("wD_bf", [8, 128, 2816], BF16).ap()
    with ExitStack() as es:
        P = Prog(nc, es)
        k = K(nc, P)
        for i in range(5):
            t = es.enter_context(nc.psum_tensor(f"psf{i}", [128, 512], F32))
            k.ring.append((t, f"psf{i}"))
        o0 = es.enter_context(nc.psum_tensor("psf5", [128, 512], F32))
        o1 = es.enter_context(nc.psum_tensor("psf6", [128, 512], F32))
        pb = es.enter_context(nc.psum_tensor("psb", [128, 1024], BF16))
        k.psO = [(o0, "psf5"), (o1, "psf6")]
        k.psB = (pb, "psb")
        k.ring.append((o0, "psf5"))
        k.ring.append((o1, "psf6"))
        if "mix" in phases:
            for i in range(8):
                k.dma("pool", T["wA_bf"][i], T["wA_l"][i], w=[f"wbf_A{i}"])
            for i in range(8):
                k.dma("pool", T["wO_bf"][i], T["wO_l"][i], w=[f"wbf_O{i}"])
            for i in range(22):
                k.dma("pool", T["wU_bf"][i], T["wU_l"][i], w=[f"wbf_U{i}"])
            for i in range(8):
                k.dma("pool", T["wD_bf"][i], T["wD_l"][i], w=[f"wbf_Dn{i}"])
        if "s5" in phases:
            phase_s5(k, T)
        if "mem" in phases:
            phase_mem(k, T)
        if "att" in phases:
            phase_att(k, T)
        if "mix" in phases:
            phase_mix(k, T)
        else:
            P.wait_all("sp", [kk_ for kk_ in P.lastw if kk_.startswith("br")])
            P.emit()
    return nc


def tile_lhsT(W):
    Kd, N = W.shape
    return np.ascontiguousarray(W.reshape(Kd // 128, 128, N // 128, 128).transpose(2, 1, 0, 3).reshape(N // 128, 128, (Kd // 128) * 128))


def rhs_layout(W):
    Kd, N = W.shape
    return np.ascontiguousarray(W.reshape(Kd // 128, 128, N).transpose(1, 0, 2))


def prep_shared(inp):
    f = np.float32
    sh = {}
    w_in = inp["w_in"][0]
    def hc(base, h):
        return base + h * 64 + np.arange(64)
    def sw(c):
        return np.concatenate([c[32:], c[:32]])
    tiles = []
    for j in range(4):
        a, b = hc(0, 2 * j), hc(0, 2 * j + 1)
        tiles += [np.concatenate([a, b]), np.concatenate([sw(a), sw(b)])]
    for g in range(2):
        a = hc(512, g)
        tiles += [np.concatenate([a, a]), np.concatenate([sw(a), sw(a)])]
    for j in range(2):
        a, b = hc(768, 2 * j), hc(768, 2 * j + 1)
        tiles += [np.concatenate([a, b]), np.concatenate([sw(a), sw(b)])]
    a = 1024 + np.arange(64)
    tiles += [np.concatenate([a, a]), np.concatenate([sw(a), sw(a)])]
    cols = np.concatenate(tiles)
    sh["w_att_l"] = tile_lhsT(w_in[:, cols])
    sh["w_vw_l"] = rhs_layout(np.concatenate([w_in[:, 640:768], w_in[:, 1088:1092]], axis=1))
    sh["w_u_l"] = np.ascontiguousarray(tile_lhsT(w_in[:, 1092:1604]).transpose(1, 0, 2))
    sh["w_qmem_l"] = np.ascontiguousarray(tile_lhsT(w_in[:, 1604:2116]).transpose(1, 0, 2))
    wkv = inp["w_mem_kv"][0]
    sh["w_kmem_l"] = np.ascontiguousarray(tile_lhsT(wkv[:, 0:512]).transpose(1, 0, 2))
    sh["w_vmem_l"] = rhs_layout(wkv[:, 512:1024])
    sh["w_glu_l"] = np.ascontiguousarray(tile_lhsT(inp["w_glu"][0]).transpose(1, 0, 2))
    sh["b_glu_l"] = np.ascontiguousarray(inp["b_glu"][0].reshape(4, 128).T)
    p = np.arange(128)
    inv_freq = (10000.0 ** (-(np.arange(32, dtype=np.float32)) / np.float32(32))).astype(np.float32)
    ifr = np.zeros((128, 2), f)
    ifr[:, 0] = (inv_freq[p % 32].astype(np.float64) / TWO_PI).astype(f)
    ifr[:, 1] = np.where((p % 64) < 32, -1.0, 1.0)
    sh["ifr"] = ifr
    sh["nwrow"] = (-(2.0 ** -(np.arange(24, dtype=np.float64) + 2.0))).astype(f).reshape(1, 24)
    dup = lambda a_: np.ascontiguousarray(np.concatenate([a_, a_], axis=0).astype(f))
    sh["lre"] = dup(inp["s5_lam_re"][0].T)
    sh["lim"] = dup(inp["s5_lam_im"][0].T)
    sh["ldt"] = np.ascontiguousarray(np.broadcast_to(inp["s5_log_dt"][0][None, :], (128, 32)).astype(f))
    sh["Br"] = dup(inp["s5_b_re"][0].transpose(1, 0, 2))
    sh["Bi"] = dup(inp["s5_b_im"][0].transpose(1, 0, 2))
    sh["Cr"] = dup(inp["s5_c_re"][0].transpose(2, 0, 1))
    sh["Ci"] = dup(inp["s5_c_im"][0].transpose(2, 0, 1))
    sh["dP"] = np.ascontiguousarray(np.tile(inp["s5_d"][0].reshape(32, 16).T, (8, 1)).astype(f))
    sh["I128"] = np.eye(128, dtype=f)
    J = np.zeros((128, 128), f)
    for q in range(64):
        J[q, 64 + q] = 1.0
        J[64 + q, q] = -1.0
    sh["J128"] = J
    tau = np.arange(128) // 16
    sh["Mc"] = (tau[None, :] >= tau[:, None]).astype(f)
    Sel = np.zeros((128, 64, 128), f)
    SelI = np.zeros((128, 64, 128), f)
    for gl in range(8):
        for t in range(8):
            for j in range(16):
                Sel[gl * 16 + j, gl * 8 + t, t * 16 + j] = 1.0
                SelI[t * 16 + j, gl * 8 + t, gl * 16 + j] = 1.0
    sh["Sel"] = Sel
    sh["SelI"] = SelI
    wg = inp["w_gate"][0]
    pa = [tile_lhsT(inp[n][0]) for n in ("w_proj_a", "w_proj_b", "w_proj_c")]
    ga = [tile_lhsT(wg[:, i * 1024:(i + 1) * 1024]) for i in range(3)]
    sh["wA_l"] = np.ascontiguousarray(np.concatenate(pa + ga, axis=2))
    sh["wO_l"] = tile_lhsT(inp["w_out"][0])
    wu = tile_lhsT(inp["w_up"][0])
    sh["wU_l"] = np.ascontiguousarray(np.concatenate([wu[0:22], wu[22:44]], axis=2))
    sh["wD_l"] = tile_lhsT(inp["w_down"][0])
    sh["b_gate_l"] = np.ascontiguousarray(inp["b_gate"][0].reshape(3, 8, 128).transpose(2, 0, 1).reshape(128, 24))
    vec = lambda v: v.reshape(8, 128).T
    sh["ln_l"] = np.ascontiguousarray(np.concatenate([vec(inp["ln1_g"][0]), vec(inp["ln1_b"][0]),
                                                      vec(inp["ln2_g"][0]), vec(inp["ln2_b"][0])], axis=1))
    sh["conv_w_l"] = np.ascontiguousarray(inp["conv_w"][0].T.reshape(44, 128, 3).transpose(1, 0, 2))
    sh["conv_b_l"] = np.ascontiguousarray(inp["conv_b"][0].reshape(44, 128).T)
    return {k_: np.ascontiguousarray(v.astype(f)) for k_, v in sh.items()}


def prep_core(inp, b):
    return {
        "xT": np.ascontiguousarray(inp["x"][b].T),
        "memT": np.ascontiguousarray(inp["mem"][b].T),
        "pos": np.ascontiguousarray(inp["positions"][b].reshape(1, S).astype(np.int32)),
    }


def kernel(**inputs):
    inp = {k_: np.asarray(v) for k_, v in inputs.items()}
    shared = prep_shared(inp)
    B = inp["x"].shape[0]
    in_maps = []
    for b in range(B):
        m = dict(shared)
        m.update(prep_core(inp, b))
        in_maps.append(m)
    shapes = {k_: (v.shape, I32 if v.dtype == np.int32 else F32) for k_, v in in_maps[0].items()}
    nc = build_program(shapes)
    res = run_bass_kernel_spmd(nc, in_maps, core_ids=list(range(B)))
    out = np.stack([np.asarray(r["outT"]).T for r in res.results], axis=0)
    return np.ascontiguousarray(out.astype(np.float32))
```
